# Optimizing a Trainium2 kernel written in Bass

```python
import math
import jax, jax.numpy as jnp
from jax import lax
import numpy as np

D_MODEL = 1024
BATCH = 8
SEQ = 2048
DEPTH = 2
DEC_BATCH = 128
DEC_SEQ = 8
PAST_LEN = 16384
PAGE_SIZE = 128

D_MIX = D_MODEL
D_CONV = D_MIX // 4
CONV_W = 3
GDN_HEADS = 4
GDN_DK = D_MIX // 8
GDN_DV = D_MIX // 8
D_GDN = GDN_HEADS * GDN_DV
D_GDN_QKV = 2 * GDN_HEADS * GDN_DK + D_GDN
GDN_CONV_W = 4
ML_HEADS = 4
ML_DH = D_MIX // 16
D_ML = ML_HEADS * ML_DH
D_FF = (11 * D_MODEL) // 4
FFN_CONV_W = 3
CHUNK = 64
ALPHA = (2.0 * DEPTH) ** 0.25
BETA = (8.0 * DEPTH) ** -0.25
LN_EPS = 1e-5
NORM_EPS = 1e-6
NEG = -1e30
IN_SIZES = (D_CONV, D_CONV, D_CONV,
            GDN_HEADS * GDN_DK, GDN_HEADS * GDN_DK, D_GDN, D_GDN, GDN_HEADS, GDN_HEADS,
            D_ML, D_ML, D_ML, D_ML, ML_HEADS, ML_HEADS)
D_IN = sum(IN_SIZES)

kernel_name = 'hybrid_conv_gdn_mlstm_deepnorm_step'


def layer_norm(x, g, b):
    xf = x.astype(jnp.float32)
    mu = jnp.mean(xf, -1, keepdims=True)
    var = jnp.mean(jnp.square(xf - mu), -1, keepdims=True)
    return ((xf - mu) * lax.rsqrt(var + LN_EPS) * g.astype(jnp.float32) + b.astype(jnp.float32)).astype(x.dtype)


def l2norm(x):
    return x * lax.rsqrt(jnp.sum(x * x, -1, keepdims=True) + NORM_EPS)


def causal_dwconv(x, buf, w):
    width = w.shape[0]
    T = x.shape[1]
    xp = jnp.concatenate([buf.astype(x.dtype), x], axis=1)
    y = xp[:, 0:T] * w[0]
    for j in range(1, width):
        y = y + xp[:, j:j + T] * w[j]
    return y, xp[:, -(width - 1):]


def pad_time(a, pad, value=0.0):
    return jnp.pad(a, [(0, 0), (0, pad)] + [(0, 0)] * (a.ndim - 2), constant_values=value)


def to_chunks(a, L):
    Bn, T = a.shape[:2]
    a = a.reshape((Bn, T // L, L) + a.shape[2:])
    return jnp.moveaxis(jnp.moveaxis(a, 3, 2), 1, 0)


def from_chunks(o, T):
    n, Bn, H, L, D = o.shape
    return jnp.transpose(o, (1, 0, 3, 2, 4)).reshape(Bn, n * L, H, D)[:, :T]


def gated_delta_chunked(q, k, v, g, beta, s0):
    T = q.shape[1]
    DV = v.shape[-1]
    L = min(CHUNK, T)
    n = -(-T // L)
    pad = n * L - T
    qc, kc, vc, gc, bc = [to_chunks(pad_time(a, pad), L) for a in (q, k, v, g, beta)]
    G = jnp.cumsum(gc, axis=-1)
    causal = jnp.tril(jnp.ones((L, L), bool))
    strict = jnp.tril(jnp.ones((L, L), bool), -1)
    decay = jnp.exp(jnp.where(causal, G[..., :, None] - G[..., None, :], -jnp.inf))
    kb = kc * bc[..., None]
    kk = jnp.einsum('nbhik,nbhjk->nbhij', kb, kc) * decay
    M = jnp.eye(L, dtype=kk.dtype) + jnp.where(strict, kk, 0.0)
    rhs = jnp.concatenate([vc * bc[..., None], kb * jnp.exp(G)[..., None]], axis=-1)
    sol = lax.linalg.triangular_solve(M, rhs, left_side=True, lower=True, unit_diagonal=True)
    w_v, w_k = sol[..., :DV], sol[..., DV:]
    qk = jnp.einsum('nbhik,nbhjk->nbhij', qc, kc) * decay

    def step(S, inp):
        wv_i, wk_i, q_i, k_i, qk_i, G_i = inp
        u = wv_i - jnp.einsum('bhik,bhkv->bhiv', wk_i, S)
        o = (jnp.einsum('bhik,bhkv->bhiv', q_i * jnp.exp(G_i)[..., None], S)
             + jnp.einsum('bhij,bhjv->bhiv', qk_i, u))
        g_last = G_i[..., -1]
        S = (S * jnp.exp(g_last)[..., None, None]
             + jnp.einsum('bhik,bhiv->bhkv', k_i * jnp.exp(g_last[..., None] - G_i)[..., None], u))
        return S, o

    S, o = lax.scan(step, s0, (w_v, w_k, qc, kc, qk, G))
    return from_chunks(o, T), S


def mlstm_chunked(q, k, v, ig, lf, c0, n0, m0):
    T = q.shape[1]
    L = min(CHUNK, T)
    n = -(-T // L)
    pad = n * L - T
    qc, kc, vc, fc = [to_chunks(pad_time(a, pad), L) for a in (q, k, v, lf)]
    ic = to_chunks(pad_time(ig, pad, NEG), L)
    F = jnp.cumsum(fc, axis=-1)
    causal = jnp.tril(jnp.ones((L, L), bool))

    def step(carry, inp):
        C, nv, m = carry
        q_i, k_i, v_i, ig_i, F_i = inp
        D = jnp.where(causal, F_i[..., :, None] - F_i[..., None, :] + ig_i[..., None, :], -jnp.inf)
        inter = F_i + m[..., None]
        m_i = jnp.maximum(inter, jnp.max(D, -1))
        wD = jnp.exp(D - m_i[..., None])
        wI = jnp.exp(inter - m_i)
        s = jnp.einsum('bhid,bhjd->bhij', q_i, k_i) * wD
        num = wI[..., None] * jnp.einsum('bhid,bhde->bhie', q_i, C) + jnp.einsum('bhij,bhje->bhie', s, v_i)
        qn = wI * jnp.einsum('bhid,bhd->bhi', q_i, nv) + jnp.sum(s, -1)
        h = num / jnp.maximum(jnp.abs(qn), jnp.exp(-m_i))[..., None]
        m_new = m_i[..., -1]
        wk = jnp.exp(F_i[..., -1:] - F_i + ig_i - m_new[..., None])
        dc = jnp.exp(F_i[..., -1] + m - m_new)
        C = dc[..., None, None] * C + jnp.einsum('bhjd,bhje->bhde', k_i * wk[..., None], v_i)
        nv = dc[..., None] * nv + jnp.einsum('bhjd,bhj->bhd', k_i, wk)
        return (C, nv, m_new), h

    (C, nv, m), h = lax.scan(step, (c0, n0, m0), (qc, kc, vc, ic, F))
    return from_chunks(h, T), C, nv, m


def layer_forward(x, conv_buf, gdn_buf, gdn_s, ml_c, ml_n, ml_m, ffn_buf,
                  w_in, conv_w, gdn_conv_w, gdn_a_log, gdn_dt_bias, gdn_norm_w,
                  ml_i_bias, ml_f_bias, ml_norm_w, w_o, ln1_g, ln1_b,
                  w_up, ffn_conv_w, w_down, ln2_g, ln2_b):
    dt = x.dtype
    f32 = jnp.float32
    Bn, T, _ = x.shape
    proj = jnp.einsum('btd,de->bte', x, w_in)
    idx = np.cumsum(IN_SIZES)[:-1].tolist()
    (a_b, a_c, a_h, g_q, g_k, g_v, g_z, g_a, g_b,
     m_q, m_k, m_v, m_o, m_i, m_f) = jnp.split(proj, idx, axis=-1)
    a_conv, conv_buf_new = causal_dwconv(a_c * a_h, conv_buf, conv_w)
    y_a = (a_b * a_conv).astype(dt)
    qkv, gdn_buf_new = causal_dwconv(jnp.concatenate([g_q, g_k, g_v], -1), gdn_buf, gdn_conv_w)
    qkv = jax.nn.silu(qkv.astype(f32))
    q, k, v = jnp.split(qkv, [GDN_HEADS * GDN_DK, 2 * GDN_HEADS * GDN_DK], axis=-1)
    q = l2norm(q.reshape(Bn, T, GDN_HEADS, GDN_DK)) * (GDN_DK ** -0.5)
    k = l2norm(k.reshape(Bn, T, GDN_HEADS, GDN_DK))
    v = v.reshape(Bn, T, GDN_HEADS, GDN_DV)
    beta = jax.nn.sigmoid(g_b.astype(f32))
    g = -jnp.exp(gdn_a_log.astype(f32)) * jax.nn.softplus(g_a.astype(f32) + gdn_dt_bias.astype(f32))
    o, gdn_s_new = gated_delta_chunked(q, k, v, g, beta, gdn_s.astype(f32))
    o = o * lax.rsqrt(jnp.mean(o * o, -1, keepdims=True) + NORM_EPS) * gdn_norm_w.astype(f32)
    o = o * jax.nn.silu(g_z.astype(f32).reshape(Bn, T, GDN_HEADS, GDN_DV))
    y_b = o.reshape(Bn, T, D_GDN).astype(dt)
    mq = m_q.astype(f32).reshape(Bn, T, ML_HEADS, ML_DH)
    mk = m_k.astype(f32).reshape(Bn, T, ML_HEADS, ML_DH) * (ML_DH ** -0.5)
    mv = m_v.astype(f32).reshape(Bn, T, ML_HEADS, ML_DH)
    ig = m_i.astype(f32) + ml_i_bias.astype(f32)
    lf = jax.nn.log_sigmoid(m_f.astype(f32) + ml_f_bias.astype(f32))
    h, ml_c_new, ml_n_new, ml_m_new = mlstm_chunked(mq, mk, mv, ig, lf, ml_c.astype(f32),
                                                     ml_n.astype(f32), ml_m.astype(f32))
    h = jax.nn.sigmoid(m_o.astype(f32)).reshape(Bn, T, ML_HEADS, ML_DH) * h
    mu = jnp.mean(h, -1, keepdims=True)
    var = jnp.mean(jnp.square(h - mu), -1, keepdims=True)
    h = (h - mu) * lax.rsqrt(var + LN_EPS) * ml_norm_w.astype(f32).reshape(ML_HEADS, ML_DH)
    y_c = h.reshape(Bn, T, D_ML).astype(dt)
    mix = jnp.einsum('bte,ed->btd', jnp.concatenate([y_a, y_b, y_c], -1), w_o)
    x = layer_norm(ALPHA * x + mix, ln1_g, ln1_b)
    up = jnp.einsum('btd,df->btf', x, w_up)
    f_gate, f_val = jnp.split(up, 2, axis=-1)
    f_gate, ffn_buf_new = causal_dwconv(f_gate, ffn_buf, ffn_conv_w)
    ffn = jnp.einsum('btf,fd->btd', jax.nn.silu(f_gate) * f_val, w_down)
    x = layer_norm(ALPHA * x + ffn, ln2_g, ln2_b)
    return (x, conv_buf_new.astype(dt), gdn_buf_new.astype(dt), gdn_s_new.astype(dt),
            ml_c_new.astype(dt), ml_n_new.astype(dt), ml_m_new.astype(dt), ffn_buf_new.astype(dt))


def run_trunk(x, conv_buf, gdn_buf, gdn_s, ml_c, ml_n, ml_m, ffn_buf, weights):
    new = []
    for l in range(DEPTH):
        x, *st = layer_forward(x, conv_buf[l], gdn_buf[l], gdn_s[l], ml_c[l], ml_n[l], ml_m[l], ffn_buf[l],
                               *[w[l] for w in weights])
        new.append(st)
    return x, [jnp.stack([s[i] for s in new]) for i in range(len(new[0]))]


def setup_inputs(seed: int = 0) -> dict:
    key = jax.random.key(seed)
    ks = jax.random.split(key, 32)
    f32 = jnp.float32

    def nrm(k, shape, s):
        return jax.random.normal(k, shape, f32) * s

    dtv = jnp.exp(jax.random.uniform(ks[13], (DEPTH, GDN_HEADS), f32, math.log(1e-3), math.log(1e-1)))
    return {
        'x_prompt': nrm(ks[0], (BATCH, SEQ, D_MODEL), 1.0),
        'x_sample': nrm(ks[1], (DEC_BATCH, DEC_SEQ, D_MODEL), 1.0),
        'state_conv_mix': nrm(ks[2], (DEPTH, DEC_BATCH, CONV_W - 1, D_CONV), 1.0),
        'state_gdn_conv': nrm(ks[3], (DEPTH, DEC_BATCH, GDN_CONV_W - 1, D_GDN_QKV), 1.0),
        'state_gdn': nrm(ks[4], (DEPTH, DEC_BATCH, GDN_HEADS, GDN_DK, GDN_DV), 0.1),
        'state_mlstm_c': nrm(ks[5], (DEPTH, DEC_BATCH, ML_HEADS, ML_DH, ML_DH), 0.1),
        'state_mlstm_n': nrm(ks[6], (DEPTH, DEC_BATCH, ML_HEADS, ML_DH), 0.1),
        'state_mlstm_m': jax.random.uniform(ks[7], (DEPTH, DEC_BATCH, ML_HEADS), f32, 0.0, 4.0),
        'state_ffn_conv': nrm(ks[8], (DEPTH, DEC_BATCH, FFN_CONV_W - 1, D_FF), 1.0),
        'w_in': nrm(ks[9], (DEPTH, D_MODEL, D_IN), D_MODEL ** -0.5),
        'conv_w': nrm(ks[10], (DEPTH, CONV_W, D_CONV), CONV_W ** -0.5),
        'gdn_conv_w': nrm(ks[11], (DEPTH, GDN_CONV_W, D_GDN_QKV), GDN_CONV_W ** -0.5),
        'gdn_a_log': jnp.log(jax.random.uniform(ks[12], (DEPTH, GDN_HEADS), f32, 1.0, 16.0)),
        'gdn_dt_bias': dtv + jnp.log(-jnp.expm1(-dtv)),
        'gdn_norm_w': 1.0 + nrm(ks[14], (DEPTH, GDN_DV), 0.02),
        'ml_i_bias': nrm(ks[15], (DEPTH, ML_HEADS), 0.1),
        'ml_f_bias': jax.random.uniform(ks[16], (DEPTH, ML_HEADS), f32, 3.0, 6.0),
        'ml_norm_w': 1.0 + nrm(ks[17], (DEPTH, D_ML), 0.02),
        'w_o': nrm(ks[18], (DEPTH, D_MIX, D_MODEL), (D_MIX ** -0.5) * BETA),
        'ln1_g': 1.0 + nrm(ks[19], (DEPTH, D_MODEL), 0.02),
        'ln1_b': nrm(ks[20], (DEPTH, D_MODEL), 0.02),
        'w_up': nrm(ks[21], (DEPTH, D_MODEL, 2 * D_FF), D_MODEL ** -0.5),
        'ffn_conv_w': nrm(ks[22], (DEPTH, FFN_CONV_W, D_FF), FFN_CONV_W ** -0.5),
        'w_down': nrm(ks[23], (DEPTH, D_FF, D_MODEL), (D_FF ** -0.5) * BETA),
        'ln2_g': 1.0 + nrm(ks[24], (DEPTH, D_MODEL), 0.02),
        'ln2_b': nrm(ks[25], (DEPTH, D_MODEL), 0.02),
    }


def reference(x_prompt, x_sample, state_conv_mix, state_gdn_conv, state_gdn, state_mlstm_c,
              state_mlstm_n, state_mlstm_m, state_ffn_conv,
              w_in, conv_w, gdn_conv_w, gdn_a_log, gdn_dt_bias, gdn_norm_w,
              ml_i_bias, ml_f_bias, ml_norm_w, w_o, ln1_g, ln1_b,
              w_up, ffn_conv_w, w_down, ln2_g, ln2_b):
    weights = (w_in, conv_w, gdn_conv_w, gdn_a_log, gdn_dt_bias, gdn_norm_w,
               ml_i_bias, ml_f_bias, ml_norm_w, w_o, ln1_g, ln1_b,
               w_up, ffn_conv_w, w_down, ln2_g, ln2_b)
    Bp = x_prompt.shape[0]
    dtp = x_prompt.dtype
    y_prompt, (p_conv, p_gconv, p_gdn, p_c, p_n, p_m, p_ffn) = run_trunk(
        x_prompt,
        jnp.zeros((DEPTH, Bp, CONV_W - 1, D_CONV), dtp),
        jnp.zeros((DEPTH, Bp, GDN_CONV_W - 1, D_GDN_QKV), dtp),
        jnp.zeros((DEPTH, Bp, GDN_HEADS, GDN_DK, GDN_DV), dtp),
        jnp.zeros((DEPTH, Bp, ML_HEADS, ML_DH, ML_DH), dtp),
        jnp.zeros((DEPTH, Bp, ML_HEADS, ML_DH), dtp),
        jnp.zeros((DEPTH, Bp, ML_HEADS), dtp),
        jnp.zeros((DEPTH, Bp, FFN_CONV_W - 1, D_FF), dtp),
        weights)
    y_sample, (s_conv, s_gconv, s_gdn, s_c, s_n, s_m, s_ffn) = run_trunk(
        x_sample, state_conv_mix, state_gdn_conv, state_gdn, state_mlstm_c, state_mlstm_n,
        state_mlstm_m, state_ffn_conv, weights)
    return (y_prompt, y_sample, p_conv, p_gconv, p_gdn, p_c, p_n, p_m, p_ffn,
            s_conv, s_gconv, s_gdn, s_c, s_n, s_m, s_ffn)
```

```python
import numpy as np
import concourse.bass as bass
import concourse.mybir as mybir

F32 = mybir.dt.float32
BF16 = mybir.dt.bfloat16
AF = mybir.ActivationFunctionType
ALU = mybir.AluOpType
AX = mybir.AxisListType
ESZ = {F32: 4, BF16: 2}
BUCK = 512


def box(ap):
    t = ap.tensor
    es = ESZ[ap.dtype]
    dims = [(int(s), int(c)) for s, c in ap.ap]
    off = int(ap.offset)
    cls = type(t).__name__
    if cls.startswith("DRam"):
        ext = sum((c - 1) * abs(s) for s, c in dims) + 1
        return (t.name, 0, 1, off * es, (off + ext) * es)
    rowlen = 1
    for s in list(t.shape)[1:]:
        rowlen *= int(s)
    p0 = off // rowlen
    f0 = off % rowlen
    if dims[0][0] == rowlen or dims[0][1] == 1:
        pc = dims[0][1]
        rest = dims[1:]
    elif dims[0][0] == 0:
        pc = 1
        rest = dims[1:]
    else:
        pc = 1
        rest = dims
    ext = sum((c - 1) * abs(s) for s, c in rest) + 1
    if cls.startswith("PSum") or cls.startswith("Psum") or cls.startswith("PS"):
        b0 = (f0 * es) // 2048 * 2048
        b1 = ((f0 + ext) * es + 2047) // 2048 * 2048
        return (t.name, 0, 128, b0, b1)
    return (t.name, p0, p0 + pc, f0 * es, (f0 + ext) * es)


class Sched:
    def __init__(self, nc, n_sp_sems=16, n_pool_sems=8):
        self.nc = nc
        self.ops = []
        self.engs = {"pe": nc.tensor, "dve": nc.vector, "act": nc.scalar,
                     "pool": nc.gpsimd, "sp": nc.sync}
        self.esem = {e: nc.semaphore("sem_" + e).__enter__() for e in self.engs}
        self.dsems = {"sp": [nc.semaphore("dsp%d" % i).__enter__() for i in range(n_sp_sems)],
                      "pool": [nc.semaphore("dpl%d" % i).__enter__() for i in range(n_pool_sems)],
                      "act": []}

    limit = None

    def add(self, eng, fn, r, w, dma=False):
        if self.limit is not None and len(self.ops) >= self.limit and not getattr(self, 'closing', False):
            raise StopIteration("limit")
        rb = [box(a) for a in r]
        wb = [box(a) for a in w]
        wb = wb + [b for b in rb if b[0] == 'psum' or b[0] == 'pa']
        self.ops.append((eng, fn, rb, wb, dma))

    def finalize(self):
        ops = self.ops
        n = len(ops)
        recs = {}
        deps = [None] * n
        pos = [0] * n
        cnt = {e: 0 for e in self.engs}
        dma_n = {q: 0 for q in self.dsems}
        dma_hist = {q: [] for q in self.dsems}
        dsem = [None] * n
        for i, (eng, fn, R, W, dma) in enumerate(ops):
            pos[i] = cnt[eng]
            cnt[eng] += 1
            d = set()
            for (nm, p0, p1, b0, b1) in R:
                for bk in range(b0 // BUCK, (b1 - 1) // BUCK + 1):
                    for rec in recs.get((nm, bk), ()):
                        if rec[5] and rec[0] < p1 and p0 < rec[1] and rec[2] < b1 and b0 < rec[3]:
                            d.add(rec[4])
            for (nm, p0, p1, b0, b1) in W:
                for bk in range(b0 // BUCK, (b1 - 1) // BUCK + 1):
                    for rec in recs.get((nm, bk), ()):
                        if rec[0] < p1 and p0 < rec[1] and rec[2] < b1 and b0 < rec[3]:
                            d.add(rec[4])
            for (nm, p0, p1, b0, b1) in W:
                for bk in range(b0 // BUCK, (b1 - 1) // BUCK + 1):
                    L = recs.setdefault((nm, bk), [])
                    lo = max(b0, bk * BUCK)
                    hi = min(b1, (bk + 1) * BUCK)
                    L[:] = [rec for rec in L if not (p0 <= rec[0] and rec[1] <= p1 and
                                                     lo <= max(rec[2], bk * BUCK) and
                                                     min(rec[3], (bk + 1) * BUCK) <= hi)]
                    L.append((p0, p1, b0, b1, i, True))
            for (nm, p0, p1, b0, b1) in R:
                for bk in range(b0 // BUCK, (b1 - 1) // BUCK + 1):
                    L = recs.setdefault((nm, bk), [])
                    if not dma:
                        lo = max(b0, bk * BUCK)
                        hi = min(b1, (bk + 1) * BUCK)
                        L[:] = [rec for rec in L if rec[5] or ops[rec[4]][4] or ops[rec[4]][0] != eng or not (
                            p0 <= rec[0] and rec[1] <= p1 and lo <= max(rec[2], bk * BUCK) and
                            min(rec[3], (bk + 1) * BUCK) <= hi)]
                    L.append((p0, p1, b0, b1, i, False))
            if dma:
                q = eng
                k = dma_n[q]
                P = len(self.dsems[q])
                dsem[i] = (q, k % P, 16 * (k // P + 1))
                if k >= P:
                    d.add(dma_hist[q][k - P])
                dma_hist[q].append(i)
                dma_n[q] += 1
            d.discard(i)
            deps[i] = d
        known = {e: {} for e in self.engs}
        snap = [None] * n
        sig = [False] * n
        waits = [None] * n
        for i, (eng, fn, R, W, dma) in enumerate(ops):
            kn = known[eng]
            need = {}
            for d in deps[i]:
                de = ops[d][0]
                ddma = ops[d][4]
                if ddma:
                    q, si, val = dsem[d]
                    key = ("D", q, si)
                    if kn.get(key, 0) >= val:
                        continue
                    if key not in need or dsem[need[key]][2] < val:
                        need[key] = d
                else:
                    if de == "pe" and eng == "pe" and not dma:
                        continue
                    if kn.get(de, -1) >= pos[d]:
                        continue
                    if de not in need or pos[need[de]] < pos[d]:
                        need[de] = d
            wl = []
            for key, d in need.items():
                if ops[d][4]:
                    if kn.get(key, 0) >= dsem[d][2]:
                        continue
                else:
                    if kn.get(key, -1) >= pos[d]:
                        continue
                wl.append(d)
                sig[d] = True
                for k2, v2 in snap[d].items():
                    if kn.get(k2, -1) < v2:
                        kn[k2] = v2
            waits[i] = wl
            s = dict(kn)
            if dma:
                q, si, val = dsem[i]
                s[("D", q, si)] = val
            else:
                s[eng] = pos[i]
            snap[i] = s
        cum = [0] * n
        c = {e: 0 for e in self.engs}
        for i, (eng, fn, R, W, dma) in enumerate(ops):
            if not dma and sig[i]:
                c[eng] += 1
            cum[i] = c[eng]
        nw = 0
        for i, (eng, fn, R, W, dma) in enumerate(ops):
            E = self.engs[eng]
            for d in waits[i]:
                if ops[d][4]:
                    q, si, val = dsem[d]
                    E.wait_ge(self.dsems[q][si], val)
                else:
                    E.wait_ge(self.esem[ops[d][0]], cum[d])
                nw += 1
            ins = fn()
            if ins is None:
                continue
            if dma:
                q, si, val = dsem[i]
                ins.then_inc(self.dsems[q][si], 16)
            elif sig[i]:
                ins.then_inc(self.esem[eng], 1)
        self.dbginfo = (waits, sig, cum, pos, dsem)
        self.stats = dict(n_ops=n, n_waits=nw, per_eng=cnt, sigs=c)
        return self.stats


class K:
    def __init__(self, nc, S):
        self.nc = nc
        self.S = S

    def mm(self, out, lhsT, rhs, start=True, stop=True):
        nc = self.nc
        self.S.add("pe", lambda: nc.tensor.matmul(out, lhsT, rhs, start=start, stop=stop),
                   [lhsT, rhs], [out])

    def tr(self, out, in_, ident):
        nc = self.nc
        self.S.add("pe", lambda: nc.tensor.transpose(out, in_, ident), [in_, ident], [out])

    def act(self, out, in_, func, bias=None, scale=None, accum=None, eng="act"):
        nc = self.nc
        kw = {}
        r = [in_]
        if bias is not None:
            kw["bias"] = bias
            if not isinstance(bias, (int, float)):
                r.append(bias)
        if scale is not None:
            kw["scale"] = scale
            if not isinstance(scale, (int, float)):
                r.append(scale)
        w = [out]
        if accum is not None:
            kw["accum_out"] = accum
            w.append(accum)
        self.S.add("act", lambda: nc.scalar.activation(out, in_, func, **kw), r, w)

    def tt(self, out, a, b, op, eng="dve"):
        E = self.S.engs[eng]
        self.S.add(eng, lambda: E.tensor_tensor(out, a, b, op), [a, b], [out])

    def ts(self, out, a, s1, op0, s2=None, op1=None, eng="dve", accum=None):
        E = self.S.engs[eng]
        r = [a]
        if not isinstance(s1, (int, float)):
            r.append(s1)
        if s2 is not None and not isinstance(s2, (int, float)):
            r.append(s2)
        w = [out]
        kw = {}
        if accum is not None:
            kw["accum_out"] = accum
            w.append(accum)
        if op1 is None:
            self.S.add(eng, lambda: E.tensor_scalar(out, a, s1, None, op0, **kw), r, w)
        else:
            self.S.add(eng, lambda: E.tensor_scalar(out, a, s1, s2, op0, op1, **kw), r, w)

    def stt(self, out, in0, scalar, in1, op0, op1):
        nc = self.nc
        r = [in0, in1]
        if not isinstance(scalar, (int, float)):
            r.append(scalar)
        self.S.add("dve", lambda: nc.vector.scalar_tensor_tensor(out, in0, scalar, in1, op0, op1), r, [out])

    def cp(self, out, in_, eng="act"):
        nc = self.nc
        if eng == "act":
            self.S.add("act", lambda: nc.scalar.copy(out, in_), [in_], [out])
        else:
            E = self.S.engs[eng]
            self.S.add(eng, lambda: E.tensor_copy(out, in_), [in_], [out])

    def memset(self, ap, v, eng="dve"):
        E = self.S.engs[eng]
        self.S.add(eng, lambda: E.memset(ap, v), [], [ap])

    def scan(self, out, d0, d1, init, op0, op1):
        nc = self.nc
        r = [d0, d1]
        if not isinstance(init, (int, float)):
            r.append(init)
        self.S.add("dve", lambda: nc.vector.tensor_tensor_scan(out, d0, d1, init, op0, op1), r, [out])

    def bn_stats(self, out, in_):
        nc = self.nc
        self.S.add("dve", lambda: nc.vector.bn_stats(out, in_), [in_], [out])

    def bn_aggr(self, out, in_):
        nc = self.nc
        self.S.add("dve", lambda: nc.vector.bn_aggr(out, in_), [in_], [out])

    def recip(self, out, in_):
        nc = self.nc
        self.S.add("dve", lambda: nc.vector.reciprocal(out, in_), [in_], [out])

    def dma(self, out, in_, q="sp", **kw):
        E = self.S.engs[q]
        self.S.add(q, lambda: E.dma_start(out=out, in_=in_, **kw), [in_], [out], dma=True)

    def fence(self, aps, q="sp"):
        self.S.add(q, lambda: None, list(aps), [])

from concourse.bass_utils import run_bass_kernel_spmd
import math

D = 1024
DIN = 3856
DFF = 2816
NEG = -1e30
LN_EPS = 1e-5
NORM_EPS = 1e-6
ALPHA = 4.0 ** 0.25
NTOK = 2176
NSP = 24
import os
CPENG = os.environ.get('CPENG', 'act')
BLOCKS = [(0, 512, False), (512, 512, False), (1024, 512, False), (1536, 512, False), (2048, 128, True)]

C_ID, C_MAIP, C_MASP, C_MAIS, C_MASS, C_SEL, C_CMP, C_CMS, C_ROWM, C_EH, C_ONE = \
    0, 128, 256, 384, 512, 640, 1152, 1280, 1408, 1424, 1440
NCST = 1448


def make_consts():
    c = np.zeros((128, NCST), np.float32)
    ii = np.arange(128)
    c[:, C_ID:C_ID + 128] = np.eye(128)
    J, I = np.meshgrid(ii, ii, indexing="ij")
    same = (J // 8) == (I // 8)
    c[:, C_MAIP:C_MAIP + 128] = np.where(I >= J, 0, NEG)
    c[:, C_MASP:C_MASP + 128] = np.where(I > J, 0, NEG)
    c[:, C_MAIS:C_MAIS + 128] = np.where((I >= J) & same, 0, NEG)
    c[:, C_MASS:C_MASS + 128] = np.where((I > J) & same, 0, NEG)
    for h in range(4):
        c[h, C_SEL + h * 128:C_SEL + (h + 1) * 128] = 1.0
        c[:, C_EH + h * 4 + h] = 1.0
    c[0:4, C_CMP:C_CMP + 128] = 1.0
    c[0:4, C_CMP] = 0.0
    c[0:4, C_CMS:C_CMS + 128] = 1.0
    c[0:4, C_CMS:C_CMS + 128:8] = 0.0
    for s in range(16):
        c[s * 8:s * 8 + 8, C_ROWM + s] = 1.0
    c[:, C_ONE] = 1.0
    return c


def build(dbg=None, nblocks=5, nlayers=2, stage=9, limit=None, bsel=None):
    dbg = dbg or {}
    nc = bass.Bass('TRN2', target_bir_lowering=False)
    S = Sched(nc, n_sp_sems=NSP, n_pool_sems=64)
    k = K(nc, S)

    def din(name, shape):
        return nc.dram_tensor(name, list(shape), F32, kind="ExternalInput").ap()

    def dout(name, shape):
        return nc.dram_tensor(name, list(shape), F32, kind="ExternalOutput").ap()

    x_in = din("x_in", [NTOK, D])
    st_conv = din("st_conv", [2, 32, 256])
    st_gconv = din("st_gconv", [2, 48, 1536])
    st_gdn = din("st_gdn", [2, 16, 4, 128, 128])
    st_c = din("st_c", [2, 16, 4, 64, 64])
    st_n = din("st_n", [2, 16, 4, 64])
    st_m = din("st_m", [2, 16, 4])
    st_ffn = din("st_ffn", [2, 32, DFF])
    w_in = din("w_in", [2, D, DIN])
    conv_w = din("conv_w", [2, 3, 256])
    gdn_conv_w = din("gdn_conv_w", [2, 4, 1536])
    gdn_a_log = din("gdn_a_log", [2, 4])
    gdn_dt_bias = din("gdn_dt_bias", [2, 4])
    gdn_norm_w = din("gdn_norm_w", [2, 128])
    ml_i_bias = din("ml_i_bias", [2, 4])
    ml_f_bias = din("ml_f_bias", [2, 4])
    ml_norm_w = din("ml_norm_w", [2, 256])
    w_o = din("w_o", [2, D, D])
    ln1_g = din("ln1_g", [2, D])
    ln1_b = din("ln1_b", [2, D])
    w_up = din("w_up", [2, D, 2 * DFF])
    ffn_conv_w = din("ffn_conv_w", [2, 3, DFF])
    w_down = din("w_down", [2, DFF, D])
    ln2_g = din("ln2_g", [2, D])
    ln2_b = din("ln2_b", [2, D])
    consts = din("consts", [128, NCST])

    y = dout("y", [NTOK, D])
    p_conv = dout("p_conv", [2, 2, 256])
    p_gconv = dout("p_gconv", [2, 3, 1536])
    p_gdn = dout("p_gdn", [2, 4, 128, 128])
    p_c = dout("p_c", [2, 4, 64, 64])
    p_n = dout("p_n", [2, 4, 64])
    p_m = dout("p_m", [2, 4])
    p_ffn = dout("p_ffn", [2, 2, DFF])
    s_conv = dout("s_conv", [2, 32, 256])
    s_gconv = dout("s_gconv", [2, 48, 1536])
    s_gdn = dout("s_gdn", [2, 16, 4, 128, 128])
    s_c = dout("s_c", [2, 16, 4, 64, 64])
    s_n = dout("s_n", [2, 16, 4, 64])
    s_m = dout("s_m", [2, 16, 4])
    s_ffn = dout("s_ffn", [2, 32, DFF])
    outs_all = [y, p_conv, p_gconv, p_gdn, p_c, p_n, p_m, p_ffn, s_conv, s_gconv, s_gdn, s_c, s_n, s_m, s_ffn]
    dbg_out = {}
    for nm, shp in dbg.items():
        dbg_out[nm] = dout("dbg_" + nm, shp)
        outs_all.append(dbg_out[nm])

    def dump(nm, ap):
        if nm in dbg_out:
            k.dma(dbg_out[nm], ap)

    wb_in = nc.dram_tensor("wb_in", [2, D, DIN], BF16, kind="Internal").ap()
    wb_o = nc.dram_tensor("wb_o", [2, D, D], BF16, kind="Internal").ap()
    wb_up = nc.dram_tensor("wb_up", [2, D, 2 * DFF], BF16, kind="Internal").ap()
    wb_down = nc.dram_tensor("wb_down", [2, DFF, D], BF16, kind="Internal").ap()

    NCOL = 44000
    A = nc.sbuf_tensor("arena", [128, NCOL], F32).__enter__()
    PS = nc.psum_tensor("psum", [128, 4096], F32).__enter__()
    cur = [0]

    def al(n):
        o = cur[0]
        cur[0] += n
        assert cur[0] <= NCOL, cur[0]
        return o

    def V(o, n, p0=0, p1=128):
        return A[p0:p1, o:o + n]

    def Vb(o, n, p0=0, p1=128):
        return A[p0:p1, o:o + n].bitcast(BF16)

    for l in range(nlayers if os.environ.get('NOCAST') is None else 0):
        for c0 in range(0, DIN, 482):
            k.dma(wb_in[l][:, c0:c0 + 482], w_in[l][:, c0:c0 + 482], q="pool")
        for c0 in range(0, D, 512):
            k.dma(wb_o[l][:, c0:c0 + 512], w_o[l][:, c0:c0 + 512], q="pool")
        for c0 in range(0, 2 * DFF, 512):
            k.dma(wb_up[l][:, c0:c0 + 512], w_up[l][:, c0:c0 + 512], q="pool")
        for r0 in range(0, DFF, 256):
            k.dma(wb_down[l][r0:r0 + 256, :], w_down[l][r0:r0 + 256, :], q="pool")

    if stage == -1:
        k.fence([wb_in, wb_o, wb_up, wb_down])
        print(S.finalize())
        return nc
    o_cst = al(NCST)
    k.dma(V(o_cst, NCST), consts)
    ident = V(o_cst + C_ID, 128)

    def sel(h):
        return V(o_cst + C_SEL + h * 128, 128, 0, 4)

    def eh(h):
        return V(o_cst + C_EH + h * 4, 4)

    rowmask = V(o_cst + C_ROWM, 16)

    dctr = [0]

    def psd():
        b = dctr[0] % 4
        dctr[0] += 1
        return PS[:, b * 512:(b + 1) * 512]

    qctr = [0]

    def psq():
        q = qctr[0] % 4
        qctr[0] += 1
        return PS[:, 2048 + q * 512: 2048 + q * 512 + 128]

    o_cwa = al(2 * 2 * 3)
    o_cwg = al(2 * 12 * 4)
    o_cwf = al(2 * 22 * 3)
    o_lnf = al(2 * 4 * 8)
    o_gnw = al(2)
    o_gb = al(2 * 8)
    o_mnw = al(2 * 256)
    o_S = al(2 * 2 * 512)
    o_C = al(2 * 2 * 130)
    o_ha = al(2 * 2 * 2)
    o_hg = al(2 * 12 * 3)
    o_hf = al(2 * 22 * 2)
    o_mp = al(2)
    o_st_end = cur[0]
    o_xtok = al(4096)
    o_xT = al(2048)
    o_qkv = al(6144)
    o_ab = al(1024)
    o_sz = al(1024)
    o_mqk = al(2048)
    o_mvo = al(4 * 516)
    o_yc = al(2048)
    o_xe = al(2 * 516)
    o_acc = al(2 * 512)
    o_wg = al(64)
    o_graw = al(4 * 128)
    o_rows = al(26 * 128)
    o_cols = al(128)
    o_scan = al(6144)
    o_wbuf = al(5 * 1024)
    o_misc = al(640)
    o_lnb = o_scan + 2048
    print("arena cols used", cur[0])
    o_ptmp = o_scan
    o_sst = o_xtok + 1024
    o_sso = o_xtok + 1024 + 704
    o_S0 = o_qkv + 1536
    o_msk = o_S0 + 2048
    o_kdm = o_msk + 2176

    def cwa(l, u):
        return V(o_cwa + (l * 2 + u) * 3, 3)

    def cwg(l, u):
        return V(o_cwg + (l * 12 + u) * 4, 4)

    def cwf(l, u):
        return V(o_cwf + (l * 22 + u) * 3, 3)

    def lnf(l, which, kk):
        return V(o_lnf + (l * 4 + which) * 8 + kk, 1)

    def gbv(l, i):
        return V(o_gb + l * 8 + i, 1, 0, 4)

    ptmp = V(o_ptmp, DFF)
    k.memset(ptmp, 0.0)
    for l in range(nlayers):
        for (src, W_, nu, fn) in ((conv_w, 3, 2, cwa), (gdn_conv_w, 4, 12, cwg), (ffn_conv_w, 3, 22, cwf)):
            C_ = nu * 128
            k.dma(ptmp[0:W_, 0:C_], src[l])
            for u0 in range(0, nu, 4):
                ps = psd()
                n4 = min(4, nu - u0)
                for u in range(u0, u0 + n4):
                    k.tr(ps[:, (u - u0) * 128:(u - u0 + 1) * 128], ptmp[:, u * 128:(u + 1) * 128], ident)
                for u in range(u0, u0 + n4):
                    k.cp(fn(l, u), ps[:, (u - u0) * 128:(u - u0) * 128 + W_])
        for wi, src in enumerate((ln1_g, ln1_b, ln2_g, ln2_b)):
            k.dma(ptmp[0:8, 0:128], src[l].rearrange("(k p) -> k p", p=128))
            ps = psd()
            k.tr(ps[:, 0:128], ptmp[:, 0:128], ident)
            k.cp(V(o_lnf + (l * 4 + wi) * 8, 8), ps[:, 0:8])
        k.dma(V(o_gnw + l, 1), gdn_norm_w[l].rearrange("(p o) -> p o", o=1))
        for i, src in enumerate((gdn_a_log, gdn_dt_bias, ml_i_bias, ml_f_bias)):
            k.dma(gbv(l, i), src[l].rearrange("(p o) -> p o", o=1))
        k.act(gbv(l, 4), gbv(l, 0), AF.Exp)
        k.ts(gbv(l, 4), gbv(l, 4), -1.0, ALU.mult)
        k.ts(gbv(l, 5), gbv(l, 3), -1.0, ALU.mult)
        k.dma(V(o_mnw + l * 256, 256), ml_norm_w[l:l + 1, :].partition_broadcast(128))

    def Sst(l, par, h):
        return V(o_S + (l * 2 + par) * 512 + h * 128, 128)

    def Cst(l, par, h):
        po = (h % 2) * 64
        return V(o_C + (l * 2 + par) * 130 + (h // 2) * 65, 65, po, po + 64)

    k.memset(V(o_S, o_st_end - o_S), 0.0)
    k.memset(V(o_rows, 26 * 128), 0.0)

    def halo_a(l, u):
        return V(o_ha + (l * 2 + u) * 2, 2)

    def halo_g(l, u):
        return V(o_hg + (l * 12 + u) * 3, 3)

    def halo_f(l, u):
        return V(o_hf + (l * 22 + u) * 2, 2)

    def mprev(l):
        return V(o_mp + l, 1, 0, 4)

    def xtok(t):
        return V(o_xtok + t * 1024, 1024)

    def wbuf_next(ctr=[0]):
        i = ctr[0]
        ctr[0] += 1
        return Vb(o_wbuf + (i % 5) * 1024, 1024)

    def row(i, p0=0, p1=4):
        return V(o_rows + i * 128, 128, p0, p1)

    def slot(h, i):
        return V(o_scan + (h * 12 + i) * 128, 128)

    if stage == 0:
        k.dma(y[0:128, 0:512], V(o_cwa, 512))
        k.fence(outs_all)
        print(S.finalize())
        return nc
    def main_loops():
        for bi, (t0, Tb, is_s) in enumerate(BLOCKS[:nblocks]):
            if bsel is not None and bi not in bsel:
                continue
            nt = Tb // 128
            L = 8 if is_s else 128
            nch = 128 // L
            nch2 = max(nch, 2)
            MAI = V(o_cst + (C_MAIS if is_s else C_MAIP), 128)
            MAS = V(o_cst + (C_MASS if is_s else C_MASP), 128)
            cmask = V(o_cst + (C_CMS if is_s else C_CMP), 128, 0, 4)
            nlev = 3 if is_s else 7
            xTv = Vb(o_xT, 4 * Tb).rearrange("p (k t) -> p k t", k=8)
            ycv = Vb(o_yc, 4 * Tb).rearrange("p (k t) -> p k t", k=8)
            hTv = Vb(o_qkv, 11 * Tb).rearrange("p (k t) -> p k t", k=22)
            szv = Vb(o_sz, 2 * Tb).rearrange("p (u t) -> p u t", u=4)
            ab = V(o_ab, 2 * Tb).rearrange("p (u t) -> p u t", u=2)
            mqk = V(o_mqk, 4 * Tb).rearrange("p (u t) -> p u t", u=4)

            def qkv(u):
                return V(o_qkv + u * Tb, Tb)

            def vext(t):
                return V(o_mvo + t * 516, 260).rearrange("p (h e) -> p h e", h=4)

            def sigo(t):
                return V(o_mvo + t * 516 + 260, 256)

            def tview(ap):
                if not is_s:
                    return ap
                return ap.rearrange("p (s t) -> p s t", s=16)

            for t in range(nt):
                k.dma(xtok(t), x_in[t0 + t * 128:t0 + (t + 1) * 128, :])

            def transpose_to_xT(src_tok, t, scale_l=None, which=None):
                for k0 in range(0, 8, 4):
                    ps = psd()
                    for kk in range(k0, k0 + 4):
                        k.tr(ps[:, (kk - k0) * 128:(kk - k0 + 1) * 128], src_tok[:, kk * 128:(kk + 1) * 128], ident)
                    for kk in range(k0, k0 + 4):
                        o_ = xTv[:, kk, t * 128:(t + 1) * 128]
                        i_ = ps[:, (kk - k0) * 128:(kk - k0 + 1) * 128]
                        if scale_l is None:
                            k.cp(o_, i_, eng=CPENG)
                        else:
                            k.act(o_, i_, AF.Identity, scale=lnf(scale_l, which, kk), bias=lnf(scale_l, which + 1, kk))

            for t in range(nt):
                transpose_to_xT(xtok(t), t)

            xectr = [0]

            def conv_unit(src, W_, cw, halo, st_src, st_rows, mul_by=None):
                i = xectr[0]
                xectr[0] += 1
                wl = W_ - 1
                if not is_s:
                    full = V(o_xe + (i % 2) * 516, wl + Tb)
                    data, hv, tail = full[:, wl:wl + Tb], full[:, 0:wl], full[:, Tb:Tb + wl]
                    win = lambda j: full[:, j:j + Tb]
                else:
                    full = V(o_xe + (i % 2) * 516, 16 * (wl + 8)).rearrange("p (s t) -> p s t", s=16)
                    data, hv, tail = full[:, :, wl:wl + 8], full[:, :, 0:wl], full[:, :, 8:8 + wl]
                    win = lambda j: full[:, :, j:j + 8]
                if mul_by is None:
                    k.cp(data, tview(src))
                else:
                    k.tt(data, tview(src), tview(mul_by), ALU.mult)
                if not is_s:
                    k.cp(hv, halo, eng="pool")
                else:
                    k.cp(hv, st_src.rearrange("p (s t) -> p s t", s=16), eng="pool")
                acc = V(o_acc + (i % 2) * 512, Tb)
                accv = tview(acc)
                k.ts(accv, win(0), cw[:, 0:1], ALU.mult)
                for j in range(1, W_):
                    k.stt(accv, win(j), cw[:, j:j + 1], accv, ALU.mult, ALU.add)
                if not is_s:
                    k.cp(halo, tail, eng="pool")
                else:
                    k.cp(st_rows.rearrange("p (s t) -> p s t", s=16), tail, eng="pool")
                return acc

            def load_state_T(src, R, nu, dst_off):
                for c0 in range(0, nu, 4):
                    n4 = min(4, nu - c0)
                    k.dma(ptmp[0:R, 0:n4 * 128], src[:, c0 * 128:(c0 + n4) * 128])
                    ps = psd()
                    for u in range(c0, c0 + n4):
                        k.tr(ps[:, (u - c0) * 128:(u - c0 + 1) * 128], ptmp[:, (u - c0) * 128:(u - c0 + 1) * 128], ident)
                    for u in range(c0, c0 + n4):
                        k.cp(V(dst_off + u * R, R), ps[:, (u - c0) * 128:(u - c0) * 128 + R])

            def rows_out(get_in, nu, R, dram):
                for c0 in range(0, nu, 4):
                    n4 = min(4, nu - c0)
                    ps = psd()
                    for u in range(c0, c0 + n4):
                        gi = get_in(u)
                        k.tr(ps[:, (u - c0) * 128:(u - c0 + 1) * 128], A[:, int(gi.offset) % NCOL:int(gi.offset) % NCOL + 128], ident)
                    rb = V(o_misc + 128, 512, 0, R)
                    k.cp(rb[:, 0:n4 * 128], ps[0:R, 0:n4 * 128])
                    k.dma(dram[:, c0 * 128:(c0 + n4) * 128], rb[:, 0:n4 * 128])

            for l in range(nlayers):
                last_l = (l == nlayers - 1)
                xectr[0] = 0

                def load_w(src, c0, n):
                    wb = wbuf_next()
                    wv = wb[:, 0:8 * n].rearrange("p (k e) -> p k e", k=8)
                    k.dma(wv, src.rearrange("(k p) e -> p k e", p=128)[:, :, c0:c0 + n])
                    return wv

                def unit_mm(ps, wv, e0, M):
                    for kk in range(8):
                        k.mm(ps[0:M, 0:Tb], wv[:, kk, e0:e0 + M], xTv[:, kk, :], start=(kk == 0), stop=(kk == 7))

                wsrc = wb_in[l]
                actmp = V(o_lnb, 2 * Tb).rearrange("p (u t) -> p u t", u=2)
                wv = load_w(wsrc, 0, 256)
                for u in range(2):
                    ps = psd()
                    unit_mm(ps, wv, u * 128, 128)
                    k.cp(ab[:, u, :], ps[:, 0:Tb])
                wv = load_w(wsrc, 256, 256)
                for u in range(2):
                    ps = psd()
                    unit_mm(ps, wv, u * 128, 128)
                    k.cp(actmp[:, u, :], ps[:, 0:Tb], eng="dve")
                if is_s:
                    load_state_T(st_conv[l], 32, 2, o_sst)
                wv = load_w(wsrc, 512, 256)
                for u in range(2):
                    ps = psd()
                    unit_mm(ps, wv, u * 128, 128)
                    acc = conv_unit(ps[:, 0:Tb], 3, cwa(l, u), halo_a(l, u), V(o_sst + u * 32, 32), V(o_sso + u * 32, 32),
                                    mul_by=actmp[:, u, :])
                    k.tt(ycv[:, u, :], acc, ab[:, u, :], ALU.mult)
                if is_s:
                    rows_out(lambda u: V(o_sso + u * 32, 32), 2, 32, s_conv[l])
                elif bi == 3:
                    rows_out(lambda u: halo_a(l, u), 2, 2, p_conv[l])
                if is_s:
                    load_state_T(st_gconv[l], 48, 12, o_sst)
                for g in range(6):
                    wv = load_w(wsrc, 768 + 256 * g, 256)
                    for uu in range(2):
                        j = 2 * g + uu
                        ps = psd()
                        unit_mm(ps, wv, uu * 128, 128)
                        acc = conv_unit(ps[:, 0:Tb], 4, cwg(l, j), halo_g(l, j), V(o_sst + j * 48, 48), V(o_sso + j * 48, 48))
                        k.act(qkv(j), acc, AF.Silu)
                if is_s:
                    rows_out(lambda u: V(o_sso + u * 48, 48), 12, 48, s_gconv[l])
                elif bi == 3:
                    rows_out(lambda u: halo_g(l, u), 12, 3, p_gconv[l])
                for g in range(2):
                    wv = load_w(wsrc, 2304 + 256 * g, 256)
                    for uu in range(2):
                        ps = psd()
                        unit_mm(ps, wv, uu * 128, 128)
                        k.act(szv[:, 2 * g + uu, :], ps[:, 0:Tb], AF.Silu)
                for g in range(2):
                    wv = load_w(wsrc, 2824 + 256 * g, 256)
                    for uu in range(2):
                        ps = psd()
                        unit_mm(ps, wv, uu * 128, 128)
                        k.cp(mqk[:, 2 * g + uu, :], ps[:, 0:Tb])
                wv = load_w(wsrc, 3336, 256)
                for t in range(nt):
                    ps = psd()
                    for kk in range(8):
                        k.mm(ps[:, 0:256], xTv[:, kk, t * 128:(t + 1) * 128], wv[:, kk, :], start=(kk == 0), stop=(kk == 7))
                    k.cp(vext(t)[:, :, 0:64], ps[:, 0:256].rearrange("p (h e) -> p h e", h=4))
                    k.memset(vext(t)[:, :, 64:65], 1.0, eng="pool")
                wv = load_w(wsrc, 3592, 256)
                for t in range(nt):
                    ps = psd()
                    for kk in range(8):
                        k.mm(ps[:, 0:256], xTv[:, kk, t * 128:(t + 1) * 128], wv[:, kk, :], start=(kk == 0), stop=(kk == 7))
                    k.act(sigo(t), ps[:, 0:256], AF.Sigmoid)
                wg = Vb(o_wg, 64).rearrange("p (k e) -> p k e", k=8)
                wsr = wsrc.rearrange("(k p) e -> p k e", p=128)
                k.dma(wg[:, :, 0:8], wsr[:, :, 2816:2824])
                k.dma(wg[:, :, 8:16], wsr[:, :, 3848:3856])
                if bi == 0 and l == 0:
                    dump("qkv", V(o_qkv, 12 * Tb))
                    dump("ycA", V(o_yc, 2048))

                for t in range(nt if stage >= 2 else 0):
                    ts_ = slice(t * 128, (t + 1) * 128)
                    gti = bi * 4 + t
                    par = gti % 2 if not is_s else 0
                    graw = [V(o_graw + i * 128, 128, 0, 4) for i in range(4)]
                    for i in range(4):
                        ps = psq()
                        for kk in range(8):
                            k.mm(ps[0:4, :], wg[:, kk, i * 4:i * 4 + 4], xTv[:, kk, ts_], start=(kk == 0), stop=(kk == 7))
                        k.cp(graw[i], ps[0:4, :], eng="dve")
                    r_G, r_lb, r_lrq, r_lrk, r_ra, r_rq, r_cj, r_kd, r_tmp, r_GL = [row(i) for i in range(10)]
                    k.act(r_tmp, graw[0], AF.Exp, bias=gbv(l, 1))
                    k.act(r_tmp, r_tmp, AF.Ln, bias=1.0)
                    k.ts(r_tmp, r_tmp, gbv(l, 4), ALU.mult)
                    k.scan(r_G, cmask, r_tmp, 0.0, ALU.mult, ALU.add)
                    k.act(r_lb, graw[1], AF.Exp, scale=-1.0)
                    k.act(r_lb, r_lb, AF.Ln, bias=1.0)
                    k.ts(r_lb, r_lb, -1.0, ALU.mult)
                    for qi, rr in ((0, r_lrq), (1, r_lrk)):
                        ps = psq()
                        for h in range(4):
                            sq = slot(h, 3 + qi)
                            k.act(sq, qkv(qi * 4 + h)[:, ts_], AF.Square)
                            k.mm(ps[0:4, :], eh(h), sq, start=(h == 0), stop=(h == 3))
                        k.act(rr, ps[0:4, :], AF.Ln, bias=NORM_EPS)
                        k.ts(rr, rr, -0.5, ALU.mult)
                    k.tt(r_ra, r_G, r_lb, ALU.add)
                    k.tt(r_ra, r_ra, r_lrk, ALU.add)
                    k.ts(r_rq, r_lrq, math.log(128.0 ** -0.5), ALU.add)
                    k.tt(r_rq, r_rq, r_G, ALU.add)
                    k.tt(r_cj, r_lrk, r_G, ALU.subtract)
                    G3 = r_G.rearrange("p (c l) -> p c l", l=L)
                    k.tt(r_kd.rearrange("p (c l) -> p c l", l=L), r_cj.rearrange("p (c l) -> p c l", l=L),
                         G3[:, :, L - 1:L].broadcast_to([4, nch, L]), ALU.add)
                    k.cp(r_GL[:, 0:nch].rearrange("p (c o) -> p c o", o=1), G3[:, :, L - 1:L], eng="dve")
                    craw = V(o_cols, 20)
                    cexp = V(o_cols + 20, 20)
                    ps = psq()
                    for ci, rr in enumerate((r_cj, r_lb, r_ra, r_kd, r_rq)):
                        k.mm(ps[:, ci * 4:ci * 4 + 4], rr, ident[0:4, 0:4], start=True, stop=True)
                    k.cp(craw, ps[:, 0:20], eng="dve")
                    k.act(cexp, ps[:, 0:20], AF.Exp)
                    elast = V(o_misc, 4 * nch2)
                    ps = psq()
                    for h in range(4):
                        k.mm(ps[:, h * nch2:h * nch2 + nch2], sel(h), r_GL[:, 0:nch2], start=True, stop=True)
                    k.act(elast, ps[:, 0:4 * nch2], AF.Exp)
                    if bi == 0 and l == 0 and t == 0:
                        dump("rows", V(o_rows, 10 * 128, 0, 4))
                        dump("cexp", V(o_cols, 40))

                    qT_ = [qkv(h)[:, ts_] for h in range(4)]
                    kT_ = [qkv(4 + h)[:, ts_] for h in range(4)]
                    vT_ = [qkv(8 + h)[:, ts_] for h in range(4)]
                    for h in range(4):
                        ps = psq()
                        k.mm(ps, sel(h), r_ra, start=True, stop=False)
                        k.mm(ps, ident, MAS, start=False, stop=True)
                        k.act(slot(h, 0), ps, AF.Exp, bias=craw[:, h:h + 1])
                        ps2 = psq()
                        k.mm(ps2, kT_[h], kT_[h])
                        k.stt(slot(h, 3), ps2, -1.0, slot(h, 0), ALU.mult, ALU.mult)
                    for h in range(4):
                        ps = psq()
                        k.mm(ps, sel(h), r_rq, start=True, stop=False)
                        k.mm(ps, ident, MAI, start=False, stop=True)
                        k.act(slot(h, 1), ps, AF.Exp, bias=craw[:, h:h + 1])
                        ps2 = psq()
                        k.mm(ps2, kT_[h], qT_[h])
                        k.tt(slot(h, 2), ps2, slot(h, 1), ALU.mult)
                    for h in range(4):
                        ps = psq()
                        k.tr(ps, slot(h, 3), ident)
                        k.cp(slot(h, 5), ps)
                        k.tt(slot(h, 7), slot(h, 3), ident, ALU.add, eng="pool")
                    cP = [(3, 5, 7)] * 4
                    for lev in range(1, nlev):
                        lastlev = (lev == nlev - 1)
                        nP = []
                        for h in range(4):
                            ipt, ipm, itt = cP[h]
                            npt, npm, ntt = 7 - ipt, 11 - ipm, 15 - itt
                            ps = psq()
                            k.mm(ps, slot(h, ipt), slot(h, ipm))
                            k.cp(slot(h, npm), ps)
                            if not lastlev:
                                ps2 = psq()
                                k.mm(ps2, slot(h, ipm), slot(h, ipt))
                                k.cp(slot(h, npt), ps2, eng="dve")
                            ps3 = psq()
                            k.mm(ps3, slot(h, npm), slot(h, itt))
                            k.tt(slot(h, ntt), ps3, slot(h, itt), ALU.add)
                            nP.append((npt, npm, ntt))
                        cP = nP
                    TTf = [slot(h, cP[h][2]) for h in range(4)]
                    for h in range(4):
                        ps = psq()
                        k.tr(ps, kT_[h], ident)
                        k.ts(slot(h, 0), ps, cexp[:, 8 + h:9 + h], ALU.mult)
                        k.act(slot(h, 1), ps, AF.Identity, scale=cexp[:, 12 + h:13 + h])
                        ps2 = psq()
                        k.tr(ps2, vT_[h], ident)
                        k.ts(slot(h, 9), ps2, cexp[:, 4 + h:5 + h], ALU.mult)
                    for h in range(4):
                        ps = psq()
                        k.mm(ps, slot(h, 0), TTf[h])
                        k.act(slot(h, 10), ps, AF.Identity, scale=-1.0)
                    O_ = [slot(h, 3) for h in range(4)]
                    if not is_s:
                        for h in range(4):
                            Sc, Sn = Sst(l, par, h), Sst(l, 1 - par, h)
                            ps = psq()
                            k.mm(ps, TTf[h], slot(h, 9), start=True, stop=False)
                            k.mm(ps, slot(h, 10), Sc, start=False, stop=True)
                            k.cp(slot(h, 11), ps)
                            ps1 = psq()
                            k.mm(ps1, qT_[h], Sc)
                            k.act(slot(h, 0), ps1, AF.Identity, scale=cexp[:, 16 + h:17 + h])
                            ps2 = psq()
                            k.mm(ps2, slot(h, 2), slot(h, 11))
                            k.tt(O_[h], ps2, slot(h, 0), ALU.add)
                            ps3 = psq()
                            k.mm(ps3, slot(h, 1), slot(h, 11))
                            k.stt(Sn, Sc, elast[:, h * nch2:h * nch2 + 1], ps3, ALU.mult, ALU.add)
                            if gti == 15:
                                k.dma(p_gdn[l, h], Sn)
                    else:
                        S0v = V(o_S0, 2048).rearrange("p (s v) -> p s v", s=16)
                        mskd = V(o_msk, 2176).rearrange("p (s r) -> p s r", r=136)[:, :, 0:8]
                        mskf = V(o_msk, 2048).rearrange("p (s i) -> p s i", s=16)
                        for h in range(4):
                            k.dma(S0v, st_gdn[l, :, h].rearrange("s k v -> k s v"))
                            if h == 0 and l == 0:
                                k.memset(V(o_msk, 2176), 0.0, eng="pool")
                            k.cp(mskd, slot(h, 10).rearrange("p (s r) -> p s r", r=8), eng="pool")
                            ps = psq()
                            k.mm(ps, TTf[h], slot(h, 9), start=True, stop=False)
                            for s in range(16):
                                k.mm(ps, mskf[:, s, :], S0v[:, s, :], start=False, stop=(s == 15))
                            k.cp(slot(h, 11), ps)
                            k.cp(mskd, qT_[h].rearrange("p (s r) -> p s r", r=8), eng="pool")
                            ps1 = psq()
                            for s in range(16):
                                k.mm(ps1, mskf[:, s, :], S0v[:, s, :], start=(s == 0), stop=(s == 15))
                            k.act(slot(h, 0), ps1, AF.Identity, scale=cexp[:, 16 + h:17 + h])
                            ps2 = psq()
                            k.mm(ps2, slot(h, 2), slot(h, 11))
                            k.tt(O_[h], ps2, slot(h, 0), ALU.add)
                            for s in range(16):
                                kdm = V(o_kdm + (s % 2) * 128, 128)
                                so = V(o_kdm + 256 + (s % 2) * 128, 128)
                                k.ts(kdm, slot(h, 1), rowmask[:, s:s + 1], ALU.mult, eng="pool")
                                ps3 = psq()
                                k.mm(ps3, kdm, slot(h, 11))
                                k.stt(so, S0v[:, s, :], elast[:, h * 16 + s:h * 16 + s + 1], ps3, ALU.mult, ALU.add)
                                k.dma(s_gdn[l, s, h], so)
                    if bi == 0 and l == 0 and t == 0:
                        dump("o_gdn", V(o_scan + 3 * 128, 128))
                        dump("TT0", TTf[0])
                    ss = V(o_cols + 40, 4)
                    for h in range(4):
                        k.act(slot(h, 0), O_[h], AF.Square, accum=ss[:, h:h + 1])
                    k.ts(ss, ss, 1.0 / 128, ALU.mult, NORM_EPS, ALU.add)
                    k.act(ss, ss, AF.Sqrt)
                    k.recip(ss, ss)
                    for h in range(4):
                        k.ts(slot(h, 9), O_[h], ss[:, h:h + 1], ALU.mult)
                        ps = psq()
                        k.tr(ps, slot(h, 9), ident)
                        k.stt(ycv[:, 2 + h, ts_], ps, V(o_gnw + l, 1), szv[:, h, ts_], ALU.mult, ALU.mult)

                    r_ig, r_lf, r_F, r_m, r_rD, r_cD, r_rI, r_em, r_kw, r_mp, r_t2, r_ch = [row(10 + i) for i in range(12)]
                    k.ts(r_ig, graw[2], gbv(l, 2), ALU.add)
                    k.act(r_lf, graw[3], AF.Exp, scale=-1.0, bias=gbv(l, 5))
                    k.act(r_lf, r_lf, AF.Ln, bias=1.0)
                    k.ts(r_lf, r_lf, -1.0, ALU.mult)
                    k.scan(r_F, cmask, r_lf, 0.0, ALU.mult, ALU.add)
                    F3 = r_F.rearrange("p (c l) -> p c l", l=L)
                    m3 = r_m.rearrange("p (c l) -> p c l", l=L)
                    if not is_s:
                        k.scan(r_m, r_lf, r_ig, mprev(l), ALU.add, ALU.max)
                        k.cp(r_mp, mprev(l).broadcast_to([4, 128]), eng="dve")
                    else:
                        m0 = r_ch[:, 64:80]
                        k.dma(m0, st_m[l].rearrange("s h -> h s"), allow_slow_non_contiguous=True)
                        k.cp(r_mp.rearrange("p (c l) -> p c l", l=8), m0.rearrange("p (c o) -> p c o", o=1).broadcast_to([4, 16, 8]), eng="dve")
                        ig3 = r_ig.rearrange("p (c l) -> p c l", l=8)
                        lf3 = r_lf.rearrange("p (c l) -> p c l", l=8)
                        t23 = r_t2.rearrange("p (c l) -> p c l", l=8)
                        k.cp(r_t2, r_ig, eng="dve")
                        k.tt(t23[:, :, 0:1], lf3[:, :, 0:1], m0.rearrange("p (c o) -> p c o", o=1), ALU.add)
                        k.tt(t23[:, :, 0:1], t23[:, :, 0:1], ig3[:, :, 0:1], ALU.max)
                        r_lf2 = row(25)
                        k.cp(r_lf2, r_lf, eng="dve")
                        k.memset(r_lf2.rearrange("p (c l) -> p c l", l=8)[:, :, 0:1], NEG)
                        k.scan(r_m, r_lf2, r_t2, 0.0, ALU.add, ALU.max)
                    k.tt(r_rD, r_F, r_m, ALU.subtract)
                    k.tt(r_cD, r_ig, r_F, ALU.subtract)
                    k.ts(r_cD, r_cD, math.log(0.125), ALU.add)
                    k.tt(r_rI, r_rD, r_mp, ALU.add)
                    k.ts(r_em, r_m, -1.0, ALU.mult)
                    chA = r_ch[:, 0:nch].rearrange("p (c o) -> p c o", o=1)
                    chB = r_ch[:, 16:16 + nch].rearrange("p (c o) -> p c o", o=1)
                    k.tt(chA, F3[:, :, L - 1:L], m3[:, :, L - 1:L], ALU.subtract)
                    k.tt(chB, chA, r_mp.rearrange("p (c l) -> p c l", l=L)[:, :, 0:1], ALU.add)
                    k.tt(r_kw.rearrange("p (c l) -> p c l", l=L), r_cD.rearrange("p (c l) -> p c l", l=L),
                         chA.broadcast_to([4, nch, L]), ALU.add)
                    if not is_s:
                        k.cp(mprev(l), r_m[:, 127:128], eng="dve")
                        if gti == 15:
                            k.dma(p_m[l].rearrange("(p o) -> p o", o=1), r_m[:, 127:128])
                    else:
                        k.dma(s_m[l].rearrange("s h -> h s"), m3[:, :, 7], allow_slow_non_contiguous=True)
                    mraw = V(o_cols + 48, 16)
                    mexp = V(o_cols + 64, 16)
                    ps = psq()
                    for ci, rr in enumerate((r_cD, r_em, r_kw, r_rI)):
                        k.mm(ps[:, ci * 4:ci * 4 + 4], rr, ident[0:4, 0:4], start=True, stop=True)
                    k.cp(mraw, ps[:, 0:16], eng="dve")
                    k.act(mexp, ps[:, 0:16], AF.Exp)
                    dcb = V(o_misc + 64, 4 * nch2)
                    ps = psq()
                    for h in range(4):
                        k.mm(ps[:, h * nch2:h * nch2 + nch2], sel(h), r_ch[:, 16:16 + nch2], start=True, stop=True)
                    k.act(dcb, ps[:, 0:4 * nch2], AF.Exp)
                    nq = V(o_scan + 48 * 128 - 4 * 65, 260).rearrange("p (h e) -> p h e", h=4)
                    kwt = [None] * 4
                    for hc in range(2):
                        ps = psq()
                        k.tr(ps, mqk[:, 2 + hc, ts_], ident)
                        for hh in range(2):
                            h = hc * 2 + hh
                            kwt[h] = slot(h, 4)[:, 0:64]
                            k.ts(kwt[h], ps[:, hh * 64:(hh + 1) * 64], mexp[:, 8 + h:9 + h], ALU.mult)
                    for h in range(4):
                        po = (h % 2) * 64
                        qTh = mqk[po:po + 64, h // 2, ts_]
                        kTh = mqk[po:po + 64, 2 + h // 2, ts_]
                        ps = psq()
                        k.mm(ps, sel(h), r_rD, start=True, stop=False)
                        k.mm(ps, ident, MAI, start=False, stop=True)
                        k.act(slot(h, 0), ps, AF.Exp, bias=mraw[:, h:h + 1])
                        ps2 = psq()
                        k.mm(ps2, kTh, qTh)
                        k.tt(slot(h, 1), ps2, slot(h, 0), ALU.mult)
                        if not is_s:
                            Cc, Cn = Cst(l, par, h), Cst(l, 1 - par, h)
                            ps1 = psq()
                            k.mm(ps1[:, 0:65], qTh, Cc)
                            k.act(slot(h, 2)[:, 0:65], ps1[:, 0:65], AF.Identity, scale=mexp[:, 12 + h:13 + h])
                            ps3 = psq()
                            k.mm(ps3[:, 0:65], slot(h, 1), vext(t)[:, h, :])
                            k.tt(nq[:, h, :], ps3[:, 0:65], slot(h, 2)[:, 0:65], ALU.add)
                            ps4 = psq()
                            k.mm(ps4[po:po + 64, 0:65], kwt[h], vext(t)[:, h, :])
                            k.stt(Cn, Cc, dcb[po:po + 64, h * nch2:h * nch2 + 1], ps4[po:po + 64, 0:65], ALU.mult, ALU.add)
                            if gti == 15:
                                k.dma(p_c[l, h], Cn[:, 0:64])
                                k.dma(p_n[l, h].rearrange("(p o) -> p o", o=1), Cn[:, 64:65])
                        else:
                            C0v = V(o_S0, 16 * 65, po, po + 64).rearrange("p (s e) -> p s e", s=16)
                            k.dma(C0v[:, :, 0:64], st_c[l, :, h].rearrange("s d e -> d s e"))
                            k.dma(C0v[:, :, 64:65], st_n[l, :, h].rearrange("s (d o) -> d s o", o=1), allow_slow_non_contiguous=True)
                            mskd = V(o_msk, 2176, po, po + 64).rearrange("p (s r) -> p s r", r=136)[:, :, 0:8]
                            mskf = V(o_msk, 2048, po, po + 64).rearrange("p (s i) -> p s i", s=16)
                            k.cp(mskd, qTh.rearrange("p (s r) -> p s r", r=8), eng="pool")
                            ps1 = psq()
                            for s in range(16):
                                k.mm(ps1[:, 0:65], mskf[:, s, :], C0v[:, s, :], start=(s == 0), stop=(s == 15))
                            k.act(slot(h, 2)[:, 0:65], ps1[:, 0:65], AF.Identity, scale=mexp[:, 12 + h:13 + h])
                            ps3 = psq()
                            k.mm(ps3[:, 0:65], slot(h, 1), vext(t)[:, h, :])
                            k.tt(nq[:, h, :], ps3[:, 0:65], slot(h, 2)[:, 0:65], ALU.add)
                            for s in range(16):
                                kwm = V(o_kdm + (s % 2) * 128, 64)
                                co = V(o_kdm + 256 + (s % 2) * 128, 65, po, po + 64)
                                k.ts(kwm, kwt[h], rowmask[:, s:s + 1], ALU.mult, eng="pool")
                                ps4 = psq()
                                k.mm(ps4[po:po + 64, 0:65], kwm, vext(t)[:, h, :])
                                k.stt(co, C0v[:, s, :], dcb[po:po + 64, h * 16 + s:h * 16 + s + 1], ps4[po:po + 64, 0:65], ALU.mult, ALU.add)
                                k.dma(s_c[l, s, h], co[:, 0:64])
                                k.dma(s_n[l, s, h].rearrange("(p o) -> p o", o=1), co[:, 64:65])
                    den = V(o_cols + 80, 4)
                    qn = nq[:, :, 64]
                    k.stt(den, qn, -1.0, qn, ALU.mult, ALU.max)
                    k.tt(den, den, mexp[:, 4:8], ALU.max)
                    k.recip(den, den)
                    hbuf = slot(0, 5)[:, 0:128]
                    hb2 = V(o_scan + 5 * 128, 128)
                    hb3 = V(o_scan + 6 * 128, 128)
                    hall = V(o_scan + 5 * 128, 256)
                    stats = V(o_cols + 84, 24)
                    mv = V(o_cols + 108, 8)
                    rstd = V(o_cols + 116, 4)
                    for h in range(4):
                        hh_ = hall[:, h * 64:(h + 1) * 64]
                        k.stt(hh_, nq[:, h, 0:64], den[:, h:h + 1], sigo(t)[:, h * 64:(h + 1) * 64], ALU.mult, ALU.mult)
                        k.bn_stats(stats[:, h * 6:(h + 1) * 6], hh_)
                        k.bn_aggr(mv[:, h * 2:(h + 1) * 2], stats[:, h * 6:(h + 1) * 6])
                    mv3 = mv.rearrange("p (h two) -> p h two", two=2)
                    k.ts(rstd.rearrange("p (h o) -> p h o", o=1), mv3[:, :, 1:2], LN_EPS, ALU.add)
                    k.act(rstd, rstd, AF.Sqrt)
                    k.recip(rstd, rstd)
                    for h in range(4):
                        hh_ = hall[:, h * 64:(h + 1) * 64]
                        k.ts(hh_, hh_, mv[:, 2 * h:2 * h + 1], ALU.subtract, rstd[:, h:h + 1], ALU.mult)
                    k.tt(hall, hall, V(o_mnw + l * 256, 256), ALU.mult, eng="pool")
                    for hc in range(2):
                        ps = psq()
                        k.tr(ps, hall[:, hc * 128:(hc + 1) * 128], ident)
                        k.cp(ycv[:, 6 + hc, ts_], ps)
                    if bi == 0 and l == 0 and t == 0:
                        dump("hml", hall)

                if bi == 0 and l == 0:
                    dump("ycat", V(o_yc, 2048))
                if stage < 3:
                    continue
                def ln_resid(t, pss):
                    xt_ = xtok(t)
                    for hf in range(4):
                        k.stt(xt_[:, hf * 256:(hf + 1) * 256], xt_[:, hf * 256:(hf + 1) * 256], ALPHA, pss[hf], ALU.mult, ALU.add)

                def ln_tile(t, pss, g_src, b_src, which, final_out=None, need_T=True):
                    xt_ = xtok(t)
                    if pss is not None:
                        ln_resid(t, pss)
                    stats = V(o_cols + 84, 12)
                    mv = V(o_cols + 108, 2)
                    rs = V(o_cols + 116, 2)
                    k.bn_stats(stats[:, 0:6], xt_[:, 0:512])
                    k.bn_stats(stats[:, 6:12], xt_[:, 512:1024])
                    k.bn_aggr(mv, stats)
                    k.ts(rs[:, 0:1], mv[:, 1:2], LN_EPS, ALU.add)
                    k.act(rs[:, 0:1], rs[:, 0:1], AF.Sqrt)
                    k.recip(rs[:, 0:1], rs[:, 0:1])
                    k.stt(rs[:, 1:2], mv[:, 0:1], -1.0, rs[:, 0:1], ALU.mult, ALU.mult)
                    ntok_ = V(o_scan + (t % 2) * 1024, 1024)
                    k.act(ntok_, xt_, AF.Identity, scale=rs[:, 0:1], bias=rs[:, 1:2])
                    if need_T:
                        transpose_to_xT(ntok_, t, scale_l=l, which=which)
                    gB = V(o_lnb, 1024)
                    bB = V(o_lnb + 1024, 1024)
                    k.tt(xt_, ntok_, gB, ALU.mult, eng="pool")
                    k.tt(xt_, xt_, bB, ALU.add, eng="pool")
                    if final_out is not None:
                        k.dma(final_out, xt_)

                k.dma(V(o_lnb, 1024), ln1_g[l:l + 1, :].partition_broadcast(128))
                k.dma(V(o_lnb + 1024, 1024), ln1_b[l:l + 1, :].partition_broadcast(128))
                wos = []
                for g in range(4):
                    wb = wbuf_next()
                    wv = wb[:, 0:2048].rearrange("p (k e) -> p k e", k=8)
                    k.dma(wv, wb_o[l].rearrange("(k p) e -> p k e", p=128)[:, :, g * 256:(g + 1) * 256])
                    wos.append(wv)
                for t in range(nt):
                    ps = psd()
                    pss = []
                    for g in range(4):
                        if g == 2:
                            ps = psd()
                        pp = ps[:, (g % 2) * 256:(g % 2 + 1) * 256]
                        for kk in range(8):
                            k.mm(pp, ycv[:, kk, t * 128:(t + 1) * 128], wos[g][:, kk, :], start=(kk == 0), stop=(kk == 7))
                        pss.append(pp)
                    ln_tile(t, pss, ln1_g, ln1_b, 0)
                if bi == 0 and l == 0:
                    dump("x1", V(o_xtok, 4096))
                if stage < 4:
                    continue
                if is_s:
                    load_state_T(st_ffn[l], 32, 22, o_sst)
                wup = wb_up[l]
                for g in range(11):
                    wvg = load_w(wup, g * 256, 256)
                    wvv = load_w(wup, DFF + g * 256, 256)
                    for uu in range(2):
                        j = 2 * g + uu
                        ps = psd()
                        unit_mm(ps, wvg, uu * 128, 128)
                        acc = conv_unit(ps[:, 0:Tb], 3, cwf(l, j), halo_f(l, j), V(o_sst + j * 32, 32), V(o_sso + j * 32, 32))
                        k.act(acc, acc, AF.Silu)
                        ps2 = psd()
                        unit_mm(ps2, wvv, uu * 128, 128)
                        k.tt(hTv[:, j, :], ps2[:, 0:Tb], acc, ALU.mult)
                if is_s:
                    rows_out(lambda u: V(o_sso + u * 32, 32), 22, 32, s_ffn[l])
                elif bi == 3:
                    rows_out(lambda u: halo_f(l, u), 22, 2, p_ffn[l])
                k.dma(V(o_lnb, 1024), ln2_g[l:l + 1, :].partition_broadcast(128))
                k.dma(V(o_lnb + 1024, 1024), ln2_b[l:l + 1, :].partition_broadcast(128))
                for g in range(11):
                    wb = wbuf_next()
                    wv = wb[:, 0:2048].rearrange("p (k e) -> p k e", k=2)
                    k.dma(wv, wb_down[l][g * 256:(g + 1) * 256, :].rearrange("(k p) e -> p k e", p=128))
                    for kk in range(2):
                        kc = 2 * g + kk
                        for t in range(nt):
                            for hf in range(2):
                                k.mm(PS[:, (t * 2 + hf) * 512:(t * 2 + hf + 1) * 512], hTv[:, kc, t * 128:(t + 1) * 128],
                                     wv[:, kk, hf * 512:(hf + 1) * 512], start=(kc == 0), stop=(kc == 21))
                for t in range(nt):
                    pss = [PS[:, (t * 2) * 512 + q * 256:(t * 2) * 512 + (q + 1) * 256] for q in range(4)]
                    ln_resid(t, pss)
                for t in range(nt):
                    fo = y[t0 + t * 128:t0 + (t + 1) * 128, :] if last_l else None
                    ln_tile(t, None, ln2_g, ln2_b, 2, final_out=fo, need_T=not last_l)

    S.limit = limit
    print('main loop starts at op', len(S.ops))
    try:
        main_loops()
    except StopIteration:
        print('LIMIT reached at', len(S.ops))
    S.closing = True
    k.fence(outs_all + [wb_in, wb_o, wb_up, wb_down])
    st = S.finalize()
    print(st)
    return nc


def _in_maps(inp, nlayers=2):
    consts = make_consts()
    maps = []
    f = lambda a: np.ascontiguousarray(np.asarray(a, dtype=np.float32))
    for c in range(8):
        sl = slice(16 * c, 16 * c + 16)
        m = {
            "x_in": f(np.concatenate([inp["x_prompt"][c], np.asarray(inp["x_sample"])[sl].reshape(128, D)], 0)),
            "st_conv": f(np.asarray(inp["state_conv_mix"])[:, sl].reshape(2, 32, 256)),
            "st_gconv": f(np.asarray(inp["state_gdn_conv"])[:, sl].reshape(2, 48, 1536)),
            "st_gdn": f(np.asarray(inp["state_gdn"])[:, sl]),
            "st_c": f(np.asarray(inp["state_mlstm_c"])[:, sl]),
            "st_n": f(np.asarray(inp["state_mlstm_n"])[:, sl]),
            "st_m": f(np.asarray(inp["state_mlstm_m"])[:, sl]),
            "st_ffn": f(np.asarray(inp["state_ffn_conv"])[:, sl].reshape(2, 32, DFF)),
            "consts": consts,
        }
        for nm in ("w_in", "conv_w", "gdn_conv_w", "gdn_a_log", "gdn_dt_bias", "gdn_norm_w", "ml_i_bias",
                   "ml_f_bias", "ml_norm_w", "w_o", "ln1_g", "ln1_b", "w_up", "ffn_conv_w", "w_down",
                   "ln2_g", "ln2_b"):
            m[nm] = f(inp[nm])
        maps.append(m)
    return maps


def kernel(**inp):
    nc = build()
    maps = _in_maps(inp)
    res = run_bass_kernel_spmd(nc, maps, core_ids=list(range(8)))
    R = res.results
    yp = np.stack([R[c]["y"][0:2048] for c in range(8)], 0)
    ys = np.concatenate([R[c]["y"][2048:].reshape(16, 8, D) for c in range(8)], 0)

    def pst(nm, shp):
        return np.stack([R[c][nm].reshape(shp) for c in range(8)], 1)

    def sst(nm, shp):
        return np.concatenate([R[c][nm].reshape(shp) for c in range(8)], 1)

    outs = (yp, ys,
            pst("p_conv", (2, 2, 256)), pst("p_gconv", (2, 3, 1536)), pst("p_gdn", (2, 4, 128, 128)),
            pst("p_c", (2, 4, 64, 64)), pst("p_n", (2, 4, 64)), pst("p_m", (2, 4)), pst("p_ffn", (2, 2, DFF)),
            sst("s_conv", (2, 16, 2, 256)), sst("s_gconv", (2, 16, 3, 1536)), sst("s_gdn", (2, 16, 4, 128, 128)),
            sst("s_c", (2, 16, 4, 64, 64)), sst("s_n", (2, 16, 4, 64)), sst("s_m", (2, 16, 4)),
            sst("s_ffn", (2, 16, 2, DFF)))
    return tuple(np.ascontiguousarray(o.astype(np.float32)) for o in outs)
```

```python
import numpy as np
import concourse.bass as bass
import concourse.mybir as mybir

F32 = mybir.dt.float32
BF16 = mybir.dt.bfloat16
AF = mybir.ActivationFunctionType
ALU = mybir.AluOpType
AX = mybir.AxisListType
ESZ = {F32: 4, BF16: 2, mybir.dt.float32r: 4}
BUCK = 512


def box(ap):
    t = ap.tensor
    es = ESZ[ap.dtype]
    dims = [(int(s), int(c)) for s, c in ap.ap]
    off = int(ap.offset)
    cls = type(t).__name__
    if cls.startswith("DRam"):
        ext = sum((c - 1) * abs(s) for s, c in dims) + 1
        return (t.name, 0, 1, off * es, (off + ext) * es)
    rowlen = 1
    for s in list(t.shape)[1:]:
        rowlen *= int(s)
    p0 = off // rowlen
    f0 = off % rowlen
    if dims[0][0] == rowlen or dims[0][1] == 1:
        pc = dims[0][1]
        rest = dims[1:]
    elif dims[0][0] == 0:
        pc = 1
        rest = dims[1:]
    else:
        pc = 1
        rest = dims
    ext = sum((c - 1) * abs(s) for s, c in rest) + 1
    if cls.startswith("PSum") or cls.startswith("Psum") or cls.startswith("PS"):
        b0 = (f0 * es) // 2048 * 2048
        b1 = ((f0 + ext) * es + 2047) // 2048 * 2048
        return (t.name, 0, 128, b0, b1)
    return (t.name, p0, p0 + pc, f0 * es, (f0 + ext) * es)


class Sched:
    def __init__(self, nc, n_sp_sems=16, n_pool_sems=8):
        self.nc = nc
        self.ops = []
        self.engs = {"pe": nc.tensor, "dve": nc.vector, "act": nc.scalar,
                     "pool": nc.gpsimd, "sp": nc.sync}
        self.esem = {e: nc.semaphore("sem_" + e).__enter__() for e in self.engs}
        self.dsems = {"sp": [nc.semaphore("dsp%d" % i).__enter__() for i in range(n_sp_sems)],
                      "pool": [nc.semaphore("dpl%d" % i).__enter__() for i in range(n_pool_sems)],
                      "act": []}

    limit = None

    def add(self, eng, fn, r, w, dma=False):
        if self.limit is not None and len(self.ops) >= self.limit and not getattr(self, 'closing', False):
            raise StopIteration("limit")
        rb = [box(a) for a in r]
        wb = [box(a) for a in w]
        wb = wb + [b for b in rb if b[0] == 'psum' or b[0] == 'pa']
        self.ops.append((eng, fn, rb, wb, dma))

    def finalize(self):
        ops = self.ops
        n = len(ops)
        recs = {}
        deps = [None] * n
        pos = [0] * n
        cnt = {e: 0 for e in self.engs}
        dma_n = {q: 0 for q in self.dsems}
        dma_hist = {q: [] for q in self.dsems}
        dsem = [None] * n
        for i, (eng, fn, R, W, dma) in enumerate(ops):
            pos[i] = cnt[eng]
            cnt[eng] += 1
            d = set()
            for (nm, p0, p1, b0, b1) in R:
                for bk in range(b0 // BUCK, (b1 - 1) // BUCK + 1):
                    for rec in recs.get((nm, bk), ()):
                        if rec[5] and rec[0] < p1 and p0 < rec[1] and rec[2] < b1 and b0 < rec[3]:
                            d.add(rec[4])
            for (nm, p0, p1, b0, b1) in W:
                for bk in range(b0 // BUCK, (b1 - 1) // BUCK + 1):
                    for rec in recs.get((nm, bk), ()):
                        if rec[0] < p1 and p0 < rec[1] and rec[2] < b1 and b0 < rec[3]:
                            d.add(rec[4])
            for (nm, p0, p1, b0, b1) in W:
                for bk in range(b0 // BUCK, (b1 - 1) // BUCK + 1):
                    L = recs.setdefault((nm, bk), [])
                    lo = max(b0, bk * BUCK)
                    hi = min(b1, (bk + 1) * BUCK)
                    L[:] = [rec for rec in L if not (p0 <= rec[0] and rec[1] <= p1 and
                                                     lo <= max(rec[2], bk * BUCK) and
                                                     min(rec[3], (bk + 1) * BUCK) <= hi)]
                    L.append((p0, p1, b0, b1, i, True))
            for (nm, p0, p1, b0, b1) in R:
                for bk in range(b0 // BUCK, (b1 - 1) // BUCK + 1):
                    L = recs.setdefault((nm, bk), [])
                    if not dma:
                        lo = max(b0, bk * BUCK)
                        hi = min(b1, (bk + 1) * BUCK)
                        L[:] = [rec for rec in L if rec[5] or ops[rec[4]][4] or ops[rec[4]][0] != eng or not (
                            p0 <= rec[0] and rec[1] <= p1 and lo <= max(rec[2], bk * BUCK) and
                            min(rec[3], (bk + 1) * BUCK) <= hi)]
                    L.append((p0, p1, b0, b1, i, False))
            if dma:
                q = eng
                k = dma_n[q]
                P = len(self.dsems[q])
                dsem[i] = (q, k % P, 16 * (k // P + 1))
                if k >= P:
                    d.add(dma_hist[q][k - P])
                dma_hist[q].append(i)
                dma_n[q] += 1
            d.discard(i)
            deps[i] = d
        known = {e: {} for e in self.engs}
        snap = [None] * n
        sig = [False] * n
        waits = [None] * n
        for i, (eng, fn, R, W, dma) in enumerate(ops):
            kn = known[eng]
            need = {}
            for d in deps[i]:
                de = ops[d][0]
                ddma = ops[d][4]
                if ddma:
                    q, si, val = dsem[d]
                    key = ("D", q, si)
                    if kn.get(key, 0) >= val:
                        continue
                    if key not in need or dsem[need[key]][2] < val:
                        need[key] = d
                else:
                    if de == "pe" and eng == "pe" and not dma:
                        continue
                    if kn.get(de, -1) >= pos[d]:
                        continue
                    if de not in need or pos[need[de]] < pos[d]:
                        need[de] = d
            wl = []
            for key, d in need.items():
                if ops[d][4]:
                    if kn.get(key, 0) >= dsem[d][2]:
                        continue
                else:
                    if kn.get(key, -1) >= pos[d]:
                        continue
                wl.append(d)
                sig[d] = True
                for k2, v2 in snap[d].items():
                    if kn.get(k2, -1) < v2:
                        kn[k2] = v2
            waits[i] = wl
            s = dict(kn)
            if dma:
                q, si, val = dsem[i]
                s[("D", q, si)] = val
            else:
                s[eng] = pos[i]
            snap[i] = s
        cum = [0] * n
        c = {e: 0 for e in self.engs}
        for i, (eng, fn, R, W, dma) in enumerate(ops):
            if not dma and sig[i]:
                c[eng] += 1
            cum[i] = c[eng]
        nw = 0
        for i, (eng, fn, R, W, dma) in enumerate(ops):
            E = self.engs[eng]
            for d in waits[i]:
                if ops[d][4]:
                    q, si, val = dsem[d]
                    E.wait_ge(self.dsems[q][si], val)
                else:
                    E.wait_ge(self.esem[ops[d][0]], cum[d])
                nw += 1
            ins = fn()
            if ins is None:
                continue
            if dma:
                q, si, val = dsem[i]
                ins.then_inc(self.dsems[q][si], 16)
            elif sig[i]:
                ins.then_inc(self.esem[eng], 1)
        self.dbginfo = (waits, sig, cum, pos, dsem)
        self.stats = dict(n_ops=n, n_waits=nw, per_eng=cnt, sigs=c)
        return self.stats


class K:
    def __init__(self, nc, S):
        self.nc = nc
        self.S = S

    def mm(self, out, lhsT, rhs, start=True, stop=True):
        nc = self.nc
        self.S.add("pe", lambda: nc.tensor.matmul(out, lhsT, rhs, start=start, stop=stop),
                   [lhsT, rhs], [out])

    def tr(self, out, in_, ident):
        nc = self.nc
        self.S.add("pe", lambda: nc.tensor.transpose(out, in_, ident), [in_, ident], [out])

    def act(self, out, in_, func, bias=None, scale=None, accum=None, eng="act"):
        nc = self.nc
        kw = {}
        r = [in_]
        if bias is not None:
            kw["bias"] = bias
            if not isinstance(bias, (int, float)):
                r.append(bias)
        if scale is not None:
            kw["scale"] = scale
            if not isinstance(scale, (int, float)):
                r.append(scale)
        w = [out]
        if accum is not None:
            kw["accum_out"] = accum
            w.append(accum)
        self.S.add("act", lambda: nc.scalar.activation(out, in_, func, **kw), r, w)

    def tt(self, out, a, b, op, eng="dve"):
        E = self.S.engs[eng]
        self.S.add(eng, lambda: E.tensor_tensor(out, a, b, op), [a, b], [out])

    def ts(self, out, a, s1, op0, s2=None, op1=None, eng="dve", accum=None):
        E = self.S.engs[eng]
        r = [a]
        if not isinstance(s1, (int, float)):
            r.append(s1)
        if s2 is not None and not isinstance(s2, (int, float)):
            r.append(s2)
        w = [out]
        kw = {}
        if accum is not None:
            kw["accum_out"] = accum
            w.append(accum)
        if op1 is None:
            self.S.add(eng, lambda: E.tensor_scalar(out, a, s1, None, op0, **kw), r, w)
        else:
            self.S.add(eng, lambda: E.tensor_scalar(out, a, s1, s2, op0, op1, **kw), r, w)

    def stt(self, out, in0, scalar, in1, op0, op1):
        nc = self.nc
        r = [in0, in1]
        if not isinstance(scalar, (int, float)):
            r.append(scalar)
        self.S.add("dve", lambda: nc.vector.scalar_tensor_tensor(out, in0, scalar, in1, op0, op1), r, [out])

    def cp(self, out, in_, eng="act"):
        nc = self.nc
        if eng == "act":
            self.S.add("act", lambda: nc.scalar.copy(out, in_), [in_], [out])
        else:
            E = self.S.engs[eng]
            self.S.add(eng, lambda: E.tensor_copy(out, in_), [in_], [out])

    def memset(self, ap, v, eng="dve"):
        E = self.S.engs[eng]
        self.S.add(eng, lambda: E.memset(ap, v), [], [ap])

    def scan(self, out, d0, d1, init, op0, op1):
        nc = self.nc
        r = [d0, d1]
        if not isinstance(init, (int, float)):
            r.append(init)
        self.S.add("dve", lambda: nc.vector.tensor_tensor_scan(out, d0, d1, init, op0, op1), r, [out])

    def bn_stats(self, out, in_):
        nc = self.nc
        self.S.add("dve", lambda: nc.vector.bn_stats(out, in_), [in_], [out])

    def bn_aggr(self, out, in_):
        nc = self.nc
        self.S.add("dve", lambda: nc.vector.bn_aggr(out, in_), [in_], [out])

    def recip(self, out, in_):
        nc = self.nc
        self.S.add("dve", lambda: nc.vector.reciprocal(out, in_), [in_], [out])

    def dma(self, out, in_, q="sp", **kw):
        E = self.S.engs[q]
        self.S.add(q, lambda: E.dma_start(out=out, in_=in_, **kw), [in_], [out], dma=True)

    def fence(self, aps, q="sp"):
        self.S.add(q, lambda: None, list(aps), [])

from concourse.bass_utils import run_bass_kernel_spmd
import math

D = 1024
DIN = 3856
DFF = 2816
NEG = -1e30
LN_EPS = 1e-5
NORM_EPS = 1e-6
ALPHA = 4.0 ** 0.25
NTOK = 2176
NSP = 24
import os
CPENG = os.environ.get('CPENG', 'act')
BLOCKS = [(0, 512, False), (512, 512, False), (1024, 512, False), (1536, 512, False), (2048, 128, True)]

C_ID, C_MAIP, C_MASP, C_MAIS, C_MASS, C_SEL, C_CMP, C_CMS, C_ROWM, C_EH, C_ONE = \
    0, 128, 256, 384, 512, 640, 1152, 1280, 1408, 1424, 1440
NCST = 1448


def make_consts():
    c = np.zeros((128, NCST), np.float32)
    ii = np.arange(128)
    c[:, C_ID:C_ID + 128] = np.eye(128)
    J, I = np.meshgrid(ii, ii, indexing="ij")
    same = (J // 8) == (I // 8)
    c[:, C_MAIP:C_MAIP + 128] = np.where(I >= J, 0, NEG)
    c[:, C_MASP:C_MASP + 128] = np.where(I > J, 0, NEG)
    c[:, C_MAIS:C_MAIS + 128] = np.where((I >= J) & same, 0, NEG)
    c[:, C_MASS:C_MASS + 128] = np.where((I > J) & same, 0, NEG)
    for h in range(4):
        c[h, C_SEL + h * 128:C_SEL + (h + 1) * 128] = 1.0
        c[:, C_EH + h * 4 + h] = 1.0
    c[0:4, C_CMP:C_CMP + 128] = 1.0
    c[0:4, C_CMP] = 0.0
    c[0:4, C_CMS:C_CMS + 128] = 1.0
    c[0:4, C_CMS:C_CMS + 128:8] = 0.0
    for s in range(16):
        c[s * 8:s * 8 + 8, C_ROWM + s] = 1.0
    c[:, C_ONE] = 1.0
    return c


def build(dbg=None, nblocks=5, nlayers=2, stage=9, limit=None, bsel=None):
    dbg = dbg or {}
    nc = bass.Bass('TRN2', target_bir_lowering=False)
    S = Sched(nc, n_sp_sems=NSP, n_pool_sems=64)
    k = K(nc, S)

    def din(name, shape):
        return nc.dram_tensor(name, list(shape), F32, kind="ExternalInput").ap()

    def dout(name, shape):
        return nc.dram_tensor(name, list(shape), F32, kind="ExternalOutput").ap()

    x_in = din("x_in", [NTOK, D])
    st_conv = din("st_conv", [2, 32, 256])
    st_gconv = din("st_gconv", [2, 48, 1536])
    st_gdn = din("st_gdn", [2, 16, 4, 128, 128])
    st_c = din("st_c", [2, 16, 4, 64, 64])
    st_n = din("st_n", [2, 16, 4, 64])
    st_m = din("st_m", [2, 16, 4])
    st_ffn = din("st_ffn", [2, 32, DFF])
    w_in = din("w_in", [2, D, DIN])
    conv_w = din("conv_w", [2, 3, 256])
    gdn_conv_w = din("gdn_conv_w", [2, 4, 1536])
    gdn_a_log = din("gdn_a_log", [2, 4])
    gdn_dt_bias = din("gdn_dt_bias", [2, 4])
    gdn_norm_w = din("gdn_norm_w", [2, 128])
    ml_i_bias = din("ml_i_bias", [2, 4])
    ml_f_bias = din("ml_f_bias", [2, 4])
    ml_norm_w = din("ml_norm_w", [2, 256])
    w_o = din("w_o", [2, D, D])
    ln1_g = din("ln1_g", [2, D])
    ln1_b = din("ln1_b", [2, D])
    w_up = din("w_up", [2, D, 2 * DFF])
    ffn_conv_w = din("ffn_conv_w", [2, 3, DFF])
    w_down = din("w_down", [2, DFF, D])
    ln2_g = din("ln2_g", [2, D])
    ln2_b = din("ln2_b", [2, D])
    consts = din("consts", [128, NCST])

    y = dout("y", [NTOK, D])
    p_conv = dout("p_conv", [2, 2, 256])
    p_gconv = dout("p_gconv", [2, 3, 1536])
    p_gdn = dout("p_gdn", [2, 4, 128, 128])
    p_c = dout("p_c", [2, 4, 64, 64])
    p_n = dout("p_n", [2, 4, 64])
    p_m = dout("p_m", [2, 4])
    p_ffn = dout("p_ffn", [2, 2, DFF])
    s_conv = dout("s_conv", [2, 32, 256])
    s_gconv = dout("s_gconv", [2, 48, 1536])
    s_gdn = dout("s_gdn", [2, 16, 4, 128, 128])
    s_c = dout("s_c", [2, 16, 4, 64, 64])
    s_n = dout("s_n", [2, 16, 4, 64])
    s_m = dout("s_m", [2, 16, 4])
    s_ffn = dout("s_ffn", [2, 32, DFF])
    outs_all = [y, p_conv, p_gconv, p_gdn, p_c, p_n, p_m, p_ffn, s_conv, s_gconv, s_gdn, s_c, s_n, s_m, s_ffn]
    dbg_out = {}
    for nm, shp in dbg.items():
        dbg_out[nm] = dout("dbg_" + nm, shp)
        outs_all.append(dbg_out[nm])

    def dump(nm, ap):
        if nm in dbg_out:
            k.dma(dbg_out[nm], ap)

    wb_in = nc.dram_tensor("wb_in", [2, D, DIN], BF16, kind="Internal").ap()
    wb_o = nc.dram_tensor("wb_o", [2, D, D], BF16, kind="Internal").ap()
    wb_up = nc.dram_tensor("wb_up", [2, D, 2 * DFF], BF16, kind="Internal").ap()
    wb_down = nc.dram_tensor("wb_down", [2, DFF, D], BF16, kind="Internal").ap()

    NCOL = 44000
    A = nc.sbuf_tensor("arena", [128, NCOL], F32).__enter__()
    PS = nc.psum_tensor("psum", [128, 4096], F32).__enter__()
    cur = [0]

    def al(n):
        o = cur[0]
        cur[0] += n
        assert cur[0] <= NCOL, cur[0]
        return o

    def V(o, n, p0=0, p1=128):
        return A[p0:p1, o:o + n]

    def Vb(o, n, p0=0, p1=128):
        return A[p0:p1, o:o + n].bitcast(BF16)

    cast_jobs = []
    job_index = {}
    for l in range(nlayers):
        for j, (c0, n) in enumerate(((0, 768), (768, 768), (1536, 768), (2304, 520), (2824, 512), (3336, 520))):
            job_index[("in", l, j)] = len(cast_jobs)
            cast_jobs.append([(wb_in[l][:, c0:c0 + n], w_in[l][:, c0:c0 + n])])
        job_index[("o", l, 0)] = len(cast_jobs)
        cast_jobs.append([(wb_o[l][:, 0:512], w_o[l][:, 0:512]), (wb_o[l][:, 512:1024], w_o[l][:, 512:1024])])
        for j in range(6):
            n = 512 if j < 5 else 256
            job_index[("up", l, j)] = len(cast_jobs)
            cast_jobs.append([(wb_up[l][:, 512 * j:512 * j + n], w_up[l][:, 512 * j:512 * j + n]),
                              (wb_up[l][:, DFF + 512 * j:DFF + 512 * j + n], w_up[l][:, DFF + 512 * j:DFF + 512 * j + n])])
        for j in range(4):
            r0 = 768 * j
            r1 = min(DFF, r0 + 768)
            job_index[("down", l, j)] = len(cast_jobs)
            cast_jobs.append([(wb_down[l][r0:r1, :], w_down[l][r0:r1, :])])
    cast_ptr = [0]

    def need(kind, l, j, la=2):
        tgt = min(len(cast_jobs), job_index[(kind, l, j)] + 1 + la)
        while cast_ptr[0] < tgt:
            for (o_, i_) in cast_jobs[cast_ptr[0]]:
                k.dma(o_, i_, q="pool")
            cast_ptr[0] += 1

    def in_job(c0):
        for j, (a0, n) in enumerate(((0, 768), (768, 768), (1536, 768), (2304, 520), (2824, 512), (3336, 520))):
            if a0 <= c0 < a0 + n:
                return j
        raise ValueError(c0)

    if os.environ.get('NOCAST') is None:
        need("in", 0, 0, la=1)
    if stage == -1:
        need('down', nlayers - 1, 3)
        k.fence([wb_in, wb_o, wb_up, wb_down])
        print(S.finalize())
        return nc
    o_cst = al(NCST)
    k.dma(V(o_cst, NCST), consts)
    ident = V(o_cst + C_ID, 128)

    def sel(h):
        return V(o_cst + C_SEL + h * 128, 128, 0, 4)

    def eh(h):
        return V(o_cst + C_EH + h * 4, 4)

    rowmask = V(o_cst + C_ROWM, 16)

    dctr = [0]

    def psd():
        b = dctr[0] % 4
        dctr[0] += 1
        return PS[:, b * 512:(b + 1) * 512]

    qctr = [0]

    def psq():
        q = qctr[0] % 4
        qctr[0] += 1
        return PS[:, 2048 + q * 512: 2048 + q * 512 + 128]

    o_cwa = al(2 * 2 * 3)
    o_cwg = al(2 * 12 * 4)
    o_cwf = al(2 * 22 * 3)
    o_lnf = al(2 * 4 * 8)
    o_gnw = al(2)
    o_gb = al(2 * 8)
    o_mnw = al(2 * 256)
    o_S = al(2 * 2 * 512)
    o_C = al(2 * 2 * 130)
    o_ha = al(2 * 2 * 2)
    o_hg = al(2 * 12 * 3)
    o_hf = al(2 * 22 * 2)
    o_mp = al(2)
    o_st_end = cur[0]
    o_xtok = al(4096)
    o_xT = al(2048)
    o_qkv = al(6144)
    o_ab = al(1024)
    o_sz = al(1024)
    o_mqk = al(2048)
    o_mvo = al(4 * 516)
    o_yc = al(2048)
    o_xe = al(2 * 516)
    o_acc = al(2 * 512)
    o_wg = al(64)
    o_graw = al(4 * 128)
    o_rows = al(26 * 128)
    o_cols = al(128)
    o_scan = al(6144)
    o_wbuf = al(5 * 1024)
    o_misc = al(640)
    o_lnb = o_scan + 2048
    print("arena cols used", cur[0])
    o_ptmp = o_scan
    o_sst = o_xtok + 1024
    o_sso = o_xtok + 1024 + 704
    o_S0 = o_qkv + 1536
    o_msk = o_S0 + 2048
    o_kdm = o_msk + 2176

    def cwa(l, u):
        return V(o_cwa + (l * 2 + u) * 3, 3)

    def cwg(l, u):
        return V(o_cwg + (l * 12 + u) * 4, 4)

    def cwf(l, u):
        return V(o_cwf + (l * 22 + u) * 3, 3)

    def lnf(l, which, kk):
        return V(o_lnf + (l * 4 + which) * 8 + kk, 1)

    def gbv(l, i):
        return V(o_gb + l * 8 + i, 1, 0, 4)

    ptmp = V(o_ptmp, DFF)
    k.memset(ptmp, 0.0)
    for l in range(nlayers):
        for (src, W_, nu, fn) in ((conv_w, 3, 2, cwa), (gdn_conv_w, 4, 12, cwg), (ffn_conv_w, 3, 22, cwf)):
            C_ = nu * 128
            k.dma(ptmp[0:W_, 0:C_], src[l])
            for u0 in range(0, nu, 4):
                ps = psd()
                n4 = min(4, nu - u0)
                for u in range(u0, u0 + n4):
                    k.tr(ps[:, (u - u0) * 128:(u - u0 + 1) * 128], ptmp[:, u * 128:(u + 1) * 128], ident)
                for u in range(u0, u0 + n4):
                    k.cp(fn(l, u), ps[:, (u - u0) * 128:(u - u0) * 128 + W_])
        for wi, src in enumerate((ln1_g, ln1_b, ln2_g, ln2_b)):
            k.dma(ptmp[0:8, 0:128], src[l].rearrange("(k p) -> k p", p=128))
            ps = psd()
            k.tr(ps[:, 0:128], ptmp[:, 0:128], ident)
            k.cp(V(o_lnf + (l * 4 + wi) * 8, 8), ps[:, 0:8])
        k.dma(V(o_gnw + l, 1), gdn_norm_w[l].rearrange("(p o) -> p o", o=1))
        for i, src in enumerate((gdn_a_log, gdn_dt_bias, ml_i_bias, ml_f_bias)):
            k.dma(gbv(l, i), src[l].rearrange("(p o) -> p o", o=1))
        k.act(gbv(l, 4), gbv(l, 0), AF.Exp)
        k.ts(gbv(l, 4), gbv(l, 4), -1.0, ALU.mult)
        k.ts(gbv(l, 5), gbv(l, 3), -1.0, ALU.mult)
        k.dma(V(o_mnw + l * 256, 256), ml_norm_w[l:l + 1, :].partition_broadcast(128))

    def Sst(l, par, h):
        return V(o_S + (l * 2 + par) * 512 + h * 128, 128)

    def Cst(l, par, h):
        po = (h % 2) * 64
        return V(o_C + (l * 2 + par) * 130 + (h // 2) * 65, 65, po, po + 64)

    k.memset(V(o_S, o_st_end - o_S), 0.0)
    k.memset(V(o_rows, 26 * 128), 0.0)

    def halo_a(l, u):
        return V(o_ha + (l * 2 + u) * 2, 2)

    def halo_g(l, u):
        return V(o_hg + (l * 12 + u) * 3, 3)

    def halo_f(l, u):
        return V(o_hf + (l * 22 + u) * 2, 2)

    def mprev(l):
        return V(o_mp + l, 1, 0, 4)

    def xtok(t):
        return V(o_xtok + t * 1024, 1024)

    def wbuf_next(ctr=[0]):
        i = ctr[0]
        ctr[0] += 1
        return Vb(o_wbuf + (i % 5) * 1024, 1024)

    def row(i, p0=0, p1=4):
        return V(o_rows + i * 128, 128, p0, p1)

    def slot(h, i):
        return V(o_scan + (h * 12 + i) * 128, 128)

    if stage == 0:
        k.dma(y[0:128, 0:512], V(o_cwa, 512))
        k.fence(outs_all)
        print(S.finalize())
        return nc
    def main_loops():
        for bi, (t0, Tb, is_s) in enumerate(BLOCKS[:nblocks]):
            if bsel is not None and bi not in bsel:
                continue
            nt = Tb // 128
            L = 8 if is_s else 128
            nch = 128 // L
            nch2 = max(nch, 2)
            MAI = V(o_cst + (C_MAIS if is_s else C_MAIP), 128)
            MAS = V(o_cst + (C_MASS if is_s else C_MASP), 128)
            cmask = V(o_cst + (C_CMS if is_s else C_CMP), 128, 0, 4)
            nlev = 3 if is_s else 7
            xTv = Vb(o_xT, 4 * Tb).rearrange("p (k t) -> p k t", k=8)
            ycv = Vb(o_yc, 4 * Tb).rearrange("p (k t) -> p k t", k=8)
            hTv = Vb(o_qkv, 11 * Tb).rearrange("p (k t) -> p k t", k=22)
            szv = Vb(o_sz, 2 * Tb).rearrange("p (u t) -> p u t", u=4)
            ab = V(o_ab, 2 * Tb).rearrange("p (u t) -> p u t", u=2)
            mqk = V(o_mqk, 4 * Tb).rearrange("p (u t) -> p u t", u=4)

            def qkv(u):
                return V(o_qkv + u * Tb, Tb)

            def vext(t):
                return V(o_mvo + t * 516, 260).rearrange("p (h e) -> p h e", h=4)

            def sigo(t):
                return V(o_mvo + t * 516 + 260, 256)

            def tview(ap):
                if not is_s:
                    return ap
                return ap.rearrange("p (s t) -> p s t", s=16)

            for t in range(nt):
                k.dma(xtok(t), x_in[t0 + t * 128:t0 + (t + 1) * 128, :])

            def transpose_to_xT(src_tok, t, scale_l=None, which=None):
                for k0 in range(0, 8, 4):
                    ps = psd()
                    for kk in range(k0, k0 + 4):
                        k.tr(ps[:, (kk - k0) * 128:(kk - k0 + 1) * 128], src_tok[:, kk * 128:(kk + 1) * 128], ident)
                    for kk in range(k0, k0 + 4):
                        o_ = xTv[:, kk, t * 128:(t + 1) * 128]
                        i_ = ps[:, (kk - k0) * 128:(kk - k0 + 1) * 128]
                        if scale_l is None:
                            k.cp(o_, i_, eng=CPENG)
                        else:
                            k.act(o_, i_, AF.Identity, scale=lnf(scale_l, which, kk), bias=lnf(scale_l, which + 1, kk))

            for t in range(nt):
                transpose_to_xT(xtok(t), t)

            xectr = [0]

            def conv_unit(src, W_, cw, halo, st_src, st_rows, mul_by=None):
                i = xectr[0]
                xectr[0] += 1
                wl = W_ - 1
                if not is_s:
                    full = V(o_xe + (i % 2) * 516, wl + Tb)
                    data, hv, tail = full[:, wl:wl + Tb], full[:, 0:wl], full[:, Tb:Tb + wl]
                    win = lambda j: full[:, j:j + Tb]
                else:
                    full = V(o_xe + (i % 2) * 516, 16 * (wl + 8)).rearrange("p (s t) -> p s t", s=16)
                    data, hv, tail = full[:, :, wl:wl + 8], full[:, :, 0:wl], full[:, :, 8:8 + wl]
                    win = lambda j: full[:, :, j:j + 8]
                if mul_by is None:
                    k.cp(data, tview(src))
                else:
                    k.tt(data, tview(src), tview(mul_by), ALU.mult)
                if not is_s:
                    k.cp(hv, halo, eng="pool")
                else:
                    k.cp(hv, st_src.rearrange("p (s t) -> p s t", s=16), eng="pool")
                acc = V(o_acc + (i % 2) * 512, Tb)
                accv = tview(acc)
                k.ts(accv, win(0), cw[:, 0:1], ALU.mult)
                for j in range(1, W_):
                    k.stt(accv, win(j), cw[:, j:j + 1], accv, ALU.mult, ALU.add)
                if not is_s:
                    k.cp(halo, tail, eng="pool")
                else:
                    k.cp(st_rows.rearrange("p (s t) -> p s t", s=16), tail, eng="pool")
                return acc

            def load_state_T(src, R, nu, dst_off):
                for c0 in range(0, nu, 4):
                    n4 = min(4, nu - c0)
                    k.dma(ptmp[0:R, 0:n4 * 128], src[:, c0 * 128:(c0 + n4) * 128])
                    ps = psd()
                    for u in range(c0, c0 + n4):
                        k.tr(ps[:, (u - c0) * 128:(u - c0 + 1) * 128], ptmp[:, (u - c0) * 128:(u - c0 + 1) * 128], ident)
                    for u in range(c0, c0 + n4):
                        k.cp(V(dst_off + u * R, R), ps[:, (u - c0) * 128:(u - c0) * 128 + R])

            def rows_out(get_in, nu, R, dram):
                for c0 in range(0, nu, 4):
                    n4 = min(4, nu - c0)
                    ps = psd()
                    for u in range(c0, c0 + n4):
                        gi = get_in(u)
                        k.tr(ps[:, (u - c0) * 128:(u - c0 + 1) * 128], A[:, int(gi.offset) % NCOL:int(gi.offset) % NCOL + 128], ident)
                    rb = V(o_misc + 128, 512, 0, R)
                    k.cp(rb[:, 0:n4 * 128], ps[0:R, 0:n4 * 128])
                    k.dma(dram[:, c0 * 128:(c0 + n4) * 128], rb[:, 0:n4 * 128])

            for l in range(nlayers):
                last_l = (l == nlayers - 1)
                xectr[0] = 0

                wsrc = wb_in[l]
                wup = wb_up[l]

                def load_w(src, c0, n):
                    if src is wsrc:
                        need("in", l, in_job(c0))
                    else:
                        need("up", l, (c0 % DFF) // 512)
                    wb = wbuf_next()
                    wv = wb[:, 0:8 * n].rearrange("p (k e) -> p k e", k=8)
                    k.dma(wv, src.rearrange("(k p) e -> p k e", p=128)[:, :, c0:c0 + n])
                    return wv

                def unit_mm(ps, wv, e0, M):
                    for kk in range(8):
                        k.mm(ps[0:M, 0:Tb], wv[:, kk, e0:e0 + M], xTv[:, kk, :], start=(kk == 0), stop=(kk == 7))

                wsrc = wb_in[l]
                actmp = V(o_lnb, 2 * Tb).rearrange("p (u t) -> p u t", u=2)
                wv = load_w(wsrc, 0, 256)
                for u in range(2):
                    ps = psd()
                    unit_mm(ps, wv, u * 128, 128)
                    k.cp(ab[:, u, :], ps[:, 0:Tb])
                wv = load_w(wsrc, 256, 256)
                for u in range(2):
                    ps = psd()
                    unit_mm(ps, wv, u * 128, 128)
                    k.cp(actmp[:, u, :], ps[:, 0:Tb], eng="dve")
                if is_s:
                    load_state_T(st_conv[l], 32, 2, o_sst)
                wv = load_w(wsrc, 512, 256)
                for u in range(2):
                    ps = psd()
                    unit_mm(ps, wv, u * 128, 128)
                    acc = conv_unit(ps[:, 0:Tb], 3, cwa(l, u), halo_a(l, u), V(o_sst + u * 32, 32), V(o_sso + u * 32, 32),
                                    mul_by=actmp[:, u, :])
                    k.tt(ycv[:, u, :], acc, ab[:, u, :], ALU.mult)
                if is_s:
                    rows_out(lambda u: V(o_sso + u * 32, 32), 2, 32, s_conv[l])
                elif bi == 3:
                    rows_out(lambda u: halo_a(l, u), 2, 2, p_conv[l])
                if is_s:
                    load_state_T(st_gconv[l], 48, 12, o_sst)
                for g in range(6):
                    wv = load_w(wsrc, 768 + 256 * g, 256)
                    for uu in range(2):
                        j = 2 * g + uu
                        ps = psd()
                        unit_mm(ps, wv, uu * 128, 128)
                        acc = conv_unit(ps[:, 0:Tb], 4, cwg(l, j), halo_g(l, j), V(o_sst + j * 48, 48), V(o_sso + j * 48, 48))
                        k.act(qkv(j), acc, AF.Silu)
                if is_s:
                    rows_out(lambda u: V(o_sso + u * 48, 48), 12, 48, s_gconv[l])
                elif bi == 3:
                    rows_out(lambda u: halo_g(l, u), 12, 3, p_gconv[l])
                for g in range(2):
                    wv = load_w(wsrc, 2304 + 256 * g, 256)
                    for uu in range(2):
                        ps = psd()
                        unit_mm(ps, wv, uu * 128, 128)
                        k.act(szv[:, 2 * g + uu, :], ps[:, 0:Tb], AF.Silu)
                for g in range(2):
                    wv = load_w(wsrc, 2824 + 256 * g, 256)
                    for uu in range(2):
                        ps = psd()
                        unit_mm(ps, wv, uu * 128, 128)
                        k.cp(mqk[:, 2 * g + uu, :], ps[:, 0:Tb])
                wv = load_w(wsrc, 3336, 256)
                for t in range(nt):
                    ps = psd()
                    for kk in range(8):
                        k.mm(ps[:, 0:256], xTv[:, kk, t * 128:(t + 1) * 128], wv[:, kk, :], start=(kk == 0), stop=(kk == 7))
                    k.cp(vext(t)[:, :, 0:64], ps[:, 0:256].rearrange("p (h e) -> p h e", h=4))
                    k.memset(vext(t)[:, :, 64:65], 1.0, eng="pool")
                wv = load_w(wsrc, 3592, 256)
                for t in range(nt):
                    ps = psd()
                    for kk in range(8):
                        k.mm(ps[:, 0:256], xTv[:, kk, t * 128:(t + 1) * 128], wv[:, kk, :], start=(kk == 0), stop=(kk == 7))
                    k.act(sigo(t), ps[:, 0:256], AF.Sigmoid)
                wg = Vb(o_wg, 64).rearrange("p (k e) -> p k e", k=8)
                wsr = wsrc.rearrange("(k p) e -> p k e", p=128)
                need("in", l, 5)
                k.dma(wg[:, :, 0:8], wsr[:, :, 2816:2824])
                k.dma(wg[:, :, 8:16], wsr[:, :, 3848:3856])
                if bi == 0 and l == 0:
                    dump("qkv", V(o_qkv, 12 * Tb))
                    dump("ycA", V(o_yc, 2048))

                for t in range(nt if stage >= 2 else 0):
                    ts_ = slice(t * 128, (t + 1) * 128)
                    gti = bi * 4 + t
                    par = gti % 2 if not is_s else 0
                    graw = [V(o_graw + i * 128, 128, 0, 4) for i in range(4)]
                    for i in range(4):
                        ps = psq()
                        for kk in range(8):
                            k.mm(ps[0:4, :], wg[:, kk, i * 4:i * 4 + 4], xTv[:, kk, ts_], start=(kk == 0), stop=(kk == 7))
                        k.cp(graw[i], ps[0:4, :], eng="dve")
                    r_G, r_lb, r_lrq, r_lrk, r_ra, r_rq, r_cj, r_kd, r_tmp, r_GL = [row(i) for i in range(10)]
                    k.act(r_tmp, graw[0], AF.Exp, bias=gbv(l, 1))
                    k.act(r_tmp, r_tmp, AF.Ln, bias=1.0)
                    k.ts(r_tmp, r_tmp, gbv(l, 4), ALU.mult)
                    k.scan(r_G, cmask, r_tmp, 0.0, ALU.mult, ALU.add)
                    k.act(r_lb, graw[1], AF.Exp, scale=-1.0)
                    k.act(r_lb, r_lb, AF.Ln, bias=1.0)
                    k.ts(r_lb, r_lb, -1.0, ALU.mult)
                    for qi, rr in ((0, r_lrq), (1, r_lrk)):
                        ps = psq()
                        for h in range(4):
                            sq = slot(h, 3 + qi)
                            k.act(sq, qkv(qi * 4 + h)[:, ts_], AF.Square)
                            k.mm(ps[0:4, :], eh(h), sq, start=(h == 0), stop=(h == 3))
                        k.act(rr, ps[0:4, :], AF.Ln, bias=NORM_EPS)
                        k.ts(rr, rr, -0.5, ALU.mult)
                    k.tt(r_ra, r_G, r_lb, ALU.add)
                    k.tt(r_ra, r_ra, r_lrk, ALU.add)
                    k.ts(r_rq, r_lrq, math.log(128.0 ** -0.5), ALU.add)
                    k.tt(r_rq, r_rq, r_G, ALU.add)
                    k.tt(r_cj, r_lrk, r_G, ALU.subtract)
                    G3 = r_G.rearrange("p (c l) -> p c l", l=L)
                    k.tt(r_kd.rearrange("p (c l) -> p c l", l=L), r_cj.rearrange("p (c l) -> p c l", l=L),
                         G3[:, :, L - 1:L].broadcast_to([4, nch, L]), ALU.add)
                    k.cp(r_GL[:, 0:nch].rearrange("p (c o) -> p c o", o=1), G3[:, :, L - 1:L], eng="dve")
                    craw = V(o_cols, 20)
                    cexp = V(o_cols + 20, 20)
                    ps = psq()
                    for ci, rr in enumerate((r_cj, r_lb, r_ra, r_kd, r_rq)):
                        k.mm(ps[:, ci * 4:ci * 4 + 4], rr, ident[0:4, 0:4], start=True, stop=True)
                    k.cp(craw, ps[:, 0:20], eng="dve")
                    k.act(cexp, ps[:, 0:20], AF.Exp)
                    elast = V(o_misc, 4 * nch2)
                    ps = psq()
                    for h in range(4):
                        k.mm(ps[:, h * nch2:h * nch2 + nch2], sel(h), r_GL[:, 0:nch2], start=True, stop=True)
                    k.act(elast, ps[:, 0:4 * nch2], AF.Exp)
                    if bi == 0 and l == 0 and t == 0:
                        dump("rows", V(o_rows, 10 * 128, 0, 4))
                        dump("cexp", V(o_cols, 40))

                    qT_ = [qkv(h)[:, ts_] for h in range(4)]
                    kT_ = [qkv(4 + h)[:, ts_] for h in range(4)]
                    vT_ = [qkv(8 + h)[:, ts_] for h in range(4)]
                    for h in range(4):
                        ps = psq()
                        k.mm(ps, sel(h), r_ra, start=True, stop=False)
                        k.mm(ps, ident, MAS, start=False, stop=True)
                        k.act(slot(h, 0), ps, AF.Exp, bias=craw[:, h:h + 1])
                        ps2 = psq()
                        k.mm(ps2, kT_[h], kT_[h])
                        k.stt(slot(h, 3), ps2, -1.0, slot(h, 0), ALU.mult, ALU.mult)
                    for h in range(4):
                        ps = psq()
                        k.mm(ps, sel(h), r_rq, start=True, stop=False)
                        k.mm(ps, ident, MAI, start=False, stop=True)
                        k.act(slot(h, 1), ps, AF.Exp, bias=craw[:, h:h + 1])
                        ps2 = psq()
                        k.mm(ps2, kT_[h], qT_[h])
                        k.tt(slot(h, 2), ps2, slot(h, 1), ALU.mult)
                    for h in range(4):
                        ps = psq()
                        k.tr(ps, slot(h, 3), ident)
                        k.cp(slot(h, 5), ps)
                        k.tt(slot(h, 7), slot(h, 3), ident, ALU.add, eng="pool")
                    cP = [(3, 5, 7)] * 4
                    for lev in range(1, nlev):
                        lastlev = (lev == nlev - 1)
                        nP = []
                        for h in range(4):
                            ipt, ipm, itt = cP[h]
                            npt, npm, ntt = 7 - ipt, 11 - ipm, 15 - itt
                            ps = psq()
                            k.mm(ps, slot(h, ipt), slot(h, ipm))
                            k.cp(slot(h, npm), ps)
                            if not lastlev:
                                ps2 = psq()
                                k.mm(ps2, slot(h, ipm), slot(h, ipt))
                                k.cp(slot(h, npt), ps2, eng="dve")
                            ps3 = psq()
                            k.mm(ps3, slot(h, npm), slot(h, itt))
                            k.tt(slot(h, ntt), ps3, slot(h, itt), ALU.add)
                            nP.append((npt, npm, ntt))
                        cP = nP
                    TTf = [slot(h, cP[h][2]) for h in range(4)]
                    for h in range(4):
                        ps = psq()
                        k.tr(ps, kT_[h], ident)
                        k.ts(slot(h, 0), ps, cexp[:, 8 + h:9 + h], ALU.mult)
                        k.act(slot(h, 1), ps, AF.Identity, scale=cexp[:, 12 + h:13 + h])
                        ps2 = psq()
                        k.tr(ps2, vT_[h], ident)
                        k.ts(slot(h, 9), ps2, cexp[:, 4 + h:5 + h], ALU.mult)
                    for h in range(4):
                        ps = psq()
                        k.mm(ps, slot(h, 0), TTf[h])
                        k.act(slot(h, 10), ps, AF.Identity, scale=-1.0)
                    O_ = [slot(h, 3) for h in range(4)]
                    if not is_s:
                        for h in range(4):
                            Sc, Sn = Sst(l, par, h), Sst(l, 1 - par, h)
                            ps = psq()
                            k.mm(ps, TTf[h], slot(h, 9), start=True, stop=False)
                            k.mm(ps, slot(h, 10), Sc, start=False, stop=True)
                            k.cp(slot(h, 11), ps)
                            ps1 = psq()
                            k.mm(ps1, qT_[h], Sc)
                            k.act(slot(h, 0), ps1, AF.Identity, scale=cexp[:, 16 + h:17 + h])
                            ps2 = psq()
                            k.mm(ps2, slot(h, 2), slot(h, 11))
                            k.tt(O_[h], ps2, slot(h, 0), ALU.add)
                            ps3 = psq()
                            k.mm(ps3, slot(h, 1), slot(h, 11))
                            k.stt(Sn, Sc, elast[:, h * nch2:h * nch2 + 1], ps3, ALU.mult, ALU.add)
                            if gti == 15:
                                k.dma(p_gdn[l, h], Sn)
                    else:
                        S0v = V(o_S0, 2048).rearrange("p (s v) -> p s v", s=16)
                        mskd = V(o_msk, 2176).rearrange("p (s r) -> p s r", r=136)[:, :, 0:8]
                        mskf = V(o_msk, 2048).rearrange("p (s i) -> p s i", s=16)
                        for h in range(4):
                            k.dma(S0v, st_gdn[l, :, h].rearrange("s k v -> k s v"))
                            if h == 0 and l == 0:
                                k.memset(V(o_msk, 2176), 0.0, eng="pool")
                            k.cp(mskd, slot(h, 10).rearrange("p (s r) -> p s r", r=8), eng="pool")
                            ps = psq()
                            k.mm(ps, TTf[h], slot(h, 9), start=True, stop=False)
                            for s in range(16):
                                k.mm(ps, mskf[:, s, :], S0v[:, s, :], start=False, stop=(s == 15))
                            k.cp(slot(h, 11), ps)
                            k.cp(mskd, qT_[h].rearrange("p (s r) -> p s r", r=8), eng="pool")
                            ps1 = psq()
                            for s in range(16):
                                k.mm(ps1, mskf[:, s, :], S0v[:, s, :], start=(s == 0), stop=(s == 15))
                            k.act(slot(h, 0), ps1, AF.Identity, scale=cexp[:, 16 + h:17 + h])
                            ps2 = psq()
                            k.mm(ps2, slot(h, 2), slot(h, 11))
                            k.tt(O_[h], ps2, slot(h, 0), ALU.add)
                            for s in range(16):
                                kdm = V(o_kdm + (s % 2) * 128, 128)
                                so = V(o_kdm + 256 + (s % 2) * 128, 128)
                                k.ts(kdm, slot(h, 1), rowmask[:, s:s + 1], ALU.mult)
                                ps3 = psq()
                                k.mm(ps3, kdm, slot(h, 11))
                                k.stt(so, S0v[:, s, :], elast[:, h * 16 + s:h * 16 + s + 1], ps3, ALU.mult, ALU.add)
                                k.dma(s_gdn[l, s, h], so)
                    if bi == 0 and l == 0 and t == 0:
                        dump("o_gdn", V(o_scan + 3 * 128, 128))
                        dump("TT0", TTf[0])
                    ss = V(o_cols + 40, 4)
                    for h in range(4):
                        k.act(slot(h, 0), O_[h], AF.Square, accum=ss[:, h:h + 1])
                    k.ts(ss, ss, 1.0 / 128, ALU.mult, NORM_EPS, ALU.add)
                    k.act(ss, ss, AF.Sqrt)
                    k.recip(ss, ss)
                    for h in range(4):
                        k.ts(slot(h, 9), O_[h], ss[:, h:h + 1], ALU.mult)
                        ps = psq()
                        k.tr(ps, slot(h, 9), ident)
                        k.stt(ycv[:, 2 + h, ts_], ps, V(o_gnw + l, 1), szv[:, h, ts_], ALU.mult, ALU.mult)

                    r_ig, r_lf, r_F, r_m, r_rD, r_cD, r_rI, r_em, r_kw, r_mp, r_t2, r_ch = [row(10 + i) for i in range(12)]
                    k.ts(r_ig, graw[2], gbv(l, 2), ALU.add)
                    k.act(r_lf, graw[3], AF.Exp, scale=-1.0, bias=gbv(l, 5))
                    k.act(r_lf, r_lf, AF.Ln, bias=1.0)
                    k.ts(r_lf, r_lf, -1.0, ALU.mult)
                    k.scan(r_F, cmask, r_lf, 0.0, ALU.mult, ALU.add)
                    F3 = r_F.rearrange("p (c l) -> p c l", l=L)
                    m3 = r_m.rearrange("p (c l) -> p c l", l=L)
                    if not is_s:
                        k.scan(r_m, r_lf, r_ig, mprev(l), ALU.add, ALU.max)
                        k.cp(r_mp, mprev(l).broadcast_to([4, 128]), eng="dve")
                    else:
                        m0 = r_ch[:, 64:80]
                        k.dma(m0, st_m[l].rearrange("s h -> h s"), allow_slow_non_contiguous=True)
                        k.cp(r_mp.rearrange("p (c l) -> p c l", l=8), m0.rearrange("p (c o) -> p c o", o=1).broadcast_to([4, 16, 8]), eng="dve")
                        ig3 = r_ig.rearrange("p (c l) -> p c l", l=8)
                        lf3 = r_lf.rearrange("p (c l) -> p c l", l=8)
                        t23 = r_t2.rearrange("p (c l) -> p c l", l=8)
                        k.cp(r_t2, r_ig, eng="dve")
                        k.tt(t23[:, :, 0:1], lf3[:, :, 0:1], m0.rearrange("p (c o) -> p c o", o=1), ALU.add)
                        k.tt(t23[:, :, 0:1], t23[:, :, 0:1], ig3[:, :, 0:1], ALU.max)
                        r_lf2 = row(25)
                        k.cp(r_lf2, r_lf, eng="dve")
                        k.memset(r_lf2.rearrange("p (c l) -> p c l", l=8)[:, :, 0:1], NEG)
                        k.scan(r_m, r_lf2, r_t2, 0.0, ALU.add, ALU.max)
                    k.tt(r_rD, r_F, r_m, ALU.subtract)
                    k.tt(r_cD, r_ig, r_F, ALU.subtract)
                    k.ts(r_cD, r_cD, math.log(0.125), ALU.add)
                    k.tt(r_rI, r_rD, r_mp, ALU.add)
                    k.ts(r_em, r_m, -1.0, ALU.mult)
                    chA = r_ch[:, 0:nch].rearrange("p (c o) -> p c o", o=1)
                    chB = r_ch[:, 16:16 + nch].rearrange("p (c o) -> p c o", o=1)
                    k.tt(chA, F3[:, :, L - 1:L], m3[:, :, L - 1:L], ALU.subtract)
                    k.tt(chB, chA, r_mp.rearrange("p (c l) -> p c l", l=L)[:, :, 0:1], ALU.add)
                    k.tt(r_kw.rearrange("p (c l) -> p c l", l=L), r_cD.rearrange("p (c l) -> p c l", l=L),
                         chA.broadcast_to([4, nch, L]), ALU.add)
                    if not is_s:
                        k.cp(mprev(l), r_m[:, 127:128], eng="dve")
                        if gti == 15:
                            k.dma(p_m[l].rearrange("(p o) -> p o", o=1), r_m[:, 127:128])
                    else:
                        k.dma(s_m[l].rearrange("s h -> h s"), m3[:, :, 7], allow_slow_non_contiguous=True)
                    mraw = V(o_cols + 48, 16)
                    mexp = V(o_cols + 64, 16)
                    ps = psq()
                    for ci, rr in enumerate((r_cD, r_em, r_kw, r_rI)):
                        k.mm(ps[:, ci * 4:ci * 4 + 4], rr, ident[0:4, 0:4], start=True, stop=True)
                    k.cp(mraw, ps[:, 0:16], eng="dve")
                    k.act(mexp, ps[:, 0:16], AF.Exp)
                    dcb = V(o_misc + 64, 4 * nch2)
                    ps = psq()
                    for h in range(4):
                        k.mm(ps[:, h * nch2:h * nch2 + nch2], sel(h), r_ch[:, 16:16 + nch2], start=True, stop=True)
                    k.act(dcb, ps[:, 0:4 * nch2], AF.Exp)
                    nq = V(o_scan + 48 * 128 - 4 * 65, 260).rearrange("p (h e) -> p h e", h=4)
                    kwt = [None] * 4
                    for hc in range(2):
                        ps = psq()
                        k.tr(ps, mqk[:, 2 + hc, ts_], ident)
                        for hh in range(2):
                            h = hc * 2 + hh
                            kwt[h] = slot(h, 4)[:, 0:64]
                            k.ts(kwt[h], ps[:, hh * 64:(hh + 1) * 64], mexp[:, 8 + h:9 + h], ALU.mult)
                    for h in range(4):
                        po = (h % 2) * 64
                        qTh = mqk[po:po + 64, h // 2, ts_]
                        kTh = mqk[po:po + 64, 2 + h // 2, ts_]
                        ps = psq()
                        k.mm(ps, sel(h), r_rD, start=True, stop=False)
                        k.mm(ps, ident, MAI, start=False, stop=True)
                        k.act(slot(h, 0), ps, AF.Exp, bias=mraw[:, h:h + 1])
                        ps2 = psq()
                        k.mm(ps2, kTh, qTh)
                        k.tt(slot(h, 1), ps2, slot(h, 0), ALU.mult)
                        if not is_s:
                            Cc, Cn = Cst(l, par, h), Cst(l, 1 - par, h)
                            ps1 = psq()
                            k.mm(ps1[:, 0:65], qTh, Cc)
                            k.act(slot(h, 2)[:, 0:65], ps1[:, 0:65], AF.Identity, scale=mexp[:, 12 + h:13 + h])
                            ps3 = psq()
                            k.mm(ps3[:, 0:65], slot(h, 1), vext(t)[:, h, :])
                            k.tt(nq[:, h, :], ps3[:, 0:65], slot(h, 2)[:, 0:65], ALU.add)
                            ps4 = psq()
                            k.mm(ps4[po:po + 64, 0:65], kwt[h], vext(t)[:, h, :])
                            k.stt(Cn, Cc, dcb[po:po + 64, h * nch2:h * nch2 + 1], ps4[po:po + 64, 0:65], ALU.mult, ALU.add)
                            if gti == 15:
                                k.dma(p_c[l, h], Cn[:, 0:64])
                                k.dma(p_n[l, h].rearrange("(p o) -> p o", o=1), Cn[:, 64:65])
                        else:
                            C0v = V(o_S0, 16 * 65, po, po + 64).rearrange("p (s e) -> p s e", s=16)
                            k.dma(C0v[:, :, 0:64], st_c[l, :, h].rearrange("s d e -> d s e"))
                            k.dma(C0v[:, :, 64:65], st_n[l, :, h].rearrange("s (d o) -> d s o", o=1), allow_slow_non_contiguous=True)
                            mskd = V(o_msk, 2176, po, po + 64).rearrange("p (s r) -> p s r", r=136)[:, :, 0:8]
                            mskf = V(o_msk, 2048, po, po + 64).rearrange("p (s i) -> p s i", s=16)
                            k.cp(mskd, qTh.rearrange("p (s r) -> p s r", r=8), eng="pool")
                            ps1 = psq()
                            for s in range(16):
                                k.mm(ps1[:, 0:65], mskf[:, s, :], C0v[:, s, :], start=(s == 0), stop=(s == 15))
                            k.act(slot(h, 2)[:, 0:65], ps1[:, 0:65], AF.Identity, scale=mexp[:, 12 + h:13 + h])
                            ps3 = psq()
                            k.mm(ps3[:, 0:65], slot(h, 1), vext(t)[:, h, :])
                            k.tt(nq[:, h, :], ps3[:, 0:65], slot(h, 2)[:, 0:65], ALU.add)
                            for s in range(16):
                                kwm = V(o_kdm + (s % 2) * 128, 64)
                                co = V(o_kdm + 256 + (s % 2) * 128, 65, po, po + 64)
                                k.ts(kwm, kwt[h], rowmask[:, s:s + 1], ALU.mult)
                                ps4 = psq()
                                k.mm(ps4[po:po + 64, 0:65], kwm, vext(t)[:, h, :])
                                k.stt(co, C0v[:, s, :], dcb[po:po + 64, h * 16 + s:h * 16 + s + 1], ps4[po:po + 64, 0:65], ALU.mult, ALU.add)
                                k.dma(s_c[l, s, h], co[:, 0:64])
                                k.dma(s_n[l, s, h].rearrange("(p o) -> p o", o=1), co[:, 64:65])
                    den = V(o_cols + 80, 4)
                    qn = nq[:, :, 64]
                    k.stt(den, qn, -1.0, qn, ALU.mult, ALU.max)
                    k.tt(den, den, mexp[:, 4:8], ALU.max)
                    k.recip(den, den)
                    hbuf = slot(0, 5)[:, 0:128]
                    hb2 = V(o_scan + 5 * 128, 128)
                    hb3 = V(o_scan + 6 * 128, 128)
                    hall = V(o_scan + 5 * 128, 256)
                    stats = V(o_cols + 84, 24)
                    mv = V(o_cols + 108, 8)
                    rstd = V(o_cols + 116, 4)
                    for h in range(4):
                        hh_ = hall[:, h * 64:(h + 1) * 64]
                        k.stt(hh_, nq[:, h, 0:64], den[:, h:h + 1], sigo(t)[:, h * 64:(h + 1) * 64], ALU.mult, ALU.mult)
                        k.bn_stats(stats[:, h * 6:(h + 1) * 6], hh_)
                        k.bn_aggr(mv[:, h * 2:(h + 1) * 2], stats[:, h * 6:(h + 1) * 6])
                    mv3 = mv.rearrange("p (h two) -> p h two", two=2)
                    k.ts(rstd.rearrange("p (h o) -> p h o", o=1), mv3[:, :, 1:2], LN_EPS, ALU.add)
                    k.act(rstd, rstd, AF.Sqrt)
                    k.recip(rstd, rstd)
                    for h in range(4):
                        hh_ = hall[:, h * 64:(h + 1) * 64]
                        k.ts(hh_, hh_, mv[:, 2 * h:2 * h + 1], ALU.subtract, rstd[:, h:h + 1], ALU.mult)
                    k.tt(hall, hall, V(o_mnw + l * 256, 256), ALU.mult, eng="pool")
                    for hc in range(2):
                        ps = psq()
                        k.tr(ps, hall[:, hc * 128:(hc + 1) * 128], ident)
                        k.cp(ycv[:, 6 + hc, ts_], ps)
                    if bi == 0 and l == 0 and t == 0:
                        dump("hml", hall)

                if bi == 0 and l == 0:
                    dump("ycat", V(o_yc, 2048))
                if stage < 3:
                    continue
                def ln_resid(t, pss):
                    xt_ = xtok(t)
                    for hf in range(4):
                        k.stt(xt_[:, hf * 256:(hf + 1) * 256], xt_[:, hf * 256:(hf + 1) * 256], ALPHA, pss[hf], ALU.mult, ALU.add)

                def ln_tile(t, pss, g_src, b_src, which, final_out=None, need_T=True):
                    xt_ = xtok(t)
                    if pss is not None:
                        ln_resid(t, pss)
                    stats = V(o_cols + 84, 12)
                    mv = V(o_cols + 108, 2)
                    rs = V(o_cols + 116, 2)
                    k.bn_stats(stats[:, 0:6], xt_[:, 0:512])
                    k.bn_stats(stats[:, 6:12], xt_[:, 512:1024])
                    k.bn_aggr(mv, stats)
                    k.ts(rs[:, 0:1], mv[:, 1:2], LN_EPS, ALU.add)
                    k.act(rs[:, 0:1], rs[:, 0:1], AF.Sqrt)
                    k.recip(rs[:, 0:1], rs[:, 0:1])
                    k.stt(rs[:, 1:2], mv[:, 0:1], -1.0, rs[:, 0:1], ALU.mult, ALU.mult)
                    ntok_ = V(o_scan + (t % 2) * 1024, 1024)
                    k.act(ntok_, xt_, AF.Identity, scale=rs[:, 0:1], bias=rs[:, 1:2])
                    if need_T:
                        transpose_to_xT(ntok_, t, scale_l=l, which=which)
                    gB = V(o_lnb, 1024)
                    bB = V(o_lnb + 1024, 1024)
                    k.tt(xt_, ntok_, gB, ALU.mult, eng="pool")
                    k.tt(xt_, xt_, bB, ALU.add, eng="pool")
                    if final_out is not None:
                        k.dma(final_out, xt_)

                k.dma(V(o_lnb, 1024), ln1_g[l:l + 1, :].partition_broadcast(128))
                k.dma(V(o_lnb + 1024, 1024), ln1_b[l:l + 1, :].partition_broadcast(128))
                need("o", l, 0)
                wos = []
                for g in range(4):
                    wb = wbuf_next()
                    wv = wb[:, 0:2048].rearrange("p (k e) -> p k e", k=8)
                    k.dma(wv, wb_o[l].rearrange("(k p) e -> p k e", p=128)[:, :, g * 256:(g + 1) * 256])
                    wos.append(wv)
                for t in range(nt):
                    ps = psd()
                    pss = []
                    for g in range(4):
                        if g == 2:
                            ps = psd()
                        pp = ps[:, (g % 2) * 256:(g % 2 + 1) * 256]
                        for kk in range(8):
                            k.mm(pp, ycv[:, kk, t * 128:(t + 1) * 128], wos[g][:, kk, :], start=(kk == 0), stop=(kk == 7))
                        pss.append(pp)
                    ln_tile(t, pss, ln1_g, ln1_b, 0)
                if bi == 0 and l == 0:
                    dump("x1", V(o_xtok, 4096))
                if stage < 4:
                    continue
                if is_s:
                    load_state_T(st_ffn[l], 32, 22, o_sst)
                wup = wb_up[l]
                for g in range(11):
                    wvg = load_w(wup, g * 256, 256)
                    wvv = load_w(wup, DFF + g * 256, 256)
                    for uu in range(2):
                        j = 2 * g + uu
                        ps = psd()
                        unit_mm(ps, wvg, uu * 128, 128)
                        acc = conv_unit(ps[:, 0:Tb], 3, cwf(l, j), halo_f(l, j), V(o_sst + j * 32, 32), V(o_sso + j * 32, 32))
                        k.act(acc, acc, AF.Silu)
                        ps2 = psd()
                        unit_mm(ps2, wvv, uu * 128, 128)
                        k.tt(hTv[:, j, :], ps2[:, 0:Tb], acc, ALU.mult)
                if is_s:
                    rows_out(lambda u: V(o_sso + u * 32, 32), 22, 32, s_ffn[l])
                elif bi == 3:
                    rows_out(lambda u: halo_f(l, u), 22, 2, p_ffn[l])
                k.dma(V(o_lnb, 1024), ln2_g[l:l + 1, :].partition_broadcast(128))
                k.dma(V(o_lnb + 1024, 1024), ln2_b[l:l + 1, :].partition_broadcast(128))
                for g in range(11):
                    need("down", l, (g * 256) // 768)
                    wb = wbuf_next()
                    wv = wb[:, 0:2048].rearrange("p (k e) -> p k e", k=2)
                    k.dma(wv, wb_down[l][g * 256:(g + 1) * 256, :].rearrange("(k p) e -> p k e", p=128))
                    for kk in range(2):
                        kc = 2 * g + kk
                        for t in range(nt):
                            for hf in range(2):
                                k.mm(PS[:, (t * 2 + hf) * 512:(t * 2 + hf + 1) * 512], hTv[:, kc, t * 128:(t + 1) * 128],
                                     wv[:, kk, hf * 512:(hf + 1) * 512], start=(kc == 0), stop=(kc == 21))
                for t in range(nt):
                    pss = [PS[:, (t * 2) * 512 + q * 256:(t * 2) * 512 + (q + 1) * 256] for q in range(4)]
                    ln_resid(t, pss)
                for t in range(nt):
                    fo = y[t0 + t * 128:t0 + (t + 1) * 128, :] if last_l else None
                    ln_tile(t, None, ln2_g, ln2_b, 2, final_out=fo, need_T=not last_l)

    S.limit = limit
    print('main loop starts at op', len(S.ops))
    try:
        main_loops()
    except StopIteration:
        print('LIMIT reached at', len(S.ops))
    S.closing = True
    k.fence(outs_all + [wb_in, wb_o, wb_up, wb_down])
    st = S.finalize()
    print(st)
    return nc


def _in_maps(inp, nlayers=2):
    consts = make_consts()
    maps = []
    f = lambda a: np.ascontiguousarray(np.asarray(a, dtype=np.float32))
    for c in range(8):
        sl = slice(16 * c, 16 * c + 16)
        m = {
            "x_in": f(np.concatenate([inp["x_prompt"][c], np.asarray(inp["x_sample"])[sl].reshape(128, D)], 0)),
            "st_conv": f(np.asarray(inp["state_conv_mix"])[:, sl].reshape(2, 32, 256)),
            "st_gconv": f(np.asarray(inp["state_gdn_conv"])[:, sl].reshape(2, 48, 1536)),
            "st_gdn": f(np.asarray(inp["state_gdn"])[:, sl]),
            "st_c": f(np.asarray(inp["state_mlstm_c"])[:, sl]),
            "st_n": f(np.asarray(inp["state_mlstm_n"])[:, sl]),
            "st_m": f(np.asarray(inp["state_mlstm_m"])[:, sl]),
            "st_ffn": f(np.asarray(inp["state_ffn_conv"])[:, sl].reshape(2, 32, DFF)),
            "consts": consts,
        }
        for nm in ("w_in", "conv_w", "gdn_conv_w", "gdn_a_log", "gdn_dt_bias", "gdn_norm_w", "ml_i_bias",
                   "ml_f_bias", "ml_norm_w", "w_o", "ln1_g", "ln1_b", "w_up", "ffn_conv_w", "w_down",
                   "ln2_g", "ln2_b"):
            m[nm] = f(inp[nm])
        maps.append(m)
    return maps


def kernel(**inp):
    nc = build()
    maps = _in_maps(inp)
    res = run_bass_kernel_spmd(nc, maps, core_ids=list(range(8)))
    R = res.results
    yp = np.stack([R[c]["y"][0:2048] for c in range(8)], 0)
    ys = np.concatenate([R[c]["y"][2048:].reshape(16, 8, D) for c in range(8)], 0)

    def pst(nm, shp):
        return np.stack([R[c][nm].reshape(shp) for c in range(8)], 1)

    def sst(nm, shp):
        return np.concatenate([R[c][nm].reshape(shp) for c in range(8)], 1)

    outs = (yp, ys,
            pst("p_conv", (2, 2, 256)), pst("p_gconv", (2, 3, 1536)), pst("p_gdn", (2, 4, 128, 128)),
            pst("p_c", (2, 4, 64, 64)), pst("p_n", (2, 4, 64)), pst("p_m", (2, 4)), pst("p_ffn", (2, 2, DFF)),
            sst("s_conv", (2, 16, 2, 256)), sst("s_gconv", (2, 16, 3, 1536)), sst("s_gdn", (2, 16, 4, 128, 128)),
            sst("s_c", (2, 16, 4, 64, 64)), sst("s_n", (2, 16, 4, 64)), sst("s_m", (2, 16, 4)),
            sst("s_ffn", (2, 16, 2, DFF)))
    return tuple(np.ascontiguousarray(o.astype(np.float32)) for o in outs)
```

```python
import numpy as np
import concourse.bass as bass
import concourse.mybir as mybir

F32 = mybir.dt.float32
BF16 = mybir.dt.bfloat16
AF = mybir.ActivationFunctionType
ALU = mybir.AluOpType
AX = mybir.AxisListType
ESZ = {F32: 4, BF16: 2, mybir.dt.float32r: 4}
BUCK = 512


def box(ap):
    t = ap.tensor
    es = ESZ[ap.dtype]
    dims = [(int(s), int(c)) for s, c in ap.ap]
    off = int(ap.offset)
    cls = type(t).__name__
    if cls.startswith("DRam"):
        ext = sum((c - 1) * abs(s) for s, c in dims) + 1
        return (t.name, 0, 1, off * es, (off + ext) * es)
    rowlen = 1
    for s in list(t.shape)[1:]:
        rowlen *= int(s)
    p0 = off // rowlen
    f0 = off % rowlen
    if dims[0][0] == rowlen or dims[0][1] == 1:
        pc = dims[0][1]
        rest = dims[1:]
    elif dims[0][0] == 0:
        pc = 1
        rest = dims[1:]
    else:
        pc = 1
        rest = dims
    ext = sum((c - 1) * abs(s) for s, c in rest) + 1
    if cls.startswith("PSum") or cls.startswith("Psum") or cls.startswith("PS"):
        b0 = (f0 * es) // 2048 * 2048
        b1 = ((f0 + ext) * es + 2047) // 2048 * 2048
        return (t.name, 0, 128, b0, b1)
    return (t.name, p0, p0 + pc, f0 * es, (f0 + ext) * es)


class Sched:
    def __init__(self, nc, n_sp_sems=16, n_pool_sems=8):
        self.nc = nc
        self.ops = []
        self.engs = {"pe": nc.tensor, "dve": nc.vector, "act": nc.scalar,
                     "pool": nc.gpsimd, "sp": nc.sync}
        self.esem = {e: nc.semaphore("sem_" + e).__enter__() for e in self.engs}
        self.dsems = {"sp": [nc.semaphore("dsp%d" % i).__enter__() for i in range(n_sp_sems)],
                      "pool": [nc.semaphore("dpl%d" % i).__enter__() for i in range(n_pool_sems)],
                      "act": []}

    limit = None

    def add(self, eng, fn, r, w, dma=False):
        if self.limit is not None and len(self.ops) >= self.limit and not getattr(self, 'closing', False):
            raise StopIteration("limit")
        rb = [box(a) for a in r]
        wb = [box(a) for a in w]
        wb = wb + [b for b in rb if b[0] == 'psum' or b[0] == 'pa']
        self.ops.append((eng, fn, rb, wb, dma))

    def finalize(self):
        ops = self.ops
        n = len(ops)
        recs = {}
        deps = [None] * n
        pos = [0] * n
        cnt = {e: 0 for e in self.engs}
        dma_n = {q: 0 for q in self.dsems}
        dma_hist = {q: [] for q in self.dsems}
        dsem = [None] * n
        for i, (eng, fn, R, W, dma) in enumerate(ops):
            pos[i] = cnt[eng]
            cnt[eng] += 1
            d = set()
            for (nm, p0, p1, b0, b1) in R:
                for bk in range(b0 // BUCK, (b1 - 1) // BUCK + 1):
                    for rec in recs.get((nm, bk), ()):
                        if rec[5] and rec[0] < p1 and p0 < rec[1] and rec[2] < b1 and b0 < rec[3]:
                            d.add(rec[4])
            for (nm, p0, p1, b0, b1) in W:
                for bk in range(b0 // BUCK, (b1 - 1) // BUCK + 1):
                    for rec in recs.get((nm, bk), ()):
                        if rec[0] < p1 and p0 < rec[1] and rec[2] < b1 and b0 < rec[3]:
                            d.add(rec[4])
            for (nm, p0, p1, b0, b1) in W:
                for bk in range(b0 // BUCK, (b1 - 1) // BUCK + 1):
                    L = recs.setdefault((nm, bk), [])
                    lo = max(b0, bk * BUCK)
                    hi = min(b1, (bk + 1) * BUCK)
                    L[:] = [rec for rec in L if not (p0 <= rec[0] and rec[1] <= p1 and
                                                     lo <= max(rec[2], bk * BUCK) and
                                                     min(rec[3], (bk + 1) * BUCK) <= hi)]
                    L.append((p0, p1, b0, b1, i, True))
            for (nm, p0, p1, b0, b1) in R:
                for bk in range(b0 // BUCK, (b1 - 1) // BUCK + 1):
                    L = recs.setdefault((nm, bk), [])
                    if not dma:
                        lo = max(b0, bk * BUCK)
                        hi = min(b1, (bk + 1) * BUCK)
                        L[:] = [rec for rec in L if rec[5] or ops[rec[4]][4] or ops[rec[4]][0] != eng or not (
                            p0 <= rec[0] and rec[1] <= p1 and lo <= max(rec[2], bk * BUCK) and
                            min(rec[3], (bk + 1) * BUCK) <= hi)]
                    L.append((p0, p1, b0, b1, i, False))
            if dma:
                q = eng
                k = dma_n[q]
                P = len(self.dsems[q])
                dsem[i] = (q, k % P, 16 * (k // P + 1))
                if k >= P:
                    d.add(dma_hist[q][k - P])
                dma_hist[q].append(i)
                dma_n[q] += 1
            d.discard(i)
            deps[i] = d
        known = {e: {} for e in self.engs}
        snap = [None] * n
        sig = [False] * n
        waits = [None] * n
        for i, (eng, fn, R, W, dma) in enumerate(ops):
            kn = known[eng]
            need = {}
            for d in deps[i]:
                de = ops[d][0]
                ddma = ops[d][4]
                if ddma:
                    q, si, val = dsem[d]
                    key = ("D", q, si)
                    if kn.get(key, 0) >= val:
                        continue
                    if key not in need or dsem[need[key]][2] < val:
                        need[key] = d
                else:
                    if de == "pe" and eng == "pe" and not dma:
                        continue
                    if kn.get(de, -1) >= pos[d]:
                        continue
                    if de not in need or pos[need[de]] < pos[d]:
                        need[de] = d
            wl = []
            for key, d in need.items():
                if ops[d][4]:
                    if kn.get(key, 0) >= dsem[d][2]:
                        continue
                else:
                    if kn.get(key, -1) >= pos[d]:
                        continue
                wl.append(d)
                sig[d] = True
                for k2, v2 in snap[d].items():
                    if kn.get(k2, -1) < v2:
                        kn[k2] = v2
            waits[i] = wl
            s = dict(kn)
            if dma:
                q, si, val = dsem[i]
                s[("D", q, si)] = val
            else:
                s[eng] = pos[i]
            snap[i] = s
        cum = [0] * n
        c = {e: 0 for e in self.engs}
        for i, (eng, fn, R, W, dma) in enumerate(ops):
            if not dma and sig[i]:
                c[eng] += 1
            cum[i] = c[eng]
        nw = 0
        for i, (eng, fn, R, W, dma) in enumerate(ops):
            E = self.engs[eng]
            for d in waits[i]:
                if ops[d][4]:
                    q, si, val = dsem[d]
                    E.wait_ge(self.dsems[q][si], val)
                else:
                    E.wait_ge(self.esem[ops[d][0]], cum[d])
                nw += 1
            ins = fn()
            if ins is None:
                continue
            if dma:
                q, si, val = dsem[i]
                ins.then_inc(self.dsems[q][si], 16)
            elif sig[i]:
                ins.then_inc(self.esem[eng], 1)
        self.dbginfo = (waits, sig, cum, pos, dsem)
        self.stats = dict(n_ops=n, n_waits=nw, per_eng=cnt, sigs=c)
        return self.stats


class K:
    def __init__(self, nc, S):
        self.nc = nc
        self.S = S

    def mm(self, out, lhsT, rhs, start=True, stop=True):
        nc = self.nc
        self.S.add("pe", lambda: nc.tensor.matmul(out, lhsT, rhs, start=start, stop=stop),
                   [lhsT, rhs], [out])

    def tr(self, out, in_, ident):
        nc = self.nc
        self.S.add("pe", lambda: nc.tensor.transpose(out, in_, ident), [in_, ident], [out])

    def act(self, out, in_, func, bias=None, scale=None, accum=None, eng="act"):
        nc = self.nc
        kw = {}
        r = [in_]
        if bias is not None:
            kw["bias"] = bias
            if not isinstance(bias, (int, float)):
                r.append(bias)
        if scale is not None:
            kw["scale"] = scale
            if not isinstance(scale, (int, float)):
                r.append(scale)
        w = [out]
        if accum is not None:
            kw["accum_out"] = accum
            w.append(accum)
        self.S.add("act", lambda: nc.scalar.activation(out, in_, func, **kw), r, w)

    def tt(self, out, a, b, op, eng="dve"):
        E = self.S.engs[eng]
        self.S.add(eng, lambda: E.tensor_tensor(out, a, b, op), [a, b], [out])

    def ts(self, out, a, s1, op0, s2=None, op1=None, eng="dve", accum=None):
        E = self.S.engs[eng]
        r = [a]
        if not isinstance(s1, (int, float)):
            r.append(s1)
        if s2 is not None and not isinstance(s2, (int, float)):
            r.append(s2)
        w = [out]
        kw = {}
        if accum is not None:
            kw["accum_out"] = accum
            w.append(accum)
        if op1 is None:
            self.S.add(eng, lambda: E.tensor_scalar(out, a, s1, None, op0, **kw), r, w)
        else:
            self.S.add(eng, lambda: E.tensor_scalar(out, a, s1, s2, op0, op1, **kw), r, w)

    def stt(self, out, in0, scalar, in1, op0, op1):
        nc = self.nc
        r = [in0, in1]
        if not isinstance(scalar, (int, float)):
            r.append(scalar)
        self.S.add("dve", lambda: nc.vector.scalar_tensor_tensor(out, in0, scalar, in1, op0, op1), r, [out])

    def cp(self, out, in_, eng="act"):
        nc = self.nc
        if eng == "act":
            self.S.add("act", lambda: nc.scalar.copy(out, in_), [in_], [out])
        else:
            E = self.S.engs[eng]
            self.S.add(eng, lambda: E.tensor_copy(out, in_), [in_], [out])

    def memset(self, ap, v, eng="dve"):
        E = self.S.engs[eng]
        self.S.add(eng, lambda: E.memset(ap, v), [], [ap])

    def scan(self, out, d0, d1, init, op0, op1):
        nc = self.nc
        r = [d0, d1]
        if not isinstance(init, (int, float)):
            r.append(init)
        self.S.add("dve", lambda: nc.vector.tensor_tensor_scan(out, d0, d1, init, op0, op1), r, [out])

    def bn_stats(self, out, in_):
        nc = self.nc
        self.S.add("dve", lambda: nc.vector.bn_stats(out, in_), [in_], [out])

    def bn_aggr(self, out, in_):
        nc = self.nc
        self.S.add("dve", lambda: nc.vector.bn_aggr(out, in_), [in_], [out])

    def recip(self, out, in_):
        nc = self.nc
        self.S.add("dve", lambda: nc.vector.reciprocal(out, in_), [in_], [out])

    def dma(self, out, in_, q="sp", **kw):
        E = self.S.engs[q]
        self.S.add(q, lambda: E.dma_start(out=out, in_=in_, **kw), [in_], [out], dma=True)

    def fence(self, aps, q="sp"):
        self.S.add(q, lambda: None, list(aps), [])

from concourse.bass_utils import run_bass_kernel_spmd
import math

D = 1024
DIN = 3856
DFF = 2816
NEG = -1e30
LN_EPS = 1e-5
NORM_EPS = 1e-6
ALPHA = 4.0 ** 0.25
NTOK = 2176
NSP = 24
import os
CPENG = os.environ.get('CPENG', 'act')
BLOCKS = [(0, 512, False), (512, 512, False), (1024, 512, False), (1536, 512, False), (2048, 128, True)]

C_ID, C_MAIP, C_MASP, C_MAIS, C_MASS, C_SEL, C_CMP, C_CMS, C_ROWM, C_EH, C_ONE = \
    0, 128, 256, 384, 512, 640, 1152, 1280, 1408, 1424, 1440
NCST = 1448


def make_consts():
    c = np.zeros((128, NCST), np.float32)
    ii = np.arange(128)
    c[:, C_ID:C_ID + 128] = np.eye(128)
    J, I = np.meshgrid(ii, ii, indexing="ij")
    same = (J // 8) == (I // 8)
    c[:, C_MAIP:C_MAIP + 128] = np.where(I >= J, 0, NEG)
    c[:, C_MASP:C_MASP + 128] = np.where(I > J, 0, NEG)
    c[:, C_MAIS:C_MAIS + 128] = np.where((I >= J) & same, 0, NEG)
    c[:, C_MASS:C_MASS + 128] = np.where((I > J) & same, 0, NEG)
    for h in range(4):
        c[h, C_SEL + h * 128:C_SEL + (h + 1) * 128] = 1.0
        c[:, C_EH + h * 4 + h] = 1.0
    c[0:4, C_CMP:C_CMP + 128] = 1.0
    c[0:4, C_CMP] = 0.0
    c[0:4, C_CMS:C_CMS + 128] = 1.0
    c[0:4, C_CMS:C_CMS + 128:8] = 0.0
    for s in range(16):
        c[s * 8:s * 8 + 8, C_ROWM + s] = 1.0
    c[:, C_ONE] = 1.0
    return c


def build(dbg=None, nblocks=5, nlayers=2, stage=9, limit=None, bsel=None):
    dbg = dbg or {}
    nc = bass.Bass('TRN2', target_bir_lowering=False)
    S = Sched(nc, n_sp_sems=NSP, n_pool_sems=64)
    k = K(nc, S)

    def din(name, shape):
        return nc.dram_tensor(name, list(shape), F32, kind="ExternalInput").ap()

    def dout(name, shape):
        return nc.dram_tensor(name, list(shape), F32, kind="ExternalOutput").ap()

    x_in = din("x_in", [NTOK, D])
    st_conv = din("st_conv", [2, 32, 256])
    st_gconv = din("st_gconv", [2, 48, 1536])
    st_gdn = din("st_gdn", [2, 16, 4, 128, 128])
    st_c = din("st_c", [2, 16, 4, 64, 64])
    st_n = din("st_n", [2, 16, 4, 64])
    st_m = din("st_m", [2, 16, 4])
    st_ffn = din("st_ffn", [2, 32, DFF])
    w_in = din("w_in", [2, D, DIN])
    conv_w = din("conv_w", [2, 3, 256])
    gdn_conv_w = din("gdn_conv_w", [2, 4, 1536])
    gdn_a_log = din("gdn_a_log", [2, 4])
    gdn_dt_bias = din("gdn_dt_bias", [2, 4])
    gdn_norm_w = din("gdn_norm_w", [2, 128])
    ml_i_bias = din("ml_i_bias", [2, 4])
    ml_f_bias = din("ml_f_bias", [2, 4])
    ml_norm_w = din("ml_norm_w", [2, 256])
    w_o = din("w_o", [2, D, D])
    ln1_g = din("ln1_g", [2, D])
    ln1_b = din("ln1_b", [2, D])
    w_up = din("w_up", [2, D, 2 * DFF])
    ffn_conv_w = din("ffn_conv_w", [2, 3, DFF])
    w_down = din("w_down", [2, DFF, D])
    ln2_g = din("ln2_g", [2, D])
    ln2_b = din("ln2_b", [2, D])
    consts = din("consts", [128, NCST])

    y = dout("y", [NTOK, D])
    p_conv = dout("p_conv", [2, 2, 256])
    p_gconv = dout("p_gconv", [2, 3, 1536])
    p_gdn = dout("p_gdn", [2, 4, 128, 128])
    p_c = dout("p_c", [2, 4, 64, 64])
    p_n = dout("p_n", [2, 4, 64])
    p_m = dout("p_m", [2, 4])
    p_ffn = dout("p_ffn", [2, 2, DFF])
    s_conv = dout("s_conv", [2, 32, 256])
    s_gconv = dout("s_gconv", [2, 48, 1536])
    s_gdn = dout("s_gdn", [2, 16, 4, 128, 128])
    s_c = dout("s_c", [2, 16, 4, 64, 64])
    s_n = dout("s_n", [2, 16, 4, 64])
    s_m = dout("s_m", [2, 16, 4])
    s_ffn = dout("s_ffn", [2, 32, DFF])
    outs_all = [y, p_conv, p_gconv, p_gdn, p_c, p_n, p_m, p_ffn, s_conv, s_gconv, s_gdn, s_c, s_n, s_m, s_ffn]
    dbg_out = {}
    for nm, shp in dbg.items():
        dbg_out[nm] = dout("dbg_" + nm, shp)
        outs_all.append(dbg_out[nm])

    def dump(nm, ap):
        if nm in dbg_out:
            k.dma(dbg_out[nm], ap)

    wb_in = nc.dram_tensor("wb_in", [2, D, DIN], BF16, kind="Internal").ap()
    wb_o = nc.dram_tensor("wb_o", [2, D, D], BF16, kind="Internal").ap()
    wb_up = nc.dram_tensor("wb_up", [2, D, 2 * DFF], BF16, kind="Internal").ap()
    wb_down = nc.dram_tensor("wb_down", [2, DFF, D], BF16, kind="Internal").ap()

    NCOL = 44000
    A = nc.sbuf_tensor("arena", [128, NCOL], F32).__enter__()
    PS = nc.psum_tensor("psum", [128, 4096], F32).__enter__()
    cur = [0]

    def al(n):
        o = cur[0]
        cur[0] += n
        assert cur[0] <= NCOL, cur[0]
        return o

    def V(o, n, p0=0, p1=128):
        return A[p0:p1, o:o + n]

    def Vb(o, n, p0=0, p1=128):
        return A[p0:p1, o:o + n].bitcast(BF16)

    cast_jobs = []
    job_index = {}
    for l in range(nlayers):
        for j, (c0, n) in enumerate(((0, 768), (768, 768), (1536, 768), (2304, 520), (2824, 512), (3336, 520))):
            job_index[("in", l, j)] = len(cast_jobs)
            cast_jobs.append([(wb_in[l][:, c0:c0 + n], w_in[l][:, c0:c0 + n])])
        job_index[("o", l, 0)] = len(cast_jobs)
        cast_jobs.append([(wb_o[l][:, 0:512], w_o[l][:, 0:512]), (wb_o[l][:, 512:1024], w_o[l][:, 512:1024])])
        for j in range(6):
            n = 512 if j < 5 else 256
            job_index[("up", l, j)] = len(cast_jobs)
            cast_jobs.append([(wb_up[l][:, 512 * j:512 * j + n], w_up[l][:, 512 * j:512 * j + n]),
                              (wb_up[l][:, DFF + 512 * j:DFF + 512 * j + n], w_up[l][:, DFF + 512 * j:DFF + 512 * j + n])])
        for j in range(4):
            r0 = 768 * j
            r1 = min(DFF, r0 + 768)
            job_index[("down", l, j)] = len(cast_jobs)
            cast_jobs.append([(wb_down[l][r0:r1, :], w_down[l][r0:r1, :])])
    cast_ptr = [0]

    def need(kind, l, j, la=2):
        tgt = min(len(cast_jobs), job_index[(kind, l, j)] + 1 + la)
        while cast_ptr[0] < tgt:
            for (o_, i_) in cast_jobs[cast_ptr[0]]:
                k.dma(o_, i_, q="pool")
            cast_ptr[0] += 1

    def in_job(c0):
        for j, (a0, n) in enumerate(((0, 768), (768, 768), (1536, 768), (2304, 520), (2824, 512), (3336, 520))):
            if a0 <= c0 < a0 + n:
                return j
        raise ValueError(c0)

    if os.environ.get('NOCAST') is None:
        need("in", 0, 0, la=1)
    if stage == -1:
        need('down', nlayers - 1, 3)
        k.fence([wb_in, wb_o, wb_up, wb_down])
        print(S.finalize())
        return nc
    o_cst = al(NCST)
    k.dma(V(o_cst, NCST), consts)
    ident = V(o_cst + C_ID, 128)

    def sel(h):
        return V(o_cst + C_SEL + h * 128, 128, 0, 4)

    def eh(h):
        return V(o_cst + C_EH + h * 4, 4)

    rowmask = V(o_cst + C_ROWM, 16)

    dctr = [0]

    def psd():
        b = dctr[0] % 4
        dctr[0] += 1
        return PS[:, b * 512:(b + 1) * 512]

    qctr = [0]

    def psq(n=128):
        q = qctr[0] % 4
        qctr[0] += 1
        return PS[:, 2048 + q * 512: 2048 + q * 512 + n]

    o_cwa = al(2 * 2 * 3)
    o_cwg = al(2 * 12 * 4)
    o_cwf = al(2 * 22 * 3)
    o_lnf = al(2 * 4 * 8)
    o_gnw = al(2)
    o_gb = al(2 * 8)
    o_mnw = al(2 * 256)
    o_S = al(2 * 2 * 512)
    o_C = al(2 * 2 * 130)
    o_ha = al(2 * 2 * 2)
    o_hg = al(2 * 12 * 3)
    o_hf = al(2 * 22 * 2)
    o_mp = al(2)
    o_st_end = cur[0]
    o_xtok = al(4096)
    o_xT = al(2048)
    o_qkv = al(6144)
    o_ab = al(1024)
    o_sz = al(1024)
    o_mqk = al(2048)
    o_mvo = al(4 * 516)
    o_yc = al(2048)
    o_xe = al(2 * 516)
    o_acc = al(2 * 512)
    o_wg = al(64)
    o_graw = al(4 * 128)
    o_rows = al(26 * 128)
    o_cols = al(128)
    o_scan = al(6144)
    o_wbuf = al(5 * 1024)
    o_misc = al(640)
    o_lnb = o_scan + 2048
    o_cb = al(5 * 64)
    print("arena cols used", cur[0])
    o_ptmp = o_scan
    o_sst = o_xtok + 1024
    o_sso = o_xtok + 1024 + 704
    o_S0 = o_qkv + 1536
    o_msk = o_S0 + 2048
    o_kdm = o_msk + 2176

    identb = Vb(o_cb, 64)
    for ci_, co_ in enumerate((C_ID, C_MAIP, C_MASP, C_MAIS, C_MASS)):
        k.cp(Vb(o_cb + ci_ * 64, 64), V(o_cst + co_, 128), eng="dve")

    def cwa(l, u):
        return V(o_cwa + (l * 2 + u) * 3, 3)

    def cwg(l, u):
        return V(o_cwg + (l * 12 + u) * 4, 4)

    def cwf(l, u):
        return V(o_cwf + (l * 22 + u) * 3, 3)

    def lnf(l, which, kk):
        return V(o_lnf + (l * 4 + which) * 8 + kk, 1)

    def gbv(l, i):
        return V(o_gb + l * 8 + i, 1, 0, 4)

    ptmp = V(o_ptmp, DFF)
    k.memset(ptmp, 0.0)
    for l in range(nlayers):
        for (src, W_, nu, fn) in ((conv_w, 3, 2, cwa), (gdn_conv_w, 4, 12, cwg), (ffn_conv_w, 3, 22, cwf)):
            C_ = nu * 128
            k.dma(ptmp[0:W_, 0:C_], src[l])
            for u0 in range(0, nu, 4):
                ps = psd()
                n4 = min(4, nu - u0)
                for u in range(u0, u0 + n4):
                    k.tr(ps[:, (u - u0) * 128:(u - u0 + 1) * 128], ptmp[:, u * 128:(u + 1) * 128], ident)
                for u in range(u0, u0 + n4):
                    k.cp(fn(l, u), ps[:, (u - u0) * 128:(u - u0) * 128 + W_])
        for wi, src in enumerate((ln1_g, ln1_b, ln2_g, ln2_b)):
            k.dma(ptmp[0:8, 0:128], src[l].rearrange("(k p) -> k p", p=128))
            ps = psd()
            k.tr(ps[:, 0:128], ptmp[:, 0:128], ident)
            k.cp(V(o_lnf + (l * 4 + wi) * 8, 8), ps[:, 0:8])
        k.dma(V(o_gnw + l, 1), gdn_norm_w[l].rearrange("(p o) -> p o", o=1))
        for i, src in enumerate((gdn_a_log, gdn_dt_bias, ml_i_bias, ml_f_bias)):
            k.dma(gbv(l, i), src[l].rearrange("(p o) -> p o", o=1))
        k.act(gbv(l, 4), gbv(l, 0), AF.Exp)
        k.ts(gbv(l, 4), gbv(l, 4), -1.0, ALU.mult)
        k.ts(gbv(l, 5), gbv(l, 3), -1.0, ALU.mult)
        k.dma(V(o_mnw + l * 256, 256), ml_norm_w[l:l + 1, :].partition_broadcast(128))

    def Sst(l, par, h):
        return V(o_S + (l * 2 + par) * 512 + h * 128, 128)

    def Cst(l, par, h):
        po = (h % 2) * 64
        return V(o_C + (l * 2 + par) * 130 + (h // 2) * 65, 65, po, po + 64)

    k.memset(V(o_S, o_st_end - o_S), 0.0)
    k.memset(V(o_rows, 26 * 128), 0.0)

    def halo_a(l, u):
        return V(o_ha + (l * 2 + u) * 2, 2)

    def halo_g(l, u):
        return V(o_hg + (l * 12 + u) * 3, 3)

    def halo_f(l, u):
        return V(o_hf + (l * 22 + u) * 2, 2)

    def mprev(l):
        return V(o_mp + l, 1, 0, 4)

    def xtok(t):
        return V(o_xtok + t * 1024, 1024)

    def wbuf_next(ctr=[0]):
        i = ctr[0]
        ctr[0] += 1
        return Vb(o_wbuf + (i % 5) * 1024, 1024)

    def row(i, p0=0, p1=4):
        return V(o_rows + i * 128, 128, p0, p1)

    def slot(h, i):
        return V(o_scan + (h * 12 + i) * 128, 128)

    if stage == 0:
        k.dma(y[0:128, 0:512], V(o_cwa, 512))
        k.fence(outs_all)
        print(S.finalize())
        return nc
    def main_loops():
        for bi, (t0, Tb, is_s) in enumerate(BLOCKS[:nblocks]):
            if bsel is not None and bi not in bsel:
                continue
            nt = Tb // 128
            L = 8 if is_s else 128
            nch = 128 // L
            nch2 = max(nch, 2)
            MAI = Vb(o_cb + (3 if is_s else 1) * 64, 64)
            MAS = Vb(o_cb + (4 if is_s else 2) * 64, 64)
            cmask = V(o_cst + (C_CMS if is_s else C_CMP), 128, 0, 4)
            nlev = 3 if is_s else 7
            xTv = Vb(o_xT, 4 * Tb).rearrange("p (k t) -> p k t", k=8)
            ycv = Vb(o_yc, 4 * Tb).rearrange("p (k t) -> p k t", k=8)
            hTv = Vb(o_qkv, 11 * Tb).rearrange("p (k t) -> p k t", k=22)
            szv = Vb(o_sz, 2 * Tb).rearrange("p (u t) -> p u t", u=4)
            ab = V(o_ab, 2 * Tb).rearrange("p (u t) -> p u t", u=2)
            mqk = V(o_mqk, 4 * Tb).rearrange("p (u t) -> p u t", u=4)

            def qkv(u):
                return V(o_qkv + u * Tb, Tb)

            def vext(t):
                return V(o_mvo + t * 516, 260).rearrange("p (h e) -> p h e", h=4)

            def sigo(t):
                return V(o_mvo + t * 516 + 260, 256)

            def tview(ap):
                if not is_s:
                    return ap
                return ap.rearrange("p (s t) -> p s t", s=16)

            for t in range(nt):
                k.dma(xtok(t), x_in[t0 + t * 128:t0 + (t + 1) * 128, :])

            def transpose_to_xT(src_tok, t, scale_l=None, which=None):
                for k0 in range(0, 8, 4):
                    ps = psd()
                    for kk in range(k0, k0 + 4):
                        k.tr(ps[:, (kk - k0) * 128:(kk - k0 + 1) * 128], src_tok[:, kk * 128:(kk + 1) * 128], ident)
                    for kk in range(k0, k0 + 4):
                        o_ = xTv[:, kk, t * 128:(t + 1) * 128]
                        i_ = ps[:, (kk - k0) * 128:(kk - k0 + 1) * 128]
                        if scale_l is None:
                            k.cp(o_, i_, eng=CPENG)
                        else:
                            k.act(o_, i_, AF.Identity, scale=lnf(scale_l, which, kk), bias=lnf(scale_l, which + 1, kk))

            for t in range(nt):
                transpose_to_xT(xtok(t), t)

            xectr = [0]

            def conv_unit(src, W_, cw, halo, st_src, st_rows, mul_by=None):
                i = xectr[0]
                xectr[0] += 1
                wl = W_ - 1
                if not is_s:
                    full = V(o_xe + (i % 2) * 516, wl + Tb)
                    data, hv, tail = full[:, wl:wl + Tb], full[:, 0:wl], full[:, Tb:Tb + wl]
                    win = lambda j: full[:, j:j + Tb]
                else:
                    full = V(o_xe + (i % 2) * 516, 16 * (wl + 8)).rearrange("p (s t) -> p s t", s=16)
                    data, hv, tail = full[:, :, wl:wl + 8], full[:, :, 0:wl], full[:, :, 8:8 + wl]
                    win = lambda j: full[:, :, j:j + 8]
                if mul_by is None:
                    k.cp(data, tview(src))
                else:
                    k.tt(data, tview(src), tview(mul_by), ALU.mult)
                if not is_s:
                    k.cp(hv, halo, eng="pool")
                else:
                    k.cp(hv, st_src.rearrange("p (s t) -> p s t", s=16), eng="pool")
                acc = V(o_acc + (i % 2) * 512, Tb)
                accv = tview(acc)
                k.ts(accv, win(0), cw[:, 0:1], ALU.mult)
                for j in range(1, W_):
                    k.stt(accv, win(j), cw[:, j:j + 1], accv, ALU.mult, ALU.add)
                if not is_s:
                    k.cp(halo, tail, eng="pool")
                else:
                    k.cp(st_rows.rearrange("p (s t) -> p s t", s=16), tail, eng="pool")
                return acc

            def load_state_T(src, R, nu, dst_off):
                for c0 in range(0, nu, 4):
                    n4 = min(4, nu - c0)
                    k.dma(ptmp[0:R, 0:n4 * 128], src[:, c0 * 128:(c0 + n4) * 128])
                    ps = psd()
                    for u in range(c0, c0 + n4):
                        k.tr(ps[:, (u - c0) * 128:(u - c0 + 1) * 128], ptmp[:, (u - c0) * 128:(u - c0 + 1) * 128], ident)
                    for u in range(c0, c0 + n4):
                        k.cp(V(dst_off + u * R, R), ps[:, (u - c0) * 128:(u - c0) * 128 + R])

            def rows_out(get_in, nu, R, dram):
                for c0 in range(0, nu, 4):
                    n4 = min(4, nu - c0)
                    ps = psd()
                    for u in range(c0, c0 + n4):
                        gi = get_in(u)
                        k.tr(ps[:, (u - c0) * 128:(u - c0 + 1) * 128], A[:, int(gi.offset) % NCOL:int(gi.offset) % NCOL + 128], ident)
                    rb = V(o_misc + 128, 512, 0, R)
                    k.cp(rb[:, 0:n4 * 128], ps[0:R, 0:n4 * 128])
                    k.dma(dram[:, c0 * 128:(c0 + n4) * 128], rb[:, 0:n4 * 128])

            for l in range(nlayers):
                last_l = (l == nlayers - 1)
                xectr[0] = 0

                wsrc = wb_in[l]
                wup = wb_up[l]

                def load_w(src, c0, n):
                    if src is wsrc:
                        need("in", l, in_job(c0))
                    else:
                        need("up", l, (c0 % DFF) // 512)
                    wb = wbuf_next()
                    wv = wb[:, 0:8 * n].rearrange("p (k e) -> p k e", k=8)
                    k.dma(wv, src.rearrange("(k p) e -> p k e", p=128)[:, :, c0:c0 + n])
                    return wv

                def unit_mm(ps, wv, e0, M):
                    for kk in range(8):
                        k.mm(ps[0:M, 0:Tb], wv[:, kk, e0:e0 + M], xTv[:, kk, :], start=(kk == 0), stop=(kk == 7))

                wsrc = wb_in[l]
                actmp = V(o_lnb, 2 * Tb).rearrange("p (u t) -> p u t", u=2)
                wv = load_w(wsrc, 0, 256)
                for u in range(2):
                    ps = psd()
                    unit_mm(ps, wv, u * 128, 128)
                    k.cp(ab[:, u, :], ps[:, 0:Tb])
                wv = load_w(wsrc, 256, 256)
                for u in range(2):
                    ps = psd()
                    unit_mm(ps, wv, u * 128, 128)
                    k.cp(actmp[:, u, :], ps[:, 0:Tb], eng="dve")
                if is_s:
                    load_state_T(st_conv[l], 32, 2, o_sst)
                wv = load_w(wsrc, 512, 256)
                for u in range(2):
                    ps = psd()
                    unit_mm(ps, wv, u * 128, 128)
                    acc = conv_unit(ps[:, 0:Tb], 3, cwa(l, u), halo_a(l, u), V(o_sst + u * 32, 32), V(o_sso + u * 32, 32),
                                    mul_by=actmp[:, u, :])
                    k.tt(ycv[:, u, :], acc, ab[:, u, :], ALU.mult)
                if is_s:
                    rows_out(lambda u: V(o_sso + u * 32, 32), 2, 32, s_conv[l])
                elif bi == 3:
                    rows_out(lambda u: halo_a(l, u), 2, 2, p_conv[l])
                if is_s:
                    load_state_T(st_gconv[l], 48, 12, o_sst)
                for g in range(6):
                    wv = load_w(wsrc, 768 + 256 * g, 256)
                    for uu in range(2):
                        j = 2 * g + uu
                        ps = psd()
                        unit_mm(ps, wv, uu * 128, 128)
                        acc = conv_unit(ps[:, 0:Tb], 4, cwg(l, j), halo_g(l, j), V(o_sst + j * 48, 48), V(o_sso + j * 48, 48))
                        k.act(qkv(j), acc, AF.Silu)
                if is_s:
                    rows_out(lambda u: V(o_sso + u * 48, 48), 12, 48, s_gconv[l])
                elif bi == 3:
                    rows_out(lambda u: halo_g(l, u), 12, 3, p_gconv[l])
                for g in range(2):
                    wv = load_w(wsrc, 2304 + 256 * g, 256)
                    for uu in range(2):
                        ps = psd()
                        unit_mm(ps, wv, uu * 128, 128)
                        k.act(szv[:, 2 * g + uu, :], ps[:, 0:Tb], AF.Silu)
                for g in range(2):
                    wv = load_w(wsrc, 2824 + 256 * g, 256)
                    for uu in range(2):
                        ps = psd()
                        unit_mm(ps, wv, uu * 128, 128)
                        k.cp(mqk[:, 2 * g + uu, :], ps[:, 0:Tb])
                wv = load_w(wsrc, 3336, 256)
                for t in range(nt):
                    ps = psd()
                    for kk in range(8):
                        k.mm(ps[:, 0:256], xTv[:, kk, t * 128:(t + 1) * 128], wv[:, kk, :], start=(kk == 0), stop=(kk == 7))
                    k.cp(vext(t)[:, :, 0:64], ps[:, 0:256].rearrange("p (h e) -> p h e", h=4))
                    k.memset(vext(t)[:, :, 64:65], 1.0, eng="pool")
                wv = load_w(wsrc, 3592, 256)
                for t in range(nt):
                    ps = psd()
                    for kk in range(8):
                        k.mm(ps[:, 0:256], xTv[:, kk, t * 128:(t + 1) * 128], wv[:, kk, :], start=(kk == 0), stop=(kk == 7))
                    k.act(sigo(t), ps[:, 0:256], AF.Sigmoid)
                wg = Vb(o_wg, 64).rearrange("p (k e) -> p k e", k=8)
                wsr = wsrc.rearrange("(k p) e -> p k e", p=128)
                need("in", l, 5)
                k.dma(wg[:, :, 0:8], wsr[:, :, 2816:2824])
                k.dma(wg[:, :, 8:16], wsr[:, :, 3848:3856])
                if bi == 0 and l == 0:
                    dump("qkv", V(o_qkv, 12 * Tb))
                    dump("ycA", V(o_yc, 2048))

                for t in range(nt if stage >= 2 else 0):
                    ts_ = slice(t * 128, (t + 1) * 128)
                    gti = bi * 4 + t
                    par = gti % 2 if not is_s else 0
                    graw = [V(o_graw + i * 128, 128, 0, 4) for i in range(4)]
                    for i in range(4):
                        ps = psq()
                        for kk in range(8):
                            k.mm(ps[0:4, :], wg[:, kk, i * 4:i * 4 + 4], xTv[:, kk, ts_], start=(kk == 0), stop=(kk == 7))
                        k.cp(graw[i], ps[0:4, :], eng="dve")
                    r_G, r_lb, r_lrq, r_lrk, r_ra, r_rq, r_cj, r_kd, r_tmp, r_GL = [row(i) for i in range(10)]
                    k.act(r_tmp, graw[0], AF.Exp, bias=gbv(l, 1))
                    k.act(r_tmp, r_tmp, AF.Ln, bias=1.0)
                    k.ts(r_tmp, r_tmp, gbv(l, 4), ALU.mult)
                    k.scan(r_G, cmask, r_tmp, 0.0, ALU.mult, ALU.add)
                    k.act(r_lb, graw[1], AF.Exp, scale=-1.0)
                    k.act(r_lb, r_lb, AF.Ln, bias=1.0)
                    k.ts(r_lb, r_lb, -1.0, ALU.mult)
                    for qi, rr in ((0, r_lrq), (1, r_lrk)):
                        ps = psq()
                        for h in range(4):
                            sq = slot(h, 3 + qi)
                            k.act(sq, qkv(qi * 4 + h)[:, ts_], AF.Square)
                            k.mm(ps[0:4, :], eh(h), sq, start=(h == 0), stop=(h == 3))
                        k.act(rr, ps[0:4, :], AF.Ln, bias=NORM_EPS)
                        k.ts(rr, rr, -0.5, ALU.mult)
                    k.tt(r_ra, r_G, r_lb, ALU.add)
                    k.tt(r_ra, r_ra, r_lrk, ALU.add)
                    k.ts(r_rq, r_lrq, math.log(128.0 ** -0.5), ALU.add)
                    k.tt(r_rq, r_rq, r_G, ALU.add)
                    k.tt(r_cj, r_lrk, r_G, ALU.subtract)
                    G3 = r_G.rearrange("p (c l) -> p c l", l=L)
                    k.tt(r_kd.rearrange("p (c l) -> p c l", l=L), r_cj.rearrange("p (c l) -> p c l", l=L),
                         G3[:, :, L - 1:L].broadcast_to([4, nch, L]), ALU.add)
                    k.cp(r_GL[:, 0:nch].rearrange("p (c o) -> p c o", o=1), G3[:, :, L - 1:L], eng="dve")
                    craw = V(o_cols, 20)
                    cexp = V(o_cols + 20, 20)
                    ps = psq()
                    for ci, rr in enumerate((r_cj, r_lb, r_ra, r_kd, r_rq)):
                        k.mm(ps[:, ci * 4:ci * 4 + 4], rr, ident[0:4, 0:4], start=True, stop=True)
                    k.cp(craw, ps[:, 0:20], eng="dve")
                    k.act(cexp, ps[:, 0:20], AF.Exp)
                    elast = V(o_misc, 4 * nch2)
                    ps = psq()
                    for h in range(4):
                        k.mm(ps[:, h * nch2:h * nch2 + nch2], sel(h), r_GL[:, 0:nch2], start=True, stop=True)
                    k.act(elast, ps[:, 0:4 * nch2], AF.Exp)
                    if bi == 0 and l == 0 and t == 0:
                        dump("rows", V(o_rows, 10 * 128, 0, 4))
                        dump("cexp", V(o_cols, 40))

                    qT_ = [qkv(h)[:, ts_] for h in range(4)]
                    kT_ = [qkv(4 + h)[:, ts_] for h in range(4)]
                    vT_ = [qkv(8 + h)[:, ts_] for h in range(4)]
                    for h in range(4):
                        ps = psq()
                        k.mm(ps, sel(h), r_ra, start=True, stop=False)
                        k.mm(ps, identb, MAS, start=False, stop=True)
                        k.act(slot(h, 0), ps, AF.Exp, bias=craw[:, h:h + 1])
                        ps2 = psq()
                        k.mm(ps2, kT_[h], kT_[h])
                        k.stt(slot(h, 4), ps2, -1.0, slot(h, 0), ALU.mult, ALU.mult)
                    for h in range(4):
                        ps = psq()
                        k.mm(ps, sel(h), r_rq, start=True, stop=False)
                        k.mm(ps, identb, MAI, start=False, stop=True)
                        k.act(slot(h, 1), ps, AF.Exp, bias=craw[:, h:h + 1])
                        ps2 = psq()
                        k.mm(ps2, kT_[h], qT_[h])
                        k.tt(slot(h, 2), ps2, slot(h, 1), ALU.mult)
                    for h in range(4):
                        ps = psq()
                        k.tr(ps, slot(h, 4), ident)
                        k.cp(slot(h, 5), ps)
                        k.tt(slot(h, 3), slot(h, 4), ident, ALU.add, eng="pool")
                    for h in range(4):
                        ps = psq()
                        k.mm(ps, slot(h, 4), slot(h, 5))
                        k.cp(slot(h, 6), ps)
                        ps2 = psq()
                        k.mm(ps2, slot(h, 5), slot(h, 4))
                        k.cp(slot(h, 4), ps2, eng="dve")
                    cP = [(3, 6)] * 4
                    for lev in range(1, nlev):
                        lastlev = (lev == nlev - 1)
                        nP = []
                        for h in range(4):
                            pb, ipm = cP[h]
                            npb, npm = 10 - pb, 11 - ipm
                            if not lastlev:
                                ps = psq(256)
                                k.mm(ps, slot(h, ipm), V(o_scan + (h * 12 + pb) * 128, 256))
                                k.tt(slot(h, npb), ps[:, 0:128], slot(h, pb), ALU.add)
                                k.cp(slot(h, npb + 1), ps[:, 128:256])
                                ps2 = psq()
                                k.mm(ps2, slot(h, pb + 1), slot(h, ipm))
                                k.cp(slot(h, npm), ps2, eng="dve")
                            else:
                                ps = psq()
                                k.mm(ps, slot(h, ipm), slot(h, pb))
                                k.tt(slot(h, npb), ps, slot(h, pb), ALU.add)
                            nP.append((npb, npm))
                        cP = nP
                    TTf = [slot(h, cP[h][0]) for h in range(4)]
                    for h in range(4):
                        ps = psq()
                        k.tr(ps, kT_[h], ident)
                        k.ts(slot(h, 0), ps, cexp[:, 8 + h:9 + h], ALU.mult)
                        k.act(slot(h, 1), ps, AF.Identity, scale=cexp[:, 12 + h:13 + h])
                        ps2 = psq()
                        k.tr(ps2, vT_[h], ident)
                        k.ts(slot(h, 9), ps2, cexp[:, 4 + h:5 + h], ALU.mult)
                    for h in range(4):
                        ps = psq()
                        k.mm(ps, slot(h, 0), TTf[h])
                        k.act(slot(h, 10), ps, AF.Identity, scale=-1.0)
                    O_ = [slot(h, 5) for h in range(4)]
                    if not is_s:
                        for h in range(4):
                            Sc, Sn = Sst(l, par, h), Sst(l, 1 - par, h)
                            ps = psq()
                            k.mm(ps, TTf[h], slot(h, 9), start=True, stop=False)
                            k.mm(ps, slot(h, 10), Sc, start=False, stop=True)
                            k.cp(slot(h, 11), ps)
                            ps1 = psq()
                            k.mm(ps1, qT_[h], Sc)
                            k.act(slot(h, 0), ps1, AF.Identity, scale=cexp[:, 16 + h:17 + h])
                            ps2 = psq()
                            k.mm(ps2, slot(h, 2), slot(h, 11))
                            k.tt(O_[h], ps2, slot(h, 0), ALU.add)
                            ps3 = psq()
                            k.mm(ps3, slot(h, 1), slot(h, 11))
                            k.stt(Sn, Sc, elast[:, h * nch2:h * nch2 + 1], ps3, ALU.mult, ALU.add)
                            if gti == 15:
                                k.dma(p_gdn[l, h], Sn)
                    else:
                        S0v = V(o_S0, 2048).rearrange("p (s v) -> p s v", s=16)
                        mskd = V(o_msk, 2176).rearrange("p (s r) -> p s r", r=136)[:, :, 0:8]
                        mskf = V(o_msk, 2048).rearrange("p (s i) -> p s i", s=16)
                        for h in range(4):
                            k.dma(S0v, st_gdn[l, :, h].rearrange("s k v -> k s v"))
                            if h == 0 and l == 0:
                                k.memset(V(o_msk, 2176), 0.0, eng="pool")
                            k.cp(mskd, slot(h, 10).rearrange("p (s r) -> p s r", r=8), eng="pool")
                            ps = psq()
                            k.mm(ps, TTf[h], slot(h, 9), start=True, stop=False)
                            for s in range(16):
                                k.mm(ps, mskf[:, s, :], S0v[:, s, :], start=False, stop=(s == 15))
                            k.cp(slot(h, 11), ps)
                            k.cp(mskd, qT_[h].rearrange("p (s r) -> p s r", r=8), eng="pool")
                            ps1 = psq()
                            for s in range(16):
                                k.mm(ps1, mskf[:, s, :], S0v[:, s, :], start=(s == 0), stop=(s == 15))
                            k.act(slot(h, 0), ps1, AF.Identity, scale=cexp[:, 16 + h:17 + h])
                            ps2 = psq()
                            k.mm(ps2, slot(h, 2), slot(h, 11))
                            k.tt(O_[h], ps2, slot(h, 0), ALU.add)
                            for s in range(16):
                                kdm = V(o_kdm + (s % 2) * 128, 128)
                                so = V(o_kdm + 256 + (s % 2) * 128, 128)
                                k.ts(kdm, slot(h, 1), rowmask[:, s:s + 1], ALU.mult)
                                ps3 = psq()
                                k.mm(ps3, kdm, slot(h, 11))
                                k.stt(so, S0v[:, s, :], elast[:, h * 16 + s:h * 16 + s + 1], ps3, ALU.mult, ALU.add)
                                k.dma(s_gdn[l, s, h], so)
                    if bi == 0 and l == 0 and t == 0:
                        dump("o_gdn", V(o_scan + 5 * 128, 128))
                        dump("TT0", TTf[0])
                    ss = V(o_cols + 40, 4)
                    for h in range(4):
                        k.act(slot(h, 0), O_[h], AF.Square, accum=ss[:, h:h + 1])
                    k.ts(ss, ss, 1.0 / 128, ALU.mult, NORM_EPS, ALU.add)
                    k.act(ss, ss, AF.Sqrt)
                    k.recip(ss, ss)
                    for h in range(4):
                        k.ts(slot(h, 9), O_[h], ss[:, h:h + 1], ALU.mult)
                        ps = psq()
                        k.tr(ps, slot(h, 9), ident)
                        k.stt(ycv[:, 2 + h, ts_], ps, V(o_gnw + l, 1), szv[:, h, ts_], ALU.mult, ALU.mult)

                    r_ig, r_lf, r_F, r_m, r_rD, r_cD, r_rI, r_em, r_kw, r_mp, r_t2, r_ch = [row(10 + i) for i in range(12)]
                    k.ts(r_ig, graw[2], gbv(l, 2), ALU.add)
                    k.act(r_lf, graw[3], AF.Exp, scale=-1.0, bias=gbv(l, 5))
                    k.act(r_lf, r_lf, AF.Ln, bias=1.0)
                    k.ts(r_lf, r_lf, -1.0, ALU.mult)
                    k.scan(r_F, cmask, r_lf, 0.0, ALU.mult, ALU.add)
                    F3 = r_F.rearrange("p (c l) -> p c l", l=L)
                    m3 = r_m.rearrange("p (c l) -> p c l", l=L)
                    if not is_s:
                        k.scan(r_m, r_lf, r_ig, mprev(l), ALU.add, ALU.max)
                        k.cp(r_mp, mprev(l).broadcast_to([4, 128]), eng="dve")
                    else:
                        m0 = r_ch[:, 64:80]
                        k.dma(m0, st_m[l].rearrange("s h -> h s"), allow_slow_non_contiguous=True)
                        k.cp(r_mp.rearrange("p (c l) -> p c l", l=8), m0.rearrange("p (c o) -> p c o", o=1).broadcast_to([4, 16, 8]), eng="dve")
                        ig3 = r_ig.rearrange("p (c l) -> p c l", l=8)
                        lf3 = r_lf.rearrange("p (c l) -> p c l", l=8)
                        t23 = r_t2.rearrange("p (c l) -> p c l", l=8)
                        k.cp(r_t2, r_ig, eng="dve")
                        k.tt(t23[:, :, 0:1], lf3[:, :, 0:1], m0.rearrange("p (c o) -> p c o", o=1), ALU.add)
                        k.tt(t23[:, :, 0:1], t23[:, :, 0:1], ig3[:, :, 0:1], ALU.max)
                        r_lf2 = row(25)
                        k.cp(r_lf2, r_lf, eng="dve")
                        k.memset(r_lf2.rearrange("p (c l) -> p c l", l=8)[:, :, 0:1], NEG)
                        k.scan(r_m, r_lf2, r_t2, 0.0, ALU.add, ALU.max)
                    k.tt(r_rD, r_F, r_m, ALU.subtract)
                    k.tt(r_cD, r_ig, r_F, ALU.subtract)
                    k.ts(r_cD, r_cD, math.log(0.125), ALU.add)
                    k.tt(r_rI, r_rD, r_mp, ALU.add)
                    k.ts(r_em, r_m, -1.0, ALU.mult)
                    chA = r_ch[:, 0:nch].rearrange("p (c o) -> p c o", o=1)
                    chB = r_ch[:, 16:16 + nch].rearrange("p (c o) -> p c o", o=1)
                    k.tt(chA, F3[:, :, L - 1:L], m3[:, :, L - 1:L], ALU.subtract)
                    k.tt(chB, chA, r_mp.rearrange("p (c l) -> p c l", l=L)[:, :, 0:1], ALU.add)
                    k.tt(r_kw.rearrange("p (c l) -> p c l", l=L), r_cD.rearrange("p (c l) -> p c l", l=L),
                         chA.broadcast_to([4, nch, L]), ALU.add)
                    if not is_s:
                        k.cp(mprev(l), r_m[:, 127:128], eng="dve")
                        if gti == 15:
                            k.dma(p_m[l].rearrange("(p o) -> p o", o=1), r_m[:, 127:128])
                    else:
                        k.dma(s_m[l].rearrange("s h -> h s"), m3[:, :, 7], allow_slow_non_contiguous=True)
                    mraw = V(o_cols + 48, 16)
                    mexp = V(o_cols + 64, 16)
                    ps = psq()
                    for ci, rr in enumerate((r_cD, r_em, r_kw, r_rI)):
                        k.mm(ps[:, ci * 4:ci * 4 + 4], rr, ident[0:4, 0:4], start=True, stop=True)
                    k.cp(mraw, ps[:, 0:16], eng="dve")
                    k.act(mexp, ps[:, 0:16], AF.Exp)
                    dcb = V(o_misc + 64, 4 * nch2)
                    ps = psq()
                    for h in range(4):
                        k.mm(ps[:, h * nch2:h * nch2 + nch2], sel(h), r_ch[:, 16:16 + nch2], start=True, stop=True)
                    k.act(dcb, ps[:, 0:4 * nch2], AF.Exp)
                    nq = V(o_scan + 48 * 128 - 4 * 65, 260).rearrange("p (h e) -> p h e", h=4)
                    kwt = [None] * 4
                    for hc in range(2):
                        ps = psq()
                        k.tr(ps, mqk[:, 2 + hc, ts_], ident)
                        for hh in range(2):
                            h = hc * 2 + hh
                            kwt[h] = slot(h, 4)[:, 0:64]
                            k.ts(kwt[h], ps[:, hh * 64:(hh + 1) * 64], mexp[:, 8 + h:9 + h], ALU.mult)
                    for h in range(4):
                        po = (h % 2) * 64
                        qTh = mqk[po:po + 64, h // 2, ts_]
                        kTh = mqk[po:po + 64, 2 + h // 2, ts_]
                        ps = psq()
                        k.mm(ps, sel(h), r_rD, start=True, stop=False)
                        k.mm(ps, identb, MAI, start=False, stop=True)
                        k.act(slot(h, 0), ps, AF.Exp, bias=mraw[:, h:h + 1])
                        ps2 = psq()
                        k.mm(ps2, kTh, qTh)
                        k.tt(slot(h, 1), ps2, slot(h, 0), ALU.mult)
                        if not is_s:
                            Cc, Cn = Cst(l, par, h), Cst(l, 1 - par, h)
                            ps1 = psq()
                            k.mm(ps1[:, 0:65], qTh, Cc)
                            k.act(slot(h, 2)[:, 0:65], ps1[:, 0:65], AF.Identity, scale=mexp[:, 12 + h:13 + h])
                            ps3 = psq()
                            k.mm(ps3[:, 0:65], slot(h, 1), vext(t)[:, h, :])
                            k.tt(nq[:, h, :], ps3[:, 0:65], slot(h, 2)[:, 0:65], ALU.add)
                            ps4 = psq()
                            k.mm(ps4[po:po + 64, 0:65], kwt[h], vext(t)[:, h, :])
                            k.stt(Cn, Cc, dcb[po:po + 64, h * nch2:h * nch2 + 1], ps4[po:po + 64, 0:65], ALU.mult, ALU.add)
                            if gti == 15:
                                k.dma(p_c[l, h], Cn[:, 0:64])
                                k.dma(p_n[l, h].rearrange("(p o) -> p o", o=1), Cn[:, 64:65])
                        else:
                            C0v = V(o_S0, 16 * 65, po, po + 64).rearrange("p (s e) -> p s e", s=16)
                            k.dma(C0v[:, :, 0:64], st_c[l, :, h].rearrange("s d e -> d s e"))
                            k.dma(C0v[:, :, 64:65], st_n[l, :, h].rearrange("s (d o) -> d s o", o=1), allow_slow_non_contiguous=True)
                            mskd = V(o_msk, 2176, po, po + 64).rearrange("p (s r) -> p s r", r=136)[:, :, 0:8]
                            mskf = V(o_msk, 2048, po, po + 64).rearrange("p (s i) -> p s i", s=16)
                            k.cp(mskd, qTh.rearrange("p (s r) -> p s r", r=8), eng="pool")
                            ps1 = psq()
                            for s in range(16):
                                k.mm(ps1[:, 0:65], mskf[:, s, :], C0v[:, s, :], start=(s == 0), stop=(s == 15))
                            k.act(slot(h, 2)[:, 0:65], ps1[:, 0:65], AF.Identity, scale=mexp[:, 12 + h:13 + h])
                            ps3 = psq()
                            k.mm(ps3[:, 0:65], slot(h, 1), vext(t)[:, h, :])
                            k.tt(nq[:, h, :], ps3[:, 0:65], slot(h, 2)[:, 0:65], ALU.add)
                            for s in range(16):
                                kwm = V(o_kdm + (s % 2) * 128, 64)
                                co = V(o_kdm + 256 + (s % 2) * 128, 65, po, po + 64)
                                k.ts(kwm, kwt[h], rowmask[:, s:s + 1], ALU.mult)
                                ps4 = psq()
                                k.mm(ps4[po:po + 64, 0:65], kwm, vext(t)[:, h, :])
                                k.stt(co, C0v[:, s, :], dcb[po:po + 64, h * 16 + s:h * 16 + s + 1], ps4[po:po + 64, 0:65], ALU.mult, ALU.add)
                                k.dma(s_c[l, s, h], co[:, 0:64])
                                k.dma(s_n[l, s, h].rearrange("(p o) -> p o", o=1), co[:, 64:65])
                    den = V(o_cols + 80, 4)
                    qn = nq[:, :, 64]
                    k.stt(den, qn, -1.0, qn, ALU.mult, ALU.max)
                    k.tt(den, den, mexp[:, 4:8], ALU.max)
                    k.recip(den, den)
                    hbuf = slot(0, 5)[:, 0:128]
                    hb2 = V(o_scan + 5 * 128, 128)
                    hb3 = V(o_scan + 6 * 128, 128)
                    hall = V(o_scan + 5 * 128, 256)
                    stats = V(o_cols + 84, 24)
                    mv = V(o_cols + 108, 8)
                    rstd = V(o_cols + 116, 4)
                    for h in range(4):
                        hh_ = hall[:, h * 64:(h + 1) * 64]
                        k.stt(hh_, nq[:, h, 0:64], den[:, h:h + 1], sigo(t)[:, h * 64:(h + 1) * 64], ALU.mult, ALU.mult)
                        k.bn_stats(stats[:, h * 6:(h + 1) * 6], hh_)
                        k.bn_aggr(mv[:, h * 2:(h + 1) * 2], stats[:, h * 6:(h + 1) * 6])
                    mv3 = mv.rearrange("p (h two) -> p h two", two=2)
                    k.ts(rstd.rearrange("p (h o) -> p h o", o=1), mv3[:, :, 1:2], LN_EPS, ALU.add)
                    k.act(rstd, rstd, AF.Sqrt)
                    k.recip(rstd, rstd)
                    for h in range(4):
                        hh_ = hall[:, h * 64:(h + 1) * 64]
                        k.ts(hh_, hh_, mv[:, 2 * h:2 * h + 1], ALU.subtract, rstd[:, h:h + 1], ALU.mult)
                    k.tt(hall, hall, V(o_mnw + l * 256, 256), ALU.mult, eng="pool")
                    for hc in range(2):
                        ps = psq()
                        k.tr(ps, hall[:, hc * 128:(hc + 1) * 128], ident)
                        k.cp(ycv[:, 6 + hc, ts_], ps)
                    if bi == 0 and l == 0 and t == 0:
                        dump("hml", hall)

                if bi == 0 and l == 0:
                    dump("ycat", V(o_yc, 2048))
                if stage < 3:
                    continue
                def ln_resid(t, pss):
                    xt_ = xtok(t)
                    for hf in range(4):
                        k.stt(xt_[:, hf * 256:(hf + 1) * 256], xt_[:, hf * 256:(hf + 1) * 256], ALPHA, pss[hf], ALU.mult, ALU.add)

                def ln_tile(t, pss, g_src, b_src, which, final_out=None, need_T=True):
                    xt_ = xtok(t)
                    if pss is not None:
                        ln_resid(t, pss)
                    stats = V(o_cols + 84, 12)
                    mv = V(o_cols + 108, 2)
                    rs = V(o_cols + 116, 2)
                    k.bn_stats(stats[:, 0:6], xt_[:, 0:512])
                    k.bn_stats(stats[:, 6:12], xt_[:, 512:1024])
                    k.bn_aggr(mv, stats)
                    k.ts(rs[:, 0:1], mv[:, 1:2], LN_EPS, ALU.add)
                    k.act(rs[:, 0:1], rs[:, 0:1], AF.Sqrt)
                    k.recip(rs[:, 0:1], rs[:, 0:1])
                    k.stt(rs[:, 1:2], mv[:, 0:1], -1.0, rs[:, 0:1], ALU.mult, ALU.mult)
                    ntok_ = V(o_scan + (t % 2) * 1024, 1024)
                    k.act(ntok_, xt_, AF.Identity, scale=rs[:, 0:1], bias=rs[:, 1:2])
                    if need_T:
                        transpose_to_xT(ntok_, t, scale_l=l, which=which)
                    gB = V(o_lnb, 1024)
                    bB = V(o_lnb + 1024, 1024)
                    k.tt(xt_, ntok_, gB, ALU.mult, eng="pool")
                    k.tt(xt_, xt_, bB, ALU.add, eng="pool")
                    if final_out is not None:
                        k.dma(final_out, xt_)

                k.dma(V(o_lnb, 1024), ln1_g[l:l + 1, :].partition_broadcast(128))
                k.dma(V(o_lnb + 1024, 1024), ln1_b[l:l + 1, :].partition_broadcast(128))
                need("o", l, 0)
                wos = []
                for g in range(4):
                    wb = wbuf_next()
                    wv = wb[:, 0:2048].rearrange("p (k e) -> p k e", k=8)
                    k.dma(wv, wb_o[l].rearrange("(k p) e -> p k e", p=128)[:, :, g * 256:(g + 1) * 256])
                    wos.append(wv)
                for t in range(nt):
                    ps = psd()
                    pss = []
                    for g in range(4):
                        if g == 2:
                            ps = psd()
                        pp = ps[:, (g % 2) * 256:(g % 2 + 1) * 256]
                        for kk in range(8):
                            k.mm(pp, ycv[:, kk, t * 128:(t + 1) * 128], wos[g][:, kk, :], start=(kk == 0), stop=(kk == 7))
                        pss.append(pp)
                    ln_tile(t, pss, ln1_g, ln1_b, 0)
                if bi == 0 and l == 0:
                    dump("x1", V(o_xtok, 4096))
                if stage < 4:
                    continue
                if is_s:
                    load_state_T(st_ffn[l], 32, 22, o_sst)
                wup = wb_up[l]
                for g in range(11):
                    wvg = load_w(wup, g * 256, 256)
                    wvv = load_w(wup, DFF + g * 256, 256)
                    for uu in range(2):
                        j = 2 * g + uu
                        ps = psd()
                        unit_mm(ps, wvg, uu * 128, 128)
                        acc = conv_unit(ps[:, 0:Tb], 3, cwf(l, j), halo_f(l, j), V(o_sst + j * 32, 32), V(o_sso + j * 32, 32))
                        k.act(acc, acc, AF.Silu)
                        ps2 = psd()
                        unit_mm(ps2, wvv, uu * 128, 128)
                        k.tt(hTv[:, j, :], ps2[:, 0:Tb], acc, ALU.mult)
                if is_s:
                    rows_out(lambda u: V(o_sso + u * 32, 32), 22, 32, s_ffn[l])
                elif bi == 3:
                    rows_out(lambda u: halo_f(l, u), 22, 2, p_ffn[l])
                k.dma(V(o_lnb, 1024), ln2_g[l:l + 1, :].partition_broadcast(128))
                k.dma(V(o_lnb + 1024, 1024), ln2_b[l:l + 1, :].partition_broadcast(128))
                for g in range(11):
                    need("down", l, (g * 256) // 768)
                    wb = wbuf_next()
                    wv = wb[:, 0:2048].rearrange("p (k e) -> p k e", k=2)
                    k.dma(wv, wb_down[l][g * 256:(g + 1) * 256, :].rearrange("(k p) e -> p k e", p=128))
                    for kk in range(2):
                        kc = 2 * g + kk
                        for t in range(nt):
                            for hf in range(2):
                                k.mm(PS[:, (t * 2 + hf) * 512:(t * 2 + hf + 1) * 512], hTv[:, kc, t * 128:(t + 1) * 128],
                                     wv[:, kk, hf * 512:(hf + 1) * 512], start=(kc == 0), stop=(kc == 21))
                for t in range(nt):
                    pss = [PS[:, (t * 2) * 512 + q * 256:(t * 2) * 512 + (q + 1) * 256] for q in range(4)]
                    ln_resid(t, pss)
                for t in range(nt):
                    fo = y[t0 + t * 128:t0 + (t + 1) * 128, :] if last_l else None
                    ln_tile(t, None, ln2_g, ln2_b, 2, final_out=fo, need_T=not last_l)

    S.limit = limit
    print('main loop starts at op', len(S.ops))
    try:
        main_loops()
    except StopIteration:
        print('LIMIT reached at', len(S.ops))
    S.closing = True
    k.fence(outs_all + [wb_in, wb_o, wb_up, wb_down])
    st = S.finalize()
    print(st)
    return nc


def _in_maps(inp, nlayers=2):
    consts = make_consts()
    maps = []
    f = lambda a: np.ascontiguousarray(np.asarray(a, dtype=np.float32))
    for c in range(8):
        sl = slice(16 * c, 16 * c + 16)
        m = {
            "x_in": f(np.concatenate([inp["x_prompt"][c], np.asarray(inp["x_sample"])[sl].reshape(128, D)], 0)),
            "st_conv": f(np.asarray(inp["state_conv_mix"])[:, sl].reshape(2, 32, 256)),
            "st_gconv": f(np.asarray(inp["state_gdn_conv"])[:, sl].reshape(2, 48, 1536)),
            "st_gdn": f(np.asarray(inp["state_gdn"])[:, sl]),
            "st_c": f(np.asarray(inp["state_mlstm_c"])[:, sl]),
            "st_n": f(np.asarray(inp["state_mlstm_n"])[:, sl]),
            "st_m": f(np.asarray(inp["state_mlstm_m"])[:, sl]),
            "st_ffn": f(np.asarray(inp["state_ffn_conv"])[:, sl].reshape(2, 32, DFF)),
            "consts": consts,
        }
        for nm in ("w_in", "conv_w", "gdn_conv_w", "gdn_a_log", "gdn_dt_bias", "gdn_norm_w", "ml_i_bias",
                   "ml_f_bias", "ml_norm_w", "w_o", "ln1_g", "ln1_b", "w_up", "ffn_conv_w", "w_down",
                   "ln2_g", "ln2_b"):
            m[nm] = f(inp[nm])
        maps.append(m)
    return maps


def kernel(**inp):
    nc = build()
    maps = _in_maps(inp)
    res = run_bass_kernel_spmd(nc, maps, core_ids=list(range(8)))
    R = res.results
    yp = np.stack([R[c]["y"][0:2048] for c in range(8)], 0)
    ys = np.concatenate([R[c]["y"][2048:].reshape(16, 8, D) for c in range(8)], 0)

    def pst(nm, shp):
        return np.stack([R[c][nm].reshape(shp) for c in range(8)], 1)

    def sst(nm, shp):
        return np.concatenate([R[c][nm].reshape(shp) for c in range(8)], 1)

    outs = (yp, ys,
            pst("p_conv", (2, 2, 256)), pst("p_gconv", (2, 3, 1536)), pst("p_gdn", (2, 4, 128, 128)),
            pst("p_c", (2, 4, 64, 64)), pst("p_n", (2, 4, 64)), pst("p_m", (2, 4)), pst("p_ffn", (2, 2, DFF)),
            sst("s_conv", (2, 16, 2, 256)), sst("s_gconv", (2, 16, 3, 1536)), sst("s_gdn", (2, 16, 4, 128, 128)),
            sst("s_c", (2, 16, 4, 64, 64)), sst("s_n", (2, 16, 4, 64)), sst("s_m", (2, 16, 4)),
            sst("s_ffn", (2, 16, 2, DFF)))
    return tuple(np.ascontiguousarray(o.astype(np.float32)) for o in outs)
```

```python
import numpy as np
import concourse.bass as bass
import concourse.mybir as mybir

F32 = mybir.dt.float32
BF16 = mybir.dt.bfloat16
AF = mybir.ActivationFunctionType
ALU = mybir.AluOpType
AX = mybir.AxisListType
ESZ = {F32: 4, BF16: 2, mybir.dt.float32r: 4}
BUCK = 512


def box(ap):
    t = ap.tensor
    es = ESZ[ap.dtype]
    dims = [(int(s), int(c)) for s, c in ap.ap]
    off = int(ap.offset)
    cls = type(t).__name__
    if cls.startswith("DRam"):
        ext = sum((c - 1) * abs(s) for s, c in dims) + 1
        return (t.name, 0, 1, off * es, (off + ext) * es)
    rowlen = 1
    for s in list(t.shape)[1:]:
        rowlen *= int(s)
    p0 = off // rowlen
    f0 = off % rowlen
    if dims[0][0] == rowlen or dims[0][1] == 1:
        pc = dims[0][1]
        rest = dims[1:]
    elif dims[0][0] == 0:
        pc = 1
        rest = dims[1:]
    else:
        pc = 1
        rest = dims
    ext = sum((c - 1) * abs(s) for s, c in rest) + 1
    if cls.startswith("PSum") or cls.startswith("Psum") or cls.startswith("PS"):
        b0 = (f0 * es) // 2048 * 2048
        b1 = ((f0 + ext) * es + 2047) // 2048 * 2048
        return (t.name, 0, 128, b0, b1)
    return (t.name, p0, p0 + pc, f0 * es, (f0 + ext) * es)


class Sched:
    def __init__(self, nc, n_sp_sems=16, n_pool_sems=8):
        self.nc = nc
        self.ops = []
        self.engs = {"pe": nc.tensor, "dve": nc.vector, "act": nc.scalar,
                     "pool": nc.gpsimd, "sp": nc.sync}
        self.esem = {e: nc.semaphore("sem_" + e).__enter__() for e in self.engs}
        self.dsems = {"sp": [nc.semaphore("dsp%d" % i).__enter__() for i in range(n_sp_sems)],
                      "pool": [nc.semaphore("dpl%d" % i).__enter__() for i in range(n_pool_sems)],
                      "act": []}

    limit = None

    def add(self, eng, fn, r, w, dma=False):
        if self.limit is not None and len(self.ops) >= self.limit and not getattr(self, 'closing', False):
            raise StopIteration("limit")
        rb = [box(a) for a in r]
        wb = [box(a) for a in w]
        wb = wb + [b for b in rb if b[0] == 'psum' or b[0] == 'pa']
        self.ops.append((eng, fn, rb, wb, dma))

    def finalize(self):
        ops = self.ops
        n = len(ops)
        recs = {}
        deps = [None] * n
        pos = [0] * n
        cnt = {e: 0 for e in self.engs}
        dma_n = {q: 0 for q in self.dsems}
        dma_hist = {q: [] for q in self.dsems}
        dsem = [None] * n
        for i, (eng, fn, R, W, dma) in enumerate(ops):
            pos[i] = cnt[eng]
            cnt[eng] += 1
            d = set()
            for (nm, p0, p1, b0, b1) in R:
                for bk in range(b0 // BUCK, (b1 - 1) // BUCK + 1):
                    for rec in recs.get((nm, bk), ()):
                        if rec[5] and rec[0] < p1 and p0 < rec[1] and rec[2] < b1 and b0 < rec[3]:
                            d.add(rec[4])
            for (nm, p0, p1, b0, b1) in W:
                for bk in range(b0 // BUCK, (b1 - 1) // BUCK + 1):
                    for rec in recs.get((nm, bk), ()):
                        if rec[0] < p1 and p0 < rec[1] and rec[2] < b1 and b0 < rec[3]:
                            d.add(rec[4])
            for (nm, p0, p1, b0, b1) in W:
                for bk in range(b0 // BUCK, (b1 - 1) // BUCK + 1):
                    L = recs.setdefault((nm, bk), [])
                    lo = max(b0, bk * BUCK)
                    hi = min(b1, (bk + 1) * BUCK)
                    L[:] = [rec for rec in L if not (p0 <= rec[0] and rec[1] <= p1 and
                                                     lo <= max(rec[2], bk * BUCK) and
                                                     min(rec[3], (bk + 1) * BUCK) <= hi)]
                    L.append((p0, p1, b0, b1, i, True))
            for (nm, p0, p1, b0, b1) in R:
                for bk in range(b0 // BUCK, (b1 - 1) // BUCK + 1):
                    L = recs.setdefault((nm, bk), [])
                    if not dma:
                        lo = max(b0, bk * BUCK)
                        hi = min(b1, (bk + 1) * BUCK)
                        L[:] = [rec for rec in L if rec[5] or ops[rec[4]][4] or ops[rec[4]][0] != eng or not (
                            p0 <= rec[0] and rec[1] <= p1 and lo <= max(rec[2], bk * BUCK) and
                            min(rec[3], (bk + 1) * BUCK) <= hi)]
                    L.append((p0, p1, b0, b1, i, False))
            if dma:
                q = eng
                k = dma_n[q]
                P = len(self.dsems[q])
                dsem[i] = (q, k % P, 16 * (k // P + 1))
                if k >= P:
                    d.add(dma_hist[q][k - P])
                dma_hist[q].append(i)
                dma_n[q] += 1
            d.discard(i)
            deps[i] = d
        known = {e: {} for e in self.engs}
        snap = [None] * n
        sig = [False] * n
        waits = [None] * n
        for i, (eng, fn, R, W, dma) in enumerate(ops):
            kn = known[eng]
            need = {}
            for d in deps[i]:
                de = ops[d][0]
                ddma = ops[d][4]
                if ddma:
                    q, si, val = dsem[d]
                    key = ("D", q, si)
                    if kn.get(key, 0) >= val:
                        continue
                    if key not in need or dsem[need[key]][2] < val:
                        need[key] = d
                else:
                    if de == "pe" and eng == "pe" and not dma:
                        continue
                    if kn.get(de, -1) >= pos[d]:
                        continue
                    if de not in need or pos[need[de]] < pos[d]:
                        need[de] = d
            wl = []
            for key, d in need.items():
                if ops[d][4]:
                    if kn.get(key, 0) >= dsem[d][2]:
                        continue
                else:
                    if kn.get(key, -1) >= pos[d]:
                        continue
                wl.append(d)
                sig[d] = True
                for k2, v2 in snap[d].items():
                    if kn.get(k2, -1) < v2:
                        kn[k2] = v2
            waits[i] = wl
            s = dict(kn)
            if dma:
                q, si, val = dsem[i]
                s[("D", q, si)] = val
            else:
                s[eng] = pos[i]
            snap[i] = s
        cum = [0] * n
        c = {e: 0 for e in self.engs}
        for i, (eng, fn, R, W, dma) in enumerate(ops):
            if not dma and sig[i]:
                c[eng] += 1
            cum[i] = c[eng]
        nw = 0
        for i, (eng, fn, R, W, dma) in enumerate(ops):
            E = self.engs[eng]
            for d in waits[i]:
                if ops[d][4]:
                    q, si, val = dsem[d]
                    E.wait_ge(self.dsems[q][si], val)
                else:
                    E.wait_ge(self.esem[ops[d][0]], cum[d])
                nw += 1
            ins = fn()
            if ins is None:
                continue
            if dma:
                q, si, val = dsem[i]
                ins.then_inc(self.dsems[q][si], 16)
            elif sig[i]:
                ins.then_inc(self.esem[eng], 1)
        self.dbginfo = (waits, sig, cum, pos, dsem)
        self.stats = dict(n_ops=n, n_waits=nw, per_eng=cnt, sigs=c)
        return self.stats


class K:
    def __init__(self, nc, S):
        self.nc = nc
        self.S = S

    def mm(self, out, lhsT, rhs, start=True, stop=True):
        nc = self.nc
        self.S.add("pe", lambda: nc.tensor.matmul(out, lhsT, rhs, start=start, stop=stop),
                   [lhsT, rhs], [out])

    def tr(self, out, in_, ident):
        nc = self.nc
        self.S.add("pe", lambda: nc.tensor.transpose(out, in_, ident), [in_, ident], [out])

    def act(self, out, in_, func, bias=None, scale=None, accum=None, eng="act"):
        nc = self.nc
        kw = {}
        r = [in_]
        if bias is not None:
            kw["bias"] = bias
            if not isinstance(bias, (int, float)):
                r.append(bias)
        if scale is not None:
            kw["scale"] = scale
            if not isinstance(scale, (int, float)):
                r.append(scale)
        w = [out]
        if accum is not None:
            kw["accum_out"] = accum
            w.append(accum)
        self.S.add("act", lambda: nc.scalar.activation(out, in_, func, **kw), r, w)

    def tt(self, out, a, b, op, eng="dve"):
        E = self.S.engs[eng]
        self.S.add(eng, lambda: E.tensor_tensor(out, a, b, op), [a, b], [out])

    def ts(self, out, a, s1, op0, s2=None, op1=None, eng="dve", accum=None):
        E = self.S.engs[eng]
        r = [a]
        if not isinstance(s1, (int, float)):
            r.append(s1)
        if s2 is not None and not isinstance(s2, (int, float)):
            r.append(s2)
        w = [out]
        kw = {}
        if accum is not None:
            kw["accum_out"] = accum
            w.append(accum)
        if op1 is None:
            self.S.add(eng, lambda: E.tensor_scalar(out, a, s1, None, op0, **kw), r, w)
        else:
            self.S.add(eng, lambda: E.tensor_scalar(out, a, s1, s2, op0, op1, **kw), r, w)

    def stt(self, out, in0, scalar, in1, op0, op1):
        nc = self.nc
        r = [in0, in1]
        if not isinstance(scalar, (int, float)):
            r.append(scalar)
        self.S.add("dve", lambda: nc.vector.scalar_tensor_tensor(out, in0, scalar, in1, op0, op1), r, [out])

    def cp(self, out, in_, eng="act"):
        nc = self.nc
        if eng == "act":
            self.S.add("act", lambda: nc.scalar.copy(out, in_), [in_], [out])
        else:
            E = self.S.engs[eng]
            self.S.add(eng, lambda: E.tensor_copy(out, in_), [in_], [out])

    def memset(self, ap, v, eng="dve"):
        E = self.S.engs[eng]
        self.S.add(eng, lambda: E.memset(ap, v), [], [ap])

    def scan(self, out, d0, d1, init, op0, op1):
        nc = self.nc
        r = [d0, d1]
        if not isinstance(init, (int, float)):
            r.append(init)
        self.S.add("dve", lambda: nc.vector.tensor_tensor_scan(out, d0, d1, init, op0, op1), r, [out])

    def bn_stats(self, out, in_):
        nc = self.nc
        self.S.add("dve", lambda: nc.vector.bn_stats(out, in_), [in_], [out])

    def bn_aggr(self, out, in_):
        nc = self.nc
        self.S.add("dve", lambda: nc.vector.bn_aggr(out, in_), [in_], [out])

    def recip(self, out, in_):
        nc = self.nc
        self.S.add("dve", lambda: nc.vector.reciprocal(out, in_), [in_], [out])

    def dma(self, out, in_, q="sp", **kw):
        E = self.S.engs[q]
        self.S.add(q, lambda: E.dma_start(out=out, in_=in_, **kw), [in_], [out], dma=True)

    def fence(self, aps, q="sp"):
        self.S.add(q, lambda: None, list(aps), [])

from concourse.bass_utils import run_bass_kernel_spmd
import math

D = 1024
DIN = 3856
DFF = 2816
NEG = -1e30
LN_EPS = 1e-5
NORM_EPS = 1e-6
ALPHA = 4.0 ** 0.25
NTOK = 2176
NSP = 24
import os
CPENG = os.environ.get('CPENG', 'act')
BLOCKS = [(0, 512, False), (512, 512, False), (1024, 512, False), (1536, 512, False), (2048, 128, True)]

C_ID, C_MAIP, C_MASP, C_MAIS, C_MASS, C_SEL, C_CMP, C_CMS, C_ROWM, C_EH, C_ONE = \
    0, 128, 256, 384, 512, 640, 1152, 1280, 1408, 1424, 1440
NCST = 1448


def make_consts():
    c = np.zeros((128, NCST), np.float32)
    ii = np.arange(128)
    c[:, C_ID:C_ID + 128] = np.eye(128)
    J, I = np.meshgrid(ii, ii, indexing="ij")
    same = (J // 8) == (I // 8)
    c[:, C_MAIP:C_MAIP + 128] = np.where(I >= J, 0, NEG)
    c[:, C_MASP:C_MASP + 128] = np.where(I > J, 0, NEG)
    c[:, C_MAIS:C_MAIS + 128] = np.where((I >= J) & same, 0, NEG)
    c[:, C_MASS:C_MASS + 128] = np.where((I > J) & same, 0, NEG)
    for h in range(4):
        c[h, C_SEL + h * 128:C_SEL + (h + 1) * 128] = 1.0
        c[:, C_EH + h * 4 + h] = 1.0
    c[0:4, C_CMP:C_CMP + 128] = 1.0
    c[0:4, C_CMP] = 0.0
    c[0:4, C_CMS:C_CMS + 128] = 1.0
    c[0:4, C_CMS:C_CMS + 128:8] = 0.0
    for s in range(16):
        c[s * 8:s * 8 + 8, C_ROWM + s] = 1.0
    c[:, C_ONE] = 1.0
    return c


def build(dbg=None, nblocks=5, nlayers=2, stage=9, limit=None, bsel=None):
    dbg = dbg or {}
    nc = bass.Bass('TRN2', target_bir_lowering=False)
    S = Sched(nc, n_sp_sems=NSP, n_pool_sems=64)
    k = K(nc, S)

    def din(name, shape):
        return nc.dram_tensor(name, list(shape), F32, kind="ExternalInput").ap()

    def dout(name, shape):
        return nc.dram_tensor(name, list(shape), F32, kind="ExternalOutput").ap()

    x_in = din("x_in", [NTOK, D])
    st_conv = din("st_conv", [2, 32, 256])
    st_gconv = din("st_gconv", [2, 48, 1536])
    st_gdn = din("st_gdn", [2, 16, 4, 128, 128])
    st_c = din("st_c", [2, 16, 4, 64, 64])
    st_n = din("st_n", [2, 16, 4, 64])
    st_m = din("st_m", [2, 16, 4])
    st_ffn = din("st_ffn", [2, 32, DFF])
    w_in = din("w_in", [2, D, DIN])
    conv_w = din("conv_w", [2, 3, 256])
    gdn_conv_w = din("gdn_conv_w", [2, 4, 1536])
    gdn_a_log = din("gdn_a_log", [2, 4])
    gdn_dt_bias = din("gdn_dt_bias", [2, 4])
    gdn_norm_w = din("gdn_norm_w", [2, 128])
    ml_i_bias = din("ml_i_bias", [2, 4])
    ml_f_bias = din("ml_f_bias", [2, 4])
    ml_norm_w = din("ml_norm_w", [2, 256])
    w_o = din("w_o", [2, D, D])
    ln1_g = din("ln1_g", [2, D])
    ln1_b = din("ln1_b", [2, D])
    w_up = din("w_up", [2, D, 2 * DFF])
    ffn_conv_w = din("ffn_conv_w", [2, 3, DFF])
    w_down = din("w_down", [2, DFF, D])
    ln2_g = din("ln2_g", [2, D])
    ln2_b = din("ln2_b", [2, D])
    consts = din("consts", [128, NCST])

    y = dout("y", [NTOK, D])
    p_conv = dout("p_conv", [2, 2, 256])
    p_gconv = dout("p_gconv", [2, 3, 1536])
    p_gdn = dout("p_gdn", [2, 4, 128, 128])
    p_c = dout("p_c", [2, 4, 64, 64])
    p_n = dout("p_n", [2, 4, 64])
    p_m = dout("p_m", [2, 4])
    p_ffn = dout("p_ffn", [2, 2, DFF])
    s_conv = dout("s_conv", [2, 32, 256])
    s_gconv = dout("s_gconv", [2, 48, 1536])
    s_gdn = dout("s_gdn", [2, 16, 4, 128, 128])
    s_c = dout("s_c", [2, 16, 4, 64, 64])
    s_n = dout("s_n", [2, 16, 4, 64])
    s_m = dout("s_m", [2, 16, 4])
    s_ffn = dout("s_ffn", [2, 32, DFF])
    outs_all = [y, p_conv, p_gconv, p_gdn, p_c, p_n, p_m, p_ffn, s_conv, s_gconv, s_gdn, s_c, s_n, s_m, s_ffn]
    dbg_out = {}
    for nm, shp in dbg.items():
        dbg_out[nm] = dout("dbg_" + nm, shp)
        outs_all.append(dbg_out[nm])

    def dump(nm, ap):
        if nm in dbg_out:
            k.dma(dbg_out[nm], ap)

    wb_in = nc.dram_tensor("wb_in", [2, D, DIN], BF16, kind="Internal").ap()
    wb_o = nc.dram_tensor("wb_o", [2, D, D], BF16, kind="Internal").ap()
    wb_up = nc.dram_tensor("wb_up", [2, D, 2 * DFF], BF16, kind="Internal").ap()
    wb_down = nc.dram_tensor("wb_down", [2, DFF, D], BF16, kind="Internal").ap()

    NCOL = 44000
    A = nc.sbuf_tensor("arena", [128, NCOL], F32).__enter__()
    PS = nc.psum_tensor("psum", [128, 4096], F32).__enter__()
    cur = [0]

    def al(n):
        o = cur[0]
        cur[0] += n
        assert cur[0] <= NCOL, cur[0]
        return o

    def V(o, n, p0=0, p1=128):
        return A[p0:p1, o:o + n]

    def Vb(o, n, p0=0, p1=128):
        return A[p0:p1, o:o + n].bitcast(BF16)

    cast_jobs = []
    job_index = {}
    for l in range(nlayers):
        for j, (c0, n) in enumerate(((0, 768), (768, 768), (1536, 768), (2304, 520), (2824, 512), (3336, 520))):
            job_index[("in", l, j)] = len(cast_jobs)
            cast_jobs.append([(wb_in[l][:, c0:c0 + n], w_in[l][:, c0:c0 + n])])
        job_index[("o", l, 0)] = len(cast_jobs)
        cast_jobs.append([(wb_o[l][:, 0:512], w_o[l][:, 0:512]), (wb_o[l][:, 512:1024], w_o[l][:, 512:1024])])
        for j in range(6):
            n = 512 if j < 5 else 256
            job_index[("up", l, j)] = len(cast_jobs)
            cast_jobs.append([(wb_up[l][:, 512 * j:512 * j + n], w_up[l][:, 512 * j:512 * j + n]),
                              (wb_up[l][:, DFF + 512 * j:DFF + 512 * j + n], w_up[l][:, DFF + 512 * j:DFF + 512 * j + n])])
        for j in range(4):
            r0 = 768 * j
            r1 = min(DFF, r0 + 768)
            job_index[("down", l, j)] = len(cast_jobs)
            cast_jobs.append([(wb_down[l][r0:r1, :], w_down[l][r0:r1, :])])
    cast_ptr = [0]

    def need(kind, l, j, la=2):
        tgt = min(len(cast_jobs), job_index[(kind, l, j)] + 1 + la)
        while cast_ptr[0] < tgt:
            for (o_, i_) in cast_jobs[cast_ptr[0]]:
                k.dma(o_, i_, q="pool")
            cast_ptr[0] += 1

    def in_job(c0):
        for j, (a0, n) in enumerate(((0, 768), (768, 768), (1536, 768), (2304, 520), (2824, 512), (3336, 520))):
            if a0 <= c0 < a0 + n:
                return j
        raise ValueError(c0)

    if os.environ.get('NOCAST') is None:
        need("in", 0, 0, la=1)
    if stage == -1:
        need('down', nlayers - 1, 3)
        k.fence([wb_in, wb_o, wb_up, wb_down])
        print(S.finalize())
        return nc
    o_cst = al(NCST)
    k.dma(V(o_cst, NCST), consts)
    ident = V(o_cst + C_ID, 128)

    def sel(h):
        return V(o_cst + C_SEL + h * 128, 128, 0, 4)

    def eh(h):
        return V(o_cst + C_EH + h * 4, 4)

    rowmask = V(o_cst + C_ROWM, 16)

    dctr = [0]

    def psd():
        b = dctr[0] % 4
        dctr[0] += 1
        return PS[:, b * 512:(b + 1) * 512]

    qctr = [0]

    def psq(n=128):
        q = qctr[0] % 8
        qctr[0] += 1
        q = (q + 4) % 8
        return PS[:, q * 512: q * 512 + n]

    o_cwa = al(2 * 2 * 3)
    o_cwg = al(2 * 12 * 4)
    o_cwf = al(2 * 22 * 3)
    o_lnf = al(2 * 4 * 8)
    o_gnw = al(2)
    o_gb = al(2 * 8)
    o_mnw = al(2 * 256)
    o_S = al(2 * 2 * 512)
    o_C = al(2 * 2 * 130)
    o_ha = al(2 * 2 * 2)
    o_hg = al(2 * 12 * 3)
    o_hf = al(2 * 22 * 2)
    o_mp = al(2)
    o_st_end = cur[0]
    o_xtok = al(4096)
    o_xT = al(2048)
    o_qkv = al(6144)
    o_ab = al(1024)
    o_sz = al(1024)
    o_mqk = al(2048)
    o_mvo = al(4 * 516)
    o_yc = al(2048)
    o_xe = al(2 * 516)
    o_acc = al(2 * 512)
    o_wg = al(64)
    o_graw = al(4 * 128)
    o_rows = al(26 * 128)
    o_cols = al(128)
    o_scan = al(6144)
    o_wbuf = al(5 * 1024)
    o_misc = al(640)
    o_lnb = o_scan + 2048
    o_cb = al(5 * 64)
    print("arena cols used", cur[0])
    o_ptmp = o_scan
    o_sst = o_xtok + 1024
    o_sso = o_xtok + 1024 + 704
    o_S0 = o_qkv + 1536
    o_msk = o_S0 + 2048
    o_kdm = o_msk + 2176

    identb = Vb(o_cb, 64)
    for ci_, co_ in enumerate((C_ID, C_MAIP, C_MASP, C_MAIS, C_MASS)):
        k.cp(Vb(o_cb + ci_ * 64, 64), V(o_cst + co_, 128), eng="dve")

    def cwa(l, u):
        return V(o_cwa + (l * 2 + u) * 3, 3)

    def cwg(l, u):
        return V(o_cwg + (l * 12 + u) * 4, 4)

    def cwf(l, u):
        return V(o_cwf + (l * 22 + u) * 3, 3)

    def lnf(l, which, kk):
        return V(o_lnf + (l * 4 + which) * 8 + kk, 1)

    def gbv(l, i):
        return V(o_gb + l * 8 + i, 1, 0, 4)

    ptmp = V(o_ptmp, DFF)
    k.memset(ptmp, 0.0)
    for l in range(nlayers):
        for (src, W_, nu, fn) in ((conv_w, 3, 2, cwa), (gdn_conv_w, 4, 12, cwg), (ffn_conv_w, 3, 22, cwf)):
            C_ = nu * 128
            k.dma(ptmp[0:W_, 0:C_], src[l])
            for u0 in range(0, nu, 4):
                ps = psd()
                n4 = min(4, nu - u0)
                for u in range(u0, u0 + n4):
                    k.tr(ps[:, (u - u0) * 128:(u - u0 + 1) * 128], ptmp[:, u * 128:(u + 1) * 128], ident)
                for u in range(u0, u0 + n4):
                    k.cp(fn(l, u), ps[:, (u - u0) * 128:(u - u0) * 128 + W_])
        for wi, src in enumerate((ln1_g, ln1_b, ln2_g, ln2_b)):
            k.dma(ptmp[0:8, 0:128], src[l].rearrange("(k p) -> k p", p=128))
            ps = psd()
            k.tr(ps[:, 0:128], ptmp[:, 0:128], ident)
            k.cp(V(o_lnf + (l * 4 + wi) * 8, 8), ps[:, 0:8])
        k.dma(V(o_gnw + l, 1), gdn_norm_w[l].rearrange("(p o) -> p o", o=1))
        for i, src in enumerate((gdn_a_log, gdn_dt_bias, ml_i_bias, ml_f_bias)):
            k.dma(gbv(l, i), src[l].rearrange("(p o) -> p o", o=1))
        k.act(gbv(l, 4), gbv(l, 0), AF.Exp)
        k.ts(gbv(l, 4), gbv(l, 4), -1.0, ALU.mult)
        k.ts(gbv(l, 5), gbv(l, 3), -1.0, ALU.mult)
        k.dma(V(o_mnw + l * 256, 256), ml_norm_w[l:l + 1, :].partition_broadcast(128))

    def Sst(l, par, h):
        return V(o_S + (l * 2 + par) * 512 + h * 128, 128)

    def Cst(l, par, h):
        po = (h % 2) * 64
        return V(o_C + (l * 2 + par) * 130 + (h // 2) * 65, 65, po, po + 64)

    k.memset(V(o_S, o_st_end - o_S), 0.0)
    k.memset(V(o_rows, 26 * 128), 0.0)

    def halo_a(l, u):
        return V(o_ha + (l * 2 + u) * 2, 2)

    def halo_g(l, u):
        return V(o_hg + (l * 12 + u) * 3, 3)

    def halo_f(l, u):
        return V(o_hf + (l * 22 + u) * 2, 2)

    def mprev(l):
        return V(o_mp + l, 1, 0, 4)

    def xtok(t):
        return V(o_xtok + t * 1024, 1024)

    def wbuf_next(ctr=[0]):
        i = ctr[0]
        ctr[0] += 1
        return Vb(o_wbuf + (i % 5) * 1024, 1024)

    def row(i, p0=0, p1=4):
        return V(o_rows + i * 128, 128, p0, p1)

    def slot(h, i):
        return V(o_scan + (h * 12 + i) * 128, 128)

    if stage == 0:
        k.dma(y[0:128, 0:512], V(o_cwa, 512))
        k.fence(outs_all)
        print(S.finalize())
        return nc
    def main_loops():
        for bi, (t0, Tb, is_s) in enumerate(BLOCKS[:nblocks]):
            if bsel is not None and bi not in bsel:
                continue
            nt = Tb // 128
            L = 8 if is_s else 128
            nch = 128 // L
            nch2 = max(nch, 2)
            MAI = Vb(o_cb + (3 if is_s else 1) * 64, 64)
            MAS = Vb(o_cb + (4 if is_s else 2) * 64, 64)
            cmask = V(o_cst + (C_CMS if is_s else C_CMP), 128, 0, 4)
            nlev = 3 if is_s else 7
            xTv = Vb(o_xT, 4 * Tb).rearrange("p (k t) -> p k t", k=8)
            ycv = Vb(o_yc, 4 * Tb).rearrange("p (k t) -> p k t", k=8)
            hTv = Vb(o_qkv, 11 * Tb).rearrange("p (k t) -> p k t", k=22)
            szv = Vb(o_sz, 2 * Tb).rearrange("p (u t) -> p u t", u=4)
            ab = V(o_ab, 2 * Tb).rearrange("p (u t) -> p u t", u=2)
            mqk = V(o_mqk, 4 * Tb).rearrange("p (u t) -> p u t", u=4)

            def qkv(u):
                return V(o_qkv + u * Tb, Tb)

            def vext(t):
                return V(o_mvo + t * 516, 260).rearrange("p (h e) -> p h e", h=4)

            def sigo(t):
                return V(o_mvo + t * 516 + 260, 256)

            def tview(ap):
                if not is_s:
                    return ap
                return ap.rearrange("p (s t) -> p s t", s=16)

            for t in range(nt):
                k.dma(xtok(t), x_in[t0 + t * 128:t0 + (t + 1) * 128, :])

            def transpose_to_xT(src_tok, t, scale_l=None, which=None):
                for k0 in range(0, 8, 4):
                    ps = psd()
                    for kk in range(k0, k0 + 4):
                        k.tr(ps[:, (kk - k0) * 128:(kk - k0 + 1) * 128], src_tok[:, kk * 128:(kk + 1) * 128], ident)
                    for kk in range(k0, k0 + 4):
                        o_ = xTv[:, kk, t * 128:(t + 1) * 128]
                        i_ = ps[:, (kk - k0) * 128:(kk - k0 + 1) * 128]
                        if scale_l is None:
                            k.cp(o_, i_, eng=CPENG)
                        else:
                            k.act(o_, i_, AF.Identity, scale=lnf(scale_l, which, kk), bias=lnf(scale_l, which + 1, kk))

            for t in range(nt):
                transpose_to_xT(xtok(t), t)

            xectr = [0]

            def conv_unit(src, W_, cw, halo, st_src, st_rows, mul_by=None):
                i = xectr[0]
                xectr[0] += 1
                wl = W_ - 1
                if not is_s:
                    full = V(o_xe + (i % 2) * 516, wl + Tb)
                    data, hv, tail = full[:, wl:wl + Tb], full[:, 0:wl], full[:, Tb:Tb + wl]
                    win = lambda j: full[:, j:j + Tb]
                else:
                    full = V(o_xe + (i % 2) * 516, 16 * (wl + 8)).rearrange("p (s t) -> p s t", s=16)
                    data, hv, tail = full[:, :, wl:wl + 8], full[:, :, 0:wl], full[:, :, 8:8 + wl]
                    win = lambda j: full[:, :, j:j + 8]
                if mul_by is None:
                    k.cp(data, tview(src))
                else:
                    k.tt(data, tview(src), tview(mul_by), ALU.mult)
                if not is_s:
                    k.cp(hv, halo, eng="pool")
                else:
                    k.cp(hv, st_src.rearrange("p (s t) -> p s t", s=16), eng="pool")
                acc = V(o_acc + (i % 2) * 512, Tb)
                accv = tview(acc)
                k.ts(accv, win(0), cw[:, 0:1], ALU.mult)
                for j in range(1, W_):
                    k.stt(accv, win(j), cw[:, j:j + 1], accv, ALU.mult, ALU.add)
                if not is_s:
                    k.cp(halo, tail, eng="pool")
                else:
                    k.cp(st_rows.rearrange("p (s t) -> p s t", s=16), tail, eng="pool")
                return acc

            def load_state_T(src, R, nu, dst_off):
                for c0 in range(0, nu, 4):
                    n4 = min(4, nu - c0)
                    k.dma(ptmp[0:R, 0:n4 * 128], src[:, c0 * 128:(c0 + n4) * 128])
                    ps = psd()
                    for u in range(c0, c0 + n4):
                        k.tr(ps[:, (u - c0) * 128:(u - c0 + 1) * 128], ptmp[:, (u - c0) * 128:(u - c0 + 1) * 128], ident)
                    for u in range(c0, c0 + n4):
                        k.cp(V(dst_off + u * R, R), ps[:, (u - c0) * 128:(u - c0) * 128 + R])

            def rows_out(get_in, nu, R, dram):
                for c0 in range(0, nu, 4):
                    n4 = min(4, nu - c0)
                    ps = psd()
                    for u in range(c0, c0 + n4):
                        gi = get_in(u)
                        k.tr(ps[:, (u - c0) * 128:(u - c0 + 1) * 128], A[:, int(gi.offset) % NCOL:int(gi.offset) % NCOL + 128], ident)
                    rb = V(o_misc + 128, 512, 0, R)
                    k.cp(rb[:, 0:n4 * 128], ps[0:R, 0:n4 * 128])
                    k.dma(dram[:, c0 * 128:(c0 + n4) * 128], rb[:, 0:n4 * 128])

            for l in range(nlayers):
                last_l = (l == nlayers - 1)
                xectr[0] = 0

                wsrc = wb_in[l]
                wup = wb_up[l]

                def load_w(src, c0, n):
                    if src is wsrc:
                        need("in", l, in_job(c0))
                    else:
                        need("up", l, (c0 % DFF) // 512)
                    wb = wbuf_next()
                    wv = wb[:, 0:8 * n].rearrange("p (k e) -> p k e", k=8)
                    k.dma(wv, src.rearrange("(k p) e -> p k e", p=128)[:, :, c0:c0 + n])
                    return wv

                def unit_mm(ps, wv, e0, M):
                    for kk in range(8):
                        k.mm(ps[0:M, 0:Tb], wv[:, kk, e0:e0 + M], xTv[:, kk, :], start=(kk == 0), stop=(kk == 7))

                wsrc = wb_in[l]
                actmp = V(o_lnb, 2 * Tb).rearrange("p (u t) -> p u t", u=2)
                wv = load_w(wsrc, 0, 256)
                for u in range(2):
                    ps = psd()
                    unit_mm(ps, wv, u * 128, 128)
                    k.cp(ab[:, u, :], ps[:, 0:Tb])
                wv = load_w(wsrc, 256, 256)
                for u in range(2):
                    ps = psd()
                    unit_mm(ps, wv, u * 128, 128)
                    k.cp(actmp[:, u, :], ps[:, 0:Tb], eng="dve")
                if is_s:
                    load_state_T(st_conv[l], 32, 2, o_sst)
                wv = load_w(wsrc, 512, 256)
                for u in range(2):
                    ps = psd()
                    unit_mm(ps, wv, u * 128, 128)
                    acc = conv_unit(ps[:, 0:Tb], 3, cwa(l, u), halo_a(l, u), V(o_sst + u * 32, 32), V(o_sso + u * 32, 32),
                                    mul_by=actmp[:, u, :])
                    k.tt(ycv[:, u, :], acc, ab[:, u, :], ALU.mult)
                if is_s:
                    rows_out(lambda u: V(o_sso + u * 32, 32), 2, 32, s_conv[l])
                elif bi == 3:
                    rows_out(lambda u: halo_a(l, u), 2, 2, p_conv[l])
                if is_s:
                    load_state_T(st_gconv[l], 48, 12, o_sst)
                for g in range(6):
                    wv = load_w(wsrc, 768 + 256 * g, 256)
                    for uu in range(2):
                        j = 2 * g + uu
                        ps = psd()
                        unit_mm(ps, wv, uu * 128, 128)
                        acc = conv_unit(ps[:, 0:Tb], 4, cwg(l, j), halo_g(l, j), V(o_sst + j * 48, 48), V(o_sso + j * 48, 48))
                        k.act(qkv(j), acc, AF.Silu)
                if is_s:
                    rows_out(lambda u: V(o_sso + u * 48, 48), 12, 48, s_gconv[l])
                elif bi == 3:
                    rows_out(lambda u: halo_g(l, u), 12, 3, p_gconv[l])
                for g in range(2):
                    wv = load_w(wsrc, 2304 + 256 * g, 256)
                    for uu in range(2):
                        ps = psd()
                        unit_mm(ps, wv, uu * 128, 128)
                        k.act(szv[:, 2 * g + uu, :], ps[:, 0:Tb], AF.Silu)
                for g in range(2):
                    wv = load_w(wsrc, 2824 + 256 * g, 256)
                    for uu in range(2):
                        ps = psd()
                        unit_mm(ps, wv, uu * 128, 128)
                        k.cp(mqk[:, 2 * g + uu, :], ps[:, 0:Tb])
                wv = load_w(wsrc, 3336, 256)
                for t in range(nt):
                    ps = psd()
                    for kk in range(8):
                        k.mm(ps[:, 0:256], xTv[:, kk, t * 128:(t + 1) * 128], wv[:, kk, :], start=(kk == 0), stop=(kk == 7))
                    k.cp(vext(t)[:, :, 0:64], ps[:, 0:256].rearrange("p (h e) -> p h e", h=4))
                    k.memset(vext(t)[:, :, 64:65], 1.0, eng="pool")
                wv = load_w(wsrc, 3592, 256)
                for t in range(nt):
                    ps = psd()
                    for kk in range(8):
                        k.mm(ps[:, 0:256], xTv[:, kk, t * 128:(t + 1) * 128], wv[:, kk, :], start=(kk == 0), stop=(kk == 7))
                    k.act(sigo(t), ps[:, 0:256], AF.Sigmoid)
                wg = Vb(o_wg, 64).rearrange("p (k e) -> p k e", k=8)
                wsr = wsrc.rearrange("(k p) e -> p k e", p=128)
                need("in", l, 5)
                k.dma(wg[:, :, 0:8], wsr[:, :, 2816:2824])
                k.dma(wg[:, :, 8:16], wsr[:, :, 3848:3856])
                if bi == 0 and l == 0:
                    dump("qkv", V(o_qkv, 12 * Tb))
                    dump("ycA", V(o_yc, 2048))

                for t in range(nt if stage >= 2 else 0):
                    ts_ = slice(t * 128, (t + 1) * 128)
                    gti = bi * 4 + t
                    par = gti % 2 if not is_s else 0
                    graw = [V(o_graw + i * 128, 128, 0, 4) for i in range(4)]
                    for i in range(4):
                        ps = psq()
                        for kk in range(8):
                            k.mm(ps[0:4, :], wg[:, kk, i * 4:i * 4 + 4], xTv[:, kk, ts_], start=(kk == 0), stop=(kk == 7))
                        k.cp(graw[i], ps[0:4, :], eng="dve")
                    r_G, r_lb, r_lrq, r_lrk, r_ra, r_rq, r_cj, r_kd, r_tmp, r_GL = [row(i) for i in range(10)]
                    k.act(r_tmp, graw[0], AF.Exp, bias=gbv(l, 1))
                    k.act(r_tmp, r_tmp, AF.Ln, bias=1.0)
                    k.ts(r_tmp, r_tmp, gbv(l, 4), ALU.mult)
                    k.scan(r_G, cmask, r_tmp, 0.0, ALU.mult, ALU.add)
                    k.act(r_lb, graw[1], AF.Exp, scale=-1.0)
                    k.act(r_lb, r_lb, AF.Ln, bias=1.0)
                    k.ts(r_lb, r_lb, -1.0, ALU.mult)
                    for qi, rr in ((0, r_lrq), (1, r_lrk)):
                        ps = psq()
                        for h in range(4):
                            sq = slot(h, 3 + qi)
                            k.act(sq, qkv(qi * 4 + h)[:, ts_], AF.Square)
                            k.mm(ps[0:4, :], eh(h), sq, start=(h == 0), stop=(h == 3))
                        k.act(rr, ps[0:4, :], AF.Ln, bias=NORM_EPS)
                        k.ts(rr, rr, -0.5, ALU.mult)
                    k.tt(r_ra, r_G, r_lb, ALU.add)
                    k.tt(r_ra, r_ra, r_lrk, ALU.add)
                    k.ts(r_rq, r_lrq, math.log(128.0 ** -0.5), ALU.add)
                    k.tt(r_rq, r_rq, r_G, ALU.add)
                    k.tt(r_cj, r_lrk, r_G, ALU.subtract)
                    G3 = r_G.rearrange("p (c l) -> p c l", l=L)
                    k.tt(r_kd.rearrange("p (c l) -> p c l", l=L), r_cj.rearrange("p (c l) -> p c l", l=L),
                         G3[:, :, L - 1:L].broadcast_to([4, nch, L]), ALU.add)
                    k.cp(r_GL[:, 0:nch].rearrange("p (c o) -> p c o", o=1), G3[:, :, L - 1:L], eng="dve")
                    craw = V(o_cols, 20)
                    cexp = V(o_cols + 20, 20)
                    ps = psq()
                    for ci, rr in enumerate((r_cj, r_lb, r_ra, r_kd, r_rq)):
                        k.mm(ps[:, ci * 4:ci * 4 + 4], rr, ident[0:4, 0:4], start=True, stop=True)
                    k.cp(craw, ps[:, 0:20], eng="dve")
                    k.act(cexp, ps[:, 0:20], AF.Exp)
                    elast = V(o_misc, 4 * nch2)
                    ps = psq()
                    for h in range(4):
                        k.mm(ps[:, h * nch2:h * nch2 + nch2], sel(h), r_GL[:, 0:nch2], start=True, stop=True)
                    k.act(elast, ps[:, 0:4 * nch2], AF.Exp)
                    if bi == 0 and l == 0 and t == 0:
                        dump("rows", V(o_rows, 10 * 128, 0, 4))
                        dump("cexp", V(o_cols, 40))

                    qT_ = [qkv(h)[:, ts_] for h in range(4)]
                    kT_ = [qkv(4 + h)[:, ts_] for h in range(4)]
                    vT_ = [qkv(8 + h)[:, ts_] for h in range(4)]
                    for h in range(4):
                        ps = psq()
                        k.mm(ps, sel(h), r_ra, start=True, stop=False)
                        k.mm(ps, identb, MAS, start=False, stop=True)
                        k.act(slot(h, 0), ps, AF.Exp, bias=craw[:, h:h + 1])
                        ps2 = psq()
                        k.mm(ps2, kT_[h], kT_[h])
                        k.stt(slot(h, 4), ps2, -1.0, slot(h, 0), ALU.mult, ALU.mult)
                    for h in range(4):
                        ps = psq()
                        k.mm(ps, sel(h), r_rq, start=True, stop=False)
                        k.mm(ps, identb, MAI, start=False, stop=True)
                        k.act(slot(h, 1), ps, AF.Exp, bias=craw[:, h:h + 1])
                        ps2 = psq()
                        k.mm(ps2, kT_[h], qT_[h])
                        k.tt(slot(h, 2), ps2, slot(h, 1), ALU.mult)
                    for h in range(4):
                        ps = psq()
                        k.tr(ps, slot(h, 4), ident)
                        k.cp(slot(h, 5), ps)
                        k.tt(slot(h, 3), slot(h, 4), ident, ALU.add, eng="pool")
                    for h in range(4):
                        ps = psq()
                        k.mm(ps, slot(h, 4), slot(h, 5))
                        k.cp(slot(h, 6), ps)
                        ps2 = psq()
                        k.mm(ps2, slot(h, 5), slot(h, 4))
                        k.cp(slot(h, 4), ps2, eng="dve")
                    cP = [(3, 6)] * 4
                    for lev in range(1, nlev):
                        lastlev = (lev == nlev - 1)
                        nP = []
                        for h in range(4):
                            pb, ipm = cP[h]
                            npb, npm = 10 - pb, 11 - ipm
                            if not lastlev:
                                ps = psq(256)
                                k.mm(ps, slot(h, ipm), V(o_scan + (h * 12 + pb) * 128, 256))
                                k.tt(slot(h, npb), ps[:, 0:128], slot(h, pb), ALU.add)
                                k.cp(slot(h, npb + 1), ps[:, 128:256])
                                ps2 = psq()
                                k.mm(ps2, slot(h, pb + 1), slot(h, ipm))
                                k.cp(slot(h, npm), ps2, eng="dve")
                            else:
                                ps = psq()
                                k.mm(ps, slot(h, ipm), slot(h, pb))
                                k.tt(slot(h, npb), ps, slot(h, pb), ALU.add)
                            nP.append((npb, npm))
                        cP = nP
                    TTf = [slot(h, cP[h][0]) for h in range(4)]
                    for h in range(4):
                        ps = psq()
                        k.tr(ps, kT_[h], ident)
                        k.ts(slot(h, 0), ps, cexp[:, 8 + h:9 + h], ALU.mult)
                        k.act(slot(h, 1), ps, AF.Identity, scale=cexp[:, 12 + h:13 + h])
                        ps2 = psq()
                        k.tr(ps2, vT_[h], ident)
                        k.ts(slot(h, 9), ps2, cexp[:, 4 + h:5 + h], ALU.mult)
                    for h in range(4):
                        ps = psq()
                        k.mm(ps, slot(h, 0), TTf[h])
                        k.act(slot(h, 10), ps, AF.Identity, scale=-1.0)
                    O_ = [slot(h, 5) for h in range(4)]
                    if not is_s:
                        for h in range(4):
                            Sc, Sn = Sst(l, par, h), Sst(l, 1 - par, h)
                            ps = psq()
                            k.mm(ps, TTf[h], slot(h, 9), start=True, stop=False)
                            k.mm(ps, slot(h, 10), Sc, start=False, stop=True)
                            k.cp(slot(h, 11), ps)
                            ps1 = psq()
                            k.mm(ps1, qT_[h], Sc)
                            k.act(slot(h, 0), ps1, AF.Identity, scale=cexp[:, 16 + h:17 + h])
                            ps2 = psq()
                            k.mm(ps2, slot(h, 2), slot(h, 11))
                            k.tt(O_[h], ps2, slot(h, 0), ALU.add)
                            ps3 = psq()
                            k.mm(ps3, slot(h, 1), slot(h, 11))
                            k.stt(Sn, Sc, elast[:, h * nch2:h * nch2 + 1], ps3, ALU.mult, ALU.add)
                            if gti == 15:
                                k.dma(p_gdn[l, h], Sn)
                    else:
                        S0v = V(o_S0, 2048).rearrange("p (s v) -> p s v", s=16)
                        mskd = V(o_msk, 2176).rearrange("p (s r) -> p s r", r=136)[:, :, 0:8]
                        mskf = V(o_msk, 2048).rearrange("p (s i) -> p s i", s=16)
                        for h in range(4):
                            k.dma(S0v, st_gdn[l, :, h].rearrange("s k v -> k s v"))
                            if h == 0 and l == 0:
                                k.memset(V(o_msk, 2176), 0.0, eng="pool")
                            k.cp(mskd, slot(h, 10).rearrange("p (s r) -> p s r", r=8), eng="pool")
                            ps = psq()
                            k.mm(ps, TTf[h], slot(h, 9), start=True, stop=False)
                            for s in range(16):
                                k.mm(ps, mskf[:, s, :], S0v[:, s, :], start=False, stop=(s == 15))
                            k.cp(slot(h, 11), ps)
                            k.cp(mskd, qT_[h].rearrange("p (s r) -> p s r", r=8), eng="pool")
                            ps1 = psq()
                            for s in range(16):
                                k.mm(ps1, mskf[:, s, :], S0v[:, s, :], start=(s == 0), stop=(s == 15))
                            k.act(slot(h, 0), ps1, AF.Identity, scale=cexp[:, 16 + h:17 + h])
                            ps2 = psq()
                            k.mm(ps2, slot(h, 2), slot(h, 11))
                            k.tt(O_[h], ps2, slot(h, 0), ALU.add)
                            for s in range(16):
                                kdm = V(o_kdm + (s % 3) * 128, 128)
                                k.ts(kdm, slot(h, 1), rowmask[:, s:s + 1], ALU.mult)
                                ps3 = psq()
                                k.mm(ps3, kdm, slot(h, 11))
                                k.stt(S0v[:, s, :], S0v[:, s, :], elast[:, h * 16 + s:h * 16 + s + 1], ps3, ALU.mult, ALU.add)
                            k.dma(s_gdn[l, :, h].rearrange("s k v -> k s v"), S0v)
                    if bi == 0 and l == 0 and t == 0:
                        dump("o_gdn", V(o_scan + 5 * 128, 128))
                        dump("TT0", TTf[0])
                    ss = V(o_cols + 40, 4)
                    for h in range(4):
                        k.act(slot(h, 0), O_[h], AF.Square, accum=ss[:, h:h + 1])
                    k.ts(ss, ss, 1.0 / 128, ALU.mult, NORM_EPS, ALU.add)
                    k.act(ss, ss, AF.Sqrt)
                    k.recip(ss, ss)
                    for h in range(4):
                        k.ts(slot(h, 9), O_[h], ss[:, h:h + 1], ALU.mult)
                        ps = psq()
                        k.tr(ps, slot(h, 9), ident)
                        k.stt(ycv[:, 2 + h, ts_], ps, V(o_gnw + l, 1), szv[:, h, ts_], ALU.mult, ALU.mult)

                    r_ig, r_lf, r_F, r_m, r_rD, r_cD, r_rI, r_em, r_kw, r_mp, r_t2, r_ch = [row(10 + i) for i in range(12)]
                    k.ts(r_ig, graw[2], gbv(l, 2), ALU.add)
                    k.act(r_lf, graw[3], AF.Exp, scale=-1.0, bias=gbv(l, 5))
                    k.act(r_lf, r_lf, AF.Ln, bias=1.0)
                    k.ts(r_lf, r_lf, -1.0, ALU.mult)
                    k.scan(r_F, cmask, r_lf, 0.0, ALU.mult, ALU.add)
                    F3 = r_F.rearrange("p (c l) -> p c l", l=L)
                    m3 = r_m.rearrange("p (c l) -> p c l", l=L)
                    if not is_s:
                        k.scan(r_m, r_lf, r_ig, mprev(l), ALU.add, ALU.max)
                        k.cp(r_mp, mprev(l).broadcast_to([4, 128]), eng="dve")
                    else:
                        m0 = r_ch[:, 64:80]
                        k.dma(m0, st_m[l].rearrange("s h -> h s"), allow_slow_non_contiguous=True)
                        k.cp(r_mp.rearrange("p (c l) -> p c l", l=8), m0.rearrange("p (c o) -> p c o", o=1).broadcast_to([4, 16, 8]), eng="dve")
                        ig3 = r_ig.rearrange("p (c l) -> p c l", l=8)
                        lf3 = r_lf.rearrange("p (c l) -> p c l", l=8)
                        t23 = r_t2.rearrange("p (c l) -> p c l", l=8)
                        k.cp(r_t2, r_ig, eng="dve")
                        k.tt(t23[:, :, 0:1], lf3[:, :, 0:1], m0.rearrange("p (c o) -> p c o", o=1), ALU.add)
                        k.tt(t23[:, :, 0:1], t23[:, :, 0:1], ig3[:, :, 0:1], ALU.max)
                        r_lf2 = row(25)
                        k.cp(r_lf2, r_lf, eng="dve")
                        k.memset(r_lf2.rearrange("p (c l) -> p c l", l=8)[:, :, 0:1], NEG)
                        k.scan(r_m, r_lf2, r_t2, 0.0, ALU.add, ALU.max)
                    k.tt(r_rD, r_F, r_m, ALU.subtract)
                    k.tt(r_cD, r_ig, r_F, ALU.subtract)
                    k.ts(r_cD, r_cD, math.log(0.125), ALU.add)
                    k.tt(r_rI, r_rD, r_mp, ALU.add)
                    k.ts(r_em, r_m, -1.0, ALU.mult)
                    chA = r_ch[:, 0:nch].rearrange("p (c o) -> p c o", o=1)
                    chB = r_ch[:, 16:16 + nch].rearrange("p (c o) -> p c o", o=1)
                    k.tt(chA, F3[:, :, L - 1:L], m3[:, :, L - 1:L], ALU.subtract)
                    k.tt(chB, chA, r_mp.rearrange("p (c l) -> p c l", l=L)[:, :, 0:1], ALU.add)
                    k.tt(r_kw.rearrange("p (c l) -> p c l", l=L), r_cD.rearrange("p (c l) -> p c l", l=L),
                         chA.broadcast_to([4, nch, L]), ALU.add)
                    if not is_s:
                        k.cp(mprev(l), r_m[:, 127:128], eng="dve")
                        if gti == 15:
                            k.dma(p_m[l].rearrange("(p o) -> p o", o=1), r_m[:, 127:128])
                    else:
                        k.dma(s_m[l].rearrange("s h -> h s"), m3[:, :, 7], allow_slow_non_contiguous=True)
                    mraw = V(o_cols + 48, 16)
                    mexp = V(o_cols + 64, 16)
                    ps = psq()
                    for ci, rr in enumerate((r_cD, r_em, r_kw, r_rI)):
                        k.mm(ps[:, ci * 4:ci * 4 + 4], rr, ident[0:4, 0:4], start=True, stop=True)
                    k.cp(mraw, ps[:, 0:16], eng="dve")
                    k.act(mexp, ps[:, 0:16], AF.Exp)
                    dcb = V(o_misc + 64, 4 * nch2)
                    ps = psq()
                    for h in range(4):
                        k.mm(ps[:, h * nch2:h * nch2 + nch2], sel(h), r_ch[:, 16:16 + nch2], start=True, stop=True)
                    k.act(dcb, ps[:, 0:4 * nch2], AF.Exp)
                    nq = V(o_scan + 48 * 128 - 4 * 65, 260).rearrange("p (h e) -> p h e", h=4)
                    kwt = [None] * 4
                    for hc in range(2):
                        ps = psq()
                        k.tr(ps, mqk[:, 2 + hc, ts_], ident)
                        for hh in range(2):
                            h = hc * 2 + hh
                            kwt[h] = slot(h, 4)[:, 0:64]
                            k.ts(kwt[h], ps[:, hh * 64:(hh + 1) * 64], mexp[:, 8 + h:9 + h], ALU.mult)
                    for h in range(4):
                        po = (h % 2) * 64
                        qTh = mqk[po:po + 64, h // 2, ts_]
                        kTh = mqk[po:po + 64, 2 + h // 2, ts_]
                        ps = psq()
                        k.mm(ps, sel(h), r_rD, start=True, stop=False)
                        k.mm(ps, identb, MAI, start=False, stop=True)
                        k.act(slot(h, 0), ps, AF.Exp, bias=mraw[:, h:h + 1])
                        ps2 = psq()
                        k.mm(ps2, kTh, qTh)
                        k.tt(slot(h, 1), ps2, slot(h, 0), ALU.mult)
                        if not is_s:
                            Cc, Cn = Cst(l, par, h), Cst(l, 1 - par, h)
                            ps1 = psq()
                            k.mm(ps1[:, 0:65], qTh, Cc)
                            k.act(slot(h, 2)[:, 0:65], ps1[:, 0:65], AF.Identity, scale=mexp[:, 12 + h:13 + h])
                            ps3 = psq()
                            k.mm(ps3[:, 0:65], slot(h, 1), vext(t)[:, h, :])
                            k.tt(nq[:, h, :], ps3[:, 0:65], slot(h, 2)[:, 0:65], ALU.add)
                            ps4 = psq()
                            k.mm(ps4[po:po + 64, 0:65], kwt[h], vext(t)[:, h, :])
                            k.stt(Cn, Cc, dcb[po:po + 64, h * nch2:h * nch2 + 1], ps4[po:po + 64, 0:65], ALU.mult, ALU.add)
                            if gti == 15:
                                k.dma(p_c[l, h], Cn[:, 0:64])
                                k.dma(p_n[l, h].rearrange("(p o) -> p o", o=1), Cn[:, 64:65])
                        else:
                            C0v = V(o_S0, 16 * 65, po, po + 64).rearrange("p (s e) -> p s e", s=16)
                            k.dma(C0v[:, :, 0:64], st_c[l, :, h].rearrange("s d e -> d s e"))
                            k.dma(C0v[:, :, 64:65], st_n[l, :, h].rearrange("s (d o) -> d s o", o=1), allow_slow_non_contiguous=True)
                            mskd = V(o_msk, 2176, po, po + 64).rearrange("p (s r) -> p s r", r=136)[:, :, 0:8]
                            mskf = V(o_msk, 2048, po, po + 64).rearrange("p (s i) -> p s i", s=16)
                            k.cp(mskd, qTh.rearrange("p (s r) -> p s r", r=8), eng="pool")
                            ps1 = psq()
                            for s in range(16):
                                k.mm(ps1[:, 0:65], mskf[:, s, :], C0v[:, s, :], start=(s == 0), stop=(s == 15))
                            k.act(slot(h, 2)[:, 0:65], ps1[:, 0:65], AF.Identity, scale=mexp[:, 12 + h:13 + h])
                            ps3 = psq()
                            k.mm(ps3[:, 0:65], slot(h, 1), vext(t)[:, h, :])
                            k.tt(nq[:, h, :], ps3[:, 0:65], slot(h, 2)[:, 0:65], ALU.add)
                            for s in range(16):
                                kwm = V(o_kdm + (s % 3) * 128, 64)
                                k.ts(kwm, kwt[h], rowmask[:, s:s + 1], ALU.mult)
                                ps4 = psq()
                                k.mm(ps4[po:po + 64, 0:65], kwm, vext(t)[:, h, :])
                                k.stt(C0v[:, s, :], C0v[:, s, :], dcb[po:po + 64, h * 16 + s:h * 16 + s + 1], ps4[po:po + 64, 0:65], ALU.mult, ALU.add)
                            k.dma(s_c[l, :, h].rearrange("s d e -> d s e"), C0v[:, :, 0:64])
                            k.dma(s_n[l, :, h].rearrange("s (d o) -> d s o", o=1), C0v[:, :, 64:65], allow_slow_non_contiguous=True)
                    den = V(o_cols + 80, 4)
                    qn = nq[:, :, 64]
                    k.stt(den, qn, -1.0, qn, ALU.mult, ALU.max)
                    k.tt(den, den, mexp[:, 4:8], ALU.max)
                    k.recip(den, den)
                    hbuf = slot(0, 5)[:, 0:128]
                    hb2 = V(o_scan + 5 * 128, 128)
                    hb3 = V(o_scan + 6 * 128, 128)
                    hall = V(o_scan + 5 * 128, 256)
                    stats = V(o_cols + 84, 24)
                    mv = V(o_cols + 108, 8)
                    rstd = V(o_cols + 116, 4)
                    for h in range(4):
                        hh_ = hall[:, h * 64:(h + 1) * 64]
                        k.stt(hh_, nq[:, h, 0:64], den[:, h:h + 1], sigo(t)[:, h * 64:(h + 1) * 64], ALU.mult, ALU.mult)
                        k.bn_stats(stats[:, h * 6:(h + 1) * 6], hh_)
                        k.bn_aggr(mv[:, h * 2:(h + 1) * 2], stats[:, h * 6:(h + 1) * 6])
                    mv3 = mv.rearrange("p (h two) -> p h two", two=2)
                    k.ts(rstd.rearrange("p (h o) -> p h o", o=1), mv3[:, :, 1:2], LN_EPS, ALU.add)
                    k.act(rstd, rstd, AF.Sqrt)
                    k.recip(rstd, rstd)
                    for h in range(4):
                        hh_ = hall[:, h * 64:(h + 1) * 64]
                        k.ts(hh_, hh_, mv[:, 2 * h:2 * h + 1], ALU.subtract, rstd[:, h:h + 1], ALU.mult)
                    k.tt(hall, hall, V(o_mnw + l * 256, 256), ALU.mult, eng="pool")
                    for hc in range(2):
                        ps = psq()
                        k.tr(ps, hall[:, hc * 128:(hc + 1) * 128], ident)
                        k.cp(ycv[:, 6 + hc, ts_], ps)
                    if bi == 0 and l == 0 and t == 0:
                        dump("hml", hall)

                if bi == 0 and l == 0:
                    dump("ycat", V(o_yc, 2048))
                if stage < 3:
                    continue
                def ln_resid(t, pss):
                    xt_ = xtok(t)
                    for hf in range(4):
                        k.stt(xt_[:, hf * 256:(hf + 1) * 256], xt_[:, hf * 256:(hf + 1) * 256], ALPHA, pss[hf], ALU.mult, ALU.add)

                def ln_tile(t, pss, g_src, b_src, which, final_out=None, need_T=True):
                    xt_ = xtok(t)
                    if pss is not None:
                        ln_resid(t, pss)
                    stats = V(o_cols + 84, 12)
                    mv = V(o_cols + 108, 2)
                    rs = V(o_cols + 116, 2)
                    k.bn_stats(stats[:, 0:6], xt_[:, 0:512])
                    k.bn_stats(stats[:, 6:12], xt_[:, 512:1024])
                    k.bn_aggr(mv, stats)
                    k.ts(rs[:, 0:1], mv[:, 1:2], LN_EPS, ALU.add)
                    k.act(rs[:, 0:1], rs[:, 0:1], AF.Sqrt)
                    k.recip(rs[:, 0:1], rs[:, 0:1])
                    k.stt(rs[:, 1:2], mv[:, 0:1], -1.0, rs[:, 0:1], ALU.mult, ALU.mult)
                    ntok_ = V(o_scan + (t % 2) * 1024, 1024)
                    k.act(ntok_, xt_, AF.Identity, scale=rs[:, 0:1], bias=rs[:, 1:2])
                    if need_T:
                        transpose_to_xT(ntok_, t, scale_l=l, which=which)
                    gB = V(o_lnb, 1024)
                    bB = V(o_lnb + 1024, 1024)
                    k.tt(xt_, ntok_, gB, ALU.mult, eng="pool")
                    k.tt(xt_, xt_, bB, ALU.add, eng="pool")
                    if final_out is not None:
                        k.dma(final_out, xt_)

                k.dma(V(o_lnb, 1024), ln1_g[l:l + 1, :].partition_broadcast(128))
                k.dma(V(o_lnb + 1024, 1024), ln1_b[l:l + 1, :].partition_broadcast(128))
                need("o", l, 0)
                wos = []
                for g in range(4):
                    wb = wbuf_next()
                    wv = wb[:, 0:2048].rearrange("p (k e) -> p k e", k=8)
                    k.dma(wv, wb_o[l].rearrange("(k p) e -> p k e", p=128)[:, :, g * 256:(g + 1) * 256])
                    wos.append(wv)
                for t in range(nt):
                    ps = psd()
                    pss = []
                    for g in range(4):
                        if g == 2:
                            ps = psd()
                        pp = ps[:, (g % 2) * 256:(g % 2 + 1) * 256]
                        for kk in range(8):
                            k.mm(pp, ycv[:, kk, t * 128:(t + 1) * 128], wos[g][:, kk, :], start=(kk == 0), stop=(kk == 7))
                        pss.append(pp)
                    ln_tile(t, pss, ln1_g, ln1_b, 0)
                if bi == 0 and l == 0:
                    dump("x1", V(o_xtok, 4096))
                if stage < 4:
                    continue
                if is_s:
                    load_state_T(st_ffn[l], 32, 22, o_sst)
                wup = wb_up[l]
                for g in range(11):
                    wvg = load_w(wup, g * 256, 256)
                    wvv = load_w(wup, DFF + g * 256, 256)
                    for uu in range(2):
                        j = 2 * g + uu
                        ps = psd()
                        unit_mm(ps, wvg, uu * 128, 128)
                        acc = conv_unit(ps[:, 0:Tb], 3, cwf(l, j), halo_f(l, j), V(o_sst + j * 32, 32), V(o_sso + j * 32, 32))
                        k.act(acc, acc, AF.Silu)
                        ps2 = psd()
                        unit_mm(ps2, wvv, uu * 128, 128)
                        k.tt(hTv[:, j, :], ps2[:, 0:Tb], acc, ALU.mult)
                if is_s:
                    rows_out(lambda u: V(o_sso + u * 32, 32), 22, 32, s_ffn[l])
                elif bi == 3:
                    rows_out(lambda u: halo_f(l, u), 22, 2, p_ffn[l])
                k.dma(V(o_lnb, 1024), ln2_g[l:l + 1, :].partition_broadcast(128))
                k.dma(V(o_lnb + 1024, 1024), ln2_b[l:l + 1, :].partition_broadcast(128))
                for g in range(11):
                    need("down", l, (g * 256) // 768)
                    wb = wbuf_next()
                    wv = wb[:, 0:2048].rearrange("p (k e) -> p k e", k=2)
                    k.dma(wv, wb_down[l][g * 256:(g + 1) * 256, :].rearrange("(k p) e -> p k e", p=128))
                    for kk in range(2):
                        kc = 2 * g + kk
                        for t in range(nt):
                            for hf in range(2):
                                k.mm(PS[:, (t * 2 + hf) * 512:(t * 2 + hf + 1) * 512], hTv[:, kc, t * 128:(t + 1) * 128],
                                     wv[:, kk, hf * 512:(hf + 1) * 512], start=(kc == 0), stop=(kc == 21))
                for t in range(nt):
                    pss = [PS[:, (t * 2) * 512 + q * 256:(t * 2) * 512 + (q + 1) * 256] for q in range(4)]
                    ln_resid(t, pss)
                for t in range(nt):
                    fo = y[t0 + t * 128:t0 + (t + 1) * 128, :] if last_l else None
                    ln_tile(t, None, ln2_g, ln2_b, 2, final_out=fo, need_T=not last_l)

    S.limit = limit
    print('main loop starts at op', len(S.ops))
    try:
        main_loops()
    except StopIteration:
        print('LIMIT reached at', len(S.ops))
    S.closing = True
    k.fence(outs_all + [wb_in, wb_o, wb_up, wb_down])
    st = S.finalize()
    print(st)
    return nc


def _in_maps(inp, nlayers=2):
    consts = make_consts()
    maps = []
    f = lambda a: np.ascontiguousarray(np.asarray(a, dtype=np.float32))
    for c in range(8):
        sl = slice(16 * c, 16 * c + 16)
        m = {
            "x_in": f(np.concatenate([inp["x_prompt"][c], np.asarray(inp["x_sample"])[sl].reshape(128, D)], 0)),
            "st_conv": f(np.asarray(inp["state_conv_mix"])[:, sl].reshape(2, 32, 256)),
            "st_gconv": f(np.asarray(inp["state_gdn_conv"])[:, sl].reshape(2, 48, 1536)),
            "st_gdn": f(np.asarray(inp["state_gdn"])[:, sl]),
            "st_c": f(np.asarray(inp["state_mlstm_c"])[:, sl]),
            "st_n": f(np.asarray(inp["state_mlstm_n"])[:, sl]),
            "st_m": f(np.asarray(inp["state_mlstm_m"])[:, sl]),
            "st_ffn": f(np.asarray(inp["state_ffn_conv"])[:, sl].reshape(2, 32, DFF)),
            "consts": consts,
        }
        for nm in ("w_in", "conv_w", "gdn_conv_w", "gdn_a_log", "gdn_dt_bias", "gdn_norm_w", "ml_i_bias",
                   "ml_f_bias", "ml_norm_w", "w_o", "ln1_g", "ln1_b", "w_up", "ffn_conv_w", "w_down",
                   "ln2_g", "ln2_b"):
            m[nm] = f(inp[nm])
        maps.append(m)
    return maps


def kernel(**inp):
    nc = build()
    maps = _in_maps(inp)
    res = run_bass_kernel_spmd(nc, maps, core_ids=list(range(8)))
    R = res.results
    yp = np.stack([R[c]["y"][0:2048] for c in range(8)], 0)
    ys = np.concatenate([R[c]["y"][2048:].reshape(16, 8, D) for c in range(8)], 0)

    def pst(nm, shp):
        return np.stack([R[c][nm].reshape(shp) for c in range(8)], 1)

    def sst(nm, shp):
        return np.concatenate([R[c][nm].reshape(shp) for c in range(8)], 1)

    outs = (yp, ys,
            pst("p_conv", (2, 2, 256)), pst("p_gconv", (2, 3, 1536)), pst("p_gdn", (2, 4, 128, 128)),
            pst("p_c", (2, 4, 64, 64)), pst("p_n", (2, 4, 64)), pst("p_m", (2, 4)), pst("p_ffn", (2, 2, DFF)),
            sst("s_conv", (2, 16, 2, 256)), sst("s_gconv", (2, 16, 3, 1536)), sst("s_gdn", (2, 16, 4, 128, 128)),
            sst("s_c", (2, 16, 4, 64, 64)), sst("s_n", (2, 16, 4, 64)), sst("s_m", (2, 16, 4)),
            sst("s_ffn", (2, 16, 2, DFF)))
    return tuple(np.ascontiguousarray(o.astype(np.float32)) for o in outs)
```

```python
import numpy as np
import concourse.bass as bass
import concourse.mybir as mybir

F32 = mybir.dt.float32
BF16 = mybir.dt.bfloat16
AF = mybir.ActivationFunctionType
ALU = mybir.AluOpType
AX = mybir.AxisListType
ESZ = {F32: 4, BF16: 2, mybir.dt.float32r: 4}
BUCK = 512


def box(ap):
    t = ap.tensor
    es = ESZ[ap.dtype]
    dims = [(int(s), int(c)) for s, c in ap.ap]
    off = int(ap.offset)
    cls = type(t).__name__
    if cls.startswith("DRam"):
        ext = sum((c - 1) * abs(s) for s, c in dims) + 1
        return (t.name, 0, 1, off * es, (off + ext) * es)
    rowlen = 1
    for s in list(t.shape)[1:]:
        rowlen *= int(s)
    p0 = off // rowlen
    f0 = off % rowlen
    if dims[0][0] == rowlen or dims[0][1] == 1:
        pc = dims[0][1]
        rest = dims[1:]
    elif dims[0][0] == 0:
        pc = 1
        rest = dims[1:]
    else:
        pc = 1
        rest = dims
    ext = sum((c - 1) * abs(s) for s, c in rest) + 1
    if cls.startswith("PSum") or cls.startswith("Psum") or cls.startswith("PS"):
        b0 = (f0 * es) // 2048 * 2048
        b1 = ((f0 + ext) * es + 2047) // 2048 * 2048
        return (t.name, 0, 128, b0, b1)
    return (t.name, p0, p0 + pc, f0 * es, (f0 + ext) * es)


class Sched:
    def __init__(self, nc, n_sp_sems=16, n_pool_sems=8):
        self.nc = nc
        self.ops = []
        self.engs = {"pe": nc.tensor, "dve": nc.vector, "act": nc.scalar,
                     "pool": nc.gpsimd, "sp": nc.sync}
        self.esem = {e: nc.semaphore("sem_" + e).__enter__() for e in self.engs}
        self.dsems = {"sp": [nc.semaphore("dsp%d" % i).__enter__() for i in range(n_sp_sems)],
                      "pool": [nc.semaphore("dpl%d" % i).__enter__() for i in range(n_pool_sems)],
                      "act": []}

    limit = None

    def add(self, eng, fn, r, w, dma=False):
        if self.limit is not None and len(self.ops) >= self.limit and not getattr(self, 'closing', False):
            raise StopIteration("limit")
        rb = [box(a) for a in r]
        wb = [box(a) for a in w]
        wb = wb + [b for b in rb if b[0] == 'psum' or b[0] == 'pa']
        self.ops.append((eng, fn, rb, wb, dma))

    def finalize(self):
        ops = self.ops
        n = len(ops)
        recs = {}
        deps = [None] * n
        pos = [0] * n
        cnt = {e: 0 for e in self.engs}
        dma_n = {q: 0 for q in self.dsems}
        dma_hist = {q: [] for q in self.dsems}
        dsem = [None] * n
        for i, (eng, fn, R, W, dma) in enumerate(ops):
            pos[i] = cnt[eng]
            cnt[eng] += 1
            d = set()
            for (nm, p0, p1, b0, b1) in R:
                for bk in range(b0 // BUCK, (b1 - 1) // BUCK + 1):
                    for rec in recs.get((nm, bk), ()):
                        if rec[5] and rec[0] < p1 and p0 < rec[1] and rec[2] < b1 and b0 < rec[3]:
                            d.add(rec[4])
            for (nm, p0, p1, b0, b1) in W:
                for bk in range(b0 // BUCK, (b1 - 1) // BUCK + 1):
                    for rec in recs.get((nm, bk), ()):
                        if rec[0] < p1 and p0 < rec[1] and rec[2] < b1 and b0 < rec[3]:
                            d.add(rec[4])
            for (nm, p0, p1, b0, b1) in W:
                for bk in range(b0 // BUCK, (b1 - 1) // BUCK + 1):
                    L = recs.setdefault((nm, bk), [])
                    lo = max(b0, bk * BUCK)
                    hi = min(b1, (bk + 1) * BUCK)
                    L[:] = [rec for rec in L if not (p0 <= rec[0] and rec[1] <= p1 and
                                                     lo <= max(rec[2], bk * BUCK) and
                                                     min(rec[3], (bk + 1) * BUCK) <= hi)]
                    L.append((p0, p1, b0, b1, i, True))
            for (nm, p0, p1, b0, b1) in R:
                for bk in range(b0 // BUCK, (b1 - 1) // BUCK + 1):
                    L = recs.setdefault((nm, bk), [])
                    if not dma:
                        lo = max(b0, bk * BUCK)
                        hi = min(b1, (bk + 1) * BUCK)
                        L[:] = [rec for rec in L if rec[5] or ops[rec[4]][4] or ops[rec[4]][0] != eng or not (
                            p0 <= rec[0] and rec[1] <= p1 and lo <= max(rec[2], bk * BUCK) and
                            min(rec[3], (bk + 1) * BUCK) <= hi)]
                    L.append((p0, p1, b0, b1, i, False))
            if dma:
                q = eng
                k = dma_n[q]
                P = len(self.dsems[q])
                dsem[i] = (q, k % P, 16 * (k // P + 1))
                if k >= P:
                    d.add(dma_hist[q][k - P])
                dma_hist[q].append(i)
                dma_n[q] += 1
            d.discard(i)
            deps[i] = d
        known = {e: {} for e in self.engs}
        snap = [None] * n
        sig = [False] * n
        waits = [None] * n
        for i, (eng, fn, R, W, dma) in enumerate(ops):
            kn = known[eng]
            need = {}
            for d in deps[i]:
                de = ops[d][0]
                ddma = ops[d][4]
                if ddma:
                    q, si, val = dsem[d]
                    key = ("D", q, si)
                    if kn.get(key, 0) >= val:
                        continue
                    if key not in need or dsem[need[key]][2] < val:
                        need[key] = d
                else:
                    if de == "pe" and eng == "pe" and not dma:
                        continue
                    if kn.get(de, -1) >= pos[d]:
                        continue
                    if de not in need or pos[need[de]] < pos[d]:
                        need[de] = d
            wl = []
            for key, d in need.items():
                if ops[d][4]:
                    if kn.get(key, 0) >= dsem[d][2]:
                        continue
                else:
                    if kn.get(key, -1) >= pos[d]:
                        continue
                wl.append(d)
                sig[d] = True
                for k2, v2 in snap[d].items():
                    if kn.get(k2, -1) < v2:
                        kn[k2] = v2
            waits[i] = wl
            s = dict(kn)
            if dma:
                q, si, val = dsem[i]
                s[("D", q, si)] = val
            else:
                s[eng] = pos[i]
            snap[i] = s
        cum = [0] * n
        c = {e: 0 for e in self.engs}
        for i, (eng, fn, R, W, dma) in enumerate(ops):
            if not dma and sig[i]:
                c[eng] += 1
            cum[i] = c[eng]
        nw = 0
        for i, (eng, fn, R, W, dma) in enumerate(ops):
            E = self.engs[eng]
            for d in waits[i]:
                if ops[d][4]:
                    q, si, val = dsem[d]
                    E.wait_ge(self.dsems[q][si], val)
                else:
                    E.wait_ge(self.esem[ops[d][0]], cum[d])
                nw += 1
            ins = fn()
            if ins is None:
                continue
            if dma:
                q, si, val = dsem[i]
                ins.then_inc(self.dsems[q][si], 16)
            elif sig[i]:
                ins.then_inc(self.esem[eng], 1)
        self.dbginfo = (waits, sig, cum, pos, dsem)
        self.stats = dict(n_ops=n, n_waits=nw, per_eng=cnt, sigs=c)
        return self.stats


class K:
    def __init__(self, nc, S):
        self.nc = nc
        self.S = S

    def mm(self, out, lhsT, rhs, start=True, stop=True):
        nc = self.nc
        self.S.add("pe", lambda: nc.tensor.matmul(out, lhsT, rhs, start=start, stop=stop),
                   [lhsT, rhs], [out])

    def tr(self, out, in_, ident):
        nc = self.nc
        self.S.add("pe", lambda: nc.tensor.transpose(out, in_, ident), [in_, ident], [out])

    def act(self, out, in_, func, bias=None, scale=None, accum=None, eng="act"):
        nc = self.nc
        kw = {}
        r = [in_]
        if bias is not None:
            kw["bias"] = bias
            if not isinstance(bias, (int, float)):
                r.append(bias)
        if scale is not None:
            kw["scale"] = scale
            if not isinstance(scale, (int, float)):
                r.append(scale)
        w = [out]
        if accum is not None:
            kw["accum_out"] = accum
            w.append(accum)
        self.S.add("act", lambda: nc.scalar.activation(out, in_, func, **kw), r, w)

    def tt(self, out, a, b, op, eng="dve"):
        E = self.S.engs[eng]
        self.S.add(eng, lambda: E.tensor_tensor(out, a, b, op), [a, b], [out])

    def ts(self, out, a, s1, op0, s2=None, op1=None, eng="dve", accum=None):
        E = self.S.engs[eng]
        r = [a]
        if not isinstance(s1, (int, float)):
            r.append(s1)
        if s2 is not None and not isinstance(s2, (int, float)):
            r.append(s2)
        w = [out]
        kw = {}
        if accum is not None:
            kw["accum_out"] = accum
            w.append(accum)
        if op1 is None:
            self.S.add(eng, lambda: E.tensor_scalar(out, a, s1, None, op0, **kw), r, w)
        else:
            self.S.add(eng, lambda: E.tensor_scalar(out, a, s1, s2, op0, op1, **kw), r, w)

    def stt(self, out, in0, scalar, in1, op0, op1):
        nc = self.nc
        r = [in0, in1]
        if not isinstance(scalar, (int, float)):
            r.append(scalar)
        self.S.add("dve", lambda: nc.vector.scalar_tensor_tensor(out, in0, scalar, in1, op0, op1), r, [out])

    def cp(self, out, in_, eng="act"):
        nc = self.nc
        if eng == "act":
            self.S.add("act", lambda: nc.scalar.copy(out, in_), [in_], [out])
        else:
            E = self.S.engs[eng]
            self.S.add(eng, lambda: E.tensor_copy(out, in_), [in_], [out])

    def memset(self, ap, v, eng="dve"):
        E = self.S.engs[eng]
        self.S.add(eng, lambda: E.memset(ap, v), [], [ap])

    def scan(self, out, d0, d1, init, op0, op1):
        nc = self.nc
        r = [d0, d1]
        if not isinstance(init, (int, float)):
            r.append(init)
        self.S.add("dve", lambda: nc.vector.tensor_tensor_scan(out, d0, d1, init, op0, op1), r, [out])

    def bn_stats(self, out, in_):
        nc = self.nc
        self.S.add("dve", lambda: nc.vector.bn_stats(out, in_), [in_], [out])

    def bn_aggr(self, out, in_):
        nc = self.nc
        self.S.add("dve", lambda: nc.vector.bn_aggr(out, in_), [in_], [out])

    def recip(self, out, in_):
        nc = self.nc
        self.S.add("dve", lambda: nc.vector.reciprocal(out, in_), [in_], [out])

    def dma(self, out, in_, q="sp", **kw):
        E = self.S.engs[q]
        self.S.add(q, lambda: E.dma_start(out=out, in_=in_, **kw), [in_], [out], dma=True)

    def fence(self, aps, q="sp"):
        self.S.add(q, lambda: None, list(aps), [])

from concourse.bass_utils import run_bass_kernel_spmd
import math

D = 1024
DIN = 3856
DFF = 2816
NEG = -1e30
LN_EPS = 1e-5
NORM_EPS = 1e-6
ALPHA = 4.0 ** 0.25
NTOK = 2176
NSP = 24
import os
CPENG = os.environ.get('CPENG', 'act')
BLOCKS = [(0, 512, False), (512, 512, False), (1024, 512, False), (1536, 512, False), (2048, 128, True)]

C_ID, C_MAIP, C_MASP, C_MAIS, C_MASS, C_SEL, C_CMP, C_CMS, C_ROWM, C_EH, C_ONE = \
    0, 128, 256, 384, 512, 640, 1152, 1280, 1408, 1424, 1440
NCST = 1448


def make_consts():
    c = np.zeros((128, NCST), np.float32)
    ii = np.arange(128)
    c[:, C_ID:C_ID + 128] = np.eye(128)
    J, I = np.meshgrid(ii, ii, indexing="ij")
    same = (J // 8) == (I // 8)
    c[:, C_MAIP:C_MAIP + 128] = np.where(I >= J, 0, NEG)
    c[:, C_MASP:C_MASP + 128] = np.where(I > J, 0, NEG)
    c[:, C_MAIS:C_MAIS + 128] = np.where((I >= J) & same, 0, NEG)
    c[:, C_MASS:C_MASS + 128] = np.where((I > J) & same, 0, NEG)
    for h in range(4):
        c[h, C_SEL + h * 128:C_SEL + (h + 1) * 128] = 1.0
        c[:, C_EH + h * 4 + h] = 1.0
    c[0:4, C_CMP:C_CMP + 128] = 1.0
    c[0:4, C_CMP] = 0.0
    c[0:4, C_CMS:C_CMS + 128] = 1.0
    c[0:4, C_CMS:C_CMS + 128:8] = 0.0
    for s in range(16):
        c[s * 8:s * 8 + 8, C_ROWM + s] = 1.0
    c[:, C_ONE] = 1.0
    return c


def build(dbg=None, nblocks=5, nlayers=2, stage=9, limit=None, bsel=None):
    dbg = dbg or {}
    nc = bass.Bass('TRN2', target_bir_lowering=False)
    S = Sched(nc, n_sp_sems=NSP, n_pool_sems=64)
    k = K(nc, S)

    def din(name, shape):
        return nc.dram_tensor(name, list(shape), F32, kind="ExternalInput").ap()

    def dout(name, shape):
        return nc.dram_tensor(name, list(shape), F32, kind="ExternalOutput").ap()

    x_in = din("x_in", [NTOK, D])
    st_conv = din("st_conv", [2, 32, 256])
    st_gconv = din("st_gconv", [2, 48, 1536])
    st_gdn = din("st_gdn", [2, 16, 4, 128, 128])
    st_c = din("st_c", [2, 16, 4, 64, 64])
    st_n = din("st_n", [2, 16, 4, 64])
    st_m = din("st_m", [2, 16, 4])
    st_ffn = din("st_ffn", [2, 32, DFF])
    w_in = din("w_in", [2, D, DIN])
    conv_w = din("conv_w", [2, 3, 256])
    gdn_conv_w = din("gdn_conv_w", [2, 4, 1536])
    gdn_a_log = din("gdn_a_log", [2, 4])
    gdn_dt_bias = din("gdn_dt_bias", [2, 4])
    gdn_norm_w = din("gdn_norm_w", [2, 128])
    ml_i_bias = din("ml_i_bias", [2, 4])
    ml_f_bias = din("ml_f_bias", [2, 4])
    ml_norm_w = din("ml_norm_w", [2, 256])
    w_o = din("w_o", [2, D, D])
    ln1_g = din("ln1_g", [2, D])
    ln1_b = din("ln1_b", [2, D])
    w_up = din("w_up", [2, D, 2 * DFF])
    ffn_conv_w = din("ffn_conv_w", [2, 3, DFF])
    w_down = din("w_down", [2, DFF, D])
    ln2_g = din("ln2_g", [2, D])
    ln2_b = din("ln2_b", [2, D])
    consts = din("consts", [128, NCST])

    y = dout("y", [NTOK, D])
    p_conv = dout("p_conv", [2, 2, 256])
    p_gconv = dout("p_gconv", [2, 3, 1536])
    p_gdn = dout("p_gdn", [2, 4, 128, 128])
    p_c = dout("p_c", [2, 4, 64, 64])
    p_n = dout("p_n", [2, 4, 64])
    p_m = dout("p_m", [2, 4])
    p_ffn = dout("p_ffn", [2, 2, DFF])
    s_conv = dout("s_conv", [2, 32, 256])
    s_gconv = dout("s_gconv", [2, 48, 1536])
    s_gdn = dout("s_gdn", [2, 16, 4, 128, 128])
    s_c = dout("s_c", [2, 16, 4, 64, 64])
    s_n = dout("s_n", [2, 16, 4, 64])
    s_m = dout("s_m", [2, 16, 4])
    s_ffn = dout("s_ffn", [2, 32, DFF])
    outs_all = [y, p_conv, p_gconv, p_gdn, p_c, p_n, p_m, p_ffn, s_conv, s_gconv, s_gdn, s_c, s_n, s_m, s_ffn]
    dbg_out = {}
    for nm, shp in dbg.items():
        dbg_out[nm] = dout("dbg_" + nm, shp)
        outs_all.append(dbg_out[nm])

    def dump(nm, ap):
        if nm in dbg_out:
            k.dma(dbg_out[nm], ap)

    wb_in = nc.dram_tensor("wb_in", [2, D, DIN], BF16, kind="Internal").ap()
    wb_o = nc.dram_tensor("wb_o", [2, D, D], BF16, kind="Internal").ap()
    wb_up = nc.dram_tensor("wb_up", [2, D, 2 * DFF], BF16, kind="Internal").ap()
    wb_down = nc.dram_tensor("wb_down", [2, DFF, D], BF16, kind="Internal").ap()

    NCOL = 44800
    A = nc.sbuf_tensor("arena", [128, NCOL], F32).__enter__()
    PS = nc.psum_tensor("psum", [128, 4096], F32).__enter__()
    cur = [0]

    def al(n):
        o = cur[0]
        cur[0] += n
        assert cur[0] <= NCOL, cur[0]
        return o

    def V(o, n, p0=0, p1=128):
        return A[p0:p1, o:o + n]

    def Vb(o, n, p0=0, p1=128):
        return A[p0:p1, o:o + n].bitcast(BF16)

    cast_jobs = []
    job_index = {}
    for l in range(nlayers):
        for j, (c0, n) in enumerate(((0, 768), (768, 768), (1536, 768), (2304, 520), (2824, 512), (3336, 520))):
            job_index[("in", l, j)] = len(cast_jobs)
            cast_jobs.append([(wb_in[l][:, c0:c0 + n], w_in[l][:, c0:c0 + n])])
        job_index[("o", l, 0)] = len(cast_jobs)
        cast_jobs.append([(wb_o[l][:, 0:512], w_o[l][:, 0:512]), (wb_o[l][:, 512:1024], w_o[l][:, 512:1024])])
        for j in range(6):
            n = 512 if j < 5 else 256
            job_index[("up", l, j)] = len(cast_jobs)
            cast_jobs.append([(wb_up[l][:, 512 * j:512 * j + n], w_up[l][:, 512 * j:512 * j + n]),
                              (wb_up[l][:, DFF + 512 * j:DFF + 512 * j + n], w_up[l][:, DFF + 512 * j:DFF + 512 * j + n])])
        for j in range(4):
            r0 = 768 * j
            r1 = min(DFF, r0 + 768)
            job_index[("down", l, j)] = len(cast_jobs)
            cast_jobs.append([(wb_down[l][r0:r1, :], w_down[l][r0:r1, :])])
    cast_ptr = [0]

    def need(kind, l, j, la=2):
        tgt = min(len(cast_jobs), job_index[(kind, l, j)] + 1 + la)
        while cast_ptr[0] < tgt:
            for (o_, i_) in cast_jobs[cast_ptr[0]]:
                k.dma(o_, i_, q="pool")
            cast_ptr[0] += 1

    def in_job(c0):
        for j, (a0, n) in enumerate(((0, 768), (768, 768), (1536, 768), (2304, 520), (2824, 512), (3336, 520))):
            if a0 <= c0 < a0 + n:
                return j
        raise ValueError(c0)

    if os.environ.get('NOCAST') is None:
        need("in", 0, 0, la=1)
    if stage == -1:
        need('down', nlayers - 1, 3)
        k.fence([wb_in, wb_o, wb_up, wb_down])
        print(S.finalize())
        return nc
    o_cst = al(NCST)
    k.dma(V(o_cst, NCST), consts)
    ident = V(o_cst + C_ID, 128)

    def sel(h):
        return V(o_cst + C_SEL + h * 128, 128, 0, 4)

    def eh(h):
        return V(o_cst + C_EH + h * 4, 4)

    rowmask = V(o_cst + C_ROWM, 16)

    dctr = [0]

    def psd():
        b = dctr[0] % 4
        dctr[0] += 1
        return PS[:, b * 512:(b + 1) * 512]

    qctr = [0]

    def psq(n=128):
        q = qctr[0] % 8
        qctr[0] += 1
        q = (q + 4) % 8
        return PS[:, q * 512: q * 512 + n]

    o_cwa = al(2 * 2 * 3)
    o_cwg = al(2 * 12 * 4)
    o_cwf = al(2 * 22 * 3)
    o_lnf = al(2 * 4 * 8)
    o_gnw = al(2)
    o_gb = al(2 * 8)
    o_mnw = al(2 * 256)
    o_S = al(2 * 2 * 512)
    o_C = al(2 * 2 * 130)
    o_ha = al(2 * 2 * 2)
    o_hg = al(2 * 12 * 3)
    o_hf = al(2 * 22 * 2)
    o_mp = al(2)
    o_st_end = cur[0]
    o_xtok = al(4096)
    o_xT = al(2048)
    o_qkv = al(6144)
    o_ab = al(1024)
    o_sz = al(1024)
    o_mqk = al(2048)
    o_mvo = al(4 * 516)
    o_yc = al(2048)
    o_xe = al(2 * 516)
    o_acc = al(2 * 512)
    o_wg = al(64)
    o_graw = al(4 * 128)
    o_rows = al(26 * 128)
    o_cols = al(128)
    o_scan = al(6144)
    o_wbuf = al(5 * 1024)
    o_misc = al(640)
    o_lnb = o_scan + 2048
    o_cb = al(5 * 64)
    print("arena cols used", cur[0])
    o_ptmp = o_scan
    o_sst = o_xtok + 1024
    o_sso = o_xtok + 1024 + 704
    o_S0 = o_qkv + 1536
    o_msk = o_S0 + 2048
    o_kdm = o_msk + 2176

    identb = Vb(o_cb, 64)
    for ci_, co_ in enumerate((C_ID, C_MAIP, C_MASP, C_MAIS, C_MASS)):
        k.cp(Vb(o_cb + ci_ * 64, 64), V(o_cst + co_, 128), eng="dve")

    def cwa(l, u):
        return V(o_cwa + (l * 2 + u) * 3, 3)

    def cwg(l, u):
        return V(o_cwg + (l * 12 + u) * 4, 4)

    def cwf(l, u):
        return V(o_cwf + (l * 22 + u) * 3, 3)

    def lnf(l, which, kk):
        return V(o_lnf + (l * 4 + which) * 8 + kk, 1)

    def gbv(l, i):
        return V(o_gb + l * 8 + i, 1, 0, 4)

    ptmp = V(o_ptmp, DFF)
    k.memset(ptmp, 0.0)
    for l in range(nlayers):
        for (src, W_, nu, fn) in ((conv_w, 3, 2, cwa), (gdn_conv_w, 4, 12, cwg), (ffn_conv_w, 3, 22, cwf)):
            C_ = nu * 128
            k.dma(ptmp[0:W_, 0:C_], src[l])
            for u0 in range(0, nu, 4):
                ps = psd()
                n4 = min(4, nu - u0)
                for u in range(u0, u0 + n4):
                    k.tr(ps[:, (u - u0) * 128:(u - u0 + 1) * 128], ptmp[:, u * 128:(u + 1) * 128], ident)
                for u in range(u0, u0 + n4):
                    k.cp(fn(l, u), ps[:, (u - u0) * 128:(u - u0) * 128 + W_])
        for wi, src in enumerate((ln1_g, ln1_b, ln2_g, ln2_b)):
            k.dma(ptmp[0:8, 0:128], src[l].rearrange("(k p) -> k p", p=128))
            ps = psd()
            k.tr(ps[:, 0:128], ptmp[:, 0:128], ident)
            k.cp(V(o_lnf + (l * 4 + wi) * 8, 8), ps[:, 0:8])
        k.dma(V(o_gnw + l, 1), gdn_norm_w[l].rearrange("(p o) -> p o", o=1))
        for i, src in enumerate((gdn_a_log, gdn_dt_bias, ml_i_bias, ml_f_bias)):
            k.dma(gbv(l, i), src[l].rearrange("(p o) -> p o", o=1))
        k.act(gbv(l, 4), gbv(l, 0), AF.Exp)
        k.ts(gbv(l, 4), gbv(l, 4), -1.0, ALU.mult)
        k.ts(gbv(l, 5), gbv(l, 3), -1.0, ALU.mult)
        k.dma(V(o_mnw + l * 256, 256), ml_norm_w[l:l + 1, :].partition_broadcast(128))

    def Sst(l, par, h):
        return V(o_S + (l * 2 + par) * 512 + h * 128, 128)

    def Cst(l, par, h):
        po = (h % 2) * 64
        return V(o_C + (l * 2 + par) * 130 + (h // 2) * 65, 65, po, po + 64)

    k.memset(V(o_S, o_st_end - o_S), 0.0)
    k.memset(V(o_rows, 26 * 128), 0.0)

    def halo_a(l, u):
        return V(o_ha + (l * 2 + u) * 2, 2)

    def halo_g(l, u):
        return V(o_hg + (l * 12 + u) * 3, 3)

    def halo_f(l, u):
        return V(o_hf + (l * 22 + u) * 2, 2)

    def mprev(l):
        return V(o_mp + l, 1, 0, 4)

    def xtok(t):
        return V(o_xtok + t * 1024, 1024)

    def wbuf_next(ctr=[0]):
        i = ctr[0]
        ctr[0] += 1
        return Vb(o_wbuf + (i % 5) * 1024, 1024)

    def row(i, p0=0, p1=4):
        return V(o_rows + i * 128, 128, p0, p1)

    def slot(h, i):
        return V(o_scan + (h * 12 + i) * 128, 128)

    def slotb(h, i):
        return Vb(o_scan + (h * 12 + i) * 128, 64)

    o_sb = al(256)

    def Sb(h):
        return Vb(o_sb + h * 64, 64)

    if stage == 0:
        k.dma(y[0:128, 0:512], V(o_cwa, 512))
        k.fence(outs_all)
        print(S.finalize())
        return nc
    def main_loops():
        for bi, (t0, Tb, is_s) in enumerate(BLOCKS[:nblocks]):
            if bsel is not None and bi not in bsel:
                continue
            nt = Tb // 128
            L = 8 if is_s else 128
            nch = 128 // L
            nch2 = max(nch, 2)
            MAI = Vb(o_cb + (3 if is_s else 1) * 64, 64)
            MAS = Vb(o_cb + (4 if is_s else 2) * 64, 64)
            cmask = V(o_cst + (C_CMS if is_s else C_CMP), 128, 0, 4)
            nlev = 3 if is_s else 7
            xTv = Vb(o_xT, 4 * Tb).rearrange("p (k t) -> p k t", k=8)
            ycv = Vb(o_yc, 4 * Tb).rearrange("p (k t) -> p k t", k=8)
            hTv = Vb(o_qkv, 11 * Tb).rearrange("p (k t) -> p k t", k=22)
            szv = Vb(o_sz, 2 * Tb).rearrange("p (u t) -> p u t", u=4)
            ab = V(o_ab, 2 * Tb).rearrange("p (u t) -> p u t", u=2)
            mqk = V(o_mqk, 4 * Tb).rearrange("p (u t) -> p u t", u=4)

            def qkv(u):
                return V(o_qkv + u * Tb, Tb)

            def vext(t):
                return Vb(o_mvo + t * 516, 130).rearrange("p (h e) -> p h e", h=4)

            def sigo(t):
                return V(o_mvo + t * 516 + 260, 256)

            def tview(ap):
                if not is_s:
                    return ap
                return ap.rearrange("p (s t) -> p s t", s=16)

            for t in range(nt):
                k.dma(xtok(t), x_in[t0 + t * 128:t0 + (t + 1) * 128, :])

            def transpose_to_xT(src_tok, t, scale_l=None, which=None):
                for k0 in range(0, 8, 4):
                    ps = psd()
                    for kk in range(k0, k0 + 4):
                        k.tr(ps[:, (kk - k0) * 128:(kk - k0 + 1) * 128], src_tok[:, kk * 128:(kk + 1) * 128], ident)
                    for kk in range(k0, k0 + 4):
                        o_ = xTv[:, kk, t * 128:(t + 1) * 128]
                        i_ = ps[:, (kk - k0) * 128:(kk - k0 + 1) * 128]
                        if scale_l is None:
                            k.cp(o_, i_, eng=CPENG)
                        else:
                            k.act(o_, i_, AF.Identity, scale=lnf(scale_l, which, kk), bias=lnf(scale_l, which + 1, kk))

            for t in range(nt):
                transpose_to_xT(xtok(t), t)

            xectr = [0]

            def conv_unit(src, W_, cw, halo, st_src, st_rows, mul_by=None):
                i = xectr[0]
                xectr[0] += 1
                wl = W_ - 1
                if not is_s:
                    full = V(o_xe + (i % 2) * 516, wl + Tb)
                    data, hv, tail = full[:, wl:wl + Tb], full[:, 0:wl], full[:, Tb:Tb + wl]
                    win = lambda j: full[:, j:j + Tb]
                else:
                    full = V(o_xe + (i % 2) * 516, 16 * (wl + 8)).rearrange("p (s t) -> p s t", s=16)
                    data, hv, tail = full[:, :, wl:wl + 8], full[:, :, 0:wl], full[:, :, 8:8 + wl]
                    win = lambda j: full[:, :, j:j + 8]
                if mul_by is None:
                    k.cp(data, tview(src))
                else:
                    k.tt(data, tview(src), tview(mul_by), ALU.mult)
                if not is_s:
                    k.cp(hv, halo, eng="pool")
                else:
                    k.cp(hv, st_src.rearrange("p (s t) -> p s t", s=16), eng="pool")
                acc = V(o_acc + (i % 2) * 512, Tb)
                accv = tview(acc)
                k.ts(accv, win(0), cw[:, 0:1], ALU.mult)
                for j in range(1, W_):
                    k.stt(accv, win(j), cw[:, j:j + 1], accv, ALU.mult, ALU.add)
                if not is_s:
                    k.cp(halo, tail, eng="pool")
                else:
                    k.cp(st_rows.rearrange("p (s t) -> p s t", s=16), tail, eng="pool")
                return acc

            def load_state_T(src, R, nu, dst_off):
                for c0 in range(0, nu, 4):
                    n4 = min(4, nu - c0)
                    k.dma(ptmp[0:R, 0:n4 * 128], src[:, c0 * 128:(c0 + n4) * 128])
                    ps = psd()
                    for u in range(c0, c0 + n4):
                        k.tr(ps[:, (u - c0) * 128:(u - c0 + 1) * 128], ptmp[:, (u - c0) * 128:(u - c0 + 1) * 128], ident)
                    for u in range(c0, c0 + n4):
                        k.cp(V(dst_off + u * R, R), ps[:, (u - c0) * 128:(u - c0) * 128 + R])

            def rows_out(get_in, nu, R, dram):
                for c0 in range(0, nu, 4):
                    n4 = min(4, nu - c0)
                    ps = psd()
                    for u in range(c0, c0 + n4):
                        gi = get_in(u)
                        k.tr(ps[:, (u - c0) * 128:(u - c0 + 1) * 128], A[:, int(gi.offset) % NCOL:int(gi.offset) % NCOL + 128], ident)
                    rb = V(o_misc + 128, 512, 0, R)
                    k.cp(rb[:, 0:n4 * 128], ps[0:R, 0:n4 * 128])
                    k.dma(dram[:, c0 * 128:(c0 + n4) * 128], rb[:, 0:n4 * 128])

            for l in range(nlayers):
                last_l = (l == nlayers - 1)
                xectr[0] = 0

                wsrc = wb_in[l]
                wup = wb_up[l]

                def load_w(src, c0, n):
                    if src is wsrc:
                        need("in", l, in_job(c0))
                    else:
                        need("up", l, (c0 % DFF) // 512)
                    wb = wbuf_next()
                    wv = wb[:, 0:8 * n].rearrange("p (k e) -> p k e", k=8)
                    k.dma(wv, src.rearrange("(k p) e -> p k e", p=128)[:, :, c0:c0 + n])
                    return wv

                def unit_mm(ps, wv, e0, M):
                    for kk in range(8):
                        k.mm(ps[0:M, 0:Tb], wv[:, kk, e0:e0 + M], xTv[:, kk, :], start=(kk == 0), stop=(kk == 7))

                wsrc = wb_in[l]
                actmp = V(o_lnb, 2 * Tb).rearrange("p (u t) -> p u t", u=2)
                wv = load_w(wsrc, 0, 256)
                for u in range(2):
                    ps = psd()
                    unit_mm(ps, wv, u * 128, 128)
                    k.cp(ab[:, u, :], ps[:, 0:Tb])
                wv = load_w(wsrc, 256, 256)
                for u in range(2):
                    ps = psd()
                    unit_mm(ps, wv, u * 128, 128)
                    k.cp(actmp[:, u, :], ps[:, 0:Tb], eng="dve")
                if is_s:
                    load_state_T(st_conv[l], 32, 2, o_sst)
                wv = load_w(wsrc, 512, 256)
                for u in range(2):
                    ps = psd()
                    unit_mm(ps, wv, u * 128, 128)
                    acc = conv_unit(ps[:, 0:Tb], 3, cwa(l, u), halo_a(l, u), V(o_sst + u * 32, 32), V(o_sso + u * 32, 32),
                                    mul_by=actmp[:, u, :])
                    k.tt(ycv[:, u, :], acc, ab[:, u, :], ALU.mult)
                if is_s:
                    rows_out(lambda u: V(o_sso + u * 32, 32), 2, 32, s_conv[l])
                elif bi == 3:
                    rows_out(lambda u: halo_a(l, u), 2, 2, p_conv[l])
                if is_s:
                    load_state_T(st_gconv[l], 48, 12, o_sst)
                for g in range(6):
                    wv = load_w(wsrc, 768 + 256 * g, 256)
                    for uu in range(2):
                        j = 2 * g + uu
                        ps = psd()
                        unit_mm(ps, wv, uu * 128, 128)
                        acc = conv_unit(ps[:, 0:Tb], 4, cwg(l, j), halo_g(l, j), V(o_sst + j * 48, 48), V(o_sso + j * 48, 48))
                        k.act(qkv(j), acc, AF.Silu)
                if is_s:
                    rows_out(lambda u: V(o_sso + u * 48, 48), 12, 48, s_gconv[l])
                elif bi == 3:
                    rows_out(lambda u: halo_g(l, u), 12, 3, p_gconv[l])
                for g in range(2):
                    wv = load_w(wsrc, 2304 + 256 * g, 256)
                    for uu in range(2):
                        ps = psd()
                        unit_mm(ps, wv, uu * 128, 128)
                        k.act(szv[:, 2 * g + uu, :], ps[:, 0:Tb], AF.Silu)
                for g in range(2):
                    wv = load_w(wsrc, 2824 + 256 * g, 256)
                    for uu in range(2):
                        ps = psd()
                        unit_mm(ps, wv, uu * 128, 128)
                        k.cp(mqk[:, 2 * g + uu, :], ps[:, 0:Tb])
                wv = load_w(wsrc, 3336, 256)
                for t in range(nt):
                    ps = psd()
                    for kk in range(8):
                        k.mm(ps[:, 0:256], xTv[:, kk, t * 128:(t + 1) * 128], wv[:, kk, :], start=(kk == 0), stop=(kk == 7))
                    k.cp(vext(t)[:, :, 0:64], ps[:, 0:256].rearrange("p (h e) -> p h e", h=4))
                    k.memset(vext(t)[:, :, 64:65], 1.0, eng="pool")
                wv = load_w(wsrc, 3592, 256)
                for t in range(nt):
                    ps = psd()
                    for kk in range(8):
                        k.mm(ps[:, 0:256], xTv[:, kk, t * 128:(t + 1) * 128], wv[:, kk, :], start=(kk == 0), stop=(kk == 7))
                    k.act(sigo(t), ps[:, 0:256], AF.Sigmoid)
                wg = Vb(o_wg, 64).rearrange("p (k e) -> p k e", k=8)
                wsr = wsrc.rearrange("(k p) e -> p k e", p=128)
                need("in", l, 5)
                k.dma(wg[:, :, 0:8], wsr[:, :, 2816:2824])
                k.dma(wg[:, :, 8:16], wsr[:, :, 3848:3856])
                if bi == 0 and l == 0:
                    dump("qkv", V(o_qkv, 12 * Tb))
                    dump("ycA", V(o_yc, 2048))

                for t in range(nt if stage >= 2 else 0):
                    ts_ = slice(t * 128, (t + 1) * 128)
                    gti = bi * 4 + t
                    par = gti % 2 if not is_s else 0
                    graw = [V(o_graw + i * 128, 128, 0, 4) for i in range(4)]
                    for i in range(4):
                        ps = psq()
                        for kk in range(8):
                            k.mm(ps[0:4, :], wg[:, kk, i * 4:i * 4 + 4], xTv[:, kk, ts_], start=(kk == 0), stop=(kk == 7))
                        k.cp(graw[i], ps[0:4, :], eng="dve")
                    r_G, r_lb, r_lrq, r_lrk, r_ra, r_rq, r_cj, r_kd, r_tmp, r_GL = [row(i) for i in range(10)]
                    k.act(r_tmp, graw[0], AF.Exp, bias=gbv(l, 1))
                    k.act(r_tmp, r_tmp, AF.Ln, bias=1.0)
                    k.ts(r_tmp, r_tmp, gbv(l, 4), ALU.mult)
                    k.scan(r_G, cmask, r_tmp, 0.0, ALU.mult, ALU.add)
                    k.act(r_lb, graw[1], AF.Exp, scale=-1.0)
                    k.act(r_lb, r_lb, AF.Ln, bias=1.0)
                    k.ts(r_lb, r_lb, -1.0, ALU.mult)
                    for qi, rr in ((0, r_lrq), (1, r_lrk)):
                        ps = psq()
                        for h in range(4):
                            sq = slot(h, 3 + qi)
                            k.act(sq, qkv(qi * 4 + h)[:, ts_], AF.Square)
                            k.mm(ps[0:4, :], eh(h), sq, start=(h == 0), stop=(h == 3))
                        k.act(rr, ps[0:4, :], AF.Ln, bias=NORM_EPS)
                        k.ts(rr, rr, -0.5, ALU.mult)
                    k.tt(r_ra, r_G, r_lb, ALU.add)
                    k.tt(r_ra, r_ra, r_lrk, ALU.add)
                    k.ts(r_rq, r_lrq, math.log(128.0 ** -0.5), ALU.add)
                    k.tt(r_rq, r_rq, r_G, ALU.add)
                    k.tt(r_cj, r_lrk, r_G, ALU.subtract)
                    G3 = r_G.rearrange("p (c l) -> p c l", l=L)
                    k.tt(r_kd.rearrange("p (c l) -> p c l", l=L), r_cj.rearrange("p (c l) -> p c l", l=L),
                         G3[:, :, L - 1:L].broadcast_to([4, nch, L]), ALU.add)
                    k.cp(r_GL[:, 0:nch].rearrange("p (c o) -> p c o", o=1), G3[:, :, L - 1:L], eng="dve")
                    craw = V(o_cols, 20)
                    cexp = V(o_cols + 20, 20)
                    ps = psq()
                    for ci, rr in enumerate((r_cj, r_lb, r_ra, r_kd, r_rq)):
                        k.mm(ps[:, ci * 4:ci * 4 + 4], rr, ident[0:4, 0:4], start=True, stop=True)
                    k.cp(craw, ps[:, 0:20], eng="dve")
                    k.act(cexp, ps[:, 0:20], AF.Exp)
                    elast = V(o_misc, 4 * nch2)
                    ps = psq()
                    for h in range(4):
                        k.mm(ps[:, h * nch2:h * nch2 + nch2], sel(h), r_GL[:, 0:nch2], start=True, stop=True)
                    k.act(elast, ps[:, 0:4 * nch2], AF.Exp)
                    if bi == 0 and l == 0 and t == 0:
                        dump("rows", V(o_rows, 10 * 128, 0, 4))
                        dump("cexp", V(o_cols, 40))

                    qT_ = [qkv(h)[:, ts_] for h in range(4)]
                    kT_ = [qkv(4 + h)[:, ts_] for h in range(4)]
                    vT_ = [qkv(8 + h)[:, ts_] for h in range(4)]
                    for h in range(4):
                        ps = psq()
                        k.mm(ps, sel(h), r_ra, start=True, stop=False)
                        k.mm(ps, identb, MAS, start=False, stop=True)
                        k.act(slot(h, 0), ps, AF.Exp, bias=craw[:, h:h + 1])
                        ps2 = psq()
                        k.mm(ps2, kT_[h], kT_[h])
                        k.stt(slot(h, 4), ps2, -1.0, slot(h, 0), ALU.mult, ALU.mult)
                    for h in range(4):
                        ps = psq()
                        k.mm(ps, sel(h), r_rq, start=True, stop=False)
                        k.mm(ps, identb, MAI, start=False, stop=True)
                        k.act(slot(h, 1), ps, AF.Exp, bias=craw[:, h:h + 1])
                        ps2 = psq()
                        k.mm(ps2, kT_[h], qT_[h])
                        k.tt(slotb(h, 2), ps2, slot(h, 1), ALU.mult)
                    for h in range(4):
                        ps = psq()
                        k.tr(ps, slot(h, 4), ident)
                        k.cp(slot(h, 5), ps)
                        k.tt(slot(h, 3), slot(h, 4), ident, ALU.add, eng="pool")
                    for h in range(4):
                        ps = psq()
                        k.mm(ps, slot(h, 4), slot(h, 5))
                        k.cp(slot(h, 6), ps)
                        ps2 = psq()
                        k.mm(ps2, slot(h, 5), slot(h, 4))
                        k.cp(slot(h, 4), ps2, eng="dve")
                    cP = [(3, 6)] * 4
                    for lev in range(1, nlev):
                        lastlev = (lev == nlev - 1)
                        nP = []
                        for h in range(4):
                            pb, ipm = cP[h]
                            npb, npm = 10 - pb, 11 - ipm
                            if not lastlev:
                                ps = psq(256)
                                k.mm(ps, slot(h, ipm), V(o_scan + (h * 12 + pb) * 128, 256))
                                k.tt(slot(h, npb), ps[:, 0:128], slot(h, pb), ALU.add)
                                k.cp(slot(h, npb + 1), ps[:, 128:256])
                                ps2 = psq()
                                k.mm(ps2, slot(h, pb + 1), slot(h, ipm))
                                k.cp(slot(h, npm), ps2, eng="dve")
                            else:
                                ps = psq()
                                k.mm(ps, slot(h, ipm), slot(h, pb))
                                k.tt(slotb(h, npb), ps, slot(h, pb), ALU.add)
                            nP.append((npb, npm))
                        cP = nP
                    TTf = [slotb(h, cP[h][0]) for h in range(4)]
                    for h in range(4):
                        ps = psq()
                        k.tr(ps, kT_[h], ident)
                        k.ts(slotb(h, 0), ps, cexp[:, 8 + h:9 + h], ALU.mult)
                        k.act(slotb(h, 1), ps, AF.Identity, scale=cexp[:, 12 + h:13 + h])
                        ps2 = psq()
                        k.tr(ps2, vT_[h], ident)
                        k.ts(slotb(h, 9), ps2, cexp[:, 4 + h:5 + h], ALU.mult)
                    for h in range(4):
                        ps = psq()
                        k.mm(ps, slotb(h, 0), TTf[h])
                        k.act(slot(h, 10) if is_s else slotb(h, 10), ps, AF.Identity, scale=-1.0)
                    O_ = [slot(h, 5) for h in range(4)]
                    if not is_s:
                        for h in range(4):
                            Sc, Sn = Sst(l, par, h), Sst(l, 1 - par, h)
                            k.cp(Sb(h), Sc)
                            ps = psq()
                            k.mm(ps, TTf[h], slotb(h, 9), start=True, stop=False)
                            k.mm(ps, slotb(h, 10), Sb(h), start=False, stop=True)
                            k.cp(slotb(h, 11), ps)
                            ps1 = psq()
                            k.mm(ps1, qT_[h], Sc)
                            k.act(slot(h, 0), ps1, AF.Identity, scale=cexp[:, 16 + h:17 + h])
                            ps2 = psq()
                            k.mm(ps2, slotb(h, 2), slotb(h, 11))
                            k.tt(O_[h], ps2, slot(h, 0), ALU.add)
                            ps3 = psq()
                            k.mm(ps3, slotb(h, 1), slotb(h, 11))
                            k.stt(Sn, Sc, elast[:, h * nch2:h * nch2 + 1], ps3, ALU.mult, ALU.add)
                            if gti == 15:
                                k.dma(p_gdn[l, h], Sn)
                    else:
                        S0v = V(o_S0, 2048).rearrange("p (s v) -> p s v", s=16)
                        mskd = V(o_msk, 2176).rearrange("p (s r) -> p s r", r=136)[:, :, 0:8]
                        mskf = V(o_msk, 2048).rearrange("p (s i) -> p s i", s=16)
                        for h in range(4):
                            k.dma(S0v, st_gdn[l, :, h].rearrange("s k v -> k s v"))
                            if h == 0 and l == 0:
                                k.memset(V(o_msk, 2176), 0.0, eng="pool")
                            k.cp(mskd, slot(h, 10).rearrange("p (s r) -> p s r", r=8), eng="pool")
                            ps = psq()
                            k.mm(ps, TTf[h], slotb(h, 9), start=True, stop=False)
                            for s in range(16):
                                k.mm(ps, mskf[:, s, :], S0v[:, s, :], start=False, stop=(s == 15))
                            k.cp(slotb(h, 11), ps)
                            k.cp(mskd, qT_[h].rearrange("p (s r) -> p s r", r=8), eng="pool")
                            ps1 = psq()
                            for s in range(16):
                                k.mm(ps1, mskf[:, s, :], S0v[:, s, :], start=(s == 0), stop=(s == 15))
                            k.act(slot(h, 0), ps1, AF.Identity, scale=cexp[:, 16 + h:17 + h])
                            ps2 = psq()
                            k.mm(ps2, slotb(h, 2), slotb(h, 11))
                            k.tt(O_[h], ps2, slot(h, 0), ALU.add)
                            for s in range(16):
                                kdm = Vb(o_kdm + (s % 3) * 128, 64)
                                k.ts(kdm, slotb(h, 1), rowmask[:, s:s + 1], ALU.mult)
                                ps3 = psq()
                                k.mm(ps3, kdm, slotb(h, 11))
                                k.stt(S0v[:, s, :], S0v[:, s, :], elast[:, h * 16 + s:h * 16 + s + 1], ps3, ALU.mult, ALU.add)
                            k.dma(s_gdn[l, :, h].rearrange("s k v -> k s v"), S0v)
                    if bi == 0 and l == 0 and t == 0:
                        dump("o_gdn", V(o_scan + 5 * 128, 128))
                        pass
                    ss = V(o_cols + 40, 4)
                    for h in range(4):
                        k.act(slot(h, 0), O_[h], AF.Square, accum=ss[:, h:h + 1])
                    k.ts(ss, ss, 1.0 / 128, ALU.mult, NORM_EPS, ALU.add)
                    k.act(ss, ss, AF.Sqrt)
                    k.recip(ss, ss)
                    for h in range(4):
                        k.ts(slot(h, 9), O_[h], ss[:, h:h + 1], ALU.mult)
                        ps = psq()
                        k.tr(ps, slot(h, 9), ident)
                        k.stt(ycv[:, 2 + h, ts_], ps, V(o_gnw + l, 1), szv[:, h, ts_], ALU.mult, ALU.mult)

                    r_ig, r_lf, r_F, r_m, r_rD, r_cD, r_rI, r_em, r_kw, r_mp, r_t2, r_ch = [row(10 + i) for i in range(12)]
                    k.ts(r_ig, graw[2], gbv(l, 2), ALU.add)
                    k.act(r_lf, graw[3], AF.Exp, scale=-1.0, bias=gbv(l, 5))
                    k.act(r_lf, r_lf, AF.Ln, bias=1.0)
                    k.ts(r_lf, r_lf, -1.0, ALU.mult)
                    k.scan(r_F, cmask, r_lf, 0.0, ALU.mult, ALU.add)
                    F3 = r_F.rearrange("p (c l) -> p c l", l=L)
                    m3 = r_m.rearrange("p (c l) -> p c l", l=L)
                    if not is_s:
                        k.scan(r_m, r_lf, r_ig, mprev(l), ALU.add, ALU.max)
                        k.cp(r_mp, mprev(l).broadcast_to([4, 128]), eng="dve")
                    else:
                        m0 = r_ch[:, 64:80]
                        k.dma(m0, st_m[l].rearrange("s h -> h s"), allow_slow_non_contiguous=True)
                        k.cp(r_mp.rearrange("p (c l) -> p c l", l=8), m0.rearrange("p (c o) -> p c o", o=1).broadcast_to([4, 16, 8]), eng="dve")
                        ig3 = r_ig.rearrange("p (c l) -> p c l", l=8)
                        lf3 = r_lf.rearrange("p (c l) -> p c l", l=8)
                        t23 = r_t2.rearrange("p (c l) -> p c l", l=8)
                        k.cp(r_t2, r_ig, eng="dve")
                        k.tt(t23[:, :, 0:1], lf3[:, :, 0:1], m0.rearrange("p (c o) -> p c o", o=1), ALU.add)
                        k.tt(t23[:, :, 0:1], t23[:, :, 0:1], ig3[:, :, 0:1], ALU.max)
                        r_lf2 = row(25)
                        k.cp(r_lf2, r_lf, eng="dve")
                        k.memset(r_lf2.rearrange("p (c l) -> p c l", l=8)[:, :, 0:1], NEG)
                        k.scan(r_m, r_lf2, r_t2, 0.0, ALU.add, ALU.max)
                    k.tt(r_rD, r_F, r_m, ALU.subtract)
                    k.tt(r_cD, r_ig, r_F, ALU.subtract)
                    k.ts(r_cD, r_cD, math.log(0.125), ALU.add)
                    k.tt(r_rI, r_rD, r_mp, ALU.add)
                    k.ts(r_em, r_m, -1.0, ALU.mult)
                    chA = r_ch[:, 0:nch].rearrange("p (c o) -> p c o", o=1)
                    chB = r_ch[:, 16:16 + nch].rearrange("p (c o) -> p c o", o=1)
                    k.tt(chA, F3[:, :, L - 1:L], m3[:, :, L - 1:L], ALU.subtract)
                    k.tt(chB, chA, r_mp.rearrange("p (c l) -> p c l", l=L)[:, :, 0:1], ALU.add)
                    k.tt(r_kw.rearrange("p (c l) -> p c l", l=L), r_cD.rearrange("p (c l) -> p c l", l=L),
                         chA.broadcast_to([4, nch, L]), ALU.add)
                    if not is_s:
                        k.cp(mprev(l), r_m[:, 127:128], eng="dve")
                        if gti == 15:
                            k.dma(p_m[l].rearrange("(p o) -> p o", o=1), r_m[:, 127:128])
                    else:
                        k.dma(s_m[l].rearrange("s h -> h s"), m3[:, :, 7], allow_slow_non_contiguous=True)
                    mraw = V(o_cols + 48, 16)
                    mexp = V(o_cols + 64, 16)
                    ps = psq()
                    for ci, rr in enumerate((r_cD, r_em, r_kw, r_rI)):
                        k.mm(ps[:, ci * 4:ci * 4 + 4], rr, ident[0:4, 0:4], start=True, stop=True)
                    k.cp(mraw, ps[:, 0:16], eng="dve")
                    k.act(mexp, ps[:, 0:16], AF.Exp)
                    dcb = V(o_misc + 64, 4 * nch2)
                    ps = psq()
                    for h in range(4):
                        k.mm(ps[:, h * nch2:h * nch2 + nch2], sel(h), r_ch[:, 16:16 + nch2], start=True, stop=True)
                    k.act(dcb, ps[:, 0:4 * nch2], AF.Exp)
                    nq = V(o_scan + 48 * 128 - 4 * 65, 260).rearrange("p (h e) -> p h e", h=4)
                    kwt = [None] * 4
                    for hc in range(2):
                        ps = psq()
                        k.tr(ps, mqk[:, 2 + hc, ts_], ident)
                        for hh in range(2):
                            h = hc * 2 + hh
                            kwt[h] = slotb(h, 4)[:, 0:64]
                            k.ts(kwt[h], ps[:, hh * 64:(hh + 1) * 64], mexp[:, 8 + h:9 + h], ALU.mult)
                    for h in range(4):
                        po = (h % 2) * 64
                        qTh = mqk[po:po + 64, h // 2, ts_]
                        kTh = mqk[po:po + 64, 2 + h // 2, ts_]
                        ps = psq()
                        k.mm(ps, sel(h), r_rD, start=True, stop=False)
                        k.mm(ps, identb, MAI, start=False, stop=True)
                        k.act(slot(h, 0), ps, AF.Exp, bias=mraw[:, h:h + 1])
                        ps2 = psq()
                        k.mm(ps2, kTh, qTh)
                        k.tt(slotb(h, 1), ps2, slot(h, 0), ALU.mult)
                        if not is_s:
                            Cc, Cn = Cst(l, par, h), Cst(l, 1 - par, h)
                            ps1 = psq()
                            k.mm(ps1[:, 0:65], qTh, Cc)
                            k.act(slot(h, 2)[:, 0:65], ps1[:, 0:65], AF.Identity, scale=mexp[:, 12 + h:13 + h])
                            ps3 = psq()
                            k.mm(ps3[:, 0:65], slotb(h, 1), vext(t)[:, h, :])
                            k.tt(nq[:, h, :], ps3[:, 0:65], slot(h, 2)[:, 0:65], ALU.add)
                            ps4 = psq()
                            k.mm(ps4[po:po + 64, 0:65], kwt[h], vext(t)[:, h, :])
                            k.stt(Cn, Cc, dcb[po:po + 64, h * nch2:h * nch2 + 1], ps4[po:po + 64, 0:65], ALU.mult, ALU.add)
                            if gti == 15:
                                k.dma(p_c[l, h], Cn[:, 0:64])
                                k.dma(p_n[l, h].rearrange("(p o) -> p o", o=1), Cn[:, 64:65])
                        else:
                            C0v = V(o_S0, 16 * 65, po, po + 64).rearrange("p (s e) -> p s e", s=16)
                            k.dma(C0v[:, :, 0:64], st_c[l, :, h].rearrange("s d e -> d s e"))
                            k.dma(C0v[:, :, 64:65], st_n[l, :, h].rearrange("s (d o) -> d s o", o=1), allow_slow_non_contiguous=True)
                            mskd = V(o_msk, 2176, po, po + 64).rearrange("p (s r) -> p s r", r=136)[:, :, 0:8]
                            mskf = V(o_msk, 2048, po, po + 64).rearrange("p (s i) -> p s i", s=16)
                            k.cp(mskd, qTh.rearrange("p (s r) -> p s r", r=8), eng="pool")
                            ps1 = psq()
                            for s in range(16):
                                k.mm(ps1[:, 0:65], mskf[:, s, :], C0v[:, s, :], start=(s == 0), stop=(s == 15))
                            k.act(slot(h, 2)[:, 0:65], ps1[:, 0:65], AF.Identity, scale=mexp[:, 12 + h:13 + h])
                            ps3 = psq()
                            k.mm(ps3[:, 0:65], slotb(h, 1), vext(t)[:, h, :])
                            k.tt(nq[:, h, :], ps3[:, 0:65], slot(h, 2)[:, 0:65], ALU.add)
                            for s in range(16):
                                kwm = Vb(o_kdm + (s % 3) * 128, 32)
                                k.ts(kwm, kwt[h], rowmask[:, s:s + 1], ALU.mult)
                                ps4 = psq()
                                k.mm(ps4[po:po + 64, 0:65], kwm, vext(t)[:, h, :])
                                k.stt(C0v[:, s, :], C0v[:, s, :], dcb[po:po + 64, h * 16 + s:h * 16 + s + 1], ps4[po:po + 64, 0:65], ALU.mult, ALU.add)
                            k.dma(s_c[l, :, h].rearrange("s d e -> d s e"), C0v[:, :, 0:64])
                            k.dma(s_n[l, :, h].rearrange("s (d o) -> d s o", o=1), C0v[:, :, 64:65], allow_slow_non_contiguous=True)
                    den = V(o_cols + 80, 4)
                    qn = nq[:, :, 64]
                    k.stt(den, qn, -1.0, qn, ALU.mult, ALU.max)
                    k.tt(den, den, mexp[:, 4:8], ALU.max)
                    k.recip(den, den)
                    hbuf = slot(0, 5)[:, 0:128]
                    hb2 = V(o_scan + 5 * 128, 128)
                    hb3 = V(o_scan + 6 * 128, 128)
                    hall = V(o_scan + 5 * 128, 256)
                    stats = V(o_cols + 84, 24)
                    mv = V(o_cols + 108, 8)
                    rstd = V(o_cols + 116, 4)
                    for h in range(4):
                        hh_ = hall[:, h * 64:(h + 1) * 64]
                        k.stt(hh_, nq[:, h, 0:64], den[:, h:h + 1], sigo(t)[:, h * 64:(h + 1) * 64], ALU.mult, ALU.mult)
                        k.bn_stats(stats[:, h * 6:(h + 1) * 6], hh_)
                        k.bn_aggr(mv[:, h * 2:(h + 1) * 2], stats[:, h * 6:(h + 1) * 6])
                    mv3 = mv.rearrange("p (h two) -> p h two", two=2)
                    k.ts(rstd.rearrange("p (h o) -> p h o", o=1), mv3[:, :, 1:2], LN_EPS, ALU.add)
                    k.act(rstd, rstd, AF.Sqrt)
                    k.recip(rstd, rstd)
                    for h in range(4):
                        hh_ = hall[:, h * 64:(h + 1) * 64]
                        k.ts(hh_, hh_, mv[:, 2 * h:2 * h + 1], ALU.subtract, rstd[:, h:h + 1], ALU.mult)
                    k.tt(hall, hall, V(o_mnw + l * 256, 256), ALU.mult, eng="pool")
                    for hc in range(2):
                        ps = psq()
                        k.tr(ps, hall[:, hc * 128:(hc + 1) * 128], ident)
                        k.cp(ycv[:, 6 + hc, ts_], ps)
                    if bi == 0 and l == 0 and t == 0:
                        dump("hml", hall)

                if bi == 0 and l == 0:
                    dump("ycat", V(o_yc, 2048))
                if stage < 3:
                    continue
                def ln_resid(t, pss):
                    xt_ = xtok(t)
                    for hf in range(4):
                        k.stt(xt_[:, hf * 256:(hf + 1) * 256], xt_[:, hf * 256:(hf + 1) * 256], ALPHA, pss[hf], ALU.mult, ALU.add)

                def ln_tile(t, pss, g_src, b_src, which, final_out=None, need_T=True):
                    xt_ = xtok(t)
                    if pss is not None:
                        ln_resid(t, pss)
                    stats = V(o_cols + 84, 12)
                    mv = V(o_cols + 108, 2)
                    rs = V(o_cols + 116, 2)
                    k.bn_stats(stats[:, 0:6], xt_[:, 0:512])
                    k.bn_stats(stats[:, 6:12], xt_[:, 512:1024])
                    k.bn_aggr(mv, stats)
                    k.ts(rs[:, 0:1], mv[:, 1:2], LN_EPS, ALU.add)
                    k.act(rs[:, 0:1], rs[:, 0:1], AF.Sqrt)
                    k.recip(rs[:, 0:1], rs[:, 0:1])
                    k.stt(rs[:, 1:2], mv[:, 0:1], -1.0, rs[:, 0:1], ALU.mult, ALU.mult)
                    ntok_ = V(o_scan + (t % 2) * 1024, 1024)
                    k.act(ntok_, xt_, AF.Identity, scale=rs[:, 0:1], bias=rs[:, 1:2])
                    if need_T:
                        transpose_to_xT(ntok_, t, scale_l=l, which=which)
                    gB = V(o_lnb, 1024)
                    bB = V(o_lnb + 1024, 1024)
                    k.tt(xt_, ntok_, gB, ALU.mult, eng="pool")
                    k.tt(xt_, xt_, bB, ALU.add, eng="pool")
                    if final_out is not None:
                        k.dma(final_out, xt_)

                k.dma(V(o_lnb, 1024), ln1_g[l:l + 1, :].partition_broadcast(128))
                k.dma(V(o_lnb + 1024, 1024), ln1_b[l:l + 1, :].partition_broadcast(128))
                need("o", l, 0)
                wos = []
                for g in range(4):
                    wb = wbuf_next()
                    wv = wb[:, 0:2048].rearrange("p (k e) -> p k e", k=8)
                    k.dma(wv, wb_o[l].rearrange("(k p) e -> p k e", p=128)[:, :, g * 256:(g + 1) * 256])
                    wos.append(wv)
                for t in range(nt):
                    ps = psd()
                    pss = []
                    for g in range(4):
                        if g == 2:
                            ps = psd()
                        pp = ps[:, (g % 2) * 256:(g % 2 + 1) * 256]
                        for kk in range(8):
                            k.mm(pp, ycv[:, kk, t * 128:(t + 1) * 128], wos[g][:, kk, :], start=(kk == 0), stop=(kk == 7))
                        pss.append(pp)
                    ln_tile(t, pss, ln1_g, ln1_b, 0)
                if bi == 0 and l == 0:
                    dump("x1", V(o_xtok, 4096))
                if stage < 4:
                    continue
                if is_s:
                    load_state_T(st_ffn[l], 32, 22, o_sst)
                wup = wb_up[l]
                for g in range(11):
                    wvg = load_w(wup, g * 256, 256)
                    wvv = load_w(wup, DFF + g * 256, 256)
                    for uu in range(2):
                        j = 2 * g + uu
                        ps = psd()
                        unit_mm(ps, wvg, uu * 128, 128)
                        acc = conv_unit(ps[:, 0:Tb], 3, cwf(l, j), halo_f(l, j), V(o_sst + j * 32, 32), V(o_sso + j * 32, 32))
                        k.act(acc, acc, AF.Silu)
                        ps2 = psd()
                        unit_mm(ps2, wvv, uu * 128, 128)
                        k.tt(hTv[:, j, :], ps2[:, 0:Tb], acc, ALU.mult)
                if is_s:
                    rows_out(lambda u: V(o_sso + u * 32, 32), 22, 32, s_ffn[l])
                elif bi == 3:
                    rows_out(lambda u: halo_f(l, u), 22, 2, p_ffn[l])
                k.dma(V(o_lnb, 1024), ln2_g[l:l + 1, :].partition_broadcast(128))
                k.dma(V(o_lnb + 1024, 1024), ln2_b[l:l + 1, :].partition_broadcast(128))
                for g in range(11):
                    need("down", l, (g * 256) // 768)
                    wb = wbuf_next()
                    wv = wb[:, 0:2048].rearrange("p (k e) -> p k e", k=2)
                    k.dma(wv, wb_down[l][g * 256:(g + 1) * 256, :].rearrange("(k p) e -> p k e", p=128))
                    for kk in range(2):
                        kc = 2 * g + kk
                        for t in range(nt):
                            for hf in range(2):
                                k.mm(PS[:, (t * 2 + hf) * 512:(t * 2 + hf + 1) * 512], hTv[:, kc, t * 128:(t + 1) * 128],
                                     wv[:, kk, hf * 512:(hf + 1) * 512], start=(kc == 0), stop=(kc == 21))
                for t in range(nt):
                    pss = [PS[:, (t * 2) * 512 + q * 256:(t * 2) * 512 + (q + 1) * 256] for q in range(4)]
                    ln_resid(t, pss)
                for t in range(nt):
                    fo = y[t0 + t * 128:t0 + (t + 1) * 128, :] if last_l else None
                    ln_tile(t, None, ln2_g, ln2_b, 2, final_out=fo, need_T=not last_l)

    S.limit = limit
    print('main loop starts at op', len(S.ops))
    try:
        main_loops()
    except StopIteration:
        print('LIMIT reached at', len(S.ops))
    S.closing = True
    k.fence(outs_all + [wb_in, wb_o, wb_up, wb_down])
    st = S.finalize()
    print(st)
    return nc


def _in_maps(inp, nlayers=2):
    consts = make_consts()
    maps = []
    f = lambda a: np.ascontiguousarray(np.asarray(a, dtype=np.float32))
    for c in range(8):
        sl = slice(16 * c, 16 * c + 16)
        m = {
            "x_in": f(np.concatenate([inp["x_prompt"][c], np.asarray(inp["x_sample"])[sl].reshape(128, D)], 0)),
            "st_conv": f(np.asarray(inp["state_conv_mix"])[:, sl].reshape(2, 32, 256)),
            "st_gconv": f(np.asarray(inp["state_gdn_conv"])[:, sl].reshape(2, 48, 1536)),
            "st_gdn": f(np.asarray(inp["state_gdn"])[:, sl]),
            "st_c": f(np.asarray(inp["state_mlstm_c"])[:, sl]),
            "st_n": f(np.asarray(inp["state_mlstm_n"])[:, sl]),
            "st_m": f(np.asarray(inp["state_mlstm_m"])[:, sl]),
            "st_ffn": f(np.asarray(inp["state_ffn_conv"])[:, sl].reshape(2, 32, DFF)),
            "consts": consts,
        }
        for nm in ("w_in", "conv_w", "gdn_conv_w", "gdn_a_log", "gdn_dt_bias", "gdn_norm_w", "ml_i_bias",
                   "ml_f_bias", "ml_norm_w", "w_o", "ln1_g", "ln1_b", "w_up", "ffn_conv_w", "w_down",
                   "ln2_g", "ln2_b"):
            m[nm] = f(inp[nm])
        maps.append(m)
    return maps


def kernel(**inp):
    nc = build()
    maps = _in_maps(inp)
    res = run_bass_kernel_spmd(nc, maps, core_ids=list(range(8)))
    R = res.results
    yp = np.stack([R[c]["y"][0:2048] for c in range(8)], 0)
    ys = np.concatenate([R[c]["y"][2048:].reshape(16, 8, D) for c in range(8)], 0)

    def pst(nm, shp):
        return np.stack([R[c][nm].reshape(shp) for c in range(8)], 1)

    def sst(nm, shp):
        return np.concatenate([R[c][nm].reshape(shp) for c in range(8)], 1)

    outs = (yp, ys,
            pst("p_conv", (2, 2, 256)), pst("p_gconv", (2, 3, 1536)), pst("p_gdn", (2, 4, 128, 128)),
            pst("p_c", (2, 4, 64, 64)), pst("p_n", (2, 4, 64)), pst("p_m", (2, 4)), pst("p_ffn", (2, 2, DFF)),
            sst("s_conv", (2, 16, 2, 256)), sst("s_gconv", (2, 16, 3, 1536)), sst("s_gdn", (2, 16, 4, 128, 128)),
            sst("s_c", (2, 16, 4, 64, 64)), sst("s_n", (2, 16, 4, 64)), sst("s_m", (2, 16, 4)),
            sst("s_ffn", (2, 16, 2, DFF)))
    return tuple(np.ascontiguousarray(o.astype(np.float32)) for o in outs)
```

```python
import numpy as np
import concourse.bass as bass
import concourse.mybir as mybir

F32 = mybir.dt.float32
BF16 = mybir.dt.bfloat16
AF = mybir.ActivationFunctionType
ALU = mybir.AluOpType
AX = mybir.AxisListType
ESZ = {F32: 4, BF16: 2, mybir.dt.float32r: 4}
BUCK = 512


def box(ap):
    t = ap.tensor
    es = ESZ[ap.dtype]
    dims = [(int(s), int(c)) for s, c in ap.ap]
    off = int(ap.offset)
    cls = type(t).__name__
    if cls.startswith("DRam"):
        ext = sum((c - 1) * abs(s) for s, c in dims) + 1
        return (t.name, 0, 1, off * es, (off + ext) * es)
    rowlen = 1
    for s in list(t.shape)[1:]:
        rowlen *= int(s)
    p0 = off // rowlen
    f0 = off % rowlen
    if dims[0][0] == rowlen or dims[0][1] == 1:
        pc = dims[0][1]
        rest = dims[1:]
    elif dims[0][0] == 0:
        pc = 1
        rest = dims[1:]
    else:
        pc = 1
        rest = dims
    ext = sum((c - 1) * abs(s) for s, c in rest) + 1
    if cls.startswith("PSum") or cls.startswith("Psum") or cls.startswith("PS"):
        b0 = (f0 * es) // 2048 * 2048
        b1 = ((f0 + ext) * es + 2047) // 2048 * 2048
        return (t.name, 0, 128, b0, b1)
    return (t.name, p0, p0 + pc, f0 * es, (f0 + ext) * es)


class Sched:
    def __init__(self, nc, n_sp_sems=16, n_pool_sems=8):
        self.nc = nc
        self.ops = []
        self.engs = {"pe": nc.tensor, "dve": nc.vector, "act": nc.scalar,
                     "pool": nc.gpsimd, "sp": nc.sync}
        self.esem = {e: nc.semaphore("sem_" + e).__enter__() for e in self.engs}
        self.dsems = {"sp": [nc.semaphore("dsp%d" % i).__enter__() for i in range(n_sp_sems)],
                      "pool": [nc.semaphore("dpl%d" % i).__enter__() for i in range(n_pool_sems)],
                      "act": []}

    limit = None

    def add(self, eng, fn, r, w, dma=False):
        if self.limit is not None and len(self.ops) >= self.limit and not getattr(self, 'closing', False):
            raise StopIteration("limit")
        rb = [box(a) for a in r]
        wb = [box(a) for a in w]
        wb = wb + [b for b in rb if b[0] == 'psum' or b[0] == 'pa']
        self.ops.append((eng, fn, rb, wb, dma))

    def finalize(self):
        ops = self.ops
        n = len(ops)
        recs = {}
        deps = [None] * n
        pos = [0] * n
        cnt = {e: 0 for e in self.engs}
        dma_n = {q: 0 for q in self.dsems}
        dma_hist = {q: [] for q in self.dsems}
        dsem = [None] * n
        for i, (eng, fn, R, W, dma) in enumerate(ops):
            pos[i] = cnt[eng]
            cnt[eng] += 1
            d = set()
            for (nm, p0, p1, b0, b1) in R:
                for bk in range(b0 // BUCK, (b1 - 1) // BUCK + 1):
                    for rec in recs.get((nm, bk), ()):
                        if rec[5] and rec[0] < p1 and p0 < rec[1] and rec[2] < b1 and b0 < rec[3]:
                            d.add(rec[4])
            for (nm, p0, p1, b0, b1) in W:
                for bk in range(b0 // BUCK, (b1 - 1) // BUCK + 1):
                    for rec in recs.get((nm, bk), ()):
                        if rec[0] < p1 and p0 < rec[1] and rec[2] < b1 and b0 < rec[3]:
                            d.add(rec[4])
            for (nm, p0, p1, b0, b1) in W:
                for bk in range(b0 // BUCK, (b1 - 1) // BUCK + 1):
                    L = recs.setdefault((nm, bk), [])
                    lo = max(b0, bk * BUCK)
                    hi = min(b1, (bk + 1) * BUCK)
                    L[:] = [rec for rec in L if not (p0 <= rec[0] and rec[1] <= p1 and
                                                     lo <= max(rec[2], bk * BUCK) and
                                                     min(rec[3], (bk + 1) * BUCK) <= hi)]
                    L.append((p0, p1, b0, b1, i, True))
            for (nm, p0, p1, b0, b1) in R:
                for bk in range(b0 // BUCK, (b1 - 1) // BUCK + 1):
                    L = recs.setdefault((nm, bk), [])
                    if not dma:
                        lo = max(b0, bk * BUCK)
                        hi = min(b1, (bk + 1) * BUCK)
                        L[:] = [rec for rec in L if rec[5] or ops[rec[4]][4] or ops[rec[4]][0] != eng or not (
                            p0 <= rec[0] and rec[1] <= p1 and lo <= max(rec[2], bk * BUCK) and
                            min(rec[3], (bk + 1) * BUCK) <= hi)]
                    L.append((p0, p1, b0, b1, i, False))
            if dma:
                q = eng
                k = dma_n[q]
                P = len(self.dsems[q])
                dsem[i] = (q, k % P, 16 * (k // P + 1))
                if k >= P:
                    d.add(dma_hist[q][k - P])
                dma_hist[q].append(i)
                dma_n[q] += 1
            d.discard(i)
            deps[i] = d
        known = {e: {} for e in self.engs}
        snap = [None] * n
        sig = [False] * n
        waits = [None] * n
        for i, (eng, fn, R, W, dma) in enumerate(ops):
            kn = known[eng]
            need = {}
            for d in deps[i]:
                de = ops[d][0]
                ddma = ops[d][4]
                if ddma:
                    q, si, val = dsem[d]
                    key = ("D", q, si)
                    if kn.get(key, 0) >= val:
                        continue
                    if key not in need or dsem[need[key]][2] < val:
                        need[key] = d
                else:
                    if de == "pe" and eng == "pe" and not dma:
                        continue
                    if kn.get(de, -1) >= pos[d]:
                        continue
                    if de not in need or pos[need[de]] < pos[d]:
                        need[de] = d
            wl = []
            for key, d in need.items():
                if ops[d][4]:
                    if kn.get(key, 0) >= dsem[d][2]:
                        continue
                else:
                    if kn.get(key, -1) >= pos[d]:
                        continue
                wl.append(d)
                sig[d] = True
                for k2, v2 in snap[d].items():
                    if kn.get(k2, -1) < v2:
                        kn[k2] = v2
            waits[i] = wl
            s = dict(kn)
            if dma:
                q, si, val = dsem[i]
                s[("D", q, si)] = val
            else:
                s[eng] = pos[i]
            snap[i] = s
        cum = [0] * n
        c = {e: 0 for e in self.engs}
        for i, (eng, fn, R, W, dma) in enumerate(ops):
            if not dma and sig[i]:
                c[eng] += 1
            cum[i] = c[eng]
        nw = 0
        for i, (eng, fn, R, W, dma) in enumerate(ops):
            E = self.engs[eng]
            for d in waits[i]:
                if ops[d][4]:
                    q, si, val = dsem[d]
                    E.wait_ge(self.dsems[q][si], val)
                else:
                    E.wait_ge(self.esem[ops[d][0]], cum[d])
                nw += 1
            ins = fn()
            if ins is None:
                continue
            if dma:
                q, si, val = dsem[i]
                ins.then_inc(self.dsems[q][si], 16)
            elif sig[i]:
                ins.then_inc(self.esem[eng], 1)
        self.dbginfo = (waits, sig, cum, pos, dsem)
        self.stats = dict(n_ops=n, n_waits=nw, per_eng=cnt, sigs=c)
        return self.stats


class K:
    def __init__(self, nc, S):
        self.nc = nc
        self.S = S

    def mm(self, out, lhsT, rhs, start=True, stop=True):
        nc = self.nc
        self.S.add("pe", lambda: nc.tensor.matmul(out, lhsT, rhs, start=start, stop=stop),
                   [lhsT, rhs], [out])

    def tr(self, out, in_, ident):
        nc = self.nc
        self.S.add("pe", lambda: nc.tensor.transpose(out, in_, ident), [in_, ident], [out])

    def act(self, out, in_, func, bias=None, scale=None, accum=None, eng="act"):
        nc = self.nc
        kw = {}
        r = [in_]
        if bias is not None:
            kw["bias"] = bias
            if not isinstance(bias, (int, float)):
                r.append(bias)
        if scale is not None:
            kw["scale"] = scale
            if not isinstance(scale, (int, float)):
                r.append(scale)
        w = [out]
        if accum is not None:
            kw["accum_out"] = accum
            w.append(accum)
        self.S.add("act", lambda: nc.scalar.activation(out, in_, func, **kw), r, w)

    def tt(self, out, a, b, op, eng="dve"):
        E = self.S.engs[eng]
        self.S.add(eng, lambda: E.tensor_tensor(out, a, b, op), [a, b], [out])

    def ts(self, out, a, s1, op0, s2=None, op1=None, eng="dve", accum=None):
        E = self.S.engs[eng]
        r = [a]
        if not isinstance(s1, (int, float)):
            r.append(s1)
        if s2 is not None and not isinstance(s2, (int, float)):
            r.append(s2)
        w = [out]
        kw = {}
        if accum is not None:
            kw["accum_out"] = accum
            w.append(accum)
        if op1 is None:
            self.S.add(eng, lambda: E.tensor_scalar(out, a, s1, None, op0, **kw), r, w)
        else:
            self.S.add(eng, lambda: E.tensor_scalar(out, a, s1, s2, op0, op1, **kw), r, w)

    def stt(self, out, in0, scalar, in1, op0, op1):
        nc = self.nc
        r = [in0, in1]
        if not isinstance(scalar, (int, float)):
            r.append(scalar)
        self.S.add("dve", lambda: nc.vector.scalar_tensor_tensor(out, in0, scalar, in1, op0, op1), r, [out])

    def cp(self, out, in_, eng="act"):
        nc = self.nc
        if eng == "act":
            self.S.add("act", lambda: nc.scalar.copy(out, in_), [in_], [out])
        else:
            E = self.S.engs[eng]
            self.S.add(eng, lambda: E.tensor_copy(out, in_), [in_], [out])

    def memset(self, ap, v, eng="dve"):
        E = self.S.engs[eng]
        self.S.add(eng, lambda: E.memset(ap, v), [], [ap])

    def scan(self, out, d0, d1, init, op0, op1):
        nc = self.nc
        r = [d0, d1]
        if not isinstance(init, (int, float)):
            r.append(init)
        self.S.add("dve", lambda: nc.vector.tensor_tensor_scan(out, d0, d1, init, op0, op1), r, [out])

    def bn_stats(self, out, in_):
        nc = self.nc
        self.S.add("dve", lambda: nc.vector.bn_stats(out, in_), [in_], [out])

    def bn_aggr(self, out, in_):
        nc = self.nc
        self.S.add("dve", lambda: nc.vector.bn_aggr(out, in_), [in_], [out])

    def recip(self, out, in_):
        nc = self.nc
        self.S.add("dve", lambda: nc.vector.reciprocal(out, in_), [in_], [out])

    def dma(self, out, in_, q="sp", **kw):
        E = self.S.engs[q]
        self.S.add(q, lambda: E.dma_start(out=out, in_=in_, **kw), [in_], [out], dma=True)

    def fence(self, aps, q="sp"):
        self.S.add(q, lambda: None, list(aps), [])

from concourse.bass_utils import run_bass_kernel_spmd
import math

D = 1024
DIN = 3856
DFF = 2816
NEG = -1e30
LN_EPS = 1e-5
NORM_EPS = 1e-6
ALPHA = 4.0 ** 0.25
NTOK = 2176
NSP = 24
import os
CPENG = os.environ.get('CPENG', 'act')
BLOCKS = [(0, 512, False), (512, 512, False), (1024, 512, False), (1536, 512, False), (2048, 128, True)]

C_ID, C_MAIP, C_MASP, C_MAIS, C_MASS, C_SEL, C_CMP, C_CMS, C_ROWM, C_EH, C_ONE = \
    0, 128, 256, 384, 512, 640, 1152, 1280, 1408, 1424, 1440
NCST = 1448


def make_consts():
    c = np.zeros((128, NCST), np.float32)
    ii = np.arange(128)
    c[:, C_ID:C_ID + 128] = np.eye(128)
    J, I = np.meshgrid(ii, ii, indexing="ij")
    same = (J // 8) == (I // 8)
    c[:, C_MAIP:C_MAIP + 128] = np.where(I >= J, 0, NEG)
    c[:, C_MASP:C_MASP + 128] = np.where(I > J, 0, NEG)
    c[:, C_MAIS:C_MAIS + 128] = np.where((I >= J) & same, 0, NEG)
    c[:, C_MASS:C_MASS + 128] = np.where((I > J) & same, 0, NEG)
    for h in range(4):
        c[h, C_SEL + h * 128:C_SEL + (h + 1) * 128] = 1.0
        c[:, C_EH + h * 4 + h] = 1.0
    c[0:4, C_CMP:C_CMP + 128] = 1.0
    c[0:4, C_CMP] = 0.0
    c[0:4, C_CMS:C_CMS + 128] = 1.0
    c[0:4, C_CMS:C_CMS + 128:8] = 0.0
    for s in range(16):
        c[s * 8:s * 8 + 8, C_ROWM + s] = 1.0
    c[:, C_ONE] = 1.0
    return c


def build(dbg=None, nblocks=5, nlayers=2, stage=9, limit=None, bsel=None):
    dbg = dbg or {}
    nc = bass.Bass('TRN2', target_bir_lowering=False)
    S = Sched(nc, n_sp_sems=NSP, n_pool_sems=64)
    k = K(nc, S)

    def din(name, shape):
        return nc.dram_tensor(name, list(shape), F32, kind="ExternalInput").ap()

    def dout(name, shape):
        return nc.dram_tensor(name, list(shape), F32, kind="ExternalOutput").ap()

    x_in = din("x_in", [NTOK, D])
    st_conv = din("st_conv", [2, 32, 256])
    st_gconv = din("st_gconv", [2, 48, 1536])
    st_gdn = din("st_gdn", [2, 16, 4, 128, 128])
    st_c = din("st_c", [2, 16, 4, 64, 64])
    st_n = din("st_n", [2, 16, 4, 64])
    st_m = din("st_m", [2, 16, 4])
    st_ffn = din("st_ffn", [2, 32, DFF])
    w_in = din("w_in", [2, D, DIN])
    conv_w = din("conv_w", [2, 3, 256])
    gdn_conv_w = din("gdn_conv_w", [2, 4, 1536])
    gdn_a_log = din("gdn_a_log", [2, 4])
    gdn_dt_bias = din("gdn_dt_bias", [2, 4])
    gdn_norm_w = din("gdn_norm_w", [2, 128])
    ml_i_bias = din("ml_i_bias", [2, 4])
    ml_f_bias = din("ml_f_bias", [2, 4])
    ml_norm_w = din("ml_norm_w", [2, 256])
    w_o = din("w_o", [2, D, D])
    ln1_g = din("ln1_g", [2, D])
    ln1_b = din("ln1_b", [2, D])
    w_up = din("w_up", [2, D, 2 * DFF])
    ffn_conv_w = din("ffn_conv_w", [2, 3, DFF])
    w_down = din("w_down", [2, DFF, D])
    ln2_g = din("ln2_g", [2, D])
    ln2_b = din("ln2_b", [2, D])
    consts = din("consts", [128, NCST])

    y = dout("y", [NTOK, D])
    p_conv = dout("p_conv", [2, 2, 256])
    p_gconv = dout("p_gconv", [2, 3, 1536])
    p_gdn = dout("p_gdn", [2, 4, 128, 128])
    p_c = dout("p_c", [2, 4, 64, 64])
    p_n = dout("p_n", [2, 4, 64])
    p_m = dout("p_m", [2, 4])
    p_ffn = dout("p_ffn", [2, 2, DFF])
    s_conv = dout("s_conv", [2, 32, 256])
    s_gconv = dout("s_gconv", [2, 48, 1536])
    s_gdn = dout("s_gdn", [2, 16, 4, 128, 128])
    s_c = dout("s_c", [2, 16, 4, 64, 64])
    s_n = dout("s_n", [2, 16, 4, 64])
    s_m = dout("s_m", [2, 16, 4])
    s_ffn = dout("s_ffn", [2, 32, DFF])
    outs_all = [y, p_conv, p_gconv, p_gdn, p_c, p_n, p_m, p_ffn, s_conv, s_gconv, s_gdn, s_c, s_n, s_m, s_ffn]
    dbg_out = {}
    for nm, shp in dbg.items():
        dbg_out[nm] = dout("dbg_" + nm, shp)
        outs_all.append(dbg_out[nm])

    def dump(nm, ap):
        if nm in dbg_out:
            k.dma(dbg_out[nm], ap)

    wb_in = nc.dram_tensor("wb_in", [2, D, DIN], BF16, kind="Internal").ap()
    wb_o = nc.dram_tensor("wb_o", [2, D, D], BF16, kind="Internal").ap()
    wb_up = nc.dram_tensor("wb_up", [2, D, 2 * DFF], BF16, kind="Internal").ap()
    wb_down = nc.dram_tensor("wb_down", [2, DFF, D], BF16, kind="Internal").ap()

    NCOL = 44800
    A = nc.sbuf_tensor("arena", [128, NCOL], F32).__enter__()
    PS = nc.psum_tensor("psum", [128, 4096], F32).__enter__()
    cur = [0]

    def al(n):
        o = cur[0]
        cur[0] += n
        assert cur[0] <= NCOL, cur[0]
        return o

    def V(o, n, p0=0, p1=128):
        return A[p0:p1, o:o + n]

    def Vb(o, n, p0=0, p1=128):
        return A[p0:p1, o:o + n].bitcast(BF16)

    cast_jobs = []
    job_index = {}
    for l in range(nlayers):
        for j, (c0, n) in enumerate(((0, 768), (768, 768), (1536, 768), (2304, 520), (2824, 512), (3336, 520))):
            job_index[("in", l, j)] = len(cast_jobs)
            cast_jobs.append([(wb_in[l][:, c0:c0 + n], w_in[l][:, c0:c0 + n])])
        job_index[("o", l, 0)] = len(cast_jobs)
        cast_jobs.append([(wb_o[l][:, 0:512], w_o[l][:, 0:512]), (wb_o[l][:, 512:1024], w_o[l][:, 512:1024])])
        for j in range(6):
            n = 512 if j < 5 else 256
            job_index[("up", l, j)] = len(cast_jobs)
            cast_jobs.append([(wb_up[l][:, 512 * j:512 * j + n], w_up[l][:, 512 * j:512 * j + n]),
                              (wb_up[l][:, DFF + 512 * j:DFF + 512 * j + n], w_up[l][:, DFF + 512 * j:DFF + 512 * j + n])])
        for j in range(4):
            r0 = 768 * j
            r1 = min(DFF, r0 + 768)
            job_index[("down", l, j)] = len(cast_jobs)
            cast_jobs.append([(wb_down[l][r0:r1, :], w_down[l][r0:r1, :])])
    cast_ptr = [0]

    def need(kind, l, j, la=2):
        tgt = min(len(cast_jobs), job_index[(kind, l, j)] + 1 + la)
        while cast_ptr[0] < tgt:
            for (o_, i_) in cast_jobs[cast_ptr[0]]:
                k.dma(o_, i_, q="pool")
            cast_ptr[0] += 1

    def in_job(c0):
        for j, (a0, n) in enumerate(((0, 768), (768, 768), (1536, 768), (2304, 520), (2824, 512), (3336, 520))):
            if a0 <= c0 < a0 + n:
                return j
        raise ValueError(c0)

    if os.environ.get('NOCAST') is None:
        need("in", 0, 0, la=1)
    if stage == -1:
        need('down', nlayers - 1, 3)
        k.fence([wb_in, wb_o, wb_up, wb_down])
        print(S.finalize())
        return nc
    o_cst = al(NCST)
    k.dma(V(o_cst, NCST), consts)
    ident = V(o_cst + C_ID, 128)

    def sel(h):
        return V(o_cst + C_SEL + h * 128, 128, 0, 4)

    def eh(h):
        return V(o_cst + C_EH + h * 4, 4)

    rowmask = V(o_cst + C_ROWM, 16)

    dctr = [0]

    def psd():
        b = dctr[0] % 4
        dctr[0] += 1
        return PS[:, b * 512:(b + 1) * 512]

    qctr = [0]

    def psq(n=128):
        q = qctr[0] % 8
        qctr[0] += 1
        q = (q + 4) % 8
        return PS[:, q * 512: q * 512 + n]

    o_cwa = al(2 * 2 * 3)
    o_cwg = al(2 * 12 * 4)
    o_cwf = al(2 * 22 * 3)
    o_lnf = al(2 * 4 * 8)
    o_gnw = al(2)
    o_gb = al(2 * 8)
    o_mnw = al(2 * 256)
    o_S = al(2 * 2 * 512)
    o_C = al(2 * 2 * 130)
    o_ha = al(2 * 2 * 2)
    o_hg = al(2 * 12 * 3)
    o_hf = al(2 * 22 * 2)
    o_mp = al(2)
    o_st_end = cur[0]
    o_xtok = al(4096)
    o_xT = al(2048)
    o_qkv = al(6144)
    o_ab = al(1024)
    o_sz = al(1024)
    o_mqk = al(2048)
    o_mvo = al(4 * 516)
    o_yc = al(2048)
    o_xe = al(2 * 516)
    o_acc = al(2 * 512)
    o_wg = al(64)
    o_graw = al(4 * 128)
    o_rows = al(26 * 128)
    o_cols = al(128)
    o_scan = al(6144)
    o_wbuf = al(5 * 1024)
    o_misc = al(640)
    o_lnb = o_scan + 2048
    o_cb = al(5 * 64)
    print("arena cols used", cur[0])
    o_ptmp = o_scan
    o_sst = o_xtok + 1024
    o_sso = o_xtok + 1024 + 704
    o_S0 = o_qkv + 1536
    o_msk = o_S0 + 2048
    o_kdm = o_msk + 2176

    identb = Vb(o_cb, 64)
    for ci_, co_ in enumerate((C_ID, C_MAIP, C_MASP, C_MAIS, C_MASS)):
        k.cp(Vb(o_cb + ci_ * 64, 64), V(o_cst + co_, 128), eng="dve")

    def cwa(l, u):
        return V(o_cwa + (l * 2 + u) * 3, 3)

    def cwg(l, u):
        return V(o_cwg + (l * 12 + u) * 4, 4)

    def cwf(l, u):
        return V(o_cwf + (l * 22 + u) * 3, 3)

    def lnf(l, which, kk):
        return V(o_lnf + (l * 4 + which) * 8 + kk, 1)

    def gbv(l, i):
        return V(o_gb + l * 8 + i, 1, 0, 4)

    ptmp = V(o_ptmp, DFF)
    k.memset(ptmp, 0.0)
    for l in range(nlayers):
        for (src, W_, nu, fn) in ((conv_w, 3, 2, cwa), (gdn_conv_w, 4, 12, cwg), (ffn_conv_w, 3, 22, cwf)):
            C_ = nu * 128
            k.dma(ptmp[0:W_, 0:C_], src[l])
            for u0 in range(0, nu, 4):
                ps = psd()
                n4 = min(4, nu - u0)
                for u in range(u0, u0 + n4):
                    k.tr(ps[:, (u - u0) * 128:(u - u0 + 1) * 128], ptmp[:, u * 128:(u + 1) * 128], ident)
                for u in range(u0, u0 + n4):
                    k.cp(fn(l, u), ps[:, (u - u0) * 128:(u - u0) * 128 + W_])
        for wi, src in enumerate((ln1_g, ln1_b, ln2_g, ln2_b)):
            k.dma(ptmp[0:8, 0:128], src[l].rearrange("(k p) -> k p", p=128))
            ps = psd()
            k.tr(ps[:, 0:128], ptmp[:, 0:128], ident)
            k.cp(V(o_lnf + (l * 4 + wi) * 8, 8), ps[:, 0:8])
        k.dma(V(o_gnw + l, 1), gdn_norm_w[l].rearrange("(p o) -> p o", o=1))
        for i, src in enumerate((gdn_a_log, gdn_dt_bias, ml_i_bias, ml_f_bias)):
            k.dma(gbv(l, i), src[l].rearrange("(p o) -> p o", o=1))
        k.act(gbv(l, 4), gbv(l, 0), AF.Exp)
        k.ts(gbv(l, 4), gbv(l, 4), -1.0, ALU.mult)
        k.ts(gbv(l, 5), gbv(l, 3), -1.0, ALU.mult)
        k.dma(V(o_mnw + l * 256, 256), ml_norm_w[l:l + 1, :].partition_broadcast(128))

    def Sst(l, par, h):
        return V(o_S + (l * 2 + par) * 512 + h * 128, 128)

    def Cst(l, par, h):
        po = (h % 2) * 64
        return V(o_C + (l * 2 + par) * 130 + (h // 2) * 65, 65, po, po + 64)

    k.memset(V(o_S, o_st_end - o_S), 0.0)
    k.memset(V(o_rows, 26 * 128), 0.0)

    def halo_a(l, u):
        return V(o_ha + (l * 2 + u) * 2, 2)

    def halo_g(l, u):
        return V(o_hg + (l * 12 + u) * 3, 3)

    def halo_f(l, u):
        return V(o_hf + (l * 22 + u) * 2, 2)

    def mprev(l):
        return V(o_mp + l, 1, 0, 4)

    def xtok(t):
        return V(o_xtok + t * 1024, 1024)

    def wbuf_next(ctr=[0]):
        i = ctr[0]
        ctr[0] += 1
        return Vb(o_wbuf + (i % 5) * 1024, 1024)

    def row(i, p0=0, p1=4):
        return V(o_rows + i * 128, 128, p0, p1)

    def slot(h, i):
        return V(o_scan + (h * 12 + i) * 128, 128)

    def slotb(h, i):
        return Vb(o_scan + (h * 12 + i) * 128, 64)

    o_sb = al(256)

    def Sb(h):
        return Vb(o_sb + h * 64, 64)

    if stage == 0:
        k.dma(y[0:128, 0:512], V(o_cwa, 512))
        k.fence(outs_all)
        print(S.finalize())
        return nc
    def main_loops():
        for bi, (t0, Tb, is_s) in enumerate(BLOCKS[:nblocks]):
            if bsel is not None and bi not in bsel:
                continue
            nt = Tb // 128
            L = 8 if is_s else 128
            nch = 128 // L
            nch2 = max(nch, 2)
            MAI = Vb(o_cb + (3 if is_s else 1) * 64, 64)
            MAS = Vb(o_cb + (4 if is_s else 2) * 64, 64)
            cmask = V(o_cst + (C_CMS if is_s else C_CMP), 128, 0, 4)
            nlev = 3 if is_s else 7
            xTv = Vb(o_xT, 4 * Tb).rearrange("p (k t) -> p k t", k=8)
            ycv = Vb(o_yc, 4 * Tb).rearrange("p (k t) -> p k t", k=8)
            hTv = Vb(o_qkv, 11 * Tb).rearrange("p (k t) -> p k t", k=22)
            szv = Vb(o_sz, 2 * Tb).rearrange("p (u t) -> p u t", u=4)
            ab = V(o_ab, 2 * Tb).rearrange("p (u t) -> p u t", u=2)
            mqk = V(o_mqk, 4 * Tb).rearrange("p (u t) -> p u t", u=4)

            def qkv(u):
                return V(o_qkv + u * Tb, Tb)

            def vext(t):
                return Vb(o_mvo + t * 516, 130).rearrange("p (h e) -> p h e", h=4)

            def sigo(t):
                return V(o_mvo + t * 516 + 260, 256)

            def tview(ap):
                if not is_s:
                    return ap
                return ap.rearrange("p (s t) -> p s t", s=16)

            for t in range(nt):
                k.dma(xtok(t), x_in[t0 + t * 128:t0 + (t + 1) * 128, :])

            def transpose_to_xT(src_tok, t, scale_l=None, which=None):
                for k0 in range(0, 8, 4):
                    ps = psd()
                    for kk in range(k0, k0 + 4):
                        k.tr(ps[:, (kk - k0) * 128:(kk - k0 + 1) * 128], src_tok[:, kk * 128:(kk + 1) * 128], ident)
                    for kk in range(k0, k0 + 4):
                        o_ = xTv[:, kk, t * 128:(t + 1) * 128]
                        i_ = ps[:, (kk - k0) * 128:(kk - k0 + 1) * 128]
                        if scale_l is None:
                            k.cp(o_, i_, eng=CPENG)
                        else:
                            k.act(o_, i_, AF.Identity, scale=lnf(scale_l, which, kk), bias=lnf(scale_l, which + 1, kk))

            for t in range(nt):
                transpose_to_xT(xtok(t), t)

            xectr = [0]

            def conv_unit(src, W_, cw, halo, st_src, st_rows, mul_by=None):
                i = xectr[0]
                xectr[0] += 1
                wl = W_ - 1
                if not is_s:
                    full = V(o_xe + (i % 2) * 516, wl + Tb)
                    data, hv, tail = full[:, wl:wl + Tb], full[:, 0:wl], full[:, Tb:Tb + wl]
                    win = lambda j: full[:, j:j + Tb]
                else:
                    full = V(o_xe + (i % 2) * 516, 16 * (wl + 8)).rearrange("p (s t) -> p s t", s=16)
                    data, hv, tail = full[:, :, wl:wl + 8], full[:, :, 0:wl], full[:, :, 8:8 + wl]
                    win = lambda j: full[:, :, j:j + 8]
                if mul_by is None:
                    k.cp(data, tview(src))
                else:
                    k.tt(data, tview(src), tview(mul_by), ALU.mult)
                if not is_s:
                    k.cp(hv, halo, eng="pool")
                else:
                    k.cp(hv, st_src.rearrange("p (s t) -> p s t", s=16), eng="pool")
                acc = V(o_acc + (i % 2) * 512, Tb)
                accv = tview(acc)
                k.ts(accv, win(0), cw[:, 0:1], ALU.mult)
                for j in range(1, W_):
                    k.stt(accv, win(j), cw[:, j:j + 1], accv, ALU.mult, ALU.add)
                if not is_s:
                    k.cp(halo, tail, eng="pool")
                else:
                    k.cp(st_rows.rearrange("p (s t) -> p s t", s=16), tail, eng="pool")
                return acc

            def load_state_T(src, R, nu, dst_off):
                for c0 in range(0, nu, 4):
                    n4 = min(4, nu - c0)
                    k.dma(ptmp[0:R, 0:n4 * 128], src[:, c0 * 128:(c0 + n4) * 128])
                    ps = psd()
                    for u in range(c0, c0 + n4):
                        k.tr(ps[:, (u - c0) * 128:(u - c0 + 1) * 128], ptmp[:, (u - c0) * 128:(u - c0 + 1) * 128], ident)
                    for u in range(c0, c0 + n4):
                        k.cp(V(dst_off + u * R, R), ps[:, (u - c0) * 128:(u - c0) * 128 + R])

            def rows_out(get_in, nu, R, dram):
                for c0 in range(0, nu, 4):
                    n4 = min(4, nu - c0)
                    ps = psd()
                    for u in range(c0, c0 + n4):
                        gi = get_in(u)
                        k.tr(ps[:, (u - c0) * 128:(u - c0 + 1) * 128], A[:, int(gi.offset) % NCOL:int(gi.offset) % NCOL + 128], ident)
                    rb = V(o_misc + 128, 512, 0, R)
                    k.cp(rb[:, 0:n4 * 128], ps[0:R, 0:n4 * 128])
                    k.dma(dram[:, c0 * 128:(c0 + n4) * 128], rb[:, 0:n4 * 128])

            for l in range(nlayers):
                last_l = (l == nlayers - 1)
                xectr[0] = 0

                wsrc = wb_in[l]
                wup = wb_up[l]

                def load_w(src, c0, n):
                    if src is wsrc:
                        need("in", l, in_job(c0))
                    else:
                        need("up", l, (c0 % DFF) // 512)
                    wb = wbuf_next()
                    wv = wb[:, 0:8 * n].rearrange("p (k e) -> p k e", k=8)
                    k.dma(wv, src.rearrange("(k p) e -> p k e", p=128)[:, :, c0:c0 + n])
                    return wv

                def unit_mm(ps, wv, e0, M):
                    for kk in range(8):
                        k.mm(ps[0:M, 0:Tb], wv[:, kk, e0:e0 + M], xTv[:, kk, :], start=(kk == 0), stop=(kk == 7))

                wsrc = wb_in[l]
                actmp = V(o_lnb, 2 * Tb).rearrange("p (u t) -> p u t", u=2)
                wv = load_w(wsrc, 0, 256)
                for u in range(2):
                    ps = psd()
                    unit_mm(ps, wv, u * 128, 128)
                    k.cp(ab[:, u, :], ps[:, 0:Tb])
                wv = load_w(wsrc, 256, 256)
                for u in range(2):
                    ps = psd()
                    unit_mm(ps, wv, u * 128, 128)
                    k.cp(actmp[:, u, :], ps[:, 0:Tb], eng="dve")
                if is_s:
                    load_state_T(st_conv[l], 32, 2, o_sst)
                wv = load_w(wsrc, 512, 256)
                for u in range(2):
                    ps = psd()
                    unit_mm(ps, wv, u * 128, 128)
                    acc = conv_unit(ps[:, 0:Tb], 3, cwa(l, u), halo_a(l, u), V(o_sst + u * 32, 32), V(o_sso + u * 32, 32),
                                    mul_by=actmp[:, u, :])
                    k.tt(ycv[:, u, :], acc, ab[:, u, :], ALU.mult)
                if is_s:
                    rows_out(lambda u: V(o_sso + u * 32, 32), 2, 32, s_conv[l])
                elif bi == 3:
                    rows_out(lambda u: halo_a(l, u), 2, 2, p_conv[l])
                if is_s:
                    load_state_T(st_gconv[l], 48, 12, o_sst)
                for g in range(6):
                    wv = load_w(wsrc, 768 + 256 * g, 256)
                    for uu in range(2):
                        j = 2 * g + uu
                        ps = psd()
                        unit_mm(ps, wv, uu * 128, 128)
                        acc = conv_unit(ps[:, 0:Tb], 4, cwg(l, j), halo_g(l, j), V(o_sst + j * 48, 48), V(o_sso + j * 48, 48))
                        k.act(qkv(j), acc, AF.Silu)
                if is_s:
                    rows_out(lambda u: V(o_sso + u * 48, 48), 12, 48, s_gconv[l])
                elif bi == 3:
                    rows_out(lambda u: halo_g(l, u), 12, 3, p_gconv[l])
                for g in range(2):
                    wv = load_w(wsrc, 2304 + 256 * g, 256)
                    for uu in range(2):
                        ps = psd()
                        unit_mm(ps, wv, uu * 128, 128)
                        k.act(szv[:, 2 * g + uu, :], ps[:, 0:Tb], AF.Silu)
                for g in range(2):
                    wv = load_w(wsrc, 2824 + 256 * g, 256)
                    for uu in range(2):
                        ps = psd()
                        unit_mm(ps, wv, uu * 128, 128)
                        k.cp(mqk[:, 2 * g + uu, :], ps[:, 0:Tb])
                wv = load_w(wsrc, 3336, 256)
                for t in range(nt):
                    ps = psd()
                    for kk in range(8):
                        k.mm(ps[:, 0:256], xTv[:, kk, t * 128:(t + 1) * 128], wv[:, kk, :], start=(kk == 0), stop=(kk == 7))
                    k.cp(vext(t)[:, :, 0:64], ps[:, 0:256].rearrange("p (h e) -> p h e", h=4))
                    k.memset(vext(t)[:, :, 64:65], 1.0, eng="pool")
                wv = load_w(wsrc, 3592, 256)
                for t in range(nt):
                    ps = psd()
                    for kk in range(8):
                        k.mm(ps[:, 0:256], xTv[:, kk, t * 128:(t + 1) * 128], wv[:, kk, :], start=(kk == 0), stop=(kk == 7))
                    k.act(sigo(t), ps[:, 0:256], AF.Sigmoid)
                wg = Vb(o_wg, 64).rearrange("p (k e) -> p k e", k=8)
                wsr = wsrc.rearrange("(k p) e -> p k e", p=128)
                need("in", l, 5)
                k.dma(wg[:, :, 0:8], wsr[:, :, 2816:2824])
                k.dma(wg[:, :, 8:16], wsr[:, :, 3848:3856])
                if bi == 0 and l == 0:
                    dump("qkv", V(o_qkv, 12 * Tb))
                    dump("ycA", V(o_yc, 2048))

                for t in range(nt if stage >= 2 else 0):
                    ts_ = slice(t * 128, (t + 1) * 128)
                    gti = bi * 4 + t
                    par = gti % 2 if not is_s else 0
                    graw = [V(o_graw + i * 128, 128, 0, 4) for i in range(4)]
                    for i in range(4):
                        ps = psq()
                        for kk in range(8):
                            k.mm(ps[0:4, :], wg[:, kk, i * 4:i * 4 + 4], xTv[:, kk, ts_], start=(kk == 0), stop=(kk == 7))
                        k.cp(graw[i], ps[0:4, :], eng="dve")
                    r_G, r_lb, r_lrq, r_lrk, r_ra, r_rq, r_cj, r_kd, r_tmp, r_GL = [row(i) for i in range(10)]
                    k.act(r_tmp, graw[0], AF.Exp, bias=gbv(l, 1))
                    k.act(r_tmp, r_tmp, AF.Ln, bias=1.0)
                    k.ts(r_tmp, r_tmp, gbv(l, 4), ALU.mult)
                    k.scan(r_G, cmask, r_tmp, 0.0, ALU.mult, ALU.add)
                    k.act(r_lb, graw[1], AF.Exp, scale=-1.0)
                    k.act(r_lb, r_lb, AF.Ln, bias=1.0)
                    k.ts(r_lb, r_lb, -1.0, ALU.mult)
                    for qi, rr in ((0, r_lrq), (1, r_lrk)):
                        ps = psq()
                        for h in range(4):
                            sq = slot(h, 3 + qi)
                            k.act(sq, qkv(qi * 4 + h)[:, ts_], AF.Square)
                            k.mm(ps[0:4, :], eh(h), sq, start=(h == 0), stop=(h == 3))
                        k.act(rr, ps[0:4, :], AF.Ln, bias=NORM_EPS)
                        k.ts(rr, rr, -0.5, ALU.mult)
                    k.tt(r_ra, r_G, r_lb, ALU.add)
                    k.tt(r_ra, r_ra, r_lrk, ALU.add)
                    k.ts(r_rq, r_lrq, math.log(128.0 ** -0.5), ALU.add)
                    k.tt(r_rq, r_rq, r_G, ALU.add)
                    k.tt(r_cj, r_lrk, r_G, ALU.subtract)
                    G3 = r_G.rearrange("p (c l) -> p c l", l=L)
                    k.tt(r_kd.rearrange("p (c l) -> p c l", l=L), r_cj.rearrange("p (c l) -> p c l", l=L),
                         G3[:, :, L - 1:L].broadcast_to([4, nch, L]), ALU.add)
                    k.cp(r_GL[:, 0:nch].rearrange("p (c o) -> p c o", o=1), G3[:, :, L - 1:L], eng="dve")
                    craw = V(o_cols, 20)
                    cexp = V(o_cols + 20, 20)
                    ps = psq()
                    for ci, rr in enumerate((r_cj, r_lb, r_ra, r_kd, r_rq)):
                        k.mm(ps[:, ci * 4:ci * 4 + 4], rr, ident[0:4, 0:4], start=True, stop=True)
                    k.cp(craw, ps[:, 0:20], eng="dve")
                    k.act(cexp, ps[:, 0:20], AF.Exp)
                    elast = V(o_misc, 4 * nch2)
                    ps = psq()
                    for h in range(4):
                        k.mm(ps[:, h * nch2:h * nch2 + nch2], sel(h), r_GL[:, 0:nch2], start=True, stop=True)
                    k.act(elast, ps[:, 0:4 * nch2], AF.Exp)
                    if bi == 0 and l == 0 and t == 0:
                        dump("rows", V(o_rows, 10 * 128, 0, 4))
                        dump("cexp", V(o_cols, 40))

                    qT_ = [qkv(h)[:, ts_] for h in range(4)]
                    kT_ = [qkv(4 + h)[:, ts_] for h in range(4)]
                    vT_ = [qkv(8 + h)[:, ts_] for h in range(4)]
                    for h in range(4):
                        ps = psq()
                        k.mm(ps, sel(h), r_ra, start=True, stop=False)
                        k.mm(ps, identb, MAS, start=False, stop=True)
                        k.act(slot(h, 0), ps, AF.Exp, bias=craw[:, h:h + 1])
                        ps2 = psq()
                        k.mm(ps2, kT_[h], kT_[h])
                        k.stt(slot(h, 4), ps2, -1.0, slot(h, 0), ALU.mult, ALU.mult)
                    for h in range(4):
                        ps = psq()
                        k.mm(ps, sel(h), r_rq, start=True, stop=False)
                        k.mm(ps, identb, MAI, start=False, stop=True)
                        k.act(slot(h, 1), ps, AF.Exp, bias=craw[:, h:h + 1])
                        ps2 = psq()
                        k.mm(ps2, kT_[h], qT_[h])
                        k.tt(slotb(h, 2), ps2, slot(h, 1), ALU.mult)
                    for h in range(4):
                        ps = psq()
                        k.tr(ps, slot(h, 4), ident)
                        k.cp(slot(h, 5), ps)
                        k.tt(slot(h, 3), slot(h, 4), ident, ALU.add, eng="pool")
                    for h in range(4):
                        ps = psq()
                        k.mm(ps, slot(h, 4), slot(h, 5))
                        k.cp(slot(h, 6), ps)
                        ps2 = psq()
                        k.mm(ps2, slot(h, 5), slot(h, 4))
                        k.cp(slot(h, 4), ps2, eng="dve")
                    cP = [(3, 6)] * 4
                    for lev in range(1, nlev):
                        lastlev = (lev == nlev - 1)
                        nP = []
                        for h in range(4):
                            pb, ipm = cP[h]
                            npb, npm = 10 - pb, 11 - ipm
                            if not lastlev:
                                ps = psq(256)
                                k.mm(ps, slot(h, ipm), V(o_scan + (h * 12 + pb) * 128, 256))
                                k.tt(slot(h, npb), ps[:, 0:128], slot(h, pb), ALU.add)
                                k.cp(slot(h, npb + 1), ps[:, 128:256])
                                ps2 = psq()
                                k.mm(ps2, slot(h, pb + 1), slot(h, ipm))
                                k.cp(slot(h, npm), ps2, eng="dve")
                            else:
                                ps = psq()
                                k.mm(ps, slot(h, ipm), slot(h, pb))
                                k.tt(slotb(h, npb), ps, slot(h, pb), ALU.add)
                            nP.append((npb, npm))
                        cP = nP
                    TTf = [slotb(h, cP[h][0]) for h in range(4)]
                    for h in range(4):
                        ps = psq()
                        k.tr(ps, kT_[h], ident)
                        k.ts(slotb(h, 0), ps, cexp[:, 8 + h:9 + h], ALU.mult)
                        k.act(slotb(h, 1), ps, AF.Identity, scale=cexp[:, 12 + h:13 + h])
                        ps2 = psq()
                        k.tr(ps2, vT_[h], ident)
                        k.ts(slotb(h, 9), ps2, cexp[:, 4 + h:5 + h], ALU.mult)
                    for h in range(4):
                        ps = psq()
                        k.mm(ps, slotb(h, 0), TTf[h])
                        k.act(slot(h, 10) if is_s else slotb(h, 10), ps, AF.Identity, scale=-1.0)
                    O_ = [slot(h, 5) for h in range(4)]
                    if not is_s:
                        for h in range(4):
                            Sc, Sn = Sst(l, par, h), Sst(l, 1 - par, h)
                            k.cp(Sb(h), Sc)
                            ps = psq()
                            k.mm(ps, TTf[h], slotb(h, 9), start=True, stop=False)
                            k.mm(ps, slotb(h, 10), Sb(h), start=False, stop=True)
                            k.cp(slotb(h, 11), ps)
                            ps1 = psq()
                            k.mm(ps1, qT_[h], Sc)
                            k.act(slot(h, 0), ps1, AF.Identity, scale=cexp[:, 16 + h:17 + h])
                            ps2 = psq()
                            k.mm(ps2, slotb(h, 2), slotb(h, 11))
                            k.tt(O_[h], ps2, slot(h, 0), ALU.add)
                            ps3 = psq()
                            k.mm(ps3, slotb(h, 1), slotb(h, 11))
                            k.stt(Sn, Sc, elast[:, h * nch2:h * nch2 + 1], ps3, ALU.mult, ALU.add)
                            if gti == 15:
                                k.dma(p_gdn[l, h], Sn)
                    else:
                        S0v = V(o_S0, 2048).rearrange("p (s v) -> p s v", s=16)
                        mskd = V(o_msk, 2176).rearrange("p (s r) -> p s r", r=136)[:, :, 0:8]
                        mskf = V(o_msk, 2048).rearrange("p (s i) -> p s i", s=16)
                        for h in range(4):
                            k.dma(S0v, st_gdn[l, :, h].rearrange("s k v -> k s v"))
                            if h == 0 and l == 0:
                                k.memset(V(o_msk, 2176), 0.0, eng="pool")
                            k.cp(mskd, slot(h, 10).rearrange("p (s r) -> p s r", r=8), eng="pool")
                            ps = psq()
                            k.mm(ps, TTf[h], slotb(h, 9), start=True, stop=False)
                            for s in range(16):
                                k.mm(ps, mskf[:, s, :], S0v[:, s, :], start=False, stop=(s == 15))
                            k.cp(slotb(h, 11), ps)
                            k.cp(mskd, qT_[h].rearrange("p (s r) -> p s r", r=8), eng="pool")
                            ps1 = psq()
                            for s in range(16):
                                k.mm(ps1, mskf[:, s, :], S0v[:, s, :], start=(s == 0), stop=(s == 15))
                            k.act(slot(h, 0), ps1, AF.Identity, scale=cexp[:, 16 + h:17 + h])
                            ps2 = psq()
                            k.mm(ps2, slotb(h, 2), slotb(h, 11))
                            k.tt(O_[h], ps2, slot(h, 0), ALU.add)
                            for s in range(16):
                                kdm = Vb(o_kdm + (s % 3) * 128, 64)
                                k.ts(kdm, slotb(h, 1), rowmask[:, s:s + 1], ALU.mult)
                                ps3 = psq()
                                k.mm(ps3, kdm, slotb(h, 11))
                                k.stt(S0v[:, s, :], S0v[:, s, :], elast[:, h * 16 + s:h * 16 + s + 1], ps3, ALU.mult, ALU.add)
                            k.dma(s_gdn[l, :, h].rearrange("s k v -> k s v"), S0v)
                    if bi == 0 and l == 0 and t == 0:
                        dump("o_gdn", V(o_scan + 5 * 128, 128))
                        pass
                    ss = V(o_cols + 40, 4)
                    for h in range(4):
                        k.act(slot(h, 0), O_[h], AF.Square, accum=ss[:, h:h + 1])
                    k.ts(ss, ss, 1.0 / 128, ALU.mult, NORM_EPS, ALU.add)
                    k.act(ss, ss, AF.Sqrt)
                    k.recip(ss, ss)
                    for h in range(4):
                        k.ts(slot(h, 9), O_[h], ss[:, h:h + 1], ALU.mult)
                        ps = psq()
                        k.tr(ps, slot(h, 9), ident)
                        k.stt(ycv[:, 2 + h, ts_], ps, V(o_gnw + l, 1), szv[:, h, ts_], ALU.mult, ALU.mult)

                    r_ig, r_lf, r_F, r_m, r_rD, r_cD, r_rI, r_em, r_kw, r_mp, r_t2, r_ch = [row(10 + i) for i in range(12)]
                    k.ts(r_ig, graw[2], gbv(l, 2), ALU.add)
                    k.act(r_lf, graw[3], AF.Exp, scale=-1.0, bias=gbv(l, 5))
                    k.act(r_lf, r_lf, AF.Ln, bias=1.0)
                    k.ts(r_lf, r_lf, -1.0, ALU.mult)
                    k.scan(r_F, cmask, r_lf, 0.0, ALU.mult, ALU.add)
                    F3 = r_F.rearrange("p (c l) -> p c l", l=L)
                    m3 = r_m.rearrange("p (c l) -> p c l", l=L)
                    if not is_s:
                        k.scan(r_m, r_lf, r_ig, mprev(l), ALU.add, ALU.max)
                        k.cp(r_mp, mprev(l).broadcast_to([4, 128]), eng="dve")
                    else:
                        m0 = r_ch[:, 64:80]
                        k.dma(m0, st_m[l].rearrange("s h -> h s"), allow_slow_non_contiguous=True)
                        k.cp(r_mp.rearrange("p (c l) -> p c l", l=8), m0.rearrange("p (c o) -> p c o", o=1).broadcast_to([4, 16, 8]), eng="dve")
                        ig3 = r_ig.rearrange("p (c l) -> p c l", l=8)
                        lf3 = r_lf.rearrange("p (c l) -> p c l", l=8)
                        t23 = r_t2.rearrange("p (c l) -> p c l", l=8)
                        k.cp(r_t2, r_ig, eng="dve")
                        k.tt(t23[:, :, 0:1], lf3[:, :, 0:1], m0.rearrange("p (c o) -> p c o", o=1), ALU.add)
                        k.tt(t23[:, :, 0:1], t23[:, :, 0:1], ig3[:, :, 0:1], ALU.max)
                        r_lf2 = row(25)
                        k.cp(r_lf2, r_lf, eng="dve")
                        k.memset(r_lf2.rearrange("p (c l) -> p c l", l=8)[:, :, 0:1], NEG)
                        k.scan(r_m, r_lf2, r_t2, 0.0, ALU.add, ALU.max)
                    k.tt(r_rD, r_F, r_m, ALU.subtract)
                    k.tt(r_cD, r_ig, r_F, ALU.subtract)
                    k.ts(r_cD, r_cD, math.log(0.125), ALU.add)
                    k.tt(r_rI, r_rD, r_mp, ALU.add)
                    k.ts(r_em, r_m, -1.0, ALU.mult)
                    chA = r_ch[:, 0:nch].rearrange("p (c o) -> p c o", o=1)
                    chB = r_ch[:, 16:16 + nch].rearrange("p (c o) -> p c o", o=1)
                    k.tt(chA, F3[:, :, L - 1:L], m3[:, :, L - 1:L], ALU.subtract)
                    k.tt(chB, chA, r_mp.rearrange("p (c l) -> p c l", l=L)[:, :, 0:1], ALU.add)
                    k.tt(r_kw.rearrange("p (c l) -> p c l", l=L), r_cD.rearrange("p (c l) -> p c l", l=L),
                         chA.broadcast_to([4, nch, L]), ALU.add)
                    if not is_s:
                        k.cp(mprev(l), r_m[:, 127:128], eng="dve")
                        if gti == 15:
                            k.dma(p_m[l].rearrange("(p o) -> p o", o=1), r_m[:, 127:128])
                    else:
                        k.dma(s_m[l].rearrange("s h -> h s"), m3[:, :, 7], allow_slow_non_contiguous=True)
                    mraw = V(o_cols + 48, 16)
                    mexp = V(o_cols + 64, 16)
                    ps = psq()
                    for ci, rr in enumerate((r_cD, r_em, r_kw, r_rI)):
                        k.mm(ps[:, ci * 4:ci * 4 + 4], rr, ident[0:4, 0:4], start=True, stop=True)
                    k.cp(mraw, ps[:, 0:16], eng="dve")
                    k.act(mexp, ps[:, 0:16], AF.Exp)
                    dcb = V(o_misc + 64, 4 * nch2)
                    ps = psq()
                    for h in range(4):
                        k.mm(ps[:, h * nch2:h * nch2 + nch2], sel(h), r_ch[:, 16:16 + nch2], start=True, stop=True)
                    k.act(dcb, ps[:, 0:4 * nch2], AF.Exp)
                    nq = V(o_scan + 48 * 128 - 4 * 65, 260).rearrange("p (h e) -> p h e", h=4)
                    kwt = [None] * 4
                    for hc in range(2):
                        ps = psq()
                        k.tr(ps, mqk[:, 2 + hc, ts_], ident)
                        for hh in range(2):
                            h = hc * 2 + hh
                            kwt[h] = slotb(h, 4)[:, 0:64]
                            k.ts(kwt[h], ps[:, hh * 64:(hh + 1) * 64], mexp[:, 8 + h:9 + h], ALU.mult)
                    POs = [(h % 2) * 64 for h in range(4)]
                    qThs = [mqk[POs[h]:POs[h] + 64, h // 2, ts_] for h in range(4)]
                    kThs = [mqk[POs[h]:POs[h] + 64, 2 + h // 2, ts_] for h in range(4)]
                    for h in range(4):
                        ps = psq()
                        k.mm(ps, sel(h), r_rD, start=True, stop=False)
                        k.mm(ps, identb, MAI, start=False, stop=True)
                        k.act(slot(h, 0), ps, AF.Exp, bias=mraw[:, h:h + 1])
                    for h in range(4):
                        ps2 = psq()
                        k.mm(ps2, kThs[h], qThs[h])
                        k.tt(slotb(h, 1), ps2, slot(h, 0), ALU.mult)
                    if not is_s:
                        for h in range(4):
                            Cc = Cst(l, par, h)
                            ps1 = psq()
                            k.mm(ps1[:, 0:65], qThs[h], Cc)
                            k.act(slot(h, 2)[:, 0:65], ps1[:, 0:65], AF.Identity, scale=mexp[:, 12 + h:13 + h])
                        for h in range(4):
                            ps3 = psq()
                            k.mm(ps3[:, 0:65], slotb(h, 1), vext(t)[:, h, :])
                            k.tt(nq[:, h, :], ps3[:, 0:65], slot(h, 2)[:, 0:65], ALU.add)
                        for h in range(4):
                            po = POs[h]
                            Cc, Cn = Cst(l, par, h), Cst(l, 1 - par, h)
                            ps4 = psq()
                            k.mm(ps4[po:po + 64, 0:65], kwt[h], vext(t)[:, h, :])
                            k.stt(Cn, Cc, dcb[po:po + 64, h * nch2:h * nch2 + 1], ps4[po:po + 64, 0:65], ALU.mult, ALU.add)
                            if gti == 15:
                                k.dma(p_c[l, h], Cn[:, 0:64])
                                k.dma(p_n[l, h].rearrange("(p o) -> p o", o=1), Cn[:, 64:65])
                    else:
                        for h in range(4):
                            po = POs[h]
                            qTh = qThs[h]
                            C0v = V(o_S0, 16 * 65, po, po + 64).rearrange("p (s e) -> p s e", s=16)
                            k.dma(C0v[:, :, 0:64], st_c[l, :, h].rearrange("s d e -> d s e"))
                            k.dma(C0v[:, :, 64:65], st_n[l, :, h].rearrange("s (d o) -> d s o", o=1), allow_slow_non_contiguous=True)
                            mskd = V(o_msk, 2176, po, po + 64).rearrange("p (s r) -> p s r", r=136)[:, :, 0:8]
                            mskf = V(o_msk, 2048, po, po + 64).rearrange("p (s i) -> p s i", s=16)
                            k.cp(mskd, qTh.rearrange("p (s r) -> p s r", r=8), eng="pool")
                            ps1 = psq()
                            for s in range(16):
                                k.mm(ps1[:, 0:65], mskf[:, s, :], C0v[:, s, :], start=(s == 0), stop=(s == 15))
                            k.act(slot(h, 2)[:, 0:65], ps1[:, 0:65], AF.Identity, scale=mexp[:, 12 + h:13 + h])
                            ps3 = psq()
                            k.mm(ps3[:, 0:65], slotb(h, 1), vext(t)[:, h, :])
                            k.tt(nq[:, h, :], ps3[:, 0:65], slot(h, 2)[:, 0:65], ALU.add)
                            for s in range(16):
                                kwm = Vb(o_kdm + (s % 3) * 128, 32)
                                k.ts(kwm, kwt[h], rowmask[:, s:s + 1], ALU.mult)
                                ps4 = psq()
                                k.mm(ps4[po:po + 64, 0:65], kwm, vext(t)[:, h, :])
                                k.stt(C0v[:, s, :], C0v[:, s, :], dcb[po:po + 64, h * 16 + s:h * 16 + s + 1], ps4[po:po + 64, 0:65], ALU.mult, ALU.add)
                            k.dma(s_c[l, :, h].rearrange("s d e -> d s e"), C0v[:, :, 0:64])
                            k.dma(s_n[l, :, h].rearrange("s (d o) -> d s o", o=1), C0v[:, :, 64:65], allow_slow_non_contiguous=True)
                    den = V(o_cols + 80, 4)
                    qn = nq[:, :, 64]
                    k.stt(den, qn, -1.0, qn, ALU.mult, ALU.max)
                    k.tt(den, den, mexp[:, 4:8], ALU.max)
                    k.recip(den, den)
                    hbuf = slot(0, 5)[:, 0:128]
                    hb2 = V(o_scan + 5 * 128, 128)
                    hb3 = V(o_scan + 6 * 128, 128)
                    hall = V(o_scan + 5 * 128, 256)
                    stats = V(o_cols + 84, 24)
                    mv = V(o_cols + 108, 8)
                    rstd = V(o_cols + 116, 4)
                    for h in range(4):
                        hh_ = hall[:, h * 64:(h + 1) * 64]
                        k.stt(hh_, nq[:, h, 0:64], den[:, h:h + 1], sigo(t)[:, h * 64:(h + 1) * 64], ALU.mult, ALU.mult)
                        k.bn_stats(stats[:, h * 6:(h + 1) * 6], hh_)
                        k.bn_aggr(mv[:, h * 2:(h + 1) * 2], stats[:, h * 6:(h + 1) * 6])
                    mv3 = mv.rearrange("p (h two) -> p h two", two=2)
                    k.ts(rstd.rearrange("p (h o) -> p h o", o=1), mv3[:, :, 1:2], LN_EPS, ALU.add)
                    k.act(rstd, rstd, AF.Sqrt)
                    k.recip(rstd, rstd)
                    for h in range(4):
                        hh_ = hall[:, h * 64:(h + 1) * 64]
                        k.ts(hh_, hh_, mv[:, 2 * h:2 * h + 1], ALU.subtract, rstd[:, h:h + 1], ALU.mult)
                    k.tt(hall, hall, V(o_mnw + l * 256, 256), ALU.mult, eng="pool")
                    for hc in range(2):
                        ps = psq()
                        k.tr(ps, hall[:, hc * 128:(hc + 1) * 128], ident)
                        k.cp(ycv[:, 6 + hc, ts_], ps)
                    if bi == 0 and l == 0 and t == 0:
                        dump("hml", hall)

                if bi == 0 and l == 0:
                    dump("ycat", V(o_yc, 2048))
                if stage < 3:
                    continue
                def ln_resid(t, pss):
                    xt_ = xtok(t)
                    for hf in range(4):
                        k.stt(xt_[:, hf * 256:(hf + 1) * 256], xt_[:, hf * 256:(hf + 1) * 256], ALPHA, pss[hf], ALU.mult, ALU.add)

                def ln_tile(t, pss, g_src, b_src, which, final_out=None, need_T=True):
                    xt_ = xtok(t)
                    if pss is not None:
                        ln_resid(t, pss)
                    stats = V(o_cols + 84, 12)
                    mv = V(o_cols + 108, 2)
                    rs = V(o_cols + 116, 2)
                    k.bn_stats(stats[:, 0:6], xt_[:, 0:512])
                    k.bn_stats(stats[:, 6:12], xt_[:, 512:1024])
                    k.bn_aggr(mv, stats)
                    k.ts(rs[:, 0:1], mv[:, 1:2], LN_EPS, ALU.add)
                    k.act(rs[:, 0:1], rs[:, 0:1], AF.Sqrt)
                    k.recip(rs[:, 0:1], rs[:, 0:1])
                    k.stt(rs[:, 1:2], mv[:, 0:1], -1.0, rs[:, 0:1], ALU.mult, ALU.mult)
                    ntok_ = V(o_scan + (t % 2) * 1024, 1024)
                    k.act(ntok_, xt_, AF.Identity, scale=rs[:, 0:1], bias=rs[:, 1:2])
                    if need_T:
                        transpose_to_xT(ntok_, t, scale_l=l, which=which)
                    gB = V(o_lnb, 1024)
                    bB = V(o_lnb + 1024, 1024)
                    k.tt(xt_, ntok_, gB, ALU.mult, eng="pool")
                    k.tt(xt_, xt_, bB, ALU.add, eng="pool")
                    if final_out is not None:
                        k.dma(final_out, xt_)

                k.dma(V(o_lnb, 1024), ln1_g[l:l + 1, :].partition_broadcast(128))
                k.dma(V(o_lnb + 1024, 1024), ln1_b[l:l + 1, :].partition_broadcast(128))
                need("o", l, 0)
                wos = []
                for g in range(4):
                    wb = wbuf_next()
                    wv = wb[:, 0:2048].rearrange("p (k e) -> p k e", k=8)
                    k.dma(wv, wb_o[l].rearrange("(k p) e -> p k e", p=128)[:, :, g * 256:(g + 1) * 256])
                    wos.append(wv)
                for t in range(nt):
                    ps = psd()
                    pss = []
                    for g in range(4):
                        if g == 2:
                            ps = psd()
                        pp = ps[:, (g % 2) * 256:(g % 2 + 1) * 256]
                        for kk in range(8):
                            k.mm(pp, ycv[:, kk, t * 128:(t + 1) * 128], wos[g][:, kk, :], start=(kk == 0), stop=(kk == 7))
                        pss.append(pp)
                    ln_tile(t, pss, ln1_g, ln1_b, 0)
                if bi == 0 and l == 0:
                    dump("x1", V(o_xtok, 4096))
                if stage < 4:
                    continue
                if is_s:
                    load_state_T(st_ffn[l], 32, 22, o_sst)
                wup = wb_up[l]
                for g in range(11):
                    wvg = load_w(wup, g * 256, 256)
                    wvv = load_w(wup, DFF + g * 256, 256)
                    for uu in range(2):
                        j = 2 * g + uu
                        ps = psd()
                        unit_mm(ps, wvg, uu * 128, 128)
                        acc = conv_unit(ps[:, 0:Tb], 3, cwf(l, j), halo_f(l, j), V(o_sst + j * 32, 32), V(o_sso + j * 32, 32))
                        k.act(acc, acc, AF.Silu)
                        ps2 = psd()
                        unit_mm(ps2, wvv, uu * 128, 128)
                        k.tt(hTv[:, j, :], ps2[:, 0:Tb], acc, ALU.mult)
                if is_s:
                    rows_out(lambda u: V(o_sso + u * 32, 32), 22, 32, s_ffn[l])
                elif bi == 3:
                    rows_out(lambda u: halo_f(l, u), 22, 2, p_ffn[l])
                k.dma(V(o_lnb, 1024), ln2_g[l:l + 1, :].partition_broadcast(128))
                k.dma(V(o_lnb + 1024, 1024), ln2_b[l:l + 1, :].partition_broadcast(128))
                for g in range(11):
                    need("down", l, (g * 256) // 768)
                    wb = wbuf_next()
                    wv = wb[:, 0:2048].rearrange("p (k e) -> p k e", k=2)
                    k.dma(wv, wb_down[l][g * 256:(g + 1) * 256, :].rearrange("(k p) e -> p k e", p=128))
                    for kk in range(2):
                        kc = 2 * g + kk
                        for t in range(nt):
                            for hf in range(2):
                                k.mm(PS[:, (t * 2 + hf) * 512:(t * 2 + hf + 1) * 512], hTv[:, kc, t * 128:(t + 1) * 128],
                                     wv[:, kk, hf * 512:(hf + 1) * 512], start=(kc == 0), stop=(kc == 21))
                for t in range(nt):
                    pss = [PS[:, (t * 2) * 512 + q * 256:(t * 2) * 512 + (q + 1) * 256] for q in range(4)]
                    ln_resid(t, pss)
                for t in range(nt):
                    fo = y[t0 + t * 128:t0 + (t + 1) * 128, :] if last_l else None
                    ln_tile(t, None, ln2_g, ln2_b, 2, final_out=fo, need_T=not last_l)

    S.limit = limit
    print('main loop starts at op', len(S.ops))
    try:
        main_loops()
    except StopIteration:
        print('LIMIT reached at', len(S.ops))
    S.closing = True
    k.fence(outs_all + [wb_in, wb_o, wb_up, wb_down])
    st = S.finalize()
    print(st)
    return nc


def _in_maps(inp, nlayers=2):
    consts = make_consts()
    maps = []
    f = lambda a: np.ascontiguousarray(np.asarray(a, dtype=np.float32))
    for c in range(8):
        sl = slice(16 * c, 16 * c + 16)
        m = {
            "x_in": f(np.concatenate([inp["x_prompt"][c], np.asarray(inp["x_sample"])[sl].reshape(128, D)], 0)),
            "st_conv": f(np.asarray(inp["state_conv_mix"])[:, sl].reshape(2, 32, 256)),
            "st_gconv": f(np.asarray(inp["state_gdn_conv"])[:, sl].reshape(2, 48, 1536)),
            "st_gdn": f(np.asarray(inp["state_gdn"])[:, sl]),
            "st_c": f(np.asarray(inp["state_mlstm_c"])[:, sl]),
            "st_n": f(np.asarray(inp["state_mlstm_n"])[:, sl]),
            "st_m": f(np.asarray(inp["state_mlstm_m"])[:, sl]),
            "st_ffn": f(np.asarray(inp["state_ffn_conv"])[:, sl].reshape(2, 32, DFF)),
            "consts": consts,
        }
        for nm in ("w_in", "conv_w", "gdn_conv_w", "gdn_a_log", "gdn_dt_bias", "gdn_norm_w", "ml_i_bias",
                   "ml_f_bias", "ml_norm_w", "w_o", "ln1_g", "ln1_b", "w_up", "ffn_conv_w", "w_down",
                   "ln2_g", "ln2_b"):
            m[nm] = f(inp[nm])
        maps.append(m)
    return maps


def kernel(**inp):
    nc = build()
    maps = _in_maps(inp)
    res = run_bass_kernel_spmd(nc, maps, core_ids=list(range(8)))
    R = res.results
    yp = np.stack([R[c]["y"][0:2048] for c in range(8)], 0)
    ys = np.concatenate([R[c]["y"][2048:].reshape(16, 8, D) for c in range(8)], 0)

    def pst(nm, shp):
        return np.stack([R[c][nm].reshape(shp) for c in range(8)], 1)

    def sst(nm, shp):
        return np.concatenate([R[c][nm].reshape(shp) for c in range(8)], 1)

    outs = (yp, ys,
            pst("p_conv", (2, 2, 256)), pst("p_gconv", (2, 3, 1536)), pst("p_gdn", (2, 4, 128, 128)),
            pst("p_c", (2, 4, 64, 64)), pst("p_n", (2, 4, 64)), pst("p_m", (2, 4)), pst("p_ffn", (2, 2, DFF)),
            sst("s_conv", (2, 16, 2, 256)), sst("s_gconv", (2, 16, 3, 1536)), sst("s_gdn", (2, 16, 4, 128, 128)),
            sst("s_c", (2, 16, 4, 64, 64)), sst("s_n", (2, 16, 4, 64)), sst("s_m", (2, 16, 4)),
            sst("s_ffn", (2, 16, 2, DFF)))
    return tuple(np.ascontiguousarray(o.astype(np.float32)) for o in outs)
```

```python
import numpy as np
import concourse.bass as bass
import concourse.mybir as mybir

F32 = mybir.dt.float32
BF16 = mybir.dt.bfloat16
AF = mybir.ActivationFunctionType
ALU = mybir.AluOpType
AX = mybir.AxisListType
ESZ = {F32: 4, BF16: 2, mybir.dt.float32r: 4}
BUCK = 512


def box(ap):
    t = ap.tensor
    es = ESZ[ap.dtype]
    dims = [(int(s), int(c)) for s, c in ap.ap]
    off = int(ap.offset)
    cls = type(t).__name__
    if cls.startswith("DRam"):
        ext = sum((c - 1) * abs(s) for s, c in dims) + 1
        return (t.name, 0, 1, off * es, (off + ext) * es)
    rowlen = 1
    for s in list(t.shape)[1:]:
        rowlen *= int(s)
    p0 = off // rowlen
    f0 = off % rowlen
    if dims[0][0] == rowlen or dims[0][1] == 1:
        pc = dims[0][1]
        rest = dims[1:]
    elif dims[0][0] == 0:
        pc = 1
        rest = dims[1:]
    else:
        pc = 1
        rest = dims
    ext = sum((c - 1) * abs(s) for s, c in rest) + 1
    if cls.startswith("PSum") or cls.startswith("Psum") or cls.startswith("PS"):
        b0 = (f0 * es) // 2048 * 2048
        b1 = ((f0 + ext) * es + 2047) // 2048 * 2048
        return (t.name, 0, 128, b0, b1)
    return (t.name, p0, p0 + pc, f0 * es, (f0 + ext) * es)


class Sched:
    def __init__(self, nc, n_sp_sems=16, n_pool_sems=8):
        self.nc = nc
        self.ops = []
        self.engs = {"pe": nc.tensor, "dve": nc.vector, "act": nc.scalar,
                     "pool": nc.gpsimd, "sp": nc.sync}
        self.esem = {e: nc.semaphore("sem_" + e).__enter__() for e in self.engs}
        self.dsems = {"sp": [nc.semaphore("dsp%d" % i).__enter__() for i in range(n_sp_sems)],
                      "pool": [nc.semaphore("dpl%d" % i).__enter__() for i in range(n_pool_sems)],
                      "act": []}

    limit = None

    def add(self, eng, fn, r, w, dma=False):
        if self.limit is not None and len(self.ops) >= self.limit and not getattr(self, 'closing', False):
            raise StopIteration("limit")
        rb = [box(a) for a in r]
        wb = [box(a) for a in w]
        wb = wb + [b for b in rb if b[0] == 'psum' or b[0] == 'pa']
        self.ops.append((eng, fn, rb, wb, dma))

    def finalize(self):
        ops = self.ops
        n = len(ops)
        recs = {}
        deps = [None] * n
        pos = [0] * n
        cnt = {e: 0 for e in self.engs}
        dma_n = {q: 0 for q in self.dsems}
        dma_hist = {q: [] for q in self.dsems}
        dsem = [None] * n
        for i, (eng, fn, R, W, dma) in enumerate(ops):
            pos[i] = cnt[eng]
            cnt[eng] += 1
            d = set()
            for (nm, p0, p1, b0, b1) in R:
                for bk in range(b0 // BUCK, (b1 - 1) // BUCK + 1):
                    for rec in recs.get((nm, bk), ()):
                        if rec[5] and rec[0] < p1 and p0 < rec[1] and rec[2] < b1 and b0 < rec[3]:
                            d.add(rec[4])
            for (nm, p0, p1, b0, b1) in W:
                for bk in range(b0 // BUCK, (b1 - 1) // BUCK + 1):
                    for rec in recs.get((nm, bk), ()):
                        if rec[0] < p1 and p0 < rec[1] and rec[2] < b1 and b0 < rec[3]:
                            d.add(rec[4])
            for (nm, p0, p1, b0, b1) in W:
                for bk in range(b0 // BUCK, (b1 - 1) // BUCK + 1):
                    L = recs.setdefault((nm, bk), [])
                    lo = max(b0, bk * BUCK)
                    hi = min(b1, (bk + 1) * BUCK)
                    L[:] = [rec for rec in L if not (p0 <= rec[0] and rec[1] <= p1 and
                                                     lo <= max(rec[2], bk * BUCK) and
                                                     min(rec[3], (bk + 1) * BUCK) <= hi)]
                    L.append((p0, p1, b0, b1, i, True))
            for (nm, p0, p1, b0, b1) in R:
                for bk in range(b0 // BUCK, (b1 - 1) // BUCK + 1):
                    L = recs.setdefault((nm, bk), [])
                    if not dma:
                        lo = max(b0, bk * BUCK)
                        hi = min(b1, (bk + 1) * BUCK)
                        L[:] = [rec for rec in L if rec[5] or ops[rec[4]][4] or ops[rec[4]][0] != eng or not (
                            p0 <= rec[0] and rec[1] <= p1 and lo <= max(rec[2], bk * BUCK) and
                            min(rec[3], (bk + 1) * BUCK) <= hi)]
                    L.append((p0, p1, b0, b1, i, False))
            if dma:
                q = eng
                k = dma_n[q]
                P = len(self.dsems[q])
                dsem[i] = (q, k % P, 16 * (k // P + 1))
                if k >= P:
                    d.add(dma_hist[q][k - P])
                dma_hist[q].append(i)
                dma_n[q] += 1
            d.discard(i)
            deps[i] = d
        known = {e: {} for e in self.engs}
        snap = [None] * n
        sig = [False] * n
        waits = [None] * n
        for i, (eng, fn, R, W, dma) in enumerate(ops):
            kn = known[eng]
            need = {}
            for d in deps[i]:
                de = ops[d][0]
                ddma = ops[d][4]
                if ddma:
                    q, si, val = dsem[d]
                    key = ("D", q, si)
                    if kn.get(key, 0) >= val:
                        continue
                    if key not in need or dsem[need[key]][2] < val:
                        need[key] = d
                else:
                    if de == "pe" and eng == "pe" and not dma:
                        continue
                    if kn.get(de, -1) >= pos[d]:
                        continue
                    if de not in need or pos[need[de]] < pos[d]:
                        need[de] = d
            wl = []
            for key, d in need.items():
                if ops[d][4]:
                    if kn.get(key, 0) >= dsem[d][2]:
                        continue
                else:
                    if kn.get(key, -1) >= pos[d]:
                        continue
                wl.append(d)
                sig[d] = True
                for k2, v2 in snap[d].items():
                    if kn.get(k2, -1) < v2:
                        kn[k2] = v2
            waits[i] = wl
            s = dict(kn)
            if dma:
                q, si, val = dsem[i]
                s[("D", q, si)] = val
            else:
                s[eng] = pos[i]
            snap[i] = s
        cum = [0] * n
        c = {e: 0 for e in self.engs}
        for i, (eng, fn, R, W, dma) in enumerate(ops):
            if not dma and sig[i]:
                c[eng] += 1
            cum[i] = c[eng]
        nw = 0
        for i, (eng, fn, R, W, dma) in enumerate(ops):
            E = self.engs[eng]
            for d in waits[i]:
                if ops[d][4]:
                    q, si, val = dsem[d]
                    E.wait_ge(self.dsems[q][si], val)
                else:
                    E.wait_ge(self.esem[ops[d][0]], cum[d])
                nw += 1
            ins = fn()
            if ins is None:
                continue
            if dma:
                q, si, val = dsem[i]
                ins.then_inc(self.dsems[q][si], 16)
            elif sig[i]:
                ins.then_inc(self.esem[eng], 1)
        self.dbginfo = (waits, sig, cum, pos, dsem)
        self.stats = dict(n_ops=n, n_waits=nw, per_eng=cnt, sigs=c)
        return self.stats


class K:
    def __init__(self, nc, S):
        self.nc = nc
        self.S = S

    def mm(self, out, lhsT, rhs, start=True, stop=True):
        nc = self.nc
        self.S.add("pe", lambda: nc.tensor.matmul(out, lhsT, rhs, start=start, stop=stop),
                   [lhsT, rhs], [out])

    def tr(self, out, in_, ident):
        nc = self.nc
        self.S.add("pe", lambda: nc.tensor.transpose(out, in_, ident), [in_, ident], [out])

    def act(self, out, in_, func, bias=None, scale=None, accum=None, eng="act"):
        nc = self.nc
        kw = {}
        r = [in_]
        if bias is not None:
            kw["bias"] = bias
            if not isinstance(bias, (int, float)):
                r.append(bias)
        if scale is not None:
            kw["scale"] = scale
            if not isinstance(scale, (int, float)):
                r.append(scale)
        w = [out]
        if accum is not None:
            kw["accum_out"] = accum
            w.append(accum)
        self.S.add("act", lambda: nc.scalar.activation(out, in_, func, **kw), r, w)

    def tt(self, out, a, b, op, eng="dve"):
        E = self.S.engs[eng]
        self.S.add(eng, lambda: E.tensor_tensor(out, a, b, op), [a, b], [out])

    def ts(self, out, a, s1, op0, s2=None, op1=None, eng="dve", accum=None):
        E = self.S.engs[eng]
        r = [a]
        if not isinstance(s1, (int, float)):
            r.append(s1)
        if s2 is not None and not isinstance(s2, (int, float)):
            r.append(s2)
        w = [out]
        kw = {}
        if accum is not None:
            kw["accum_out"] = accum
            w.append(accum)
        if op1 is None:
            self.S.add(eng, lambda: E.tensor_scalar(out, a, s1, None, op0, **kw), r, w)
        else:
            self.S.add(eng, lambda: E.tensor_scalar(out, a, s1, s2, op0, op1, **kw), r, w)

    def stt(self, out, in0, scalar, in1, op0, op1):
        nc = self.nc
        r = [in0, in1]
        if not isinstance(scalar, (int, float)):
            r.append(scalar)
        self.S.add("dve", lambda: nc.vector.scalar_tensor_tensor(out, in0, scalar, in1, op0, op1), r, [out])

    def cp(self, out, in_, eng="act"):
        nc = self.nc
        if eng == "act":
            self.S.add("act", lambda: nc.scalar.copy(out, in_), [in_], [out])
        else:
            E = self.S.engs[eng]
            self.S.add(eng, lambda: E.tensor_copy(out, in_), [in_], [out])

    def memset(self, ap, v, eng="dve"):
        E = self.S.engs[eng]
        self.S.add(eng, lambda: E.memset(ap, v), [], [ap])

    def scan(self, out, d0, d1, init, op0, op1):
        nc = self.nc
        r = [d0, d1]
        if not isinstance(init, (int, float)):
            r.append(init)
        self.S.add("dve", lambda: nc.vector.tensor_tensor_scan(out, d0, d1, init, op0, op1), r, [out])

    def bn_stats(self, out, in_):
        nc = self.nc
        self.S.add("dve", lambda: nc.vector.bn_stats(out, in_), [in_], [out])

    def bn_aggr(self, out, in_):
        nc = self.nc
        self.S.add("dve", lambda: nc.vector.bn_aggr(out, in_), [in_], [out])

    def recip(self, out, in_):
        nc = self.nc
        self.S.add("dve", lambda: nc.vector.reciprocal(out, in_), [in_], [out])

    def dma(self, out, in_, q="sp", **kw):
        E = self.S.engs[q]
        self.S.add(q, lambda: E.dma_start(out=out, in_=in_, **kw), [in_], [out], dma=True)

    def fence(self, aps, q="sp"):
        self.S.add(q, lambda: None, list(aps), [])

from concourse.bass_utils import run_bass_kernel_spmd
import math

D = 1024
DIN = 3856
DFF = 2816
NEG = -1e30
LN_EPS = 1e-5
NORM_EPS = 1e-6
ALPHA = 4.0 ** 0.25
NTOK = 2176
NSP = 24
import os
CPENG = os.environ.get('CPENG', 'act')
BLOCKS = [(0, 512, False), (512, 512, False), (1024, 512, False), (1536, 512, False), (2048, 128, True)]

C_ID, C_MAIP, C_MASP, C_MAIS, C_MASS, C_SEL, C_CMP, C_CMS, C_ROWM, C_EH, C_ONE = \
    0, 128, 256, 384, 512, 640, 1152, 1280, 1408, 1424, 1440
NCST = 1448


def make_consts():
    c = np.zeros((128, NCST), np.float32)
    ii = np.arange(128)
    c[:, C_ID:C_ID + 128] = np.eye(128)
    J, I = np.meshgrid(ii, ii, indexing="ij")
    same = (J // 8) == (I // 8)
    c[:, C_MAIP:C_MAIP + 128] = np.where(I >= J, 0, NEG)
    c[:, C_MASP:C_MASP + 128] = np.where(I > J, 0, NEG)
    c[:, C_MAIS:C_MAIS + 128] = np.where((I >= J) & same, 0, NEG)
    c[:, C_MASS:C_MASS + 128] = np.where((I > J) & same, 0, NEG)
    for h in range(4):
        c[h, C_SEL + h * 128:C_SEL + (h + 1) * 128] = 1.0
        c[:, C_EH + h * 4 + h] = 1.0
    c[0:4, C_CMP:C_CMP + 128] = 1.0
    c[0:4, C_CMP] = 0.0
    c[0:4, C_CMS:C_CMS + 128] = 1.0
    c[0:4, C_CMS:C_CMS + 128:8] = 0.0
    for s in range(16):
        c[s * 8:s * 8 + 8, C_ROWM + s] = 1.0
    c[:, C_ONE] = 1.0
    return c


def build(dbg=None, nblocks=5, nlayers=2, stage=9, limit=None, bsel=None):
    dbg = dbg or {}
    nc = bass.Bass('TRN2', target_bir_lowering=False)
    S = Sched(nc, n_sp_sems=NSP, n_pool_sems=64)
    k = K(nc, S)

    def din(name, shape):
        return nc.dram_tensor(name, list(shape), F32, kind="ExternalInput").ap()

    def dout(name, shape):
        return nc.dram_tensor(name, list(shape), F32, kind="ExternalOutput").ap()

    x_in = din("x_in", [NTOK, D])
    st_conv = din("st_conv", [2, 32, 256])
    st_gconv = din("st_gconv", [2, 48, 1536])
    st_gdn = din("st_gdn", [2, 16, 4, 128, 128])
    st_c = din("st_c", [2, 16, 4, 64, 64])
    st_n = din("st_n", [2, 16, 4, 64])
    st_m = din("st_m", [2, 16, 4])
    st_ffn = din("st_ffn", [2, 32, DFF])
    w_in = din("w_in", [2, D, DIN])
    conv_w = din("conv_w", [2, 3, 256])
    gdn_conv_w = din("gdn_conv_w", [2, 4, 1536])
    gdn_a_log = din("gdn_a_log", [2, 4])
    gdn_dt_bias = din("gdn_dt_bias", [2, 4])
    gdn_norm_w = din("gdn_norm_w", [2, 128])
    ml_i_bias = din("ml_i_bias", [2, 4])
    ml_f_bias = din("ml_f_bias", [2, 4])
    ml_norm_w = din("ml_norm_w", [2, 256])
    w_o = din("w_o", [2, D, D])
    ln1_g = din("ln1_g", [2, D])
    ln1_b = din("ln1_b", [2, D])
    w_up = din("w_up", [2, D, 2 * DFF])
    ffn_conv_w = din("ffn_conv_w", [2, 3, DFF])
    w_down = din("w_down", [2, DFF, D])
    ln2_g = din("ln2_g", [2, D])
    ln2_b = din("ln2_b", [2, D])
    consts = din("consts", [128, NCST])

    y = dout("y", [NTOK, D])
    p_conv = dout("p_conv", [2, 2, 256])
    p_gconv = dout("p_gconv", [2, 3, 1536])
    p_gdn = dout("p_gdn", [2, 4, 128, 128])
    p_c = dout("p_c", [2, 4, 64, 64])
    p_n = dout("p_n", [2, 4, 64])
    p_m = dout("p_m", [2, 4])
    p_ffn = dout("p_ffn", [2, 2, DFF])
    s_conv = dout("s_conv", [2, 32, 256])
    s_gconv = dout("s_gconv", [2, 48, 1536])
    s_gdn = dout("s_gdn", [2, 16, 4, 128, 128])
    s_c = dout("s_c", [2, 16, 4, 64, 64])
    s_n = dout("s_n", [2, 16, 4, 64])
    s_m = dout("s_m", [2, 16, 4])
    s_ffn = dout("s_ffn", [2, 32, DFF])
    outs_all = [y, p_conv, p_gconv, p_gdn, p_c, p_n, p_m, p_ffn, s_conv, s_gconv, s_gdn, s_c, s_n, s_m, s_ffn]
    dbg_out = {}
    for nm, shp in dbg.items():
        dbg_out[nm] = dout("dbg_" + nm, shp)
        outs_all.append(dbg_out[nm])

    def dump(nm, ap):
        if nm in dbg_out:
            k.dma(dbg_out[nm], ap)

    wb_in = nc.dram_tensor("wb_in", [2, D, DIN], BF16, kind="Internal").ap()
    wb_o = nc.dram_tensor("wb_o", [2, D, D], BF16, kind="Internal").ap()
    wb_up = nc.dram_tensor("wb_up", [2, D, 2 * DFF], BF16, kind="Internal").ap()
    wb_down = nc.dram_tensor("wb_down", [2, DFF, D], BF16, kind="Internal").ap()

    NCOL = 44800
    A = nc.sbuf_tensor("arena", [128, NCOL], F32).__enter__()
    PS = nc.psum_tensor("psum", [128, 4096], F32).__enter__()
    cur = [0]

    def al(n):
        o = cur[0]
        cur[0] += n
        assert cur[0] <= NCOL, cur[0]
        return o

    def V(o, n, p0=0, p1=128):
        return A[p0:p1, o:o + n]

    def Vb(o, n, p0=0, p1=128):
        return A[p0:p1, o:o + n].bitcast(BF16)

    cast_jobs = []
    job_index = {}
    for l in range(nlayers):
        for j, (c0, n) in enumerate(((0, 768), (768, 768), (1536, 768), (2304, 520), (2824, 512), (3336, 520))):
            job_index[("in", l, j)] = len(cast_jobs)
            cast_jobs.append([(wb_in[l][:, c0:c0 + n], w_in[l][:, c0:c0 + n])])
        job_index[("o", l, 0)] = len(cast_jobs)
        cast_jobs.append([(wb_o[l][:, 0:512], w_o[l][:, 0:512]), (wb_o[l][:, 512:1024], w_o[l][:, 512:1024])])
        for j in range(6):
            n = 512 if j < 5 else 256
            job_index[("up", l, j)] = len(cast_jobs)
            cast_jobs.append([(wb_up[l][:, 512 * j:512 * j + n], w_up[l][:, 512 * j:512 * j + n]),
                              (wb_up[l][:, DFF + 512 * j:DFF + 512 * j + n], w_up[l][:, DFF + 512 * j:DFF + 512 * j + n])])
        for j in range(4):
            r0 = 768 * j
            r1 = min(DFF, r0 + 768)
            job_index[("down", l, j)] = len(cast_jobs)
            cast_jobs.append([(wb_down[l][r0:r1, :], w_down[l][r0:r1, :])])
    cast_ptr = [0]

    def need(kind, l, j, la=2):
        tgt = min(len(cast_jobs), job_index[(kind, l, j)] + 1 + la)
        while cast_ptr[0] < tgt:
            for (o_, i_) in cast_jobs[cast_ptr[0]]:
                k.dma(o_, i_, q="pool")
            cast_ptr[0] += 1

    def in_job(c0):
        for j, (a0, n) in enumerate(((0, 768), (768, 768), (1536, 768), (2304, 520), (2824, 512), (3336, 520))):
            if a0 <= c0 < a0 + n:
                return j
        raise ValueError(c0)

    if os.environ.get('NOCAST') is None:
        need("in", 0, 0, la=1)
    if stage == -1:
        need('down', nlayers - 1, 3)
        k.fence([wb_in, wb_o, wb_up, wb_down])
        print(S.finalize())
        return nc
    o_cst = al(NCST)
    k.dma(V(o_cst, NCST), consts)
    ident = V(o_cst + C_ID, 128)

    def sel(h):
        return V(o_cst + C_SEL + h * 128, 128, 0, 4)

    def eh(h):
        return V(o_cst + C_EH + h * 4, 4)

    rowmask = V(o_cst + C_ROWM, 16)

    dctr = [0]

    def psd():
        b = dctr[0] % 4
        dctr[0] += 1
        return PS[:, b * 512:(b + 1) * 512]

    qctr = [0]

    def psq(n=128):
        q = qctr[0] % 8
        qctr[0] += 1
        q = (q + 4) % 8
        return PS[:, q * 512: q * 512 + n]

    o_cwa = al(2 * 2 * 3)
    o_cwg = al(2 * 12 * 4)
    o_cwf = al(2 * 22 * 3)
    o_lnf = al(2 * 4 * 8)
    o_gnw = al(2)
    o_gb = al(2 * 8)
    o_mnw = al(2 * 256)
    o_S = al(2 * 2 * 512)
    o_C = al(2 * 2 * 130)
    o_ha = al(2 * 2 * 2)
    o_hg = al(2 * 12 * 3)
    o_hf = al(2 * 22 * 2)
    o_mp = al(2)
    o_st_end = cur[0]
    o_xtok = al(4096)
    o_xT = al(2048)
    o_qkv = al(6144)
    o_ab = al(1024)
    o_sz = al(1024)
    o_mqk = al(2048)
    o_mvo = al(4 * 516)
    o_yc = al(2048)
    o_xe = al(2 * 516)
    o_acc = al(2 * 512)
    o_wg = al(64)
    o_graw = al(4 * 128)
    o_rows = al(26 * 128)
    o_cols = al(128)
    o_scan = al(6144)
    o_wbuf = al(4 * 1024)
    o_misc = al(640)
    o_lnb = o_scan + 2048
    o_cb = al(5 * 64)
    print("arena cols used", cur[0])
    o_ptmp = o_scan
    o_sst = o_xtok + 1024
    o_sso = o_xtok + 1024 + 704
    o_S0 = o_qkv + 1536
    o_msk = o_S0 + 2048
    o_kdm = o_msk + 2176

    identb = Vb(o_cb, 64)
    for ci_, co_ in enumerate((C_ID, C_MAIP, C_MASP, C_MAIS, C_MASS)):
        k.cp(Vb(o_cb + ci_ * 64, 64), V(o_cst + co_, 128), eng="dve")

    def cwa(l, u):
        return V(o_cwa + (l * 2 + u) * 3, 3)

    def cwg(l, u):
        return V(o_cwg + (l * 12 + u) * 4, 4)

    def cwf(l, u):
        return V(o_cwf + (l * 22 + u) * 3, 3)

    def lnf(l, which, kk):
        return V(o_lnf + (l * 4 + which) * 8 + kk, 1)

    def gbv(l, i):
        return V(o_gb + l * 8 + i, 1, 0, 4)

    ptmp = V(o_ptmp, DFF)
    k.memset(ptmp, 0.0)
    for l in range(nlayers):
        for (src, W_, nu, fn) in ((conv_w, 3, 2, cwa), (gdn_conv_w, 4, 12, cwg), (ffn_conv_w, 3, 22, cwf)):
            C_ = nu * 128
            k.dma(ptmp[0:W_, 0:C_], src[l])
            for u0 in range(0, nu, 4):
                ps = psd()
                n4 = min(4, nu - u0)
                for u in range(u0, u0 + n4):
                    k.tr(ps[:, (u - u0) * 128:(u - u0 + 1) * 128], ptmp[:, u * 128:(u + 1) * 128], ident)
                for u in range(u0, u0 + n4):
                    k.cp(fn(l, u), ps[:, (u - u0) * 128:(u - u0) * 128 + W_])
        for wi, src in enumerate((ln1_g, ln1_b, ln2_g, ln2_b)):
            k.dma(ptmp[0:8, 0:128], src[l].rearrange("(k p) -> k p", p=128))
            ps = psd()
            k.tr(ps[:, 0:128], ptmp[:, 0:128], ident)
            k.cp(V(o_lnf + (l * 4 + wi) * 8, 8), ps[:, 0:8])
        k.dma(V(o_gnw + l, 1), gdn_norm_w[l].rearrange("(p o) -> p o", o=1))
        for i, src in enumerate((gdn_a_log, gdn_dt_bias, ml_i_bias, ml_f_bias)):
            k.dma(gbv(l, i), src[l].rearrange("(p o) -> p o", o=1))
        k.act(gbv(l, 4), gbv(l, 0), AF.Exp)
        k.ts(gbv(l, 4), gbv(l, 4), -1.0, ALU.mult)
        k.ts(gbv(l, 5), gbv(l, 3), -1.0, ALU.mult)
        k.dma(V(o_mnw + l * 256, 256), ml_norm_w[l:l + 1, :].partition_broadcast(128))

    def Sst(l, par, h):
        return V(o_S + (l * 2 + par) * 512 + h * 128, 128)

    def Cst(l, par, h):
        po = (h % 2) * 64
        return V(o_C + (l * 2 + par) * 130 + (h // 2) * 65, 65, po, po + 64)

    k.memset(V(o_S, o_st_end - o_S), 0.0)
    k.memset(V(o_rows, 26 * 128), 0.0)

    def halo_a(l, u):
        return V(o_ha + (l * 2 + u) * 2, 2)

    def halo_g(l, u):
        return V(o_hg + (l * 12 + u) * 3, 3)

    def halo_f(l, u):
        return V(o_hf + (l * 22 + u) * 2, 2)

    def mprev(l):
        return V(o_mp + l, 1, 0, 4)

    def xtok(t):
        return V(o_xtok + t * 1024, 1024)

    def wbuf_next(ctr=[0]):
        i = ctr[0]
        ctr[0] += 1
        return Vb(o_wbuf + (i % 4) * 1024, 1024)

    def row(i, p0=0, p1=4):
        return V(o_rows + i * 128, 128, p0, p1)

    def slot(h, i):
        return V(o_scan + (h * 12 + i) * 128, 128)

    def slotb(h, i):
        return Vb(o_scan + (h * 12 + i) * 128, 64)

    o_sb = al(256)
    o_ml = al(966)

    def Sb(h):
        return Vb(o_sb + h * 64, 64)

    if stage == 0:
        k.dma(y[0:128, 0:512], V(o_cwa, 512))
        k.fence(outs_all)
        print(S.finalize())
        return nc
    def main_loops():
        for bi, (t0, Tb, is_s) in enumerate(BLOCKS[:nblocks]):
            if bsel is not None and bi not in bsel:
                continue
            nt = Tb // 128
            L = 8 if is_s else 128
            nch = 128 // L
            nch2 = max(nch, 2)
            MAI = Vb(o_cb + (3 if is_s else 1) * 64, 64)
            MAS = Vb(o_cb + (4 if is_s else 2) * 64, 64)
            cmask = V(o_cst + (C_CMS if is_s else C_CMP), 128, 0, 4)
            nlev = 3 if is_s else 7
            xTv = Vb(o_xT, 4 * Tb).rearrange("p (k t) -> p k t", k=8)
            ycv = Vb(o_yc, 4 * Tb).rearrange("p (k t) -> p k t", k=8)
            hTv = Vb(o_qkv, 11 * Tb).rearrange("p (k t) -> p k t", k=22)
            szv = Vb(o_sz, 2 * Tb).rearrange("p (u t) -> p u t", u=4)
            ab = V(o_ab, 2 * Tb).rearrange("p (u t) -> p u t", u=2)
            mqk = V(o_mqk, 4 * Tb).rearrange("p (u t) -> p u t", u=4)

            def qkv(u):
                return V(o_qkv + u * Tb, Tb)

            def vext(t):
                return Vb(o_mvo + t * 516, 130).rearrange("p (h e) -> p h e", h=4)

            def sigo(t):
                return V(o_mvo + t * 516 + 260, 256)

            def tview(ap):
                if not is_s:
                    return ap
                return ap.rearrange("p (s t) -> p s t", s=16)

            for t in range(nt):
                k.dma(xtok(t), x_in[t0 + t * 128:t0 + (t + 1) * 128, :])

            def transpose_to_xT(src_tok, t, scale_l=None, which=None):
                for k0 in range(0, 8, 4):
                    ps = psd()
                    for kk in range(k0, k0 + 4):
                        k.tr(ps[:, (kk - k0) * 128:(kk - k0 + 1) * 128], src_tok[:, kk * 128:(kk + 1) * 128], ident)
                    for kk in range(k0, k0 + 4):
                        o_ = xTv[:, kk, t * 128:(t + 1) * 128]
                        i_ = ps[:, (kk - k0) * 128:(kk - k0 + 1) * 128]
                        if scale_l is None:
                            k.cp(o_, i_, eng=CPENG)
                        else:
                            k.act(o_, i_, AF.Identity, scale=lnf(scale_l, which, kk), bias=lnf(scale_l, which + 1, kk))

            for t in range(nt):
                transpose_to_xT(xtok(t), t)

            xectr = [0]

            def conv_unit(src, W_, cw, halo, st_src, st_rows, mul_by=None):
                i = xectr[0]
                xectr[0] += 1
                wl = W_ - 1
                if not is_s:
                    full = V(o_xe + (i % 2) * 516, wl + Tb)
                    data, hv, tail = full[:, wl:wl + Tb], full[:, 0:wl], full[:, Tb:Tb + wl]
                    win = lambda j: full[:, j:j + Tb]
                else:
                    full = V(o_xe + (i % 2) * 516, 16 * (wl + 8)).rearrange("p (s t) -> p s t", s=16)
                    data, hv, tail = full[:, :, wl:wl + 8], full[:, :, 0:wl], full[:, :, 8:8 + wl]
                    win = lambda j: full[:, :, j:j + 8]
                if mul_by is None:
                    k.cp(data, tview(src))
                else:
                    k.tt(data, tview(src), tview(mul_by), ALU.mult)
                if not is_s:
                    k.cp(hv, halo, eng="pool")
                else:
                    k.cp(hv, st_src.rearrange("p (s t) -> p s t", s=16), eng="pool")
                acc = V(o_acc + (i % 2) * 512, Tb)
                accv = tview(acc)
                k.ts(accv, win(0), cw[:, 0:1], ALU.mult)
                for j in range(1, W_):
                    k.stt(accv, win(j), cw[:, j:j + 1], accv, ALU.mult, ALU.add)
                if not is_s:
                    k.cp(halo, tail, eng="pool")
                else:
                    k.cp(st_rows.rearrange("p (s t) -> p s t", s=16), tail, eng="pool")
                return acc

            def load_state_T(src, R, nu, dst_off):
                for c0 in range(0, nu, 4):
                    n4 = min(4, nu - c0)
                    k.dma(ptmp[0:R, 0:n4 * 128], src[:, c0 * 128:(c0 + n4) * 128])
                    ps = psd()
                    for u in range(c0, c0 + n4):
                        k.tr(ps[:, (u - c0) * 128:(u - c0 + 1) * 128], ptmp[:, (u - c0) * 128:(u - c0 + 1) * 128], ident)
                    for u in range(c0, c0 + n4):
                        k.cp(V(dst_off + u * R, R), ps[:, (u - c0) * 128:(u - c0) * 128 + R])

            def rows_out(get_in, nu, R, dram):
                for c0 in range(0, nu, 4):
                    n4 = min(4, nu - c0)
                    ps = psd()
                    for u in range(c0, c0 + n4):
                        gi = get_in(u)
                        k.tr(ps[:, (u - c0) * 128:(u - c0 + 1) * 128], A[:, int(gi.offset) % NCOL:int(gi.offset) % NCOL + 128], ident)
                    rb = V(o_misc + 128, 512, 0, R)
                    k.cp(rb[:, 0:n4 * 128], ps[0:R, 0:n4 * 128])
                    k.dma(dram[:, c0 * 128:(c0 + n4) * 128], rb[:, 0:n4 * 128])

            for l in range(nlayers):
                last_l = (l == nlayers - 1)
                xectr[0] = 0

                wsrc = wb_in[l]
                wup = wb_up[l]

                def load_w(src, c0, n):
                    if src is wsrc:
                        need("in", l, in_job(c0))
                    else:
                        need("up", l, (c0 % DFF) // 512)
                    wb = wbuf_next()
                    wv = wb[:, 0:8 * n].rearrange("p (k e) -> p k e", k=8)
                    k.dma(wv, src.rearrange("(k p) e -> p k e", p=128)[:, :, c0:c0 + n])
                    return wv

                def unit_mm(ps, wv, e0, M):
                    for kk in range(8):
                        k.mm(ps[0:M, 0:Tb], wv[:, kk, e0:e0 + M], xTv[:, kk, :], start=(kk == 0), stop=(kk == 7))

                wsrc = wb_in[l]
                actmp = V(o_lnb, 2 * Tb).rearrange("p (u t) -> p u t", u=2)
                wv = load_w(wsrc, 0, 256)
                for u in range(2):
                    ps = psd()
                    unit_mm(ps, wv, u * 128, 128)
                    k.cp(ab[:, u, :], ps[:, 0:Tb])
                wv = load_w(wsrc, 256, 256)
                for u in range(2):
                    ps = psd()
                    unit_mm(ps, wv, u * 128, 128)
                    k.cp(actmp[:, u, :], ps[:, 0:Tb], eng="dve")
                if is_s:
                    load_state_T(st_conv[l], 32, 2, o_sst)
                wv = load_w(wsrc, 512, 256)
                for u in range(2):
                    ps = psd()
                    unit_mm(ps, wv, u * 128, 128)
                    acc = conv_unit(ps[:, 0:Tb], 3, cwa(l, u), halo_a(l, u), V(o_sst + u * 32, 32), V(o_sso + u * 32, 32),
                                    mul_by=actmp[:, u, :])
                    k.tt(ycv[:, u, :], acc, ab[:, u, :], ALU.mult)
                if is_s:
                    rows_out(lambda u: V(o_sso + u * 32, 32), 2, 32, s_conv[l])
                elif bi == 3:
                    rows_out(lambda u: halo_a(l, u), 2, 2, p_conv[l])
                if is_s:
                    load_state_T(st_gconv[l], 48, 12, o_sst)
                for g in range(6):
                    wv = load_w(wsrc, 768 + 256 * g, 256)
                    for uu in range(2):
                        j = 2 * g + uu
                        ps = psd()
                        unit_mm(ps, wv, uu * 128, 128)
                        acc = conv_unit(ps[:, 0:Tb], 4, cwg(l, j), halo_g(l, j), V(o_sst + j * 48, 48), V(o_sso + j * 48, 48))
                        k.act(qkv(j), acc, AF.Silu)
                if is_s:
                    rows_out(lambda u: V(o_sso + u * 48, 48), 12, 48, s_gconv[l])
                elif bi == 3:
                    rows_out(lambda u: halo_g(l, u), 12, 3, p_gconv[l])
                for g in range(2):
                    wv = load_w(wsrc, 2304 + 256 * g, 256)
                    for uu in range(2):
                        ps = psd()
                        unit_mm(ps, wv, uu * 128, 128)
                        k.act(szv[:, 2 * g + uu, :], ps[:, 0:Tb], AF.Silu)
                for g in range(2):
                    wv = load_w(wsrc, 2824 + 256 * g, 256)
                    for uu in range(2):
                        ps = psd()
                        unit_mm(ps, wv, uu * 128, 128)
                        k.cp(mqk[:, 2 * g + uu, :], ps[:, 0:Tb])
                wv = load_w(wsrc, 3336, 256)
                for t in range(nt):
                    ps = psd()
                    for kk in range(8):
                        k.mm(ps[:, 0:256], xTv[:, kk, t * 128:(t + 1) * 128], wv[:, kk, :], start=(kk == 0), stop=(kk == 7))
                    k.cp(vext(t)[:, :, 0:64], ps[:, 0:256].rearrange("p (h e) -> p h e", h=4))
                    k.memset(vext(t)[:, :, 64:65], 1.0, eng="pool")
                wv = load_w(wsrc, 3592, 256)
                for t in range(nt):
                    ps = psd()
                    for kk in range(8):
                        k.mm(ps[:, 0:256], xTv[:, kk, t * 128:(t + 1) * 128], wv[:, kk, :], start=(kk == 0), stop=(kk == 7))
                    k.act(sigo(t), ps[:, 0:256], AF.Sigmoid)
                wg = Vb(o_wg, 64).rearrange("p (k e) -> p k e", k=8)
                wsr = wsrc.rearrange("(k p) e -> p k e", p=128)
                need("in", l, 5)
                k.dma(wg[:, :, 0:8], wsr[:, :, 2816:2824])
                k.dma(wg[:, :, 8:16], wsr[:, :, 3848:3856])
                if bi == 0 and l == 0:
                    dump("qkv", V(o_qkv, 12 * Tb))
                    dump("ycA", V(o_yc, 2048))

                for t in range(nt if stage >= 2 else 0):
                    ts_ = slice(t * 128, (t + 1) * 128)
                    gti = bi * 4 + t
                    par = gti % 2 if not is_s else 0
                    graw = [V(o_graw + i * 128, 128, 0, 4) for i in range(4)]
                    for i in range(4):
                        ps = psq()
                        for kk in range(8):
                            k.mm(ps[0:4, :], wg[:, kk, i * 4:i * 4 + 4], xTv[:, kk, ts_], start=(kk == 0), stop=(kk == 7))
                        k.cp(graw[i], ps[0:4, :], eng="dve")
                    r_G, r_lb, r_lrq, r_lrk, r_ra, r_rq, r_cj, r_kd, r_tmp, r_GL = [row(i) for i in range(10)]
                    k.act(r_tmp, graw[0], AF.Exp, bias=gbv(l, 1))
                    k.act(r_tmp, r_tmp, AF.Ln, bias=1.0)
                    k.ts(r_tmp, r_tmp, gbv(l, 4), ALU.mult)
                    k.scan(r_G, cmask, r_tmp, 0.0, ALU.mult, ALU.add)
                    k.act(r_lb, graw[1], AF.Exp, scale=-1.0)
                    k.act(r_lb, r_lb, AF.Ln, bias=1.0)
                    k.ts(r_lb, r_lb, -1.0, ALU.mult)
                    for qi, rr in ((0, r_lrq), (1, r_lrk)):
                        ps = psq()
                        for h in range(4):
                            sq = slot(h, 3 + qi)
                            k.act(sq, qkv(qi * 4 + h)[:, ts_], AF.Square)
                            k.mm(ps[0:4, :], eh(h), sq, start=(h == 0), stop=(h == 3))
                        k.act(rr, ps[0:4, :], AF.Ln, bias=NORM_EPS)
                        k.ts(rr, rr, -0.5, ALU.mult)
                    k.tt(r_ra, r_G, r_lb, ALU.add)
                    k.tt(r_ra, r_ra, r_lrk, ALU.add)
                    k.ts(r_rq, r_lrq, math.log(128.0 ** -0.5), ALU.add)
                    k.tt(r_rq, r_rq, r_G, ALU.add)
                    k.tt(r_cj, r_lrk, r_G, ALU.subtract)
                    G3 = r_G.rearrange("p (c l) -> p c l", l=L)
                    k.tt(r_kd.rearrange("p (c l) -> p c l", l=L), r_cj.rearrange("p (c l) -> p c l", l=L),
                         G3[:, :, L - 1:L].broadcast_to([4, nch, L]), ALU.add)
                    k.cp(r_GL[:, 0:nch].rearrange("p (c o) -> p c o", o=1), G3[:, :, L - 1:L], eng="dve")
                    craw = V(o_cols, 20)
                    cexp = V(o_cols + 20, 20)
                    ps = psq()
                    for ci, rr in enumerate((r_cj, r_lb, r_ra, r_kd, r_rq)):
                        k.mm(ps[:, ci * 4:ci * 4 + 4], rr, ident[0:4, 0:4], start=True, stop=True)
                    k.cp(craw, ps[:, 0:20], eng="dve")
                    k.act(cexp, ps[:, 0:20], AF.Exp)
                    elast = V(o_misc, 4 * nch2)
                    ps = psq()
                    for h in range(4):
                        k.mm(ps[:, h * nch2:h * nch2 + nch2], sel(h), r_GL[:, 0:nch2], start=True, stop=True)
                    k.act(elast, ps[:, 0:4 * nch2], AF.Exp)
                    if bi == 0 and l == 0 and t == 0:
                        dump("rows", V(o_rows, 10 * 128, 0, 4))
                        dump("cexp", V(o_cols, 40))

                    def mlW(h):
                        return slot(h, 0) if is_s else Vb(o_ml + (h % 2) * 225, 64)
                    def mlS(h):
                        return slotb(h, 1) if is_s else Vb(o_ml + (h % 2) * 225 + 64, 64)
                    def mlN(h):
                        return slot(h, 2)[:, 0:65] if is_s else V(o_ml + (h % 2) * 225 + 128, 65)
                    def mlK(h):
                        return slotb(h, 4)[:, 0:64] if is_s else Vb(o_ml + (h % 2) * 225 + 193, 32)
                    def gdn_gen():
                        qT_ = [qkv(h)[:, ts_] for h in range(4)]
                        kT_ = [qkv(4 + h)[:, ts_] for h in range(4)]
                        vT_ = [qkv(8 + h)[:, ts_] for h in range(4)]
                        yield
                        for h in range(4):
                            ps = psq()
                            k.mm(ps, sel(h), r_ra, start=True, stop=False)
                            k.mm(ps, identb, MAS, start=False, stop=True)
                            k.act(slot(h, 0), ps, AF.Exp, bias=craw[:, h:h + 1])
                            ps2 = psq()
                            k.mm(ps2, kT_[h], kT_[h])
                            k.stt(slot(h, 4), ps2, -1.0, slot(h, 0), ALU.mult, ALU.mult)
                        yield
                        for h in range(4):
                            ps = psq()
                            k.mm(ps, sel(h), r_rq, start=True, stop=False)
                            k.mm(ps, identb, MAI, start=False, stop=True)
                            k.act(slot(h, 1), ps, AF.Exp, bias=craw[:, h:h + 1])
                            ps2 = psq()
                            k.mm(ps2, kT_[h], qT_[h])
                            k.tt(slotb(h, 2), ps2, slot(h, 1), ALU.mult)
                        yield
                        for h in range(4):
                            ps = psq()
                            k.tr(ps, slot(h, 4), ident)
                            k.cp(slot(h, 5), ps)
                            k.tt(slot(h, 3), slot(h, 4), ident, ALU.add, eng="pool")
                        yield
                        for h in range(4):
                            ps = psq()
                            k.mm(ps, slot(h, 4), slot(h, 5))
                            k.cp(slot(h, 6), ps)
                            ps2 = psq()
                            k.mm(ps2, slot(h, 5), slot(h, 4))
                            k.cp(slot(h, 4), ps2, eng="dve")
                        cP = [(3, 6)] * 4
                        yield
                        for lev in range(1, nlev):
                            lastlev = (lev == nlev - 1)
                            nP = []
                            for h in range(4):
                                pb, ipm = cP[h]
                                npb, npm = 10 - pb, 11 - ipm
                                if not lastlev:
                                    ps = psq(256)
                                    k.mm(ps, slot(h, ipm), V(o_scan + (h * 12 + pb) * 128, 256))
                                    k.tt(slot(h, npb), ps[:, 0:128], slot(h, pb), ALU.add)
                                    k.cp(slot(h, npb + 1), ps[:, 128:256])
                                    ps2 = psq()
                                    k.mm(ps2, slot(h, pb + 1), slot(h, ipm))
                                    k.cp(slot(h, npm), ps2, eng="dve")
                                else:
                                    ps = psq()
                                    k.mm(ps, slot(h, ipm), slot(h, pb))
                                    k.tt(slotb(h, npb), ps, slot(h, pb), ALU.add)
                                nP.append((npb, npm))
                            cP = nP
                            yield
                        TTf = [slotb(h, cP[h][0]) for h in range(4)]
                        yield
                        for h in range(4):
                            ps = psq()
                            k.tr(ps, kT_[h], ident)
                            k.ts(slotb(h, 0), ps, cexp[:, 8 + h:9 + h], ALU.mult)
                            k.act(slotb(h, 1), ps, AF.Identity, scale=cexp[:, 12 + h:13 + h])
                            ps2 = psq()
                            k.tr(ps2, vT_[h], ident)
                            k.ts(slotb(h, 9), ps2, cexp[:, 4 + h:5 + h], ALU.mult)
                        yield
                        for h in range(4):
                            ps = psq()
                            k.mm(ps, slotb(h, 0), TTf[h])
                            k.act(slot(h, 10) if is_s else slotb(h, 10), ps, AF.Identity, scale=-1.0)
                        O_ = [slot(h, 5) for h in range(4)]
                        yield
                        if not is_s:
                            for h in range(4):
                                Sc, Sn = Sst(l, par, h), Sst(l, 1 - par, h)
                                k.cp(Sb(h), Sc)
                                ps = psq()
                                k.mm(ps, TTf[h], slotb(h, 9), start=True, stop=False)
                                k.mm(ps, slotb(h, 10), Sb(h), start=False, stop=True)
                                k.cp(slotb(h, 11), ps)
                                ps1 = psq()
                                k.mm(ps1, qT_[h], Sc)
                                k.act(slot(h, 0), ps1, AF.Identity, scale=cexp[:, 16 + h:17 + h])
                                ps2 = psq()
                                k.mm(ps2, slotb(h, 2), slotb(h, 11))
                                k.tt(O_[h], ps2, slot(h, 0), ALU.add)
                                ps3 = psq()
                                k.mm(ps3, slotb(h, 1), slotb(h, 11))
                                k.stt(Sn, Sc, elast[:, h * nch2:h * nch2 + 1], ps3, ALU.mult, ALU.add)
                                if gti == 15:
                                    k.dma(p_gdn[l, h], Sn)
                        else:
                            S0v = V(o_S0, 2048).rearrange("p (s v) -> p s v", s=16)
                            mskd = V(o_msk, 2176).rearrange("p (s r) -> p s r", r=136)[:, :, 0:8]
                            mskf = V(o_msk, 2048).rearrange("p (s i) -> p s i", s=16)
                            for h in range(4):
                                k.dma(S0v, st_gdn[l, :, h].rearrange("s k v -> k s v"))
                                if h == 0 and l == 0:
                                    k.memset(V(o_msk, 2176), 0.0, eng="pool")
                                k.cp(mskd, slot(h, 10).rearrange("p (s r) -> p s r", r=8), eng="pool")
                                ps = psq()
                                k.mm(ps, TTf[h], slotb(h, 9), start=True, stop=False)
                                for s in range(16):
                                    k.mm(ps, mskf[:, s, :], S0v[:, s, :], start=False, stop=(s == 15))
                                k.cp(slotb(h, 11), ps)
                                k.cp(mskd, qT_[h].rearrange("p (s r) -> p s r", r=8), eng="pool")
                                ps1 = psq()
                                for s in range(16):
                                    k.mm(ps1, mskf[:, s, :], S0v[:, s, :], start=(s == 0), stop=(s == 15))
                                k.act(slot(h, 0), ps1, AF.Identity, scale=cexp[:, 16 + h:17 + h])
                                ps2 = psq()
                                k.mm(ps2, slotb(h, 2), slotb(h, 11))
                                k.tt(O_[h], ps2, slot(h, 0), ALU.add)
                                for s in range(16):
                                    kdm = Vb(o_kdm + (s % 3) * 128, 64)
                                    k.ts(kdm, slotb(h, 1), rowmask[:, s:s + 1], ALU.mult)
                                    ps3 = psq()
                                    k.mm(ps3, kdm, slotb(h, 11))
                                    k.stt(S0v[:, s, :], S0v[:, s, :], elast[:, h * 16 + s:h * 16 + s + 1], ps3, ALU.mult, ALU.add)
                                k.dma(s_gdn[l, :, h].rearrange("s k v -> k s v"), S0v)
                        if bi == 0 and l == 0 and t == 0:
                            dump("o_gdn", V(o_scan + 5 * 128, 128))
                            pass
                        yield
                        ss = V(o_cols + 40, 4)
                        yield
                        for h in range(4):
                            k.act(slot(h, 0), O_[h], AF.Square, accum=ss[:, h:h + 1])
                        k.ts(ss, ss, 1.0 / 128, ALU.mult, NORM_EPS, ALU.add)
                        k.act(ss, ss, AF.Sqrt)
                        k.recip(ss, ss)
                        yield
                        for h in range(4):
                            k.ts(slot(h, 9), O_[h], ss[:, h:h + 1], ALU.mult)
                            ps = psq()
                            k.tr(ps, slot(h, 9), ident)
                            k.stt(ycv[:, 2 + h, ts_], ps, V(o_gnw + l, 1), szv[:, h, ts_], ALU.mult, ALU.mult)

                    def ml_gen():
                        r_ig, r_lf, r_F, r_m, r_rD, r_cD, r_rI, r_em, r_kw, r_mp, r_t2, r_ch = [row(10 + i) for i in range(12)]
                        k.ts(r_ig, graw[2], gbv(l, 2), ALU.add)
                        k.act(r_lf, graw[3], AF.Exp, scale=-1.0, bias=gbv(l, 5))
                        k.act(r_lf, r_lf, AF.Ln, bias=1.0)
                        k.ts(r_lf, r_lf, -1.0, ALU.mult)
                        k.scan(r_F, cmask, r_lf, 0.0, ALU.mult, ALU.add)
                        F3 = r_F.rearrange("p (c l) -> p c l", l=L)
                        m3 = r_m.rearrange("p (c l) -> p c l", l=L)
                        yield
                        if not is_s:
                            k.scan(r_m, r_lf, r_ig, mprev(l), ALU.add, ALU.max)
                            k.cp(r_mp, mprev(l).broadcast_to([4, 128]), eng="dve")
                        else:
                            m0 = r_ch[:, 64:80]
                            k.dma(m0, st_m[l].rearrange("s h -> h s"), allow_slow_non_contiguous=True)
                            k.cp(r_mp.rearrange("p (c l) -> p c l", l=8), m0.rearrange("p (c o) -> p c o", o=1).broadcast_to([4, 16, 8]), eng="dve")
                            ig3 = r_ig.rearrange("p (c l) -> p c l", l=8)
                            lf3 = r_lf.rearrange("p (c l) -> p c l", l=8)
                            t23 = r_t2.rearrange("p (c l) -> p c l", l=8)
                            k.cp(r_t2, r_ig, eng="dve")
                            k.tt(t23[:, :, 0:1], lf3[:, :, 0:1], m0.rearrange("p (c o) -> p c o", o=1), ALU.add)
                            k.tt(t23[:, :, 0:1], t23[:, :, 0:1], ig3[:, :, 0:1], ALU.max)
                            r_lf2 = row(25)
                            k.cp(r_lf2, r_lf, eng="dve")
                            k.memset(r_lf2.rearrange("p (c l) -> p c l", l=8)[:, :, 0:1], NEG)
                            k.scan(r_m, r_lf2, r_t2, 0.0, ALU.add, ALU.max)
                        yield
                        k.tt(r_rD, r_F, r_m, ALU.subtract)
                        k.tt(r_cD, r_ig, r_F, ALU.subtract)
                        k.ts(r_cD, r_cD, math.log(0.125), ALU.add)
                        k.tt(r_rI, r_rD, r_mp, ALU.add)
                        k.ts(r_em, r_m, -1.0, ALU.mult)
                        chA = r_ch[:, 0:nch].rearrange("p (c o) -> p c o", o=1)
                        chB = r_ch[:, 16:16 + nch].rearrange("p (c o) -> p c o", o=1)
                        k.tt(chA, F3[:, :, L - 1:L], m3[:, :, L - 1:L], ALU.subtract)
                        k.tt(chB, chA, r_mp.rearrange("p (c l) -> p c l", l=L)[:, :, 0:1], ALU.add)
                        k.tt(r_kw.rearrange("p (c l) -> p c l", l=L), r_cD.rearrange("p (c l) -> p c l", l=L),
                             chA.broadcast_to([4, nch, L]), ALU.add)
                        yield
                        if not is_s:
                            k.cp(mprev(l), r_m[:, 127:128], eng="dve")
                            if gti == 15:
                                k.dma(p_m[l].rearrange("(p o) -> p o", o=1), r_m[:, 127:128])
                        else:
                            k.dma(s_m[l].rearrange("s h -> h s"), m3[:, :, 7], allow_slow_non_contiguous=True)
                        yield
                        mraw = V(o_cols + 48, 16)
                        mexp = V(o_cols + 64, 16)
                        ps = psq()
                        for ci, rr in enumerate((r_cD, r_em, r_kw, r_rI)):
                            k.mm(ps[:, ci * 4:ci * 4 + 4], rr, ident[0:4, 0:4], start=True, stop=True)
                        k.cp(mraw, ps[:, 0:16], eng="dve")
                        k.act(mexp, ps[:, 0:16], AF.Exp)
                        yield
                        dcb = V(o_misc + 64, 4 * nch2)
                        ps = psq()
                        yield
                        for h in range(4):
                            k.mm(ps[:, h * nch2:h * nch2 + nch2], sel(h), r_ch[:, 16:16 + nch2], start=True, stop=True)
                        k.act(dcb, ps[:, 0:4 * nch2], AF.Exp)
                        nq = (V(o_scan + 48 * 128 - 4 * 65, 260) if is_s else V(o_ml + 706, 260)).rearrange("p (h e) -> p h e", h=4)
                        kwt = [None] * 4
                        POs = [(h % 2) * 64 for h in range(4)]
                        qThs = [mqk[POs[h]:POs[h] + 64, h // 2, ts_] for h in range(4)]
                        kThs = [mqk[POs[h]:POs[h] + 64, 2 + h // 2, ts_] for h in range(4)]

                        def kwt_stage(hcs):
                            for hc in hcs:
                                ps = psq()
                                k.tr(ps, mqk[:, 2 + hc, ts_], ident)
                                for hh in range(2):
                                    h = hc * 2 + hh
                                    kwt[h] = mlK(h)
                                    k.ts(kwt[h], ps[:, hh * 64:(hh + 1) * 64], mexp[:, 8 + h:9 + h], ALU.mult)

                        def stA(hs):
                            for h in hs:
                                ps = psq()
                                k.mm(ps, sel(h), r_rD, start=True, stop=False)
                                k.mm(ps, identb, MAI, start=False, stop=True)
                                k.act(mlW(h), ps, AF.Exp, bias=mraw[:, h:h + 1])

                        def stB(hs):
                            for h in hs:
                                ps2 = psq()
                                k.mm(ps2, kThs[h], qThs[h])
                                k.tt(mlS(h), ps2, mlW(h), ALU.mult)

                        if is_s:
                            yield
                            kwt_stage((0, 1))
                            yield
                            stA(range(4))
                            yield
                            stB(range(4))
                            yield
                        if not is_s:
                            for hp in ((0, 1), (2, 3)):
                                yield
                                kwt_stage((hp[0] // 2,))
                                stA(hp)
                                yield
                                stB(hp)
                                yield
                                for h in hp:
                                    Cc = Cst(l, par, h)
                                    ps1 = psq()
                                    k.mm(ps1[:, 0:65], qThs[h], Cc)
                                    k.act(mlN(h), ps1[:, 0:65], AF.Identity, scale=mexp[:, 12 + h:13 + h])
                                yield
                                for h in hp:
                                    ps3 = psq()
                                    k.mm(ps3[:, 0:65], mlS(h), vext(t)[:, h, :])
                                    k.tt(nq[:, h, :], ps3[:, 0:65], mlN(h), ALU.add)
                                yield
                                for h in hp:
                                    po = POs[h]
                                    Cc, Cn = Cst(l, par, h), Cst(l, 1 - par, h)
                                    ps4 = psq()
                                    k.mm(ps4[po:po + 64, 0:65], kwt[h], vext(t)[:, h, :])
                                    k.stt(Cn, Cc, dcb[po:po + 64, h * nch2:h * nch2 + 1], ps4[po:po + 64, 0:65], ALU.mult, ALU.add)
                                    if gti == 15:
                                        k.dma(p_c[l, h], Cn[:, 0:64])
                                        k.dma(p_n[l, h].rearrange("(p o) -> p o", o=1), Cn[:, 64:65])
                        else:
                            for h in range(4):
                                po = POs[h]
                                qTh = qThs[h]
                                C0v = V(o_S0, 16 * 65, po, po + 64).rearrange("p (s e) -> p s e", s=16)
                                k.dma(C0v[:, :, 0:64], st_c[l, :, h].rearrange("s d e -> d s e"))
                                k.dma(C0v[:, :, 64:65], st_n[l, :, h].rearrange("s (d o) -> d s o", o=1), allow_slow_non_contiguous=True)
                                mskd = V(o_msk, 2176, po, po + 64).rearrange("p (s r) -> p s r", r=136)[:, :, 0:8]
                                mskf = V(o_msk, 2048, po, po + 64).rearrange("p (s i) -> p s i", s=16)
                                k.cp(mskd, qTh.rearrange("p (s r) -> p s r", r=8), eng="pool")
                                ps1 = psq()
                                for s in range(16):
                                    k.mm(ps1[:, 0:65], mskf[:, s, :], C0v[:, s, :], start=(s == 0), stop=(s == 15))
                                k.act(mlN(h), ps1[:, 0:65], AF.Identity, scale=mexp[:, 12 + h:13 + h])
                                ps3 = psq()
                                k.mm(ps3[:, 0:65], mlS(h), vext(t)[:, h, :])
                                k.tt(nq[:, h, :], ps3[:, 0:65], mlN(h), ALU.add)
                                for s in range(16):
                                    kwm = Vb(o_kdm + (s % 3) * 128, 32)
                                    k.ts(kwm, kwt[h], rowmask[:, s:s + 1], ALU.mult)
                                    ps4 = psq()
                                    k.mm(ps4[po:po + 64, 0:65], kwm, vext(t)[:, h, :])
                                    k.stt(C0v[:, s, :], C0v[:, s, :], dcb[po:po + 64, h * 16 + s:h * 16 + s + 1], ps4[po:po + 64, 0:65], ALU.mult, ALU.add)
                                k.dma(s_c[l, :, h].rearrange("s d e -> d s e"), C0v[:, :, 0:64])
                                k.dma(s_n[l, :, h].rearrange("s (d o) -> d s o", o=1), C0v[:, :, 64:65], allow_slow_non_contiguous=True)
                        yield
                        den = V(o_cols + 80, 4)
                        qn = nq[:, :, 64]
                        k.stt(den, qn, -1.0, qn, ALU.mult, ALU.max)
                        k.tt(den, den, mexp[:, 4:8], ALU.max)
                        k.recip(den, den)
                        hbuf = slot(0, 5)[:, 0:128]
                        hb2 = V(o_scan + 5 * 128, 128)
                        hb3 = V(o_scan + 6 * 128, 128)
                        hall = V(o_scan + 5 * 128, 256) if is_s else V(o_ml + 450, 256)
                        stats = V(o_cols + 84, 24)
                        mv = V(o_cols + 108, 8)
                        rstd = V(o_cols + 116, 4)
                        yield
                        for h in range(4):
                            hh_ = hall[:, h * 64:(h + 1) * 64]
                            k.stt(hh_, nq[:, h, 0:64], den[:, h:h + 1], sigo(t)[:, h * 64:(h + 1) * 64], ALU.mult, ALU.mult)
                            k.bn_stats(stats[:, h * 6:(h + 1) * 6], hh_)
                            k.bn_aggr(mv[:, h * 2:(h + 1) * 2], stats[:, h * 6:(h + 1) * 6])
                        yield
                        mv3 = mv.rearrange("p (h two) -> p h two", two=2)
                        k.ts(rstd.rearrange("p (h o) -> p h o", o=1), mv3[:, :, 1:2], LN_EPS, ALU.add)
                        k.act(rstd, rstd, AF.Sqrt)
                        k.recip(rstd, rstd)
                        yield
                        for h in range(4):
                            hh_ = hall[:, h * 64:(h + 1) * 64]
                            k.ts(hh_, hh_, mv[:, 2 * h:2 * h + 1], ALU.subtract, rstd[:, h:h + 1], ALU.mult)
                        k.tt(hall, hall, V(o_mnw + l * 256, 256), ALU.mult, eng="pool")
                        yield
                        for hc in range(2):
                            ps = psq()
                            k.tr(ps, hall[:, hc * 128:(hc + 1) * 128], ident)
                            k.cp(ycv[:, 6 + hc, ts_], ps)
                        if bi == 0 and l == 0 and t == 0:
                            dump("hml", hall)
                    g1, g2 = gdn_gen(), ml_gen()
                    if is_s or os.environ.get('NOILV'):
                        for _ in g1:
                            pass
                        for _ in g2:
                            pass
                    else:
                        a1 = a2 = True
                        while a1 or a2:
                            if a1:
                                try:
                                    next(g1)
                                except StopIteration:
                                    a1 = False
                            if a2:
                                try:
                                    next(g2)
                                except StopIteration:
                                    a2 = False

                if bi == 0 and l == 0:
                    dump("ycat", V(o_yc, 2048))
                if stage < 3:
                    continue
                def ln_resid(t, pss):
                    xt_ = xtok(t)
                    for hf in range(4):
                        k.stt(xt_[:, hf * 256:(hf + 1) * 256], xt_[:, hf * 256:(hf + 1) * 256], ALPHA, pss[hf], ALU.mult, ALU.add)

                def ln_tile(t, pss, g_src, b_src, which, final_out=None, need_T=True):
                    xt_ = xtok(t)
                    if pss is not None:
                        ln_resid(t, pss)
                    stats = V(o_cols + 84, 12)
                    mv = V(o_cols + 108, 2)
                    rs = V(o_cols + 116, 2)
                    k.bn_stats(stats[:, 0:6], xt_[:, 0:512])
                    k.bn_stats(stats[:, 6:12], xt_[:, 512:1024])
                    k.bn_aggr(mv, stats)
                    k.ts(rs[:, 0:1], mv[:, 1:2], LN_EPS, ALU.add)
                    k.act(rs[:, 0:1], rs[:, 0:1], AF.Sqrt)
                    k.recip(rs[:, 0:1], rs[:, 0:1])
                    k.stt(rs[:, 1:2], mv[:, 0:1], -1.0, rs[:, 0:1], ALU.mult, ALU.mult)
                    ntok_ = V(o_scan + (t % 2) * 1024, 1024)
                    k.act(ntok_, xt_, AF.Identity, scale=rs[:, 0:1], bias=rs[:, 1:2])
                    if need_T:
                        transpose_to_xT(ntok_, t, scale_l=l, which=which)
                    gB = V(o_lnb, 1024)
                    bB = V(o_lnb + 1024, 1024)
                    k.tt(xt_, ntok_, gB, ALU.mult, eng="pool")
                    k.tt(xt_, xt_, bB, ALU.add, eng="pool")
                    if final_out is not None:
                        k.dma(final_out, xt_)

                k.dma(V(o_lnb, 1024), ln1_g[l:l + 1, :].partition_broadcast(128))
                k.dma(V(o_lnb + 1024, 1024), ln1_b[l:l + 1, :].partition_broadcast(128))
                need("o", l, 0)
                wos = []
                for g in range(4):
                    wb = wbuf_next()
                    wv = wb[:, 0:2048].rearrange("p (k e) -> p k e", k=8)
                    k.dma(wv, wb_o[l].rearrange("(k p) e -> p k e", p=128)[:, :, g * 256:(g + 1) * 256])
                    wos.append(wv)
                for t in range(nt):
                    ps = psd()
                    pss = []
                    for g in range(4):
                        if g == 2:
                            ps = psd()
                        pp = ps[:, (g % 2) * 256:(g % 2 + 1) * 256]
                        for kk in range(8):
                            k.mm(pp, ycv[:, kk, t * 128:(t + 1) * 128], wos[g][:, kk, :], start=(kk == 0), stop=(kk == 7))
                        pss.append(pp)
                    ln_tile(t, pss, ln1_g, ln1_b, 0)
                if bi == 0 and l == 0:
                    dump("x1", V(o_xtok, 4096))
                if stage < 4:
                    continue
                if is_s:
                    load_state_T(st_ffn[l], 32, 22, o_sst)
                wup = wb_up[l]
                for g in range(11):
                    wvg = load_w(wup, g * 256, 256)
                    wvv = load_w(wup, DFF + g * 256, 256)
                    for uu in range(2):
                        j = 2 * g + uu
                        ps = psd()
                        unit_mm(ps, wvg, uu * 128, 128)
                        acc = conv_unit(ps[:, 0:Tb], 3, cwf(l, j), halo_f(l, j), V(o_sst + j * 32, 32), V(o_sso + j * 32, 32))
                        k.act(acc, acc, AF.Silu)
                        ps2 = psd()
                        unit_mm(ps2, wvv, uu * 128, 128)
                        k.tt(hTv[:, j, :], ps2[:, 0:Tb], acc, ALU.mult)
                if is_s:
                    rows_out(lambda u: V(o_sso + u * 32, 32), 22, 32, s_ffn[l])
                elif bi == 3:
                    rows_out(lambda u: halo_f(l, u), 22, 2, p_ffn[l])
                k.dma(V(o_lnb, 1024), ln2_g[l:l + 1, :].partition_broadcast(128))
                k.dma(V(o_lnb + 1024, 1024), ln2_b[l:l + 1, :].partition_broadcast(128))
                for g in range(11):
                    need("down", l, (g * 256) // 768)
                    wb = wbuf_next()
                    wv = wb[:, 0:2048].rearrange("p (k e) -> p k e", k=2)
                    k.dma(wv, wb_down[l][g * 256:(g + 1) * 256, :].rearrange("(k p) e -> p k e", p=128))
                    for kk in range(2):
                        kc = 2 * g + kk
                        for t in range(nt):
                            for hf in range(2):
                                k.mm(PS[:, (t * 2 + hf) * 512:(t * 2 + hf + 1) * 512], hTv[:, kc, t * 128:(t + 1) * 128],
                                     wv[:, kk, hf * 512:(hf + 1) * 512], start=(kc == 0), stop=(kc == 21))
                for t in range(nt):
                    pss = [PS[:, (t * 2) * 512 + q * 256:(t * 2) * 512 + (q + 1) * 256] for q in range(4)]
                    ln_resid(t, pss)
                for t in range(nt):
                    fo = y[t0 + t * 128:t0 + (t + 1) * 128, :] if last_l else None
                    ln_tile(t, None, ln2_g, ln2_b, 2, final_out=fo, need_T=not last_l)

    S.limit = limit
    print('main loop starts at op', len(S.ops))
    try:
        main_loops()
    except StopIteration:
        print('LIMIT reached at', len(S.ops))
    S.closing = True
    k.fence(outs_all + [wb_in, wb_o, wb_up, wb_down])
    st = S.finalize()
    print(st)
    return nc


def _in_maps(inp, nlayers=2):
    consts = make_consts()
    maps = []
    f = lambda a: np.ascontiguousarray(np.asarray(a, dtype=np.float32))
    for c in range(8):
        sl = slice(16 * c, 16 * c + 16)
        m = {
            "x_in": f(np.concatenate([inp["x_prompt"][c], np.asarray(inp["x_sample"])[sl].reshape(128, D)], 0)),
            "st_conv": f(np.asarray(inp["state_conv_mix"])[:, sl].reshape(2, 32, 256)),
            "st_gconv": f(np.asarray(inp["state_gdn_conv"])[:, sl].reshape(2, 48, 1536)),
            "st_gdn": f(np.asarray(inp["state_gdn"])[:, sl]),
            "st_c": f(np.asarray(inp["state_mlstm_c"])[:, sl]),
            "st_n": f(np.asarray(inp["state_mlstm_n"])[:, sl]),
            "st_m": f(np.asarray(inp["state_mlstm_m"])[:, sl]),
            "st_ffn": f(np.asarray(inp["state_ffn_conv"])[:, sl].reshape(2, 32, DFF)),
            "consts": consts,
        }
        for nm in ("w_in", "conv_w", "gdn_conv_w", "gdn_a_log", "gdn_dt_bias", "gdn_norm_w", "ml_i_bias",
                   "ml_f_bias", "ml_norm_w", "w_o", "ln1_g", "ln1_b", "w_up", "ffn_conv_w", "w_down",
                   "ln2_g", "ln2_b"):
            m[nm] = f(inp[nm])
        maps.append(m)
    return maps


def kernel(**inp):
    nc = build()
    maps = _in_maps(inp)
    res = run_bass_kernel_spmd(nc, maps, core_ids=list(range(8)))
    R = res.results
    yp = np.stack([R[c]["y"][0:2048] for c in range(8)], 0)
    ys = np.concatenate([R[c]["y"][2048:].reshape(16, 8, D) for c in range(8)], 0)

    def pst(nm, shp):
        return np.stack([R[c][nm].reshape(shp) for c in range(8)], 1)

    def sst(nm, shp):
        return np.concatenate([R[c][nm].reshape(shp) for c in range(8)], 1)

    outs = (yp, ys,
            pst("p_conv", (2, 2, 256)), pst("p_gconv", (2, 3, 1536)), pst("p_gdn", (2, 4, 128, 128)),
            pst("p_c", (2, 4, 64, 64)), pst("p_n", (2, 4, 64)), pst("p_m", (2, 4)), pst("p_ffn", (2, 2, DFF)),
            sst("s_conv", (2, 16, 2, 256)), sst("s_gconv", (2, 16, 3, 1536)), sst("s_gdn", (2, 16, 4, 128, 128)),
            sst("s_c", (2, 16, 4, 64, 64)), sst("s_n", (2, 16, 4, 64)), sst("s_m", (2, 16, 4)),
            sst("s_ffn", (2, 16, 2, DFF)))
    return tuple(np.ascontiguousarray(o.astype(np.float32)) for o in outs)
```
